# Optimizing a Trainium2 kernel written in Bass

```python
import math
import jax, jax.numpy as jnp
from jax import lax
import numpy as np

D_MODEL = 1024
BATCH = 2
SEQ = 8192
DEPTH = 1

RWKV_HEAD_DIM = 64
RWKV_DIM = D_MODEL // 2
RWKV_HEADS = RWKV_DIM // RWKV_HEAD_DIM
DECAY_LORA = 64
AAA_LORA = 64
GN_EPS = 64e-5

NSA_HEAD_DIM = 64
NSA_DIM = D_MODEL // 2
NSA_Q_HEADS = NSA_DIM // NSA_HEAD_DIM
NSA_KV_GROUPS = 2
NSA_GROUP_SIZE = NSA_Q_HEADS // NSA_KV_GROUPS
NSA_KV_DIM = NSA_KV_GROUPS * NSA_HEAD_DIM
CMP_BLOCK = 32
CMP_STRIDE = 16
CMP_HIDDEN = 256
SEL_BLOCK = 64
N_SELECT = 16
WINDOW = 512
Q_BLOCK = 128

ROPE_THETA = 500000.0
ROPE_DIM = NSA_HEAD_DIM // 4

MEM_LEN = 256
X_HEADS = 4
X_HEAD_DIM = D_MODEL // X_HEADS

D_FF = 2816
CONV_WIDTH = 3

LN_EPS = 1e-5
ALPHA = (2 * DEPTH) ** 0.25
BETA = (8 * DEPTH) ** -0.25
NEG = -1e30
BIG = 1e30

RWKV_COLS = 3 * RWKV_DIM + DECAY_LORA + AAA_LORA
NSA_COLS = NSA_DIM + 6 * NSA_KV_DIM + 3 * NSA_Q_HEADS
GATE_COLS = 2 * D_MODEL
N_IN = RWKV_COLS + NSA_COLS + GATE_COLS

kernel_name = 'hybrid_rwkv7_nsa_deepnorm_layer'


def layer_norm(x, g, b):
    xf = x.astype(jnp.float32)
    mu = jnp.mean(xf, axis=-1, keepdims=True)
    var = jnp.mean(jnp.square(xf - mu), axis=-1, keepdims=True)
    return ((xf - mu) * lax.rsqrt(var + LN_EPS) * g + b).astype(x.dtype)


def partial_rope(x, pos):
    half = ROPE_DIM // 2
    inv = ROPE_THETA ** (-jnp.arange(half, dtype=jnp.float32) * 2.0 / ROPE_DIM)
    ang = pos.astype(jnp.float32)[:, None] * inv[None, :]
    cos = jnp.cos(ang)[None, :, None, :]
    sin = jnp.sin(ang)[None, :, None, :]
    xr = x[..., :ROPE_DIM].astype(jnp.float32)
    x1, x2 = xr[..., :half], xr[..., half:]
    rot = jnp.concatenate([x1 * cos - x2 * sin, x2 * cos + x1 * sin], axis=-1).astype(x.dtype)
    return jnp.concatenate([rot, x[..., ROPE_DIM:]], axis=-1)


def token_shift(p):
    return jnp.pad(p[:, :-1], ((0, 0), (1, 0), (0, 0)))


def rwkv7_time_mix(p, mu, w0, w2, a0, a2, k_k, k_a, r_k, gn_g, gn_b):
    B, S, _ = p.shape
    H, N, C = RWKV_HEADS, RWKV_HEAD_DIM, RWKV_DIM
    p = p + (token_shift(p) - p) * mu
    r, k, v = p[..., :C], p[..., C:2 * C], p[..., 2 * C:3 * C]
    wl = p[..., 3 * C:3 * C + DECAY_LORA]
    al = p[..., 3 * C + DECAY_LORA:]
    w_log = -jax.nn.softplus(-(w0 + jnp.tanh(wl) @ w2)) - 0.5
    decay = jnp.exp(-jnp.exp(w_log.astype(jnp.float32)))
    a = jax.nn.sigmoid(a0 + al @ a2).astype(jnp.float32)
    heads = lambda t: t.astype(jnp.float32).reshape(B, S, H, N)
    kk = heads(k * k_k)
    kk = kk / jnp.maximum(jnp.sqrt(jnp.sum(kk * kk, axis=-1, keepdims=True)), 1e-12)
    k = k.astype(jnp.float32) * (1.0 + (a - 1.0) * k_a)
    r4, k4, v4, a4, w4 = heads(r), heads(k), heads(v), heads(a), heads(decay)
    erase = -kk
    refill = kk * a4

    def step(state, inp):
        r_t, w_t, k_t, v_t, a_t, b_t = inp
        sa = jnp.einsum('bhvk,bhk->bhv', state, a_t)
        state = (state * w_t[:, :, None, :] + sa[..., None] * b_t[:, :, None, :]
                 + v_t[..., None] * k_t[:, :, None, :])
        return state, jnp.einsum('bhvk,bhk->bhv', state, r_t)

    tm = lambda t: jnp.moveaxis(t, 1, 0)
    state0 = jnp.zeros((B, H, N, N), jnp.float32)
    _, y = lax.scan(step, state0, (tm(r4), tm(w4), tm(k4), tm(v4), tm(erase), tm(refill)))
    y = jnp.moveaxis(y, 0, 1)
    ym = jnp.mean(y, axis=-1, keepdims=True)
    yv = jnp.mean(jnp.square(y - ym), axis=-1, keepdims=True)
    y = ((y - ym) * lax.rsqrt(yv + GN_EPS)).reshape(B, S, C) * gn_g + gn_b
    bonus = jnp.sum(r4 * k4 * r_k, axis=-1, keepdims=True) * v4
    return (y + bonus.reshape(B, S, C)).astype(p.dtype)


def nsa_attention(p, pos, pe_k, pe_v, ck_w1, ck_w2, cv_w1, cv_w2):
    B, S, _ = p.shape
    G, R, dh, Hq = NSA_KV_GROUPS, NSA_GROUP_SIZE, NSA_HEAD_DIM, NSA_Q_HEADS
    q = p[..., :NSA_DIM].reshape(B, S, Hq, dh)
    kv = p[..., NSA_DIM:NSA_DIM + 6 * NSA_KV_DIM].reshape(B, S, 6, G, dh)
    k_c, v_c, k_s, v_s, k_w, v_w = [kv[:, :, i] for i in range(6)]
    gates = jax.nn.sigmoid(p[..., NSA_DIM + 6 * NSA_KV_DIM:].reshape(B, S, Hq, 3))
    q = partial_rope(q, pos)
    k_c, k_s, k_w = partial_rope(k_c, pos), partial_rope(k_s, pos), partial_rope(k_w, pos)
    q = (q * dh ** -0.5).reshape(B, S, G, R, dh).transpose(0, 2, 3, 1, 4)

    n_cmp = (S - CMP_BLOCK) // CMP_STRIDE + 1
    cidx = jnp.arange(n_cmp)[:, None] * CMP_STRIDE + jnp.arange(CMP_BLOCK)[None, :]
    cmp_start, cmp_end = cidx[:, 0], cidx[:, -1]

    def compress(t, pe, w1, w2):
        blk = t[:, cidx] + pe[None, None, :, None, :]
        blk = blk.transpose(0, 3, 1, 2, 4).reshape(B, G, n_cmp, CMP_BLOCK * dh)
        return jax.nn.gelu(blk @ w1) @ w2

    kc = compress(k_c, pe_k, ck_w1, ck_w2)
    vc = compress(v_c, pe_v, cv_w1, cv_w2)

    n_sb = S // SEL_BLOCK
    n_sel = min(N_SELECT, n_sb)
    sb_idx = jnp.arange(n_sb)
    sel_start = sb_idx * SEL_BLOCK
    overlap = ((cmp_start[:, None] <= sel_start[None, :] + SEL_BLOCK - 1)
               & (cmp_end[:, None] >= sel_start[None, :])).astype(jnp.float32)
    ks_blocks = k_s.transpose(0, 2, 1, 3).reshape(B, G, n_sb, SEL_BLOCK, dh)
    vs_blocks = v_s.transpose(0, 2, 1, 3).reshape(B, G, n_sb, SEL_BLOCK, dh)
    gather_blocks = jax.vmap(jax.vmap(lambda blocks, ix: blocks[ix]))

    kw_pad = jnp.pad(k_w.transpose(0, 2, 1, 3), ((0, 0), (0, 0), (WINDOW, 0), (0, 0)))
    vw_pad = jnp.pad(v_w.transpose(0, 2, 1, 3), ((0, 0), (0, 0), (WINDOW, 0), (0, 0)))

    def query_block(qb):
        t0 = qb * Q_BLOCK
        qblk = lax.dynamic_slice_in_dim(q, t0, Q_BLOCK, axis=3)
        tpos = t0 + jnp.arange(Q_BLOCK)
        s_c = jnp.einsum('bgrqd,bgcd->bgrqc', qblk, kc).astype(jnp.float32)
        m_c = cmp_end[None, :] <= tpos[:, None]
        p_c = jax.nn.softmax(jnp.where(m_c, s_c, NEG), axis=-1) * jnp.any(m_c, axis=-1)[:, None]
        o_c = jnp.einsum('bgrqc,bgcd->bgrqd', p_c.astype(vc.dtype), vc)
        imp = jnp.einsum('bgrqc,cj->bgqj', p_c, overlap)
        cur = tpos // SEL_BLOCK
        valid = sel_start[None, :] <= tpos[:, None]
        forced = ((sb_idx[None, :] == 0) | (sb_idx[None, :] == cur[:, None])
                  | (sb_idx[None, :] == cur[:, None] - 1))
        imp = jnp.where(valid, jnp.where(forced, BIG, imp), NEG)
        _, idx = lax.top_k(imp, n_sel)
        ks = gather_blocks(ks_blocks, idx)
        vs = gather_blocks(vs_blocks, idx)
        s_s = jnp.einsum('bgrqd,bgqnkd->bgrqnk', qblk, ks).astype(jnp.float32)
        kpos = idx[..., None] * SEL_BLOCK + jnp.arange(SEL_BLOCK)
        m_s = (kpos <= tpos[None, None, :, None, None])[:, :, None]
        s_s = jnp.where(m_s, s_s, NEG).reshape(B, G, R, Q_BLOCK, n_sel * SEL_BLOCK)
        p_s = jax.nn.softmax(s_s, axis=-1).reshape(B, G, R, Q_BLOCK, n_sel, SEL_BLOCK)
        o_s = jnp.einsum('bgrqnk,bgqnkd->bgrqd', p_s.astype(vs.dtype), vs)
        kw = lax.dynamic_slice_in_dim(kw_pad, t0, Q_BLOCK + WINDOW, axis=2)
        vw = lax.dynamic_slice_in_dim(vw_pad, t0, Q_BLOCK + WINDOW, axis=2)
        kwpos = t0 - WINDOW + jnp.arange(Q_BLOCK + WINDOW)
        diff = tpos[:, None] - kwpos[None, :]
        m_w = (kwpos[None, :] >= 0) & (diff >= 0) & (diff < WINDOW)
        s_w = jnp.einsum('bgrqd,bgkd->bgrqk', qblk, kw).astype(jnp.float32)
        p_w = jax.nn.softmax(jnp.where(m_w, s_w, NEG), axis=-1)
        o_w = jnp.einsum('bgrqk,bgkd->bgrqd', p_w.astype(vw.dtype), vw)
        return jnp.stack([o_c, o_s, o_w], axis=-1)

    o = lax.map(query_block, jnp.arange(S // Q_BLOCK))
    o = o.transpose(1, 0, 4, 2, 3, 5, 6).reshape(B, S, Hq, dh, 3)
    return jnp.einsum('bshdc,bshc->bshd', o, gates.astype(o.dtype)).reshape(B, S, NSA_DIM)


def memory_cross_attention(x, mem, wq, wk, wv, wo):
    B, S, _ = x.shape
    M = mem.shape[1]
    q = (x @ wq).reshape(B, S, X_HEADS, X_HEAD_DIM)
    k = (mem @ wk).reshape(B, M, X_HEADS, X_HEAD_DIM)
    v = (mem @ wv).reshape(B, M, X_HEADS, X_HEAD_DIM)
    s = jnp.einsum('bshd,bmhd->bhsm', q, k).astype(jnp.float32) * X_HEAD_DIM ** -0.5
    pr = jax.nn.softmax(s, axis=-1)
    o = jnp.einsum('bhsm,bmhd->bshd', pr.astype(v.dtype), v).reshape(B, S, D_MODEL)
    return o @ wo


def conv_ffn(x, w_up, conv_w, conv_b, w_down):
    h = x @ w_up
    h = lax.conv_general_dilated(h, conv_w[:, None, :], window_strides=(1,),
                                 padding=[(CONV_WIDTH - 1, 0)],
                                 dimension_numbers=('NWC', 'WIO', 'NWC'),
                                 feature_group_count=h.shape[-1]) + conv_b
    gate, val = h[..., :D_FF], h[..., D_FF:]
    return (jax.nn.silu(gate) * val) @ w_down


def setup_inputs(seed: int = 0) -> dict:
    key = jax.random.key(seed)
    ks = jax.random.split(key, 40)
    nrm = lambda k, shape, s: jax.random.normal(k, shape, jnp.float32) * s
    L = DEPTH
    return {
        'x': nrm(ks[0], (BATCH, SEQ, D_MODEL), 1.0),
        'mem': nrm(ks[1], (BATCH, MEM_LEN, D_MODEL), 1.0),
        'w_in': nrm(ks[2], (L, D_MODEL, N_IN), D_MODEL ** -0.5),
        'rwkv_mu': jax.random.uniform(ks[3], (L, RWKV_COLS), jnp.float32),
        'rwkv_w0': jax.random.uniform(ks[4], (L, RWKV_DIM), jnp.float32, minval=-6.0, maxval=0.0),
        'rwkv_w2': nrm(ks[5], (L, DECAY_LORA, RWKV_DIM), DECAY_LORA ** -0.5),
        'rwkv_a0': nrm(ks[6], (L, RWKV_DIM), 0.1),
        'rwkv_a2': nrm(ks[7], (L, AAA_LORA, RWKV_DIM), AAA_LORA ** -0.5),
        'rwkv_k_k': 1.0 + nrm(ks[8], (L, RWKV_DIM), 0.1),
        'rwkv_k_a': 1.0 + nrm(ks[9], (L, RWKV_DIM), 0.1),
        'rwkv_r_k': nrm(ks[10], (L, RWKV_HEADS, RWKV_HEAD_DIM), 0.1),
        'rwkv_gn_g': 1.0 + nrm(ks[11], (L, RWKV_DIM), 0.01),
        'rwkv_gn_b': nrm(ks[12], (L, RWKV_DIM), 0.01),
        'nsa_pe_k': nrm(ks[13], (L, CMP_BLOCK, NSA_HEAD_DIM), 0.1),
        'nsa_pe_v': nrm(ks[14], (L, CMP_BLOCK, NSA_HEAD_DIM), 0.1),
        'nsa_ck_w1': nrm(ks[15], (L, CMP_BLOCK * NSA_HEAD_DIM, CMP_HIDDEN), (CMP_BLOCK * NSA_HEAD_DIM) ** -0.5),
        'nsa_ck_w2': nrm(ks[16], (L, CMP_HIDDEN, NSA_HEAD_DIM), CMP_HIDDEN ** -0.5),
        'nsa_cv_w1': nrm(ks[17], (L, CMP_BLOCK * NSA_HEAD_DIM, CMP_HIDDEN), (CMP_BLOCK * NSA_HEAD_DIM) ** -0.5),
        'nsa_cv_w2': nrm(ks[18], (L, CMP_HIDDEN, NSA_HEAD_DIM), CMP_HIDDEN ** -0.5),
        'merge_p_a': nrm(ks[19], (L, RWKV_DIM, D_MODEL), RWKV_DIM ** -0.5),
        'merge_p_b': nrm(ks[20], (L, NSA_DIM, D_MODEL), NSA_DIM ** -0.5),
        'mix_w_o': nrm(ks[21], (L, D_MODEL, D_MODEL), BETA * D_MODEL ** -0.5),
        'ln1_g': 1.0 + nrm(ks[22], (L, D_MODEL), 0.01),
        'ln1_b': nrm(ks[23], (L, D_MODEL), 0.01),
        'xa_wq': nrm(ks[24], (L, D_MODEL, D_MODEL), D_MODEL ** -0.5),
        'xa_wk': nrm(ks[25], (L, D_MODEL, D_MODEL), D_MODEL ** -0.5),
        'xa_wv': nrm(ks[26], (L, D_MODEL, D_MODEL), D_MODEL ** -0.5),
        'xa_wo': nrm(ks[27], (L, D_MODEL, D_MODEL), BETA * D_MODEL ** -0.5),
        'ln2_g': 1.0 + nrm(ks[28], (L, D_MODEL), 0.01),
        'ln2_b': nrm(ks[29], (L, D_MODEL), 0.01),
        'ffn_w_up': nrm(ks[30], (L, D_MODEL, 2 * D_FF), D_MODEL ** -0.5),
        'ffn_conv_w': nrm(ks[31], (L, CONV_WIDTH, 2 * D_FF), CONV_WIDTH ** -0.5),
        'ffn_conv_b': nrm(ks[32], (L, 2 * D_FF), 0.01),
        'ffn_w_down': nrm(ks[33], (L, D_FF, D_MODEL), BETA * D_FF ** -0.5),
        'ln3_g': 1.0 + nrm(ks[34], (L, D_MODEL), 0.01),
        'ln3_b': nrm(ks[35], (L, D_MODEL), 0.01),
    }


def reference(x, mem, w_in, rwkv_mu, rwkv_w0, rwkv_w2, rwkv_a0, rwkv_a2, rwkv_k_k, rwkv_k_a,
              rwkv_r_k, rwkv_gn_g, rwkv_gn_b, nsa_pe_k, nsa_pe_v, nsa_ck_w1, nsa_ck_w2,
              nsa_cv_w1, nsa_cv_w2, merge_p_a, merge_p_b, mix_w_o, ln1_g, ln1_b,
              xa_wq, xa_wk, xa_wv, xa_wo, ln2_g, ln2_b, ffn_w_up, ffn_conv_w, ffn_conv_b,
              ffn_w_down, ln3_g, ln3_b):
    S = x.shape[1]
    pos = jnp.arange(S)
    for l in range(DEPTH):
        proj = x @ w_in[l]
        p_rwkv = proj[..., :RWKV_COLS]
        p_nsa = proj[..., RWKV_COLS:RWKV_COLS + NSA_COLS]
        gates = jax.nn.sigmoid(proj[..., RWKV_COLS + NSA_COLS:])
        g_a, g_b = gates[..., :D_MODEL], gates[..., D_MODEL:]
        y_a = rwkv7_time_mix(p_rwkv, rwkv_mu[l], rwkv_w0[l], rwkv_w2[l], rwkv_a0[l], rwkv_a2[l],
                             rwkv_k_k[l], rwkv_k_a[l], rwkv_r_k[l], rwkv_gn_g[l], rwkv_gn_b[l])
        y_b = nsa_attention(p_nsa, pos, nsa_pe_k[l], nsa_pe_v[l], nsa_ck_w1[l], nsa_ck_w2[l],
                            nsa_cv_w1[l], nsa_cv_w2[l])
        mixed = (g_a * (y_a @ merge_p_a[l]) + g_b * (y_b @ merge_p_b[l])) @ mix_w_o[l]
        x = layer_norm(ALPHA * x + mixed, ln1_g[l], ln1_b[l])
        xa = memory_cross_attention(x, mem, xa_wq[l], xa_wk[l], xa_wv[l], xa_wo[l])
        x = layer_norm(ALPHA * x + xa, ln2_g[l], ln2_b[l])
        f = conv_ffn(x, ffn_w_up[l], ffn_conv_w[l], ffn_conv_b[l], ffn_w_down[l])
        x = layer_norm(ALPHA * x + f, ln3_g[l], ln3_b[l])
    return x
```

```python
import numpy as np
import concourse.bass as bass
import concourse.mybir as mybir
from concourse.bass_utils import run_bass_kernel_spmd

F32 = mybir.dt.float32
BF16 = mybir.dt.bfloat16
AF = mybir.ActivationFunctionType
ALU = mybir.AluOpType
AX = mybir.AxisListType

SAME_ENG_SYNC = False
NSLOT = 8


def _region(ap):
    t = ap.tensor
    shp = tuple(t.shape)
    sp = str(ap.space)
    off = int(ap.offset)
    pat = [(int(s), int(c)) for s, c in ap.ap]
    if 'DRAM' in sp.upper() or 'HBM' in sp.upper():
        ext = sum((c - 1) * abs(s) for s, c in pat)
        return (ap.name, 0, 0, off, off + ext)
    if 'PSUM' in sp.upper():
        return (ap.name, 0, 127, 0, 10 ** 9)
    fs = 1
    for d in shp[1:]:
        fs *= int(d)
    p0 = off // fs
    f0 = off % fs
    ps, pc = pat[0]
    p1 = p0 + (pc - 1) * (ps // fs if fs else 0)
    ext = sum((c - 1) * abs(s) for s, c in pat[1:])
    return (ap.name, p0, p1, f0, f0 + ext)


def _overlap(a, b):
    return not (a[2] < b[1] or b[2] < a[1] or a[4] < b[3] or b[4] < a[3])


def _contains(a, b):
    return a[1] <= b[1] and a[2] >= b[2] and a[3] <= b[3] and a[4] >= b[4]


class Prog:
    ENGS = ['pe', 'act', 'dve', 'pool', 'sp']

    def __init__(self, nc):
        self.nc = nc
        self.stream = {e: [] for e in self.ENGS}
        self.count = {e: 0 for e in self.ENGS}
        self.dcount = {'sp': 0, 'pool': 0, 'act': 0}
        self.hist = {}
        self.known = {e: {} for e in self.ENGS}
        self.nops = 0

    def _deps(self, eng, reads, writes, is_dma=False):
        deps = {}

        def add(tok):
            k, v, e = tok
            if e == eng and eng == 'pe' and not k.startswith('d_') and not is_dma:
                return
            if deps.get(k, (0,))[0] < v:
                deps[k] = (v, e)
        rr = [_region(a) for a in reads]
        wr = [_region(a) for a in writes]
        for r in rr:
            psum = (r[4] == 10 ** 9)
            for (reg, tok, isw) in self.hist.get(r[0], ()):
                if isw and _overlap(reg, r):
                    add(tok)
                elif psum and not isw and tok[2] != eng:
                    add(tok)
        for w in wr:
            for (reg, tok, isw) in self.hist.get(w[0], ()):
                if _overlap(reg, w):
                    add(tok)
        return deps, rr, wr

    def _update(self, tok, rr, wr):
        for w in wr:
            h = self.hist.setdefault(w[0], [])
            h[:] = [x for x in h if not _contains(w, x[0])]
            h.append((w, tok, True))
        for r in rr:
            h = self.hist.setdefault(r[0], [])
            h[:] = [x for x in h if not (not x[2] and x[1][2] == tok[2] and x[1][0] == tok[0] and x[0] == r)]
            h.append((r, tok, False))

    def _waits(self, eng, deps):
        waits = []
        kn = self.known[eng]
        for k, (v, e) in deps.items():
            if kn.get(k, 0) < v:
                kn[k] = v
                waits.append((k, v))
        return waits

    def op(self, eng, fn, reads, writes):
        deps, rr, wr = self._deps(eng, reads, writes)
        waits = self._waits(eng, deps)
        self.count[eng] += 1
        tok = ('c_' + eng, self.count[eng], eng)
        self.stream[eng].append((waits, fn, tok, 1))
        self._update(tok, rr, wr)
        self.nops += 1
        return tok

    def dma(self, out, in_, q='sp', **kw):
        eng = q
        deps, rr, wr = self._deps(eng, [in_], [out], is_dma=True)
        i = self.dcount[q]
        self.dcount[q] += 1
        slot = i % NSLOT
        val = 16 * (i // NSLOT + 1)
        key = 'd_%s_%d' % (q, slot)
        if val > 16:
            if deps.get(key, (0,))[0] < val - 16:
                deps[key] = (val - 16, q)
        waits = self._waits(eng, deps)
        tok = (key, val, eng)

        def fn(e, out=out, in_=in_, kw=kw):
            return e.dma_start(out=out, in_=in_, **kw)
        self.stream[eng].append((waits, fn, tok, 16))
        self._update(tok, rr, wr)
        self.nops += 1
        return tok

    def mm(self, out, lhsT, rhs, start=True, stop=True, **kw):
        return self.op('pe', lambda e: e.matmul(out, lhsT, rhs, start=start, stop=stop, **kw),
                       [lhsT, rhs] + ([] if start else [out]), [out])

    def tr(self, out, in_, ident):
        return self.op('pe', lambda e: e.transpose(out, in_, ident), [in_, ident], [out])

    def actf(self, out, in_, func, bias=None, scale=None, accum_out=None, eng='act'):
        kw = {}
        rd = [in_]
        wr = [out]
        if bias is not None:
            kw['bias'] = bias
            if not isinstance(bias, (int, float)):
                rd.append(bias)
        if scale is not None:
            kw['scale'] = scale
            if not isinstance(scale, (int, float)):
                rd.append(scale)
        if accum_out is not None:
            kw['accum_out'] = accum_out
            wr.append(accum_out)
        return self.op(eng, lambda e: e.activation(out, in_, func, **kw), rd, wr)

    def tt(self, out, in0, in1, op, eng='dve'):
        return self.op(eng, lambda e: e.tensor_tensor(out, in0, in1, op), [in0, in1], [out])

    def ts(self, out, in0, s1, s2, op0, op1=None, accum_out=None, eng='dve'):
        rd = [in0] + [s for s in (s1, s2) if s is not None and not isinstance(s, (int, float))]
        wr = [out] + ([accum_out] if accum_out is not None else [])
        kw = {}
        if op1 is not None:
            kw['op1'] = op1
        if accum_out is not None:
            kw['accum_out'] = accum_out
        return self.op(eng, lambda e: e.tensor_scalar(out, in0, s1, s2, op0, **kw), rd, wr)

    def stt(self, out, in0, scalar, in1, op0, op1, eng='dve'):
        rd = [in0, in1] + ([scalar] if not isinstance(scalar, (int, float)) else [])
        return self.op(eng, lambda e: e.scalar_tensor_tensor(out, in0, scalar, in1, op0, op1), rd, [out])

    def copy(self, out, in_, eng='dve'):
        if eng == 'act':
            return self.op('act', lambda e: e.copy(out, in_), [in_], [out])
        return self.op(eng, lambda e: e.tensor_copy(out, in_), [in_], [out])

    def memset(self, ap, v, eng='dve'):
        return self.op(eng, lambda e: e.memset(ap, v), [], [ap])

    def emit(self):
        nc = self.nc
        import contextlib
        es = contextlib.ExitStack()
        sems = {}

        def sem(k):
            if k not in sems:
                sems[k] = es.enter_context(nc.semaphore(k))
            return sems[k]
        for e in self.ENGS:
            sem('c_' + e)
        for q in self.dcount:
            for s in range(NSLOT):
                sem('d_%s_%d' % (q, s))
        final = []
        for e in self.ENGS:
            if e != 'sp' and self.count[e] > 0:
                final.append(('c_' + e, self.count[e]))
        for q, n in self.dcount.items():
            for s in range(NSLOT):
                cnt = (n - s + NSLOT - 1) // NSLOT if n > s else 0
                if cnt > 0:
                    final.append(('d_%s_%d' % (q, s), 16 * cnt))
        streams = self.stream

        def run(engname, e):
            for (waits, fn, tok, inc) in streams[engname]:
                for (k, v) in waits:
                    e.wait_ge(sem(k), v)
                ins = fn(e)
                ins.then_inc(sem(tok[0]), inc)
            if engname == 'sp':
                for (k, v) in final:
                    e.wait_ge(sem(k), v)
        with nc.Block() as block:
            @block.tensor
            def _(e):
                run('pe', e)

            @block.scalar
            def _(e):
                run('act', e)

            @block.vector
            def _(e):
                run('dve', e)

            @block.gpsimd
            def _(e):
                run('pool', e)

            @block.sync
            def _(e):
                run('sp', e)
        es.close()


D = 1024
ALPHA = 2.0 ** 0.25
LN_EPS = 1e-5
DFF = 2816
NT = 2050
NJ = DFF // 128


class Pools:
    def __init__(self, P, nc):
        self.P = P
        self.nc = nc
        self.banks = [nc.alloc_psum_tensor("bank%d" % i, [128, 512], F32) for i in range(8)]
        self.bi = 0
        self.stg = [nc.alloc_sbuf_tensor("stg%d" % i, [128, 2048], F32) for i in range(2)]
        self.si = 0
        self.ci = 0

    def bank(self):
        b = self.banks[self.bi % 8]
        self.bi += 1
        return b

    def nrot(self):
        self.bi += 1
        return self.bi

    def load_cast(self, dst, src, q='sp'):
        P = self.P
        shp = dst.shape
        npart = shp[0]
        if len(shp) == 2:
            n = shp[1]
            step = 2048
            for c0 in range(0, n, step):
                c1 = min(n, c0 + step)
                st = self.stg[self.si % 2]
                self.si += 1
                P.dma(st[0:npart, 0:c1 - c0], src[:, c0:c1], q=q)
                self._cast(dst[:, c0:c1], st[0:npart, 0:c1 - c0])
        else:
            a, n = shp[1], shp[2]
            assert n <= 2048
            per = max(1, 2048 // n)
            for a0 in range(0, a, per):
                a1 = min(a, a0 + per)
                st = self.stg[self.si % 2]
                self.si += 1
                sv = st[0:npart, 0:(a1 - a0) * n].rearrange("p (a n) -> p a n", n=n)
                P.dma(sv, src[:, a0:a1, :], q=q)
                self._cast(dst[:, a0:a1, :], sv)

    def _cast(self, dst, src):
        engs = ['pool', 'dve', 'act', 'pool']
        e = engs[self.ci % len(engs)]
        self.ci += 1
        self.P.copy(dst, src, eng=e)


def chan_ln(P, pl, res, resb, gcol, bcol, N, ones_b, scr_b, scr_sq, tmp):
    pm = pl.bank()
    psq = pl.bank()
    for c in range(8):
        P.actf(scr_b[:, c, 0:N], res[:, c, 0:N], AF.Copy)
        P.actf(scr_sq[:, c, 0:N], res[:, c, 0:N], AF.Square)
    for c in range(8):
        P.mm(pm[:, 0:N], ones_b[:], scr_b[:, c, 0:N], start=(c == 0), stop=(c == 7))
    for c in range(8):
        P.mm(psq[:, 0:N], ones_b[:], scr_sq[:, c, 0:N], start=(c == 0), stop=(c == 7))
    mean, msq, var, rstd = [t[:, 0:N] for t in tmp[:4]]
    P.actf(mean, pm[:, 0:N], AF.Copy, scale=1.0 / D)
    P.tt(msq, mean, mean, ALU.mult)
    P.stt(var, psq[:, 0:N], 1.0 / D, msq, ALU.mult, ALU.subtract)
    P.ts(var, var, LN_EPS, None, ALU.add)
    P.actf(var, var, AF.Sqrt)
    P.op('dve', lambda e: e.reciprocal(rstd, var), [var], [rstd])
    for c in range(8):
        P.tt(res[:, c, 0:N], res[:, c, 0:N], mean, ALU.subtract)
        P.tt(res[:, c, 0:N], res[:, c, 0:N], rstd, ALU.mult)
        P.actf(res[:, c, 0:N], res[:, c, 0:N], AF.Identity, bias=bcol[:, c:c + 1], scale=gcol[:, c:c + 1])
        P.copy(resb[:, c, 0:N], res[:, c, 0:N], eng='pool')


class _Stop(Exception):
    pass


def build_tail(stop=99):
    nc = bass.Bass("TRN2", target_bir_lowering=False)
    try:
        return _build_tail(nc, stop)
    except _Stop:
        return nc


def _build_tail(nc, stop):
    def ck(code):
        if stop == code:
            P.emit()
            raise _Stop()
    dr = lambda n, s, k="ExternalInput": nc.dram_tensor(n, s, F32, kind=k).ap()
    xT = dr("xT", [D, NT]); yaT = dr("yaT", [512, NT]); ybT = dr("ybT", [512, NT]); memT = dr("memT", [D, 256])
    wg = dr("wg", [D, 2048]); pa = dr("pa", [512, D]); pb = dr("pb", [512, D]); wo = dr("wo", [D, D])
    wq = dr("wq", [D, D]); wk = dr("wk", [D, D]); wv = dr("wv", [D, D]); xwo = dr("xwo", [D, D])
    wup = dr("wup", [D, 2 * DFF]); wdn = dr("wdn", [DFF, D])
    lnp = dr("lnp", [128, 48]); cw = dr("cw", [128, 44 * 3]); cb = dr("cb", [128, 44]); hmask = dr("hmask", [128, 1])
    outT = dr("outT", [D, 2048], "ExternalOutput")
    x2s = dr("x2s", [D, NT], "Internal")
    chunked = lambda ap: ap.rearrange("(c p) n -> p c n", p=128)

    P = Prog(nc)
    pl = Pools(P, nc)
    sb = lambda n, s, dt=F32: nc.alloc_sbuf_tensor(n, s, dt)
    NA = 256
    arena = sb("arena", [128, 49152], BF16)
    lnp_t = sb("lnp_t", [128, 48]); cw_t = sb("cw_t", [128, 132]); cb_t = sb("cb_t", [128, 44]); hm_t = sb("hm_t", [128, 1])
    ones_b = sb("ones_b", [128, 128], BF16)
    res = sb("res", [128, 8, 258]); resb = sb("resb", [128, 8, 258], BF16)
    yab = sb("yab", [128, 4, NA], BF16); ybb = sb("ybb", [128, 4, NA], BF16)
    actb2 = sb("actb2", [128, 8, NA], BF16); qT = sb("qT", [128, 8, NA], BF16)
    scr_b = sb("scr_b", [128, 8, 258], BF16); scr_sq = sb("scr_sq", [128, 8, 258], BF16)
    tmp = [sb("tmp%d" % i, [128, 258]) for i in range(8)]
    KT = sb("KT", [128, 8, 256], BF16); V = sb("V", [128, 2, D], BF16); memb = sb("memb", [128, 8, 256], BF16)
    pbuf = [sb("pbuf%d" % i, [128, NA], BF16) for i in range(2)]
    wdn_r = [sb("wdn_r%d" % i, [128, NJ, 128], BF16) for i in range(2)]
    gall = sb("gall", [128, NJ, 256], BF16)
    hs = [sb("hs%d" % i, [128, 258]) for i in range(2)]
    ostg = [sb("ostg%d" % i, [128, 256]) for i in range(2)]

    P.dma(lnp_t[:], lnp); P.dma(cw_t[:], cw); P.dma(cb_t[:], cb); P.dma(hm_t[:], hmask)
    P.memset(ones_b[:], 1.0)

    if stop <= 0:
        P.emit(); return nc
    def A(off, kc, n):
        return arena[:, off:off + kc * n].rearrange("p (c n) -> p c n", n=n)
    Wk_b = A(0, 8, D); Wv_b = A(8192, 8, D)
    pl.load_cast(memb[:], chunked(memT))
    pl.load_cast(Wk_b, chunked(wk)); pl.load_cast(Wv_b, chunked(wv), q='pool')
    if stop == 1:
        P.emit(); return nc
    for oc in range(8):
        ps = pl.bank()
        for kc in range(8):
            P.mm(ps[:, 0:256], Wk_b[:, kc, oc * 128:(oc + 1) * 128], memb[:, kc, :], start=(kc == 0), stop=(kc == 7))
        P.copy(KT[:, oc, :], ps[:, 0:256], eng='act')
    for mc in range(2):
        for hf in range(2):
            ps = pl.bank()
            for kc in range(8):
                P.mm(ps[:, :], memb[:, kc, mc * 128:(mc + 1) * 128], Wv_b[:, kc, hf * 512:(hf + 1) * 512], start=(kc == 0), stop=(kc == 7))
            P.copy(V[:, mc, hf * 512:(hf + 1) * 512], ps[:, :], eng='dve')
    if stop <= 1:
        P.dma(outT[0:128, 0:256], KT[:, 0, :].bitcast(F32)[:, 0:128] if False else tmp[0][:, 0:256]); P.emit(); return nc
    Wg_b = A(0, 8, 2048); Pa_b = A(16384, 4, D); Pb_b = A(20480, 4, D); Wo_b = A(24576, 8, D)
    Wq_b = A(32768, 8, D); XWo_b = A(40960, 8, D)
    pl.load_cast(Pa_b, chunked(pa)); pl.load_cast(Pb_b, chunked(pb), q='pool')
    pl.load_cast(Wo_b, chunked(wo)); pl.load_cast(Wq_b, chunked(wq), q='pool'); pl.load_cast(XWo_b, chunked(xwo))
    pl.load_cast(Wg_b, chunked(wg), q='pool')
    g1, b1, g2, b2, g3, b3 = [lnp_t[:, i * 8:(i + 1) * 8] for i in range(6)]

    if stop <= 2:
        P.emit(); return nc
    tiles = [(0, 2)] + [(2 + i * NA, NA) for i in range(8)]
    if stop <= 3:
        tiles = tiles[:1]
    if stop == 4 or 40 < stop < 50:
        tiles = tiles[1:2]
    xTc, yaTc, ybTc, x2sc, outTc = chunked(xT), chunked(yaT), chunked(ybT), chunked(x2s), chunked(outT)
    for (c0, N) in tiles:
        P.dma(res[:, :, 0:N], xTc[:, :, c0:c0 + N])
        st = pl.stg[pl.si % 2]; pl.si += 1
        sv = st[:, 0:8 * N].rearrange("p (a n) -> p a n", n=N)
        P.dma(sv[:, 0:4, :], yaTc[:, :, c0:c0 + N], q='pool'); P.dma(sv[:, 4:8, :], ybTc[:, :, c0:c0 + N], q='pool')
        P.copy(yab[:, :, 0:N], sv[:, 0:4, :], eng='pool'); P.copy(ybb[:, :, 0:N], sv[:, 4:8, :], eng='pool')
        for c in range(8):
            P.copy(resb[:, c, 0:N], res[:, c, 0:N], eng='act' if c % 2 else 'dve')
        ck(41)
        for oc in range(8):
            osl = slice(oc * 128, (oc + 1) * 128)
            za = pl.bank(); ga = pl.bank()
            for kc in range(4):
                P.mm(za[:, 0:N], Pa_b[:, kc, osl], yab[:, kc, 0:N], start=(kc == 0), stop=(kc == 3))
            for kc in range(4):
                P.mm(za[:, 256:256 + N], Pb_b[:, kc, osl], ybb[:, kc, 0:N], start=(kc == 0), stop=(kc == 3))
            for kc in range(8):
                P.mm(ga[:, 0:N], Wg_b[:, kc, osl], resb[:, kc, 0:N], start=(kc == 0), stop=(kc == 7))
            for kc in range(8):
                P.mm(ga[:, 256:256 + N], Wg_b[:, kc, 1024 + oc * 128:1024 + (oc + 1) * 128], resb[:, kc, 0:N], start=(kc == 0), stop=(kc == 7))
            sa, sbb, t1, t2 = tmp[4][:, 0:N], tmp[5][:, 0:N], tmp[6][:, 0:N], tmp[7][:, 0:N]
            P.actf(sa, ga[:, 0:N], AF.Sigmoid)
            P.actf(sbb, ga[:, 256:256 + N], AF.Sigmoid)
            P.tt(t1, sa, za[:, 0:N], ALU.mult)
            P.tt(t2, sbb, za[:, 256:256 + N], ALU.mult)
            P.tt(actb2[:, oc, 0:N], t1, t2, ALU.add)
        ck(42)
        for oc in range(8):
            osl = slice(oc * 128, (oc + 1) * 128)
            ps = pl.bank()
            for kc in range(8):
                P.mm(ps[:, 0:N], Wo_b[:, kc, osl], actb2[:, kc, 0:N], start=(kc == 0), stop=(kc == 7))
            P.stt(res[:, oc, 0:N], res[:, oc, 0:N], ALPHA, ps[:, 0:N], ALU.mult, ALU.add)
        ck(43)
        chan_ln(P, pl, res, resb, g1, b1, N, ones_b, scr_b, scr_sq, tmp)
        ck(44)
        for oc in range(8):
            osl = slice(oc * 128, (oc + 1) * 128)
            ps = pl.bank()
            for kc in range(8):
                P.mm(ps[:, 0:N], Wq_b[:, kc, osl], resb[:, kc, 0:N], start=(kc == 0), stop=(kc == 7))
            P.copy(qT[:, oc, 0:N], ps[:, 0:N], eng='act' if oc % 2 else 'dve')
        for hh in range(4):
            pden = pl.bank()
            for mc in range(2):
                ps = pl.bank()
                for dc in range(2):
                    P.mm(ps[:, 0:N], KT[:, 2 * hh + dc, mc * 128:(mc + 1) * 128], qT[:, 2 * hh + dc, 0:N], start=(dc == 0), stop=(dc == 1))
                P.actf(pbuf[mc][:, 0:N], ps[:, 0:N], AF.Exp, scale=1.0 / 16.0)
            for mc in range(2):
                P.mm(pden[:, 0:N], ones_b[:], pbuf[mc][:, 0:N], start=(mc == 0), stop=(mc == 1))
            rec = tmp[4][:, 0:N]
            P.op('dve', lambda e, rec=rec, pden=pden, N=N: e.reciprocal(rec, pden[:, 0:N]), [pden[:, 0:N]], [rec])
            for dc in range(2):
                po = pl.bank()
                for mc in range(2):
                    P.mm(po[:, 0:N], V[:, mc, (2 * hh + dc) * 128:(2 * hh + dc + 1) * 128], pbuf[mc][:, 0:N], start=(mc == 0), stop=(mc == 1))
                P.tt(actb2[:, 2 * hh + dc, 0:N], po[:, 0:N], rec, ALU.mult)
        for oc in range(8):
            osl = slice(oc * 128, (oc + 1) * 128)
            ps = pl.bank()
            for kc in range(8):
                P.mm(ps[:, 0:N], XWo_b[:, kc, osl], actb2[:, kc, 0:N], start=(kc == 0), stop=(kc == 7))
            P.stt(res[:, oc, 0:N], res[:, oc, 0:N], ALPHA, ps[:, 0:N], ALU.mult, ALU.add)
        ck(45)
        chan_ln(P, pl, res, resb, g2, b2, N, ones_b, scr_b, scr_sq, tmp)
        ck(46)
        P.dma(x2sc[:, :, c0:c0 + N], res[:, :, 0:N], q='pool')

    if stop <= 5 or 40 < stop < 50:
        P.emit(); return nc
    Wup_b = A(0, 8, 2 * DFF)
    wupc = chunked(wup)
    wdnc = chunked(wdn)
    for kc in range(8):
        for h0 in range(0, 2 * DFF, 2048):
            h1 = min(2 * DFF, h0 + 2048)
            pl.load_cast(Wup_b[:, kc, h0:h1], wupc[:, kc, h0:h1], q='sp' if kc % 2 else 'pool')
    cw3 = cw_t[:].rearrange("p (c k) -> p c k", k=3)
    for ti in range(8):
        c0 = ti * 256
        P.dma(res[:, :, 0:258], x2sc[:, :, c0:c0 + 258])
        for c in range(8):
            P.copy(resb[:, c, 0:258], res[:, c, 0:258], eng='act' if c % 2 else 'dve')
        for j in range(NJ):
            hp = pl.bank(); hv = pl.bank()
            for kc in range(8):
                P.mm(hp[:, 0:258], Wup_b[:, kc, j * 128:(j + 1) * 128], resb[:, kc, 0:258], start=(kc == 0), stop=(kc == 7))
            for kc in range(8):
                P.mm(hv[:, 0:258], Wup_b[:, kc, DFF + j * 128:DFF + (j + 1) * 128], resb[:, kc, 0:258], start=(kc == 0), stop=(kc == 7))
            srcs = []
            for (hsrc, k) in ((hp, 0), (hv, 1)):
                if ti == 0:
                    hb_ = hs[k]
                    P.copy(hb_[:, 0:258], hsrc[:, 0:258], eng='act')
                    P.ts(hb_[:, 0:2], hb_[:, 0:2], hm_t[:, 0:1], None, ALU.mult)
                    srcs.append(hb_)
                else:
                    srcs.append(hsrc)
            outs = []
            for k, (src, ch) in enumerate(zip(srcs, (j, NJ + j))):
                t = tmp[k * 2]; t2 = tmp[k * 2 + 1]
                P.actf(t[:, 0:256], src[:, 2:258], AF.Identity, bias=cb_t[:, ch:ch + 1], scale=cw3[:, ch, 2:3])
                P.stt(t2[:, 0:256], src[:, 1:257], cw3[:, ch, 1:2], t[:, 0:256], ALU.mult, ALU.add)
                P.stt(t[:, 0:256], src[:, 0:256], cw3[:, ch, 0:1], t2[:, 0:256], ALU.mult, ALU.add)
                outs.append(t)
            sg = tmp[4]
            P.actf(sg[:, 0:256], outs[0][:, 0:256], AF.Silu)
            P.tt(gall[:, j, :], sg[:, 0:256], outs[1][:, 0:256], ALU.mult)
        for oc in range(8):
            wr = wdn_r[oc % 2]
            pl.load_cast(wr[:, :, :], wdnc[:, :, oc * 128:(oc + 1) * 128], q='sp')
            a = pl.bank()
            for j in range(NJ):
                P.mm(a[:, 0:256], wr[:, j, :], gall[:, j, :], start=(j == 0), stop=(j == NJ - 1))
            P.stt(res[:, oc, 2:258], res[:, oc, 2:258], ALPHA, a[:, 0:256], ALU.mult, ALU.add)
        res_v = res[:, :, 2:258]; resb_v = resb[:, :, 2:258]
        chan_ln(P, pl, res_v, resb_v, g3, b3, 256, ones_b, scr_b, scr_sq, tmp)
        P.dma(outTc[:, :, ti * 256:(ti + 1) * 256], res[:, :, 2:258], q='pool')
    P.emit()
    return nc


def tail_inputs(d, ya, yb):
    l = 0
    x = d['x']
    maps = []
    RC, NC_ = 1664, 1304
    wg = np.ascontiguousarray(d['w_in'][l][:, RC + NC_:])
    lnp = np.concatenate([d[k][l].reshape(8, 128).T for k in ('ln1_g', 'ln1_b', 'ln2_g', 'ln2_b', 'ln3_g', 'ln3_b')], axis=1)
    cw = np.ascontiguousarray(d['ffn_conv_w'][l].reshape(3, 44, 128).transpose(2, 1, 0)).reshape(128, 132)
    cb = np.ascontiguousarray(d['ffn_conv_b'][l].reshape(44, 128).T)
    common = dict(wg=wg, pa=d['merge_p_a'][l], pb=d['merge_p_b'][l], wo=d['mix_w_o'][l], wq=d['xa_wq'][l], wk=d['xa_wk'][l],
                  wv=d['xa_wv'][l], xwo=d['xa_wo'][l], wup=d['ffn_w_up'][l], wdn=d['ffn_w_down'][l],
                  lnp=np.ascontiguousarray(lnp), cw=cw, cb=cb)
    common = {k: np.ascontiguousarray(v, dtype=np.float32) for k, v in common.items()}

    def halo_T(a, b, t0):
        C = a.shape[-1]
        o = np.zeros((C, NT), np.float32)
        lo = max(0, t0 - 2)
        o[:, 2 - (t0 - lo):] = a[b, lo:t0 + 2048].T
        return o
    for c in range(8):
        b, t0 = c // 4, (c % 4) * 2048
        m = dict(common)
        m['xT'] = halo_T(x, b, t0); m['yaT'] = halo_T(ya, b, t0); m['ybT'] = halo_T(yb, b, t0)
        m['memT'] = np.ascontiguousarray(d['mem'][b].T)
        m['hmask'] = np.full((128, 1), 0.0 if t0 == 0 else 1.0, np.float32)
        maps.append(m)
    return maps


_NC_CACHE = {}


def run_tail(d, ya, yb):
    if 'tail' not in _NC_CACHE:
        _NC_CACHE['tail'] = build_tail()
    nc = _NC_CACHE['tail']
    res = run_bass_kernel_spmd(nc, tail_inputs(d, ya, yb), core_ids=list(range(8)))
    out = np.zeros((2, 8192, D), np.float32)
    for c in range(8):
        b, t0 = c // 4, (c % 4) * 2048
        out[b, t0:t0 + 2048] = res.results[c]['outT'].T
    return out


SEQ = 8192
GN_EPS = 64e-5
CH = 64
NCH = 128 // CH
LV = 5


def rwkv_consts():
    idx = np.arange(128)
    same = (idx[:, None] // CH == idx[None, :] // CH)
    m_strict = (same & (idx[:, None] < idx[None, :])).astype(np.float32)
    m_incl = (same & (idx[:, None] <= idx[None, :])).astype(np.float32)
    tri_gt = (same & (idx[:, None] > idx[None, :])).astype(np.float32)
    cind = (idx[:, None] // CH == np.arange(4)[None, :]).astype(np.float32)
    ident = np.eye(128, dtype=np.float32)
    return np.concatenate([ident, m_incl, tri_gt, m_strict, m_incl, m_strict.T, m_strict.T, cind, np.zeros((128, 60), np.float32)], axis=1).astype(np.float32)


def build_rwkv(ntok=SEQ, stop=99):
    nc = bass.Bass("TRN2", target_bir_lowering=False)
    try:
        return _build_rwkv(nc, ntok, stop)
    except _Stop:
        return nc


def _build_rwkv(nc, ntok, stop):
    def ck(code):
        if stop == code:
            P.emit()
            raise _Stop()
    dr = lambda n, s, k="ExternalInput": nc.dram_tensor(n, s, F32, kind=k).ap()
    xT = dr("xT", [D, ntok]); w_r = dr("w_r", [D, 512]); cmat = dr("cmat", [128, 960])
    pcol = dr("pcol", [128, 12])
    w2a2 = dr("w2a2", [64, 256])
    w0a0 = dr("w0a0", [128, 256])
    yT = dr("yT", [128, ntok], "ExternalOutput")
    chunked = lambda ap: ap.rearrange("(c p) n -> p c n", p=128)
    P = Prog(nc)
    pl = Pools(P, nc)
    sb = lambda n, s, dt=F32: nc.alloc_sbuf_tensor(n, s, dt)
    cm = sb("cm", [128, 960]); pc = sb("pc", [128, 12]); wa = sb("wa", [64, 256]); w0 = sb("w0", [128, 256])
    P.dma(cm[:], cmat); P.dma(pc[:], pcol); P.dma(wa[:], w2a2); P.dma(w0[:], w0a0)
    ident = cm[:, 0:128]; TriInc = cm[:, 128:256]; TriGt = cm[:, 256:384]
    MSK1 = cm[:, 384:640]; Mincl = cm[:, 512:640]; MSK3 = cm[:, 640:896]; CInd = cm[:, 896:960]
    dkk = sb("dkk", [128, 128]); dka = sb("dka", [128, 128]); drk = sb("drk", [128, 128])
    P.ts(dkk[:], ident, pc[:, 4:5], None, ALU.mult)
    P.ts(dka[:], ident, pc[:, 5:6], None, ALU.mult)
    P.ts(drk[:], ident, pc[:, 6:7], None, ALU.mult)
    wrb = sb("wrb", [128, 8, 512], BF16)
    pl.load_cast(wrb[:], chunked(w_r))
    xt32 = [sb("xt32_%d" % i, [128, 8, 512]) for i in range(1)]
    xb = sb("xb", [128, 8, 512], BF16)
    pS = [[sb("pS%d_%d" % (i, c), [128, 513]) for c in range(5)] for i in range(2)]
    pL = [sb("pL%d" % c, [128, 512]) for c in range(5)]
    for c in range(5):
        P.memset(pS[1][c][:, 512:513], 0.0)
    tok = {n: sb("tk_" + n, [128, 128]) for n in ("r", "k", "v", "kkp", "kka", "rrk", "logw", "a", "kk", "kmod", "e1", "e2", "e3", "e4", "Bt", "Kt", "Rt", "Kh", "t0", "t1", "yn", "bv")}
    small = sb("small", [128, 16])
    RH1 = [sb("RH1_%d" % h, [128, 192]) for h in range(2)]
    RH2 = [sb("RH2_%d" % h, [128, 192]) for h in range(2)]
    AK = [sb("AK_%d" % h, [128, 192]) for h in range(2)]
    CM = [sb("CM_%d" % h, [64, 512]) for h in range(2)]
    XX = [[sb("XX%d_%d" % (h, i), [128, 256]) for i in range(2)] for h in range(2)]
    TT = [[sb("TT%d_%d" % (h, i), [128, 128]) for i in range(2)] for h in range(2)]
    QS = [sb("QS_%d" % h, [128, 192]) for h in range(2)]
    RpZ = [sb("RpZ_%d" % h, [128, NCH, 128]) for h in range(2)]
    msk = [sb("msk_%d" % h, [128, 2 * NCH, 64]) for h in range(2)]
    WYKP = [sb("WYKP_%d" % h, [128, 192]) for h in range(2)]
    GS = [sb("GS_%d" % h, [128, 256]) for h in range(2)]
    lamC = [sb("lamC_%d" % h, [64, 4]) for h in range(2)]
    STS = [sb("STS_%d" % h, [128, 5 * 64]) for h in range(2)]
    for h in range(2):
        P.memset(STS[h][:, :], 0.0); P.memset(GS[h][:, :], 0.0); P.memset(RpZ[h][:, :, :], 0.0)
    bnst = sb("bnst", [128, 2, 6]); bnag = sb("bnag", [128, 2, 2])
    ystg = [sb("ystg%d" % i, [128, 512]) for i in range(2)]
    lh = sb("lh", [128, 512])
    xTc = chunked(xT)
    NB = ntok // 512
    for blk in range(NB):
        par = blk % 2
        P.dma(xt32[0][:, 0:4, :], xTc[:, 0:4, blk * 512:(blk + 1) * 512])
        P.dma(xt32[0][:, 4:8, :], xTc[:, 4:8, blk * 512:(blk + 1) * 512], q='pool')
        for c in range(8):
            P.copy(xb[:, c, :], xt32[0][:, c, :], eng=('act', 'dve', 'pool')[c % 3])
        for c in range(5):
            ps = pl.bank()
            c0, c1 = ((c * 128, (c + 1) * 128) if c < 3 else ((384, 448) if c == 3 else (448, 512)))
            R = c1 - c0
            for kc in range(8):
                P.mm(ps[0:R, :], wrb[:, kc, c0:c1], xb[:, kc, :], start=(kc == 0), stop=(kc == 7))
            cur = pS[par][c]; prev = pS[1 - par][c]
            P.copy(cur[0:R, 1:513], ps[0:R, :], eng='act')
            P.copy(cur[0:R, 0:1], prev[0:R, 512:513], eng='dve')
            t = pL[c]
            mucol = pc[0:R, c:c + 1] if c < 3 else pc[0:R, 6 + c:7 + c]
            P.tt(t[0:R, :], cur[0:R, 0:512], cur[0:R, 1:513], ALU.subtract)
            P.stt(t[0:R, :], t[0:R, :], mucol, cur[0:R, 1:513], ALU.mult, ALU.add)
        P.actf(lh[0:64, :], pL[3][0:64, :], AF.Tanh)
        ys = ystg[blk % 2]
        ck(1)
        for sub in range(4):
            ts_ = slice(sub * 128, (sub + 1) * 128)
            ps = pl.bank()
            P.mm(ps[:, 0:128], pL[0][:, ts_], ident); P.mm(ps[:, 128:256], pL[1][:, ts_], ident); P.mm(ps[:, 256:384], pL[2][:, ts_], ident)
            P.copy(tok["r"][:], ps[:, 0:128], eng='act'); P.copy(tok["k"][:], ps[:, 128:256], eng='dve'); P.copy(tok["v"][:], ps[:, 256:384], eng='act')
            ps2 = pl.bank()
            P.mm(ps2[:, 0:128], pL[1][:, ts_], dkk[:]); P.mm(ps2[:, 128:256], pL[1][:, ts_], dka[:]); P.mm(ps2[:, 256:384], pL[0][:, ts_], drk[:])
            P.copy(tok["kkp"][:], ps2[:, 0:128], eng='dve'); P.copy(tok["kka"][:], ps2[:, 128:256], eng='act'); P.copy(tok["rrk"][:], ps2[:, 256:384], eng='dve')
            ps3 = pl.bank()
            P.mm(ps3[:, 0:128], lh[0:64, ts_], wa[0:64, 0:128])
            P.mm(ps3[:, 128:256], pL[4][0:64, ts_], wa[0:64, 128:256])
            P.tt(tok["t0"][:], ps3[:, 0:128], w0[:, 0:128], ALU.add)
            P.tt(tok["t1"][:], ps3[:, 128:256], w0[:, 128:256], ALU.add)
            P.actf(tok["logw"][:], tok["t0"][:], AF.Sigmoid)
            P.ts(tok["logw"][:], tok["logw"][:], -float(np.exp(-0.5)), None, ALU.mult)
            P.actf(tok["a"][:], tok["t1"][:], AF.Sigmoid)
            P.actf(tok["t0"][:], tok["kkp"][:], AF.Square)
            P.op('dve', lambda e, o=small[:, 0:2], i=tok["t0"][:].rearrange("p (h k) -> p h k", h=2): e.reduce_sum(o, i, AX.X), [tok["t0"][:]], [small[:, 0:2]])
            P.actf(small[:, 2:4], small[:, 0:2], AF.Sqrt)
            P.ts(small[:, 2:4], small[:, 2:4], 1e-12, None, ALU.max)
            P.op('dve', lambda e, o=small[:, 4:6], i=small[:, 2:4]: e.reciprocal(o, i), [small[:, 2:4]], [small[:, 4:6]])
            for h in range(2):
                hs = slice(h * 64, (h + 1) * 64)
                P.ts(tok["kk"][:, hs], tok["kkp"][:, hs], small[:, 4 + h:5 + h], None, ALU.mult)
            P.ts(tok["t0"][:], tok["a"][:], -1.0, None, ALU.add)
            P.tt(tok["t0"][:], tok["t0"][:], tok["kka"][:], ALU.mult)
            P.tt(tok["kmod"][:], tok["t0"][:], tok["k"][:], ALU.add)
            psL = pl.bank()
            P.mm(psL[:, 0:128], TriInc, tok["logw"][:]); P.mm(psL[:, 128:256], TriGt, tok["logw"][:])
            P.tt(tok["t1"][:], psL[:, 0:128], tok["logw"][:], ALU.subtract)
            P.actf(tok["e1"][:], tok["t1"][:], AF.Exp)
            P.actf(tok["e2"][:], psL[:, 0:128], AF.Exp, scale=-1.0)
            P.actf(tok["e3"][:], psL[:, 0:128], AF.Exp)
            P.actf(tok["e4"][:], psL[:, 128:256], AF.Exp)
            P.tt(tok["t0"][:], tok["kk"][:], tok["a"][:], ALU.mult)
            P.tt(tok["Bt"][:], tok["t0"][:], tok["e2"][:], ALU.mult)
            P.tt(tok["Kt"][:], tok["kmod"][:], tok["e2"][:], ALU.mult)
            P.tt(tok["Rt"][:], tok["r"][:], tok["e3"][:], ALU.mult)
            P.tt(tok["t1"][:], tok["kk"][:], tok["e1"][:], ALU.mult)
            for h in range(2):
                hs = slice(h * 64, (h + 1) * 64)
                P.ts(RH1[h][:, 0:64], tok["t1"][:, hs], -1.0, None, ALU.mult)
                P.tt(RH2[h][:, 128:192], tok["t0"][:, hs], tok["e4"][:, hs], ALU.mult)
                P.tt(AK[h][:, 128:192], tok["kmod"][:, hs], tok["e4"][:, hs], ALU.mult)
            P.tt(tok["t1"][:], tok["rrk"][:], tok["kmod"][:], ALU.mult)
            P.op('dve', lambda e, o=small[:, 6:8], i=tok["t1"][:].rearrange("p (h k) -> p h k", h=2): e.reduce_sum(o, i, AX.X), [tok["t1"][:]], [small[:, 6:8]])
            for h in range(2):
                hs = slice(h * 64, (h + 1) * 64)
                P.ts(tok["bv"][:, hs], tok["v"][:, hs], small[:, 6 + h:7 + h], None, ALU.mult)
            ck(2)
            for h in range(2):
                hs = slice(h * 64, (h + 1) * 64)
                tp = pl.bank()
                P.tr(tp[0:64, 0:128], RH1[h][:, 0:64], ident)
                P.tr(tp[0:64, 128:256], tok["Rt"][:, hs], ident)
                P.tr(tp[0:64, 256:384], tok["Bt"][:, hs], ident)
                P.tr(tp[0:64, 384:512], tok["Kt"][:, hs], ident)
                P.copy(CM[h][:, :], tp[0:64, :], eng='act')
                Ac, Rc, Bc, Kc = [CM[h][:, i * 128:(i + 1) * 128] for i in range(4)]
                m1 = pl.bank(); m2 = pl.bank(); m3 = pl.bank()
                P.mm(m1[:, 0:256], Bc, CM[h][:, 0:256])
                P.mm(m2[:, 0:128], Kc, Rc)
                P.mm(m3[:, 0:256], Ac, CM[h][:, 256:512])
                X0 = XX[h][0]
                P.tt(X0[:, 0:128], m1[:, 0:128], MSK1[:, 0:128], ALU.mult)
                P.tt(RH2[h][:, 0:128], m1[:, 128:256], Mincl, ALU.mult)
                P.tt(AK[h][:, 0:128], m2[:, 0:128], Mincl, ALU.mult)
                P.tt(X0[:, 128:256], m3[:, 0:128], MSK3[:, 0:128], ALU.mult)
                P.tt(RH1[h][:, 64:192], m3[:, 128:256], MSK3[:, 128:256], ALU.mult)
                ck(3)
                Tc = TT[h][0]
                P.tt(Tc[:, :], X0[:, 0:128], ident, ALU.add)
                cur = 0
                for lvl in range(LV):
                    Xc = XX[h][cur]; Xn = XX[h][1 - cur]
                    pp = pl.bank()
                    if lvl < LV - 1:
                        P.mm(pp[:, 0:128], Xc[:, 128:256], Xc[:, 0:128])
                    P.mm(pp[:, 128:256], Xc[:, 0:128], Xc[:, 128:256])
                    if lvl < LV - 1:
                        P.copy(Xn[:, :], pp[:, 0:256], eng='act')
                    else:
                        P.copy(Xn[:, 128:256], pp[:, 128:256], eng='act')
                    pt_ = pl.bank()
                    P.mm(pt_[:, 0:128], Xn[:, 128:256], TT[h][lvl % 2][:, :])
                    P.tt(TT[h][(lvl + 1) % 2][:, :], TT[h][lvl % 2][:, :], pt_[:, 0:128], ALU.add)
                    cur = 1 - cur
                ck(4)
                Tf = TT[h][LV % 2]
                q = pl.bank()
                P.mm(q[:, 0:192], Tf[:, :], RH1[h][:, :])
                P.copy(QS[h][:, :], q[:, 0:192], eng='act')
                rp = pl.bank()
                P.mm(rp[0:64, 0:128], QS[h][:, 0:64], RH2[h][:, 0:128])
                for c in range(NCH):
                    cs = slice(c * CH, (c + 1) * CH)
                    P.tt(RpZ[h][0:64, c, cs], rp[0:64, cs], Rc[:, cs], ALU.add)
                wk = pl.bank()
                P.mm(wk[:, 0:192], QS[h][:, 64:192], RH2[h][:, :])
                P.tt(WYKP[h][:, :], wk[:, 0:192], AK[h][:, :], ALU.add)
                gg = pl.bank()
                for c in range(NCH):
                    P.ts(msk[h][:, c, :], QS[h][:, 0:64], CInd[:, c:c + 1], None, ALU.mult)
                    P.ts(msk[h][:, NCH + c, :], WYKP[h][:, 128:192], CInd[:, c:c + 1], None, ALU.mult)
                    P.mm(gg[0:64, c * 64:(c + 1) * 64], msk[h][:, c, :], RH2[h][:, 128:192])
                P.copy(GS[h][0:64, 0:NCH * 64], gg[0:64, 0:NCH * 64], eng='act')
                lc = pl.bank()
                P.mm(lc[0:64, 0:64], tok["logw"][:, hs], CInd)
                P.actf(lamC[h][:, 0:NCH], lc[0:64, 0:NCH], AF.Exp)
                ck(5)
                for c in range(NCH):
                    cs = slice(c * CH, (c + 1) * CH)
                    sn = pl.bank()
                    STc = STS[h][:, c * 64:(c + 1) * 64]
                    P.mm(sn[0:64, 0:64], GS[h][:, c * 64:(c + 1) * 64], STc, start=True, stop=False)
                    P.mm(sn[0:64, 0:64], msk[h][:, NCH + c, :], tok["v"][:, hs], start=False, stop=True)
                    P.stt(STS[h][0:64, (c + 1) * 64:(c + 2) * 64], STS[h][0:64, c * 64:(c + 1) * 64], lamC[h][:, c:c + 1], sn[0:64, 0:64], ALU.mult, ALU.add)
                ck(6)
                yp = pl.bank()
                P.mm(yp[:, 0:64], WYKP[h][:, 0:128], tok["v"][:, hs], start=True, stop=False)
                for c in range(NCH):
                    P.mm(yp[:, 0:64], RpZ[h][:, c, :], STS[h][:, c * 64:(c + 1) * 64], start=False, stop=(c == NCH - 1))
                P.copy(STS[h][0:64, 0:64], STS[h][0:64, NCH * 64:(NCH + 1) * 64], eng='dve')
                P.copy(tok["t0"][:, hs], yp[:, 0:64], eng='act')
                P.op('dve', lambda e, o=bnst[:, h, :], i=tok["t0"][:, hs]: e.bn_stats(o, i), [tok["t0"][:, hs]], [bnst[:, h, :]])
                P.op('dve', lambda e, o=bnag[:, h, :], i=bnst[:, h, :]: e.bn_aggr(o, i), [bnst[:, h, :]], [bnag[:, h, :]])
                P.ts(small[:, 8 + h:9 + h], bnag[:, h, 1:2], GN_EPS, None, ALU.add)
                P.actf(small[:, 8 + h:9 + h], small[:, 8 + h:9 + h], AF.Sqrt)
                P.op('dve', lambda e, o=small[:, 10 + h:11 + h], i=small[:, 8 + h:9 + h]: e.reciprocal(o, i), [small[:, 8 + h:9 + h]], [small[:, 10 + h:11 + h]])
                P.ts(tok["yn"][:, hs], tok["t0"][:, hs], bnag[:, h, 0:1], small[:, 10 + h:11 + h], ALU.subtract, ALU.mult)
            ck(7)
            po = pl.bank()
            P.tr(po[:, 0:128], tok["yn"][:], ident)
            P.tr(po[:, 128:256], tok["bv"][:], ident)
            P.actf(ys[:, ts_], po[:, 0:128], AF.Identity, bias=pc[:, 8:9], scale=pc[:, 7:8])
            P.tt(ys[:, ts_], ys[:, ts_], po[:, 128:256], ALU.add)
        P.dma(yT[:, blk * 512:(blk + 1) * 512], ys[:, :], q='pool')
    P.emit()
    return nc


def rwkv_inputs(d, ntok=SEQ):
    l = 0
    W = d['w_in'][l]
    cm = rwkv_consts()
    maps = []
    for c in range(8):
        b, j = c // 4, c % 4
        cols = np.concatenate([np.arange(128 * j, 128 * j + 128), 512 + np.arange(128 * j, 128 * j + 128),
                               1024 + np.arange(128 * j, 128 * j + 128), np.arange(1536, 1664)])
        hc = np.arange(128 * j, 128 * j + 128)
        pcol = np.zeros((128, 12), np.float32)
        pcol[:, 0:4] = d['rwkv_mu'][l][cols].reshape(4, 128).T
        pcol[:, 4] = d['rwkv_k_k'][l][hc]; pcol[:, 5] = d['rwkv_k_a'][l][hc]
        pcol[:, 6] = d['rwkv_r_k'][l].reshape(-1)[hc]; pcol[:, 7] = d['rwkv_gn_g'][l][hc]; pcol[:, 8] = d['rwkv_gn_b'][l][hc]
        pcol[0:64, 9] = d['rwkv_mu'][l][1536:1600]; pcol[0:64, 10] = d['rwkv_mu'][l][1600:1664]
        w2a2 = np.concatenate([d['rwkv_w2'][l][:, hc], d['rwkv_a2'][l][:, hc]], axis=1)
        w0a0 = np.tile(np.concatenate([d['rwkv_w0'][l][hc], d['rwkv_a0'][l][hc]])[None, :], (128, 1))
        maps.append(dict(xT=np.ascontiguousarray(d['x'][b, :ntok].T), w_r=np.ascontiguousarray(W[:, cols]), cmat=cm,
                         pcol=pcol, w2a2=np.ascontiguousarray(w2a2, dtype=np.float32), w0a0=np.ascontiguousarray(w0a0, dtype=np.float32)))
    return maps


def run_rwkv(d):
    if 'rwkv' not in _NC_CACHE:
        _NC_CACHE['rwkv'] = build_rwkv()
    res = run_bass_kernel_spmd(_NC_CACHE['rwkv'], rwkv_inputs(d), core_ids=list(range(8)))
    ya = np.zeros((2, SEQ, 512), np.float32)
    for c in range(8):
        b, j = c // 4, c % 4
        ya[b, :, 128 * j:128 * j + 128] = res.results[c]['yT'].T
    return ya


ROPE_THETA = 500000.0
NEGM = 30000.0


def nsa_consts(ntok):
    pos = np.arange(ntok, dtype=np.float32)
    inv = ROPE_THETA ** (-np.arange(8, dtype=np.float32) * 2.0 / 16.0)
    ang = pos[None, :] * inv[:, None]
    cosT = np.ones((64, ntok), np.float32); sinT = np.zeros((64, ntok), np.float32)
    cosT[0:8] = np.cos(ang); cosT[8:16] = np.cos(ang)
    sinT[0:8] = -np.sin(ang); sinT[8:16] = np.sin(ang)
    p = np.arange(128)
    tri_le = (p[:, None] <= p[None, :]).astype(np.float32)
    tri_gt = (p[:, None] > p[None, :]).astype(np.float32)
    cmk = np.ones((128, 32, 128), np.float32)
    for i in range(16):
        p0 = 8 * i
        u = p[:, None] - p0
        cmk[:, 16 + i, :] = (p[None, :] >= 16 * u + 31).astype(np.float32)
        if p0 == 0:
            cmk[127, i, :] = (p >= 15).astype(np.float32)
    ncmp_pad = 512
    c = np.arange(ncmp_pad); j = np.arange(128)
    ov = ((16 * c[:, None] <= 64 * j[None, :] + 63) & (16 * c[:, None] + 31 >= 64 * j[None, :])).astype(np.float32)
    ov[(ntok - 32) // 16 + 1:] = 0.0
    ovc = ov.reshape(4, 128, 128).transpose(1, 0, 2)
    jj = np.arange(128); key = np.arange(128)
    xall = np.zeros((128, 64, 128), np.float32)
    for kb in range(64):
        xall[:, kb, :] = (jj[:, None] == 2 * kb + key[None, :] // 64)
    qcol = np.zeros((128, 4), np.float32)
    qcol[:, 0] = (p >= 64) * 1e30 + (p < 64) * -1e30
    qcol[:, 1] = (p >= 64).astype(np.float32)
    qcol[:, 2] = (p < 64) * 1e30
    return dict(cosT=cosT, sinT=sinT, tri=np.concatenate([tri_le, tri_gt, np.eye(128, dtype=np.float32)], axis=1),
                cmk=cmk.reshape(128, 32 * 128), ovc=np.ascontiguousarray(ovc).reshape(128, 512),
                xall=xall.reshape(128, 64 * 128), qcol=qcol)


def build_nsa(ntok=SEQ, stop=99):
    nc = bass.Bass("TRN2", target_bir_lowering=False)
    try:
        return _build_nsa(nc, ntok, stop)
    except _Stop:
        return nc


def _build_nsa(nc, ntok, stop):
    def ck(code):
        if stop == code:
            P.emit()
            raise _Stop()
    NB = ntok // 512; NT128 = ntok // 128; NQ = NT128 // 2
    NCMP = (ntok - 32) // 16 + 1
    dr = lambda n, s, k="ExternalInput": nc.dram_tensor(n, s, F32, kind=k).ap()
    xT = dr("xT", [D, ntok])
    wA = dr("wA", [D, 15 * 64])
    wB = dr("wB", [D, 140])
    cosT = dr("cosT", [64, ntok]); sinT = dr("sinT", [64, ntok])
    tri = dr("tri", [128, 384]); cmk = dr("cmk", [128, 2048]); ovc = dr("ovc", [128, 512]); xall = dr("xall", [128, 8192]); qcol = dr("qcol", [128, 10])
    dmask = dr("dmask", [128, 256]); wmask = dr("wmask", [128, 768])
    w1k = dr("w1k", [64, 32 * 256]); w1v = dr("w1v", [64, 32 * 256]); w2k = dr("w2k", [128, 2 * 64]); w2v = dr("w2v", [128, 2 * 64])
    pek = dr("pek", [64, 32 * 64]); pev = dr("pev", [64, 32 * 64])
    xq = dr("xq", [D, NQ * 128])
    cosq = dr("cosq", [64, NQ * 128]); sinq = dr("sinq", [64, NQ * 128])
    yb = dr("yb", [NQ * 128, 256], "ExternalOutput")
    chunked = lambda ap: ap.rearrange("(c p) n -> p c n", p=128)
    P = Prog(nc)
    pl = Pools(P, nc)
    sb = lambda n, s, dt=F32: nc.alloc_sbuf_tensor(n, s, dt)
    tri_t = sb("tri_t", [128, 384]); qc_t = sb("qc_t", [128, 10])
    P.dma(tri_t[:], tri); P.dma(qc_t[:], qcol)
    identf = tri_t[:, 256:384]
    xt32 = sb("xt32", [128, 8, 512]); xb = sb("xb", [128, 8, 512], BF16)
    pl.stg = [xt32[:, 0:4, :].rearrange("p a n -> p (a n)"), xt32[:, 4:8, :].rearrange("p a n -> p (a n)")]
    BUF2 = sb("BUF2", [128, 24576], BF16)
    cmk_b = sb("cmk_b", [128, 2048], BF16); dm_b = sb("dm_b", [128, 256], BF16); wm_b = sb("wm_b", [128, 768], BF16); ov_b = sb("ov_b", [128, 4, 128], BF16)
    xall_b = BUF2[:, 16384:24576].rearrange("p (a n) -> p a n", n=128)
    pl.load_cast(cmk_b[:], cmk); pl.load_cast(dm_b[:], dmask); pl.load_cast(wm_b[:], wmask); pl.load_cast(ov_b[:].rearrange("p a n -> p (a n)"), ovc, q='pool')
    wAb = sb("wAb", [128, 8, 960], BF16); wBb = sb("wBb", [128, 8, 140], BF16)
    pl.load_cast(wAb[:], chunked(wA)); pl.load_cast(wBb[:], chunked(wB), q='pool')
    KBUF = sb("KBUF", [64, 2, ntok], BF16)
    ksT = KBUF[:, 0, :]; kwT = KBUF[:, 1, :]; kcT = KBUF[:, 0, :]; vcT = KBUF[:, 1, :]
    vsA = sb("vsA", [128, NT128, 65], BF16); vwA = sb("vwA", [128, NT128, 65], BF16)
    qT = BUF2[0:64, 0:NQ * 512].rearrange("p (a n) -> p a n", n=512)
    GT = sb("GT", [128, NQ, 12])
    P.memset(vsA[:, :, 64:65], 1.0); P.memset(vwA[:, :, 64:65], 1.0)
    cs_t = [sb("cs_t%d" % i, [64, 1024]) for i in range(1)]
    rt = [sb("rt%d" % i, [64, 512]) for i in range(2)]
    xTc = chunked(xT); xqc = chunked(xq)

    def proj_block(src_c, cosd, sind, c0, n, groups, dests, tokmajor_tiles):
        P.dma(xt32[:, 0:4, 0:n], src_c[:, 0:4, c0:c0 + n]); P.dma(xt32[:, 4:8, 0:n], src_c[:, 4:8, c0:c0 + n], q='pool')
        for c in range(8):
            P.copy(xb[:, c, 0:n], xt32[:, c, 0:n], eng=('act', 'dve', 'pool')[c % 3])
        cst = cs_t[0]
        P.dma(cst[:, 0:n], cosd[:, c0:c0 + n]); P.dma(cst[:, 512:512 + n], sind[:, c0:c0 + n], q='pool')
        for (cb, pb), dst in zip(groups, dests):
            p1 = pl.bank()
            for kc in range(8):
                P.mm(p1[0:64, 0:n], wAb[:, kc, cb:cb + 64], xb[:, kc, 0:n], start=(kc == 0), stop=(kc == 7))
            if pb is None:
                P.copy(dst, p1[0:64, 0:n], eng='act')
                continue
            p2 = pl.bank()
            for kc in range(8):
                P.mm(p2[0:64, 0:n], wAb[:, kc, pb:pb + 64], xb[:, kc, 0:n], start=(kc == 0), stop=(kc == 7))
            P.tt(rt[0][:, 0:n], p1[0:64, 0:n], cst[:, 0:n], ALU.mult)
            P.tt(rt[1][:, 0:n], p2[0:64, 0:n], cst[:, 512:512 + n], ALU.mult)
            P.tt(dst, rt[0][:, 0:n], rt[1][:, 0:n], ALU.add, eng='pool')
        for (t128, off) in tokmajor_tiles:
            pv = pl.bank()
            for kc in range(8):
                P.mm(pv[:, 0:140], xb[:, kc, off:off + 128], wBb[:, kc, :], start=(kc == 0), stop=(kc == 7))
            yield (t128, pv)

    for blk in range(NB):
        groups = [(256 + 0, 448 + 256 + 0), (896, None)]
        sl = slice(blk * 512, (blk + 1) * 512)
        for _ in proj_block(xTc, cosT, sinT, blk * 512, 512, groups, [kcT[:, sl], vcT[:, sl]], []):
            pass
    ck(1)
    w1b = [BUF2[0:64, i * 8192:(i + 1) * 8192].rearrange("p (a n) -> p a n", n=256) for i in range(2)]
    w2b = [sb("w2b%d" % i, [128, 2, 64], BF16) for i in range(2)]
    peb = [BUF2[0:64, 16384 + i * 2048:16384 + (i + 1) * 2048].rearrange("p (a n) -> p a n", n=64) for i in range(2)]
    pl.load_cast(w1b[0], w1k.rearrange("p (a n) -> p a n", n=256)); pl.load_cast(w1b[1], w1v.rearrange("p (a n) -> p a n", n=256), q='pool')
    pl.load_cast(w2b[0][:].rearrange("p a n -> p (a n)"), w2k); pl.load_cast(w2b[1][:].rearrange("p a n -> p (a n)"), w2v, q='pool')
    pl.load_cast(peb[0], pek.rearrange("p (a n) -> p a n", n=64)); pl.load_cast(peb[1], pev.rearrange("p (a n) -> p a n", n=64), q='pool')
    KC = sb("KC", [64, 512], BF16); VCA = sb("VCA", [128, 4, 65], BF16)
    P.memset(KC[:], 0.0); P.memset(VCA[:, :, 0:64], 0.0); P.memset(VCA[:, :, 64:65], 1.0)
    GH = [[sb("GH%d_%d" % (i, hc), [128, 512], BF16) for hc in range(2)] for i in range(2)]
    gtmp = [sb("gtmp%d" % i, [128, 512]) for i in range(3)]
    bcol = sb("bcol", [128, 4])
    for which, srcT in ((0, kcT), (1, vcT)):
        for hc in range(2):
            pb_ = pl.bank()
            for l in range(32):
                P.mm(pb_[:, 0:64], w1b[which][:, l, hc * 128:(hc + 1) * 128], peb[which][:, l, :], start=(l == 0), stop=(l == 31))
            P.copy(bcol[:, which * 2 + hc:which * 2 + hc + 1], pb_[:, 0:1], eng='act')
            ph = pl.bank()
            for l in range(32):
                rhs = srcT[:, l:l + 16 * (NCMP - 1) + 1:16]
                P.mm(ph[:, 0:NCMP], w1b[which][:, l, hc * 128:(hc + 1) * 128], rhs, start=(l == 0), stop=(l == 31))
            x_ = gtmp[0][:, 0:NCMP]; u_ = gtmp[1][:, 0:NCMP]; s_ = gtmp[2][:, 0:NCMP]
            P.actf(x_, ph[:, 0:NCMP], AF.Identity, bias=bcol[:, which * 2 + hc:which * 2 + hc + 1])
            P.actf(u_, x_, AF.Square)
            P.ts(u_, u_, 0.044715, 1.0, ALU.mult, ALU.add)
            P.tt(u_, u_, x_, ALU.mult)
            P.actf(s_, u_, AF.Sigmoid, scale=2.0 * 0.7978845608028654)
            P.memset(GH[which][hc][:, NCMP:512], 0.0)
            P.tt(GH[which][hc][:, 0:NCMP], x_, s_, ALU.mult)
    pk = pl.bank()
    for hc in range(2):
        P.mm(pk[0:64, 0:NCMP], w2b[0][:, hc, :], GH[0][hc][:, 0:NCMP], start=(hc == 0), stop=(hc == 1))
    P.copy(KC[:, 0:NCMP], pk[0:64, 0:NCMP], eng='act')
    for cc in range((NCMP + 127) // 128):
        pv_ = pl.bank()
        for hc in range(2):
            P.mm(pv_[:, 0:64], GH[1][hc][:, cc * 128:(cc + 1) * 128], w2b[1][:, hc, :], start=(hc == 0), stop=(hc == 1))
        P.copy(VCA[:, cc, 0:64], pv_[:, 0:64], eng='dve')
    ck(2)
    for blk in range(NB):
        groups = [(256 + 64, 448 + 256 + 64), (256 + 128, 448 + 256 + 128)]
        sl = slice(blk * 512, (blk + 1) * 512)
        for (t128, pv) in proj_block(xTc, cosT, sinT, blk * 512, 512, groups, [ksT[:, sl], kwT[:, sl]], [(blk * 4 + i, i * 128) for i in range(4)]):
            P.copy(vsA[:, t128, 0:64], pv[:, 0:64], eng='act')
            P.copy(vwA[:, t128, 0:64], pv[:, 64:128], eng='dve')
    pl.load_cast(xall_b, xall.rearrange("p (a n) -> p a n", n=128))
    qtmp = [sb("qtmp%d" % r, [64, 512], BF16) for r in range(4)]
    for qblk in range(0, NQ, 4):
        n = min(4, NQ - qblk) * 128
        groups = [(r * 64, 448 + r * 64) for r in range(4)]
        dests = [qtmp[r][:, 0:n] for r in range(4)]
        for (t128, pv) in proj_block(xqc, cosq, sinq, qblk * 128, n, groups, dests, [(qblk + i, i * 128) for i in range(n // 128)]):
            P.actf(GT[:, t128, :], pv[:, 128:140], AF.Sigmoid)
        for i in range(n // 128):
            for r in range(4):
                P.copy(qT[:, qblk + i, r * 128:(r + 1) * 128], qtmp[r][:, i * 128:(i + 1) * 128], eng=('pool', 'dve')[r % 2])
    ck(3)
    CV = sb("CV", [128, 4, 193], BF16)
    for cc in range(4):
        P.copy(CV[:, cc, 0:65], VCA[:, cc, :], eng='dve'); P.copy(CV[:, cc, 65:193], ov_b[:, cc, :], eng='pool')
    pT = [sb("pT%d" % i, [128, 512], BF16) for i in range(3)]
    imp = sb("imp", [128, 128]); imp2 = sb("imp2", [128, 128]); m8 = sb("m8", [128, 16]); sel = sb("sel", [128, 128])
    selT = sb("selT", [128, 128], BF16); negm = sb("negm", [128, 512], BF16)
    oTs = [None] + [sb("oTs%d" % i, [65, 512]) for i in range(1, 3)]
    ytile = sb("ytile", [128, 256]); sm = sb("sm", [128, 32])
    cnum = [sb("cnum%d" % i, [128, 386]) for i in range(2)]
    identb = sb("identb", [128, 128], BF16)
    P.copy(identb[:], identf)
    pi = 0
    for i in range(NQ):
        qrhs = qT[:, i, :]
        _attend(P, pl, nc, i, qrhs, dict(KC=KC, CV=CV, cmk_b=cmk_b, pT=pT, imp=imp, imp2=imp2, m8=m8, sel=sel, selT=selT, negm=negm,
                                          oTs=oTs, ytile=ytile, sm=sm, cnum=cnum, identb=identb, identf=identf, xall_b=xall_b, dm_b=dm_b, wm_b=wm_b,
                                          ksT=ksT, kwT=kwT, vsA=vsA, vwA=vwA, GT=GT, qc_t=qc_t, yb=yb, NCMP=NCMP))
    P.emit()
    return nc


def _bc4(ap):
    return ap.unsqueeze(1).broadcast_to([ap.shape[0], 4, ap.shape[1]])


def _attend(P, pl, nc, i, qrhs, T):
    KC, CV, pT = T['KC'], T['CV'], T['pT']
    NCMP = T['NCMP']
    identf, identb = T['identf'], T['identb']
    oTs = T['oTs']
    v4 = lambda t: t[:, :].rearrange("p (r q) -> p r q", r=4)
    ccb = min((16 * i + 6) // 128, (NCMP - 1) // 128)
    cn = [pl.banks[0], pl.banks[1], pl.banks[2], pl.banks[3]]
    rot = lambda: pl.banks[4 + (pl.nrot() % 4)]
    for cc in range(ccb + 1):
        ps = rot()
        P.mm(ps[:, :], KC[:, cc * 128:(cc + 1) * 128], qrhs)
        pt = pT[(pl.bi) % 3]
        P.actf(pt[:, :], ps[:, :], AF.Exp, scale=0.125)
        if cc >= ccb - 1:
            which = 1 if cc == ccb else 0
            mk = T['cmk_b'][:, (which * 8 + (i % 8)) * 128:(which * 8 + (i % 8) + 1) * 128]
            P.tt(v4(pt), v4(pt), _bc4(mk), ALU.mult)
        for r in range(4):
            P.mm(cn[r][:, 0:193], pt[:, r * 128:(r + 1) * 128], CV[:, cc, :], start=(cc == 0), stop=(cc == ccb))
    cnum = T['cnum']
    for r in range(4):
        P.copy(cnum[r // 2][:, (r % 2) * 193:(r % 2) * 193 + 193], cn[r][:, 0:193], eng=('act', 'dve')[r % 2])
    sm = T['sm']
    for r in range(4):
        P.ts(sm[:, r:r + 1], cnum[r // 2][:, (r % 2) * 193 + 64:(r % 2) * 193 + 65], 1e-30, None, ALU.add)
    P.op('dve', lambda e: e.reciprocal(sm[:, 4:8], sm[:, 0:4]), [sm[:, 0:4]], [sm[:, 4:8]])
    imp, imp2, m8, sel, selT, negm = T['imp'], T['imp2'], T['m8'], T['sel'], T['selT'], T['negm']
    for r in range(4):
        src = cnum[r // 2][:, (r % 2) * 193 + 65:(r % 2) * 193 + 193]
        if r == 0:
            P.ts(imp[:, :], src, sm[:, 4:5], None, ALU.mult)
        else:
            P.stt(imp[:, :], src, sm[:, 4 + r:5 + r], imp[:, :], ALU.mult, ALU.add)
    qc = T['qc_t']
    w0 = max(0, 4 * i - 1); wn = 4 * i + 4 - w0; t0c = w0 - (4 * i - 1)
    P.copy(imp2[:, :], imp[:, :], eng='dve')
    P.tt(imp2[:, w0:w0 + wn], imp[:, w0:w0 + wn], qc[:, t0c:t0c + wn], ALU.mult)
    P.tt(imp2[:, w0:w0 + wn], imp2[:, w0:w0 + wn], qc[:, 5 + t0c:5 + t0c + wn], ALU.add)
    if 4 * i + 4 < 128:
        P.memset(imp2[:, 4 * i + 4:128], -1e30)
    P.memset(imp2[:, 0:1], 1e30)
    P.op('dve', lambda e: e.max(out=m8[:, 0:8], in_=imp2[:, :]), [imp2[:, :]], [m8[:, 0:8]])
    P.op('dve', lambda e: e.match_replace(out=imp[:, :], in_to_replace=m8[:, 0:8], in_values=imp2[:, :], imm_value=-3e38), [m8[:, 0:8], imp2[:, :]], [imp[:, :]])
    P.op('dve', lambda e: e.max(out=m8[:, 8:16], in_=imp[:, :]), [imp[:, :]], [m8[:, 8:16]])
    P.ts(sel[:, :], imp2[:, :], m8[:, 15:16], None, ALU.is_ge)
    pst = rot()
    P.tr(pst[:, 0:128], sel[:, :], identf)
    P.copy(selT[:, :], pst[:, 0:128], eng='act')
    for r in range(4):
        P.ts(negm[:, r * 128:(r + 1) * 128], selT[:, :], -1.0, NEGM, ALU.add, ALU.mult, eng=('dve', 'pool')[r % 2])
    ksT, vsA, xall_b, dm_b = T['ksT'], T['vsA'], T['xall_b'], T['dm_b']
    po = pl.banks[0]
    nkb = 2 * i + 2
    for kb in range(nkb):
        ps = rot()
        P.mm(ps[:, :], ksT[:, kb * 128:(kb + 1) * 128], qrhs, start=True, stop=False)
        P.mm(ps[:, :], xall_b[:, kb, :], negm[:, :], start=False, stop=True)
        pt = pT[(pl.bi) % 3]
        P.actf(pt[:, :], ps[:, :], AF.Exp, scale=0.125)
        if kb >= nkb - 2:
            mk = dm_b[:, (kb - (nkb - 2)) * 128:(kb - (nkb - 2) + 1) * 128]
            P.tt(v4(pt), v4(pt), _bc4(mk), ALU.mult)
        P.mm(po[0:65, :], vsA[:, kb, :], pt[:, :], start=(kb == 0), stop=(kb == nkb - 1))
    P.copy(oTs[1][:, :], po[0:65, :], eng='act')
    kwT, vwA, wm_b = T['kwT'], T['vwA'], T['wm_b']
    pw = pl.banks[1]
    kbs = [(m, 2 * i - 4 + m) for m in range(6) if 2 * i - 4 + m >= 0]
    for n_, (m, kb) in enumerate(kbs):
        ps = rot()
        P.mm(ps[:, :], kwT[:, kb * 128:(kb + 1) * 128], qrhs)
        pt = pT[(pl.bi) % 3]
        P.actf(pt[:, :], ps[:, :], AF.Exp, scale=0.125)
        P.tt(v4(pt), v4(pt), _bc4(wm_b[:, m * 128:(m + 1) * 128]), ALU.mult)
        P.mm(pw[0:65, :], vwA[:, kb, :], pt[:, :], start=(n_ == 0), stop=(n_ == len(kbs) - 1))
    P.copy(oTs[2][:, :], pw[0:65, :], eng='dve')
    GT, ytile = T['GT'], T['ytile']
    for r in range(4):
        P.tt(sm[:, 8 + r:9 + r], sm[:, 4 + r:5 + r], GT[:, i, r * 3:r * 3 + 1], ALU.mult)
        P.ts(ytile[:, r * 64:(r + 1) * 64], cnum[r // 2][:, (r % 2) * 193:(r % 2) * 193 + 64], sm[:, 8 + r:9 + r], None, ALU.mult)
    for br in (1, 2):
        pb_ = pl.banks[br + 1]
        for r in range(4):
            P.mm(pb_[:, r * 65:(r + 1) * 65], oTs[br][0:65, r * 128:(r + 1) * 128], identf[0:65, 0:65])
        for r in range(4):
            P.ts(sm[:, 12 + r:13 + r], pb_[:, r * 65 + 64:r * 65 + 65], 1e-30, None, ALU.add)
        P.op('dve', lambda e: e.reciprocal(sm[:, 16:20], sm[:, 12:16]), [sm[:, 12:16]], [sm[:, 16:20]])
        for r in range(4):
            P.tt(sm[:, 20 + r:21 + r], sm[:, 16 + r:17 + r], GT[:, i, r * 3 + br:r * 3 + br + 1], ALU.mult)
            P.stt(ytile[:, r * 64:(r + 1) * 64], pb_[:, r * 65:r * 65 + 64], sm[:, 20 + r:21 + r], ytile[:, r * 64:(r + 1) * 64], ALU.mult, ALU.add)
    P.dma(T['yb'][i * 128:(i + 1) * 128, :], ytile[:, :], q='pool')


def nsa_inputs(d, ntok=SEQ):
    l = 0
    W = d['w_in'][l][:, 1664:1664 + 1304]
    cst = nsa_consts(ntok)
    NQ = ntok // 256
    perm = np.arange(64); perm[0:8] = np.arange(8, 16); perm[8:16] = np.arange(0, 8)
    p = np.arange(128)
    tri_le = cst['tri'][:, 0:128]; tri_gt = cst['tri'][:, 128:256]
    one = np.ones((128, 128), np.float32); zero = np.zeros((128, 128), np.float32)
    w1k = d['nsa_ck_w1'][l].reshape(32, 64, 256).transpose(1, 0, 2).reshape(64, 32 * 256)
    w1v = d['nsa_cv_w1'][l].reshape(32, 64, 256).transpose(1, 0, 2).reshape(64, 32 * 256)
    w2k = d['nsa_ck_w2'][l].reshape(2, 128, 64).transpose(1, 0, 2).reshape(128, 128)
    w2v = d['nsa_cv_w2'][l].reshape(2, 128, 64).transpose(1, 0, 2).reshape(128, 128)
    pek = np.repeat(d['nsa_pe_k'][l].T[:, :, None], 64, axis=2).reshape(64, 32 * 64)
    pev = np.repeat(d['nsa_pe_v'][l].T[:, :, None], 64, axis=2).reshape(64, 32 * 64)
    maps = []
    for c in range(8):
        b, g, par = c // 4, (c % 4) // 2, c % 2
        qcols = [256 * g + r * 64 + np.arange(64) for r in range(4)]
        kvc = lambda idx: 512 + idx * 128 + g * 64 + np.arange(64)
        roped = qcols + [kvc(0), kvc(2), kvc(4)]
        colsA = np.concatenate(roped + [cg[perm] for cg in roped] + [kvc(1)])
        gcols = np.array([1280 + (4 * g + r) * 3 + cc for r in range(4) for cc in range(3)])
        colsB = np.concatenate([kvc(3), kvc(5), gcols])
        own = np.concatenate([np.arange((2 * i + par) * 128, (2 * i + par + 1) * 128) for i in range(NQ)])
        xTb = np.ascontiguousarray(d['x'][b, :ntok].T)
        cmk_full = cst['cmk'].reshape(128, 32, 128)
        cmkc = np.stack([cmk_full[:, wh * 16 + (2 * m + par) % 16, :] for wh in range(2) for m in range(8)], axis=1)
        if par == 0:
            dmask = np.concatenate([tri_le, zero], axis=1); wmask = np.concatenate([tri_gt, one, one, one, tri_le, zero], axis=1)
        else:
            dmask = np.concatenate([one, tri_le], axis=1); wmask = np.concatenate([zero, tri_gt, one, one, one, tri_le], axis=1)
        qcol = np.zeros((128, 10), np.float32)
        hi = (p >= 64).astype(np.float32); lo = 1.0 - hi
        for w in range(5):
            dl = w - 1 - 2 * par
            if dl < -1:
                qcol[:, w] = 1.0
            elif dl == -1:
                qcol[:, w] = hi; qcol[:, 5 + w] = lo * 1e30
            elif dl == 0:
                qcol[:, 5 + w] = 1e30
            elif dl == 1:
                qcol[:, 5 + w] = hi * 1e30 - lo * 1e30
            else:
                qcol[:, 5 + w] = -1e30
        m = dict(xT=xTb, wA=W[:, colsA], wB=W[:, colsB], cosT=cst['cosT'], sinT=cst['sinT'], tri=cst['tri'], cmk=cmkc.reshape(128, 2048),
                 ovc=cst['ovc'], xall=cst['xall'], qcol=qcol, dmask=dmask, wmask=wmask, w1k=w1k, w1v=w1v, w2k=w2k, w2v=w2v, pek=pek, pev=pev,
                 xq=xTb[:, own], cosq=cst['cosT'][:, own], sinq=cst['sinT'][:, own])
        maps.append({k: np.ascontiguousarray(v, dtype=np.float32) for k, v in m.items()})
    return maps


def run_nsa(d):
    if 'nsa' not in _NC_CACHE:
        _NC_CACHE['nsa'] = build_nsa()
    res = run_bass_kernel_spmd(_NC_CACHE['nsa'], nsa_inputs(d), core_ids=list(range(8)))
    yb = np.zeros((2, SEQ, 512), np.float32)
    for c in range(8):
        b, g, par = c // 4, (c % 4) // 2, c % 2
        o = res.results[c]['yb'].reshape(SEQ // 256, 128, 256)
        yb[b].reshape(SEQ // 256, 2, 128, 512)[:, par, :, g * 256:(g + 1) * 256] = o
    return yb


def kernel(**inputs):
    d = {k: np.asarray(v) for k, v in inputs.items()}
    ya = run_rwkv(d)
    yb = run_nsa(d)
    out = run_tail(d, ya, yb)
    return out.astype(np.float32)
```

```python
import numpy as np
import concourse.bass as bass
import concourse.mybir as mybir
from concourse.bass_utils import run_bass_kernel_spmd

F32 = mybir.dt.float32
BF16 = mybir.dt.bfloat16
AF = mybir.ActivationFunctionType
ALU = mybir.AluOpType
AX = mybir.AxisListType

SAME_ENG_SYNC = False
NSLOT = 8


def _region(ap):
    t = ap.tensor
    shp = tuple(t.shape)
    sp = str(ap.space)
    off = int(ap.offset)
    pat = [(int(s), int(c)) for s, c in ap.ap]
    if 'DRAM' in sp.upper() or 'HBM' in sp.upper():
        ext = sum((c - 1) * abs(s) for s, c in pat)
        return (ap.name, 0, 0, off, off + ext)
    if 'PSUM' in sp.upper():
        return (ap.name, 0, 127, 0, 10 ** 9)
    fs = 1
    for d in shp[1:]:
        fs *= int(d)
    p0 = off // fs
    f0 = off % fs
    ps, pc = pat[0]
    p1 = p0 + (pc - 1) * (ps // fs if fs else 0)
    ext = sum((c - 1) * abs(s) for s, c in pat[1:])
    return (ap.name, p0, p1, f0, f0 + ext)


def _overlap(a, b):
    return not (a[2] < b[1] or b[2] < a[1] or a[4] < b[3] or b[4] < a[3])


def _contains(a, b):
    return a[1] <= b[1] and a[2] >= b[2] and a[3] <= b[3] and a[4] >= b[4]


class Prog:
    ENGS = ['pe', 'act', 'dve', 'pool', 'sp']

    def __init__(self, nc):
        self.nc = nc
        self.stream = {e: [] for e in self.ENGS}
        self.count = {e: 0 for e in self.ENGS}
        self.dcount = {'sp': 0, 'pool': 0, 'act': 0}
        self.hist = {}
        self.known = {e: {} for e in self.ENGS}
        self.nops = 0

    def _deps(self, eng, reads, writes, is_dma=False):
        deps = {}

        def add(tok):
            k, v, e = tok
            if e == eng and eng == 'pe' and not k.startswith('d_') and not is_dma:
                return
            if deps.get(k, (0,))[0] < v:
                deps[k] = (v, e)
        rr = [_region(a) for a in reads]
        wr = [_region(a) for a in writes]
        for r in rr:
            psum = (r[4] == 10 ** 9)
            for (reg, tok, isw) in self.hist.get(r[0], ()):
                if isw and _overlap(reg, r):
                    add(tok)
                elif psum and not isw and tok[2] != eng:
                    add(tok)
        for w in wr:
            for (reg, tok, isw) in self.hist.get(w[0], ()):
                if _overlap(reg, w):
                    add(tok)
        return deps, rr, wr

    def _update(self, tok, rr, wr):
        for w in wr:
            h = self.hist.setdefault(w[0], [])
            h[:] = [x for x in h if not _contains(w, x[0])]
            h.append((w, tok, True))
        for r in rr:
            h = self.hist.setdefault(r[0], [])
            h[:] = [x for x in h if not (not x[2] and x[1][2] == tok[2] and x[1][0] == tok[0] and x[0] == r)]
            h.append((r, tok, False))

    def _waits(self, eng, deps):
        waits = []
        kn = self.known[eng]
        for k, (v, e) in deps.items():
            if kn.get(k, 0) < v:
                kn[k] = v
                waits.append((k, v))
        return waits

    def op(self, eng, fn, reads, writes):
        deps, rr, wr = self._deps(eng, reads, writes)
        waits = self._waits(eng, deps)
        self.count[eng] += 1
        tok = ('c_' + eng, self.count[eng], eng)
        self.stream[eng].append((waits, fn, tok, 1))
        self._update(tok, rr, wr)
        self.nops += 1
        return tok

    def dma(self, out, in_, q='sp', **kw):
        eng = q
        deps, rr, wr = self._deps(eng, [in_], [out], is_dma=True)
        i = self.dcount[q]
        self.dcount[q] += 1
        slot = i % NSLOT
        val = 16 * (i // NSLOT + 1)
        key = 'd_%s_%d' % (q, slot)
        if val > 16:
            if deps.get(key, (0,))[0] < val - 16:
                deps[key] = (val - 16, q)
        waits = self._waits(eng, deps)
        tok = (key, val, eng)

        def fn(e, out=out, in_=in_, kw=kw):
            return e.dma_start(out=out, in_=in_, **kw)
        self.stream[eng].append((waits, fn, tok, 16))
        self._update(tok, rr, wr)
        self.nops += 1
        return tok

    def dma_like(self, fn, reads, writes, q='pool'):
        eng = q
        deps, rr, wr = self._deps(eng, reads, writes, is_dma=True)
        i = self.dcount[q]
        self.dcount[q] += 1
        slot = i % NSLOT
        val = 16 * (i // NSLOT + 1)
        key = 'd_%s_%d' % (q, slot)
        if val > 16:
            if deps.get(key, (0,))[0] < val - 16:
                deps[key] = (val - 16, q)
        waits = self._waits(eng, deps)
        tok = (key, val, eng)
        self.stream[eng].append((waits, fn, tok, 16))
        self._update(tok, rr, wr)
        self.nops += 1
        return tok

    def mm(self, out, lhsT, rhs, start=True, stop=True, **kw):
        return self.op('pe', lambda e: e.matmul(out, lhsT, rhs, start=start, stop=stop, **kw),
                       [lhsT, rhs] + ([] if start else [out]), [out])

    def tr(self, out, in_, ident):
        return self.op('pe', lambda e: e.transpose(out, in_, ident), [in_, ident], [out])

    def actf(self, out, in_, func, bias=None, scale=None, accum_out=None, eng='act'):
        kw = {}
        rd = [in_]
        wr = [out]
        if bias is not None:
            kw['bias'] = bias
            if not isinstance(bias, (int, float)):
                rd.append(bias)
        if scale is not None:
            kw['scale'] = scale
            if not isinstance(scale, (int, float)):
                rd.append(scale)
        if accum_out is not None:
            kw['accum_out'] = accum_out
            wr.append(accum_out)
        return self.op(eng, lambda e: e.activation(out, in_, func, **kw), rd, wr)

    def tt(self, out, in0, in1, op, eng='dve'):
        return self.op(eng, lambda e: e.tensor_tensor(out, in0, in1, op), [in0, in1], [out])

    def ts(self, out, in0, s1, s2, op0, op1=None, accum_out=None, eng='dve'):
        rd = [in0] + [s for s in (s1, s2) if s is not None and not isinstance(s, (int, float))]
        wr = [out] + ([accum_out] if accum_out is not None else [])
        kw = {}
        if op1 is not None:
            kw['op1'] = op1
        if accum_out is not None:
            kw['accum_out'] = accum_out
        return self.op(eng, lambda e: e.tensor_scalar(out, in0, s1, s2, op0, **kw), rd, wr)

    def stt(self, out, in0, scalar, in1, op0, op1, eng='dve'):
        rd = [in0, in1] + ([scalar] if not isinstance(scalar, (int, float)) else [])
        return self.op(eng, lambda e: e.scalar_tensor_tensor(out, in0, scalar, in1, op0, op1), rd, [out])

    def copy(self, out, in_, eng='dve'):
        if eng == 'act':
            return self.op('act', lambda e: e.copy(out, in_), [in_], [out])
        return self.op(eng, lambda e: e.tensor_copy(out, in_), [in_], [out])

    def memset(self, ap, v, eng='dve'):
        return self.op(eng, lambda e: e.memset(ap, v), [], [ap])

    def emit(self):
        nc = self.nc
        import contextlib
        es = contextlib.ExitStack()
        sems = {}

        def sem(k):
            if k not in sems:
                sems[k] = es.enter_context(nc.semaphore(k))
            return sems[k]
        for e in self.ENGS:
            sem('c_' + e)
        for q in self.dcount:
            for s in range(NSLOT):
                sem('d_%s_%d' % (q, s))
        final = []
        for e in self.ENGS:
            if e != 'sp' and self.count[e] > 0:
                final.append(('c_' + e, self.count[e]))
        for q, n in self.dcount.items():
            for s in range(NSLOT):
                cnt = (n - s + NSLOT - 1) // NSLOT if n > s else 0
                if cnt > 0:
                    final.append(('d_%s_%d' % (q, s), 16 * cnt))
        streams = self.stream

        def run(engname, e):
            for (waits, fn, tok, inc) in streams[engname]:
                for (k, v) in waits:
                    e.wait_ge(sem(k), v)
                ins = fn(e)
                ins.then_inc(sem(tok[0]), inc)
            if engname == 'sp':
                for (k, v) in final:
                    e.wait_ge(sem(k), v)
        with nc.Block() as block:
            @block.tensor
            def _(e):
                run('pe', e)

            @block.scalar
            def _(e):
                run('act', e)

            @block.vector
            def _(e):
                run('dve', e)

            @block.gpsimd
            def _(e):
                run('pool', e)

            @block.sync
            def _(e):
                run('sp', e)
        es.close()


D = 1024
ALPHA = 2.0 ** 0.25
LN_EPS = 1e-5
DFF = 2816
NT = 2050
NJ = DFF // 128


class Pools:
    def __init__(self, P, nc):
        self.P = P
        self.nc = nc
        self.banks = [nc.alloc_psum_tensor("bank%d" % i, [128, 512], F32) for i in range(8)]
        self.bi = 0
        self.stg = [nc.alloc_sbuf_tensor("stg%d" % i, [128, 2048], F32) for i in range(2)]
        self.si = 0
        self.ci = 0

    def bank(self):
        b = self.banks[self.bi % 8]
        self.bi += 1
        return b

    def nrot(self):
        self.bi += 1
        return self.bi

    def load_cast(self, dst, src, q='sp'):
        P = self.P
        shp = dst.shape
        npart = shp[0]
        if len(shp) == 2:
            n = shp[1]
            step = 2048
            for c0 in range(0, n, step):
                c1 = min(n, c0 + step)
                st = self.stg[self.si % 2]
                self.si += 1
                P.dma(st[0:npart, 0:c1 - c0], src[:, c0:c1], q=q)
                self._cast(dst[:, c0:c1], st[0:npart, 0:c1 - c0])
        else:
            a, n = shp[1], shp[2]
            assert n <= 2048
            per = max(1, 2048 // n)
            for a0 in range(0, a, per):
                a1 = min(a, a0 + per)
                st = self.stg[self.si % 2]
                self.si += 1
                sv = st[0:npart, 0:(a1 - a0) * n].rearrange("p (a n) -> p a n", n=n)
                P.dma(sv, src[:, a0:a1, :], q=q)
                self._cast(dst[:, a0:a1, :], sv)

    def _cast(self, dst, src):
        engs = ['pool', 'dve', 'act', 'pool']
        e = engs[self.ci % len(engs)]
        self.ci += 1
        self.P.copy(dst, src, eng=e)


def chan_ln(P, pl, res, resb, gcol, bcol, N, ones_b, scr_b, scr_sq, tmp):
    pm = pl.bank()
    psq = pl.bank()
    for c in range(8):
        P.actf(scr_b[:, c, 0:N], res[:, c, 0:N], AF.Copy)
        P.actf(scr_sq[:, c, 0:N], res[:, c, 0:N], AF.Square)
    for c in range(8):
        P.mm(pm[:, 0:N], ones_b[:], scr_b[:, c, 0:N], start=(c == 0), stop=(c == 7))
    for c in range(8):
        P.mm(psq[:, 0:N], ones_b[:], scr_sq[:, c, 0:N], start=(c == 0), stop=(c == 7))
    mean, msq, var, rstd = [t[:, 0:N] for t in tmp[:4]]
    P.actf(mean, pm[:, 0:N], AF.Copy, scale=1.0 / D)
    P.tt(msq, mean, mean, ALU.mult)
    P.stt(var, psq[:, 0:N], 1.0 / D, msq, ALU.mult, ALU.subtract)
    P.ts(var, var, LN_EPS, None, ALU.add)
    P.actf(var, var, AF.Sqrt)
    P.op('dve', lambda e: e.reciprocal(rstd, var), [var], [rstd])
    for c in range(8):
        P.tt(res[:, c, 0:N], res[:, c, 0:N], mean, ALU.subtract)
        P.tt(res[:, c, 0:N], res[:, c, 0:N], rstd, ALU.mult)
        P.actf(res[:, c, 0:N], res[:, c, 0:N], AF.Identity, bias=bcol[:, c:c + 1], scale=gcol[:, c:c + 1])
        P.copy(resb[:, c, 0:N], res[:, c, 0:N], eng='pool')


class _Stop(Exception):
    pass


def build_tail(stop=99):
    nc = bass.Bass("TRN2", target_bir_lowering=False)
    try:
        return _build_tail(nc, stop)
    except _Stop:
        return nc


def _build_tail(nc, stop):
    def ck(code):
        if stop == code:
            P.emit()
            raise _Stop()
    dr = lambda n, s, k="ExternalInput": nc.dram_tensor(n, s, F32, kind=k).ap()
    xT = dr("xT", [D, NT]); yaT = dr("yaT", [512, NT]); ybT = dr("ybT", [512, NT]); memT = dr("memT", [D, 256])
    wg = dr("wg", [D, 2048]); pa = dr("pa", [512, D]); pb = dr("pb", [512, D]); wo = dr("wo", [D, D])
    wq = dr("wq", [D, D]); wk = dr("wk", [D, D]); wv = dr("wv", [D, D]); xwo = dr("xwo", [D, D])
    wup = dr("wup", [D, 2 * DFF]); wdn = dr("wdn", [DFF, D])
    lnp = dr("lnp", [128, 48]); cw = dr("cw", [128, 44 * 3]); cb = dr("cb", [128, 44]); hmask = dr("hmask", [128, 1])
    outT = dr("outT", [D, 2048], "ExternalOutput")
    x2s = dr("x2s", [D, NT], "Internal")
    chunked = lambda ap: ap.rearrange("(c p) n -> p c n", p=128)

    P = Prog(nc)
    pl = Pools(P, nc)
    sb = lambda n, s, dt=F32: nc.alloc_sbuf_tensor(n, s, dt)
    NA = 256
    arena = sb("arena", [128, 49152], BF16)
    lnp_t = sb("lnp_t", [128, 48]); cw_t = sb("cw_t", [128, 132]); cb_t = sb("cb_t", [128, 44]); hm_t = sb("hm_t", [128, 1])
    ones_b = sb("ones_b", [128, 128], BF16)
    res = sb("res", [128, 8, 258]); resb = sb("resb", [128, 8, 258], BF16)
    yab = sb("yab", [128, 4, NA], BF16); ybb = sb("ybb", [128, 4, NA], BF16)
    actb2 = sb("actb2", [128, 8, NA], BF16); qT = sb("qT", [128, 8, NA], BF16)
    scr_b = sb("scr_b", [128, 8, 258], BF16); scr_sq = sb("scr_sq", [128, 8, 258], BF16)
    tmp = [sb("tmp%d" % i, [128, 258]) for i in range(8)]
    KT = sb("KT", [128, 8, 256], BF16); V = sb("V", [128, 2, D], BF16); memb = sb("memb", [128, 8, 256], BF16)
    pbuf = [sb("pbuf%d" % i, [128, NA], BF16) for i in range(2)]
    wdn_r = [sb("wdn_r%d" % i, [128, NJ, 128], BF16) for i in range(2)]
    gall = sb("gall", [128, NJ, 256], BF16)
    hs = [sb("hs%d" % i, [128, 258]) for i in range(2)]
    ostg = [sb("ostg%d" % i, [128, 256]) for i in range(2)]

    P.dma(lnp_t[:], lnp); P.dma(cw_t[:], cw); P.dma(cb_t[:], cb); P.dma(hm_t[:], hmask)
    P.memset(ones_b[:], 1.0)

    if stop <= 0:
        P.emit(); return nc
    def A(off, kc, n):
        return arena[:, off:off + kc * n].rearrange("p (c n) -> p c n", n=n)
    Wk_b = A(0, 8, D); Wv_b = A(8192, 8, D)
    pl.load_cast(memb[:], chunked(memT))
    pl.load_cast(Wk_b, chunked(wk)); pl.load_cast(Wv_b, chunked(wv), q='pool')
    if stop == 1:
        P.emit(); return nc
    for oc in range(8):
        ps = pl.bank()
        for kc in range(8):
            P.mm(ps[:, 0:256], Wk_b[:, kc, oc * 128:(oc + 1) * 128], memb[:, kc, :], start=(kc == 0), stop=(kc == 7))
        P.copy(KT[:, oc, :], ps[:, 0:256], eng='act')
    for mc in range(2):
        for hf in range(2):
            ps = pl.bank()
            for kc in range(8):
                P.mm(ps[:, :], memb[:, kc, mc * 128:(mc + 1) * 128], Wv_b[:, kc, hf * 512:(hf + 1) * 512], start=(kc == 0), stop=(kc == 7))
            P.copy(V[:, mc, hf * 512:(hf + 1) * 512], ps[:, :], eng='dve')
    if stop <= 1:
        P.dma(outT[0:128, 0:256], KT[:, 0, :].bitcast(F32)[:, 0:128] if False else tmp[0][:, 0:256]); P.emit(); return nc
    Wg_b = A(0, 8, 2048); Pa_b = A(16384, 4, D); Pb_b = A(20480, 4, D); Wo_b = A(24576, 8, D)
    Wq_b = A(32768, 8, D); XWo_b = A(40960, 8, D)
    pl.load_cast(Pa_b, chunked(pa)); pl.load_cast(Pb_b, chunked(pb), q='pool')
    pl.load_cast(Wo_b, chunked(wo)); pl.load_cast(Wq_b, chunked(wq), q='pool'); pl.load_cast(XWo_b, chunked(xwo))
    pl.load_cast(Wg_b, chunked(wg), q='pool')
    g1, b1, g2, b2, g3, b3 = [lnp_t[:, i * 8:(i + 1) * 8] for i in range(6)]

    if stop <= 2:
        P.emit(); return nc
    tiles = [(0, 2)] + [(2 + i * NA, NA) for i in range(8)]
    if stop <= 3:
        tiles = tiles[:1]
    if stop == 4 or 40 < stop < 50:
        tiles = tiles[1:2]
    xTc, yaTc, ybTc, x2sc, outTc = chunked(xT), chunked(yaT), chunked(ybT), chunked(x2s), chunked(outT)
    for (c0, N) in tiles:
        P.dma(res[:, :, 0:N], xTc[:, :, c0:c0 + N])
        st = pl.stg[pl.si % 2]; pl.si += 1
        sv = st[:, 0:8 * N].rearrange("p (a n) -> p a n", n=N)
        P.dma(sv[:, 0:4, :], yaTc[:, :, c0:c0 + N], q='pool'); P.dma(sv[:, 4:8, :], ybTc[:, :, c0:c0 + N], q='pool')
        P.copy(yab[:, :, 0:N], sv[:, 0:4, :], eng='pool'); P.copy(ybb[:, :, 0:N], sv[:, 4:8, :], eng='pool')
        for c in range(8):
            P.copy(resb[:, c, 0:N], res[:, c, 0:N], eng='act' if c % 2 else 'dve')
        ck(41)
        for oc in range(8):
            osl = slice(oc * 128, (oc + 1) * 128)
            za = pl.bank(); ga = pl.bank()
            for kc in range(4):
                P.mm(za[:, 0:N], Pa_b[:, kc, osl], yab[:, kc, 0:N], start=(kc == 0), stop=(kc == 3))
            for kc in range(4):
                P.mm(za[:, 256:256 + N], Pb_b[:, kc, osl], ybb[:, kc, 0:N], start=(kc == 0), stop=(kc == 3))
            for kc in range(8):
                P.mm(ga[:, 0:N], Wg_b[:, kc, osl], resb[:, kc, 0:N], start=(kc == 0), stop=(kc == 7))
            for kc in range(8):
                P.mm(ga[:, 256:256 + N], Wg_b[:, kc, 1024 + oc * 128:1024 + (oc + 1) * 128], resb[:, kc, 0:N], start=(kc == 0), stop=(kc == 7))
            sa, sbb, t1, t2 = tmp[4][:, 0:N], tmp[5][:, 0:N], tmp[6][:, 0:N], tmp[7][:, 0:N]
            P.actf(sa, ga[:, 0:N], AF.Sigmoid)
            P.actf(sbb, ga[:, 256:256 + N], AF.Sigmoid)
            P.tt(t1, sa, za[:, 0:N], ALU.mult)
            P.tt(t2, sbb, za[:, 256:256 + N], ALU.mult)
            P.tt(actb2[:, oc, 0:N], t1, t2, ALU.add)
        ck(42)
        for oc in range(8):
            osl = slice(oc * 128, (oc + 1) * 128)
            ps = pl.bank()
            for kc in range(8):
                P.mm(ps[:, 0:N], Wo_b[:, kc, osl], actb2[:, kc, 0:N], start=(kc == 0), stop=(kc == 7))
            P.stt(res[:, oc, 0:N], res[:, oc, 0:N], ALPHA, ps[:, 0:N], ALU.mult, ALU.add)
        ck(43)
        chan_ln(P, pl, res, resb, g1, b1, N, ones_b, scr_b, scr_sq, tmp)
        ck(44)
        for oc in range(8):
            osl = slice(oc * 128, (oc + 1) * 128)
            ps = pl.bank()
            for kc in range(8):
                P.mm(ps[:, 0:N], Wq_b[:, kc, osl], resb[:, kc, 0:N], start=(kc == 0), stop=(kc == 7))
            P.copy(qT[:, oc, 0:N], ps[:, 0:N], eng='act' if oc % 2 else 'dve')
        for hh in range(4):
            pden = pl.bank()
            for mc in range(2):
                ps = pl.bank()
                for dc in range(2):
                    P.mm(ps[:, 0:N], KT[:, 2 * hh + dc, mc * 128:(mc + 1) * 128], qT[:, 2 * hh + dc, 0:N], start=(dc == 0), stop=(dc == 1))
                P.actf(pbuf[mc][:, 0:N], ps[:, 0:N], AF.Exp, scale=1.0 / 16.0)
            for mc in range(2):
                P.mm(pden[:, 0:N], ones_b[:], pbuf[mc][:, 0:N], start=(mc == 0), stop=(mc == 1))
            rec = tmp[4][:, 0:N]
            P.op('dve', lambda e, rec=rec, pden=pden, N=N: e.reciprocal(rec, pden[:, 0:N]), [pden[:, 0:N]], [rec])
            for dc in range(2):
                po = pl.bank()
                for mc in range(2):
                    P.mm(po[:, 0:N], V[:, mc, (2 * hh + dc) * 128:(2 * hh + dc + 1) * 128], pbuf[mc][:, 0:N], start=(mc == 0), stop=(mc == 1))
                P.tt(actb2[:, 2 * hh + dc, 0:N], po[:, 0:N], rec, ALU.mult)
        for oc in range(8):
            osl = slice(oc * 128, (oc + 1) * 128)
            ps = pl.bank()
            for kc in range(8):
                P.mm(ps[:, 0:N], XWo_b[:, kc, osl], actb2[:, kc, 0:N], start=(kc == 0), stop=(kc == 7))
            P.stt(res[:, oc, 0:N], res[:, oc, 0:N], ALPHA, ps[:, 0:N], ALU.mult, ALU.add)
        ck(45)
        chan_ln(P, pl, res, resb, g2, b2, N, ones_b, scr_b, scr_sq, tmp)
        ck(46)
        P.dma(x2sc[:, :, c0:c0 + N], res[:, :, 0:N], q='pool')

    if stop <= 5 or 40 < stop < 50:
        P.emit(); return nc
    Wup_b = A(0, 8, 2 * DFF)
    wupc = chunked(wup)
    wdnc = chunked(wdn)
    for kc in range(8):
        for h0 in range(0, 2 * DFF, 2048):
            h1 = min(2 * DFF, h0 + 2048)
            pl.load_cast(Wup_b[:, kc, h0:h1], wupc[:, kc, h0:h1], q='sp' if kc % 2 else 'pool')
    cw3 = cw_t[:].rearrange("p (c k) -> p c k", k=3)
    for ti in range(8):
        c0 = ti * 256
        P.dma(res[:, :, 0:258], x2sc[:, :, c0:c0 + 258])
        for c in range(8):
            P.copy(resb[:, c, 0:258], res[:, c, 0:258], eng='act' if c % 2 else 'dve')
        for j in range(NJ):
            hp = pl.bank(); hv = pl.bank()
            for kc in range(8):
                P.mm(hp[:, 0:258], Wup_b[:, kc, j * 128:(j + 1) * 128], resb[:, kc, 0:258], start=(kc == 0), stop=(kc == 7))
            for kc in range(8):
                P.mm(hv[:, 0:258], Wup_b[:, kc, DFF + j * 128:DFF + (j + 1) * 128], resb[:, kc, 0:258], start=(kc == 0), stop=(kc == 7))
            srcs = []
            for (hsrc, k) in ((hp, 0), (hv, 1)):
                if ti == 0:
                    hb_ = hs[k]
                    P.copy(hb_[:, 0:258], hsrc[:, 0:258], eng='act')
                    P.ts(hb_[:, 0:2], hb_[:, 0:2], hm_t[:, 0:1], None, ALU.mult)
                    srcs.append(hb_)
                else:
                    srcs.append(hsrc)
            outs = []
            for k, (src, ch) in enumerate(zip(srcs, (j, NJ + j))):
                t = tmp[k * 2]; t2 = tmp[k * 2 + 1]
                P.actf(t[:, 0:256], src[:, 2:258], AF.Identity, bias=cb_t[:, ch:ch + 1], scale=cw3[:, ch, 2:3])
                P.stt(t2[:, 0:256], src[:, 1:257], cw3[:, ch, 1:2], t[:, 0:256], ALU.mult, ALU.add)
                P.stt(t[:, 0:256], src[:, 0:256], cw3[:, ch, 0:1], t2[:, 0:256], ALU.mult, ALU.add)
                outs.append(t)
            sg = tmp[4]
            P.actf(sg[:, 0:256], outs[0][:, 0:256], AF.Silu)
            P.tt(gall[:, j, :], sg[:, 0:256], outs[1][:, 0:256], ALU.mult)
        for oc in range(8):
            wr = wdn_r[oc % 2]
            pl.load_cast(wr[:, :, :], wdnc[:, :, oc * 128:(oc + 1) * 128], q='sp')
            a = pl.bank()
            for j in range(NJ):
                P.mm(a[:, 0:256], wr[:, j, :], gall[:, j, :], start=(j == 0), stop=(j == NJ - 1))
            P.stt(res[:, oc, 2:258], res[:, oc, 2:258], ALPHA, a[:, 0:256], ALU.mult, ALU.add)
        res_v = res[:, :, 2:258]; resb_v = resb[:, :, 2:258]
        chan_ln(P, pl, res_v, resb_v, g3, b3, 256, ones_b, scr_b, scr_sq, tmp)
        P.dma(outTc[:, :, ti * 256:(ti + 1) * 256], res[:, :, 2:258], q='pool')
    P.emit()
    return nc


def tail_inputs(d, ya, yb):
    l = 0
    x = d['x']
    maps = []
    RC, NC_ = 1664, 1304
    wg = np.ascontiguousarray(d['w_in'][l][:, RC + NC_:])
    lnp = np.concatenate([d[k][l].reshape(8, 128).T for k in ('ln1_g', 'ln1_b', 'ln2_g', 'ln2_b', 'ln3_g', 'ln3_b')], axis=1)
    cw = np.ascontiguousarray(d['ffn_conv_w'][l].reshape(3, 44, 128).transpose(2, 1, 0)).reshape(128, 132)
    cb = np.ascontiguousarray(d['ffn_conv_b'][l].reshape(44, 128).T)
    common = dict(wg=wg, pa=d['merge_p_a'][l], pb=d['merge_p_b'][l], wo=d['mix_w_o'][l], wq=d['xa_wq'][l], wk=d['xa_wk'][l],
                  wv=d['xa_wv'][l], xwo=d['xa_wo'][l], wup=d['ffn_w_up'][l], wdn=d['ffn_w_down'][l],
                  lnp=np.ascontiguousarray(lnp), cw=cw, cb=cb)
    common = {k: np.ascontiguousarray(v, dtype=np.float32) for k, v in common.items()}

    def halo_T(a, b, t0):
        C = a.shape[-1]
        o = np.zeros((C, NT), np.float32)
        lo = max(0, t0 - 2)
        o[:, 2 - (t0 - lo):] = a[b, lo:t0 + 2048].T
        return o
    for c in range(8):
        b, t0 = c // 4, (c % 4) * 2048
        m = dict(common)
        m['xT'] = halo_T(x, b, t0); m['yaT'] = halo_T(ya, b, t0); m['ybT'] = halo_T(yb, b, t0)
        m['memT'] = np.ascontiguousarray(d['mem'][b].T)
        m['hmask'] = np.full((128, 1), 0.0 if t0 == 0 else 1.0, np.float32)
        maps.append(m)
    return maps


_NC_CACHE = {}


def run_tail(d, ya, yb):
    if 'tail' not in _NC_CACHE:
        _NC_CACHE['tail'] = build_tail()
    nc = _NC_CACHE['tail']
    res = run_bass_kernel_spmd(nc, tail_inputs(d, ya, yb), core_ids=list(range(8)))
    out = np.zeros((2, 8192, D), np.float32)
    for c in range(8):
        b, t0 = c // 4, (c % 4) * 2048
        out[b, t0:t0 + 2048] = res.results[c]['outT'].T
    return out


SEQ = 8192
GN_EPS = 64e-5
CH = 64
NCH = 128 // CH
LV = 5


def rwkv_consts():
    idx = np.arange(128)
    same = (idx[:, None] // CH == idx[None, :] // CH)
    m_strict = (same & (idx[:, None] < idx[None, :])).astype(np.float32)
    m_incl = (same & (idx[:, None] <= idx[None, :])).astype(np.float32)
    tri_gt = (same & (idx[:, None] > idx[None, :])).astype(np.float32)
    cind = (idx[:, None] // CH == np.arange(4)[None, :]).astype(np.float32)
    ident = np.eye(128, dtype=np.float32)
    return np.concatenate([ident, m_incl, tri_gt, m_strict, m_incl, m_strict.T, m_strict.T, cind, np.zeros((128, 60), np.float32)], axis=1).astype(np.float32)


def build_rwkv(ntok=SEQ, stop=99):
    nc = bass.Bass("TRN2", target_bir_lowering=False)
    try:
        return _build_rwkv(nc, ntok, stop)
    except _Stop:
        return nc


def _build_rwkv(nc, ntok, stop):
    def ck(code):
        if stop == code:
            P.emit()
            raise _Stop()
    dr = lambda n, s, k="ExternalInput": nc.dram_tensor(n, s, F32, kind=k).ap()
    xT = dr("xT", [D, ntok]); w_r = dr("w_r", [D, 512]); cmat = dr("cmat", [128, 960])
    pcol = dr("pcol", [128, 12])
    w2a2 = dr("w2a2", [64, 256])
    w0a0 = dr("w0a0", [128, 256])
    yT = dr("yT", [128, ntok], "ExternalOutput")
    chunked = lambda ap: ap.rearrange("(c p) n -> p c n", p=128)
    P = Prog(nc)
    pl = Pools(P, nc)
    sb = lambda n, s, dt=F32: nc.alloc_sbuf_tensor(n, s, dt)
    cm = sb("cm", [128, 960]); pc = sb("pc", [128, 12]); wa = sb("wa", [64, 256]); w0 = sb("w0", [128, 256])
    P.dma(cm[:], cmat); P.dma(pc[:], pcol); P.dma(wa[:], w2a2); P.dma(w0[:], w0a0)
    ident = cm[:, 0:128]; TriInc = cm[:, 128:256]; TriGt = cm[:, 256:384]
    MSK1 = cm[:, 384:640]; Mincl = cm[:, 512:640]; MSK3 = cm[:, 640:896]; CInd = cm[:, 896:960]
    dkk = sb("dkk", [128, 128]); dka = sb("dka", [128, 128]); drk = sb("drk", [128, 128])
    P.ts(dkk[:], ident, pc[:, 4:5], None, ALU.mult)
    P.ts(dka[:], ident, pc[:, 5:6], None, ALU.mult)
    P.ts(drk[:], ident, pc[:, 6:7], None, ALU.mult)
    wrb = sb("wrb", [128, 8, 512], BF16)
    pl.load_cast(wrb[:], chunked(w_r))
    xt32 = [sb("xt32_%d" % i, [128, 8, 512]) for i in range(1)]
    xb = sb("xb", [128, 8, 512], BF16)
    pS = [[sb("pS%d_%d" % (i, c), [128, 513]) for c in range(5)] for i in range(2)]
    pL = [sb("pL%d" % c, [128, 512]) for c in range(5)]
    for c in range(5):
        P.memset(pS[1][c][:, 512:513], 0.0)
    tok = {n: sb("tk_" + n, [128, 128]) for n in ("r", "k", "v", "kkp", "kka", "rrk", "logw", "a", "kk", "kmod", "e1", "e2", "e3", "e4", "Bt", "Kt", "Rt", "Kh", "t0", "t1", "yn", "bv")}
    small = sb("small", [128, 16])
    RH1 = [sb("RH1_%d" % h, [128, 192]) for h in range(2)]
    RH2 = [sb("RH2_%d" % h, [128, 192]) for h in range(2)]
    AK = [sb("AK_%d" % h, [128, 192]) for h in range(2)]
    CM = [sb("CM_%d" % h, [64, 512]) for h in range(2)]
    XX = [[sb("XX%d_%d" % (h, i), [128, 256]) for i in range(2)] for h in range(2)]
    TT = [[sb("TT%d_%d" % (h, i), [128, 128]) for i in range(2)] for h in range(2)]
    QS = [sb("QS_%d" % h, [128, 192]) for h in range(2)]
    RpZ = [sb("RpZ_%d" % h, [128, NCH, 128]) for h in range(2)]
    msk = [sb("msk_%d" % h, [128, 2 * NCH, 64]) for h in range(2)]
    WYKP = [sb("WYKP_%d" % h, [128, 192]) for h in range(2)]
    GS = [sb("GS_%d" % h, [128, 256]) for h in range(2)]
    lamC = [sb("lamC_%d" % h, [64, 4]) for h in range(2)]
    STS = [sb("STS_%d" % h, [128, 5 * 64]) for h in range(2)]
    for h in range(2):
        P.memset(STS[h][:, :], 0.0); P.memset(GS[h][:, :], 0.0); P.memset(RpZ[h][:, :, :], 0.0)
    bnst = sb("bnst", [128, 2, 6]); bnag = sb("bnag", [128, 2, 2])
    ystg = [sb("ystg%d" % i, [128, 512]) for i in range(2)]
    lh = sb("lh", [128, 512])
    xTc = chunked(xT)
    NB = ntok // 512
    for blk in range(NB):
        par = blk % 2
        P.dma(xt32[0][:, 0:4, :], xTc[:, 0:4, blk * 512:(blk + 1) * 512])
        P.dma(xt32[0][:, 4:8, :], xTc[:, 4:8, blk * 512:(blk + 1) * 512], q='pool')
        for c in range(8):
            P.copy(xb[:, c, :], xt32[0][:, c, :], eng=('act', 'dve', 'pool')[c % 3])
        for c in range(5):
            ps = pl.bank()
            c0, c1 = ((c * 128, (c + 1) * 128) if c < 3 else ((384, 448) if c == 3 else (448, 512)))
            R = c1 - c0
            for kc in range(8):
                P.mm(ps[0:R, :], wrb[:, kc, c0:c1], xb[:, kc, :], start=(kc == 0), stop=(kc == 7))
            cur = pS[par][c]; prev = pS[1 - par][c]
            P.copy(cur[0:R, 1:513], ps[0:R, :], eng='act')
            P.copy(cur[0:R, 0:1], prev[0:R, 512:513], eng='dve')
            t = pL[c]
            mucol = pc[0:R, c:c + 1] if c < 3 else pc[0:R, 6 + c:7 + c]
            P.tt(t[0:R, :], cur[0:R, 0:512], cur[0:R, 1:513], ALU.subtract)
            P.stt(t[0:R, :], t[0:R, :], mucol, cur[0:R, 1:513], ALU.mult, ALU.add)
        P.actf(lh[0:64, :], pL[3][0:64, :], AF.Tanh)
        ys = ystg[blk % 2]
        ck(1)
        for sub in range(4):
            ts_ = slice(sub * 128, (sub + 1) * 128)
            ps = pl.bank()
            P.mm(ps[:, 0:128], pL[0][:, ts_], ident); P.mm(ps[:, 128:256], pL[1][:, ts_], ident); P.mm(ps[:, 256:384], pL[2][:, ts_], ident)
            P.copy(tok["r"][:], ps[:, 0:128], eng='act'); P.copy(tok["k"][:], ps[:, 128:256], eng='dve'); P.copy(tok["v"][:], ps[:, 256:384], eng='act')
            ps2 = pl.bank()
            P.mm(ps2[:, 0:128], pL[1][:, ts_], dkk[:]); P.mm(ps2[:, 128:256], pL[1][:, ts_], dka[:]); P.mm(ps2[:, 256:384], pL[0][:, ts_], drk[:])
            P.copy(tok["kkp"][:], ps2[:, 0:128], eng='dve'); P.copy(tok["kka"][:], ps2[:, 128:256], eng='act'); P.copy(tok["rrk"][:], ps2[:, 256:384], eng='dve')
            ps3 = pl.bank()
            P.mm(ps3[:, 0:128], lh[0:64, ts_], wa[0:64, 0:128])
            P.mm(ps3[:, 128:256], pL[4][0:64, ts_], wa[0:64, 128:256])
            P.tt(tok["t0"][:], ps3[:, 0:128], w0[:, 0:128], ALU.add)
            P.tt(tok["t1"][:], ps3[:, 128:256], w0[:, 128:256], ALU.add)
            P.actf(tok["logw"][:], tok["t0"][:], AF.Sigmoid)
            P.ts(tok["logw"][:], tok["logw"][:], -float(np.exp(-0.5)), None, ALU.mult)
            P.actf(tok["a"][:], tok["t1"][:], AF.Sigmoid)
            P.actf(tok["t0"][:], tok["kkp"][:], AF.Square)
            P.op('dve', lambda e, o=small[:, 0:2], i=tok["t0"][:].rearrange("p (h k) -> p h k", h=2): e.reduce_sum(o, i, AX.X), [tok["t0"][:]], [small[:, 0:2]])
            P.actf(small[:, 2:4], small[:, 0:2], AF.Sqrt)
            P.ts(small[:, 2:4], small[:, 2:4], 1e-12, None, ALU.max)
            P.op('dve', lambda e, o=small[:, 4:6], i=small[:, 2:4]: e.reciprocal(o, i), [small[:, 2:4]], [small[:, 4:6]])
            for h in range(2):
                hs = slice(h * 64, (h + 1) * 64)
                P.ts(tok["kk"][:, hs], tok["kkp"][:, hs], small[:, 4 + h:5 + h], None, ALU.mult)
            P.ts(tok["t0"][:], tok["a"][:], -1.0, None, ALU.add)
            P.tt(tok["t0"][:], tok["t0"][:], tok["kka"][:], ALU.mult)
            P.tt(tok["kmod"][:], tok["t0"][:], tok["k"][:], ALU.add)
            psL = pl.bank()
            P.mm(psL[:, 0:128], TriInc, tok["logw"][:]); P.mm(psL[:, 128:256], TriGt, tok["logw"][:])
            P.tt(tok["t1"][:], psL[:, 0:128], tok["logw"][:], ALU.subtract)
            P.actf(tok["e1"][:], tok["t1"][:], AF.Exp)
            P.actf(tok["e2"][:], psL[:, 0:128], AF.Exp, scale=-1.0)
            P.actf(tok["e3"][:], psL[:, 0:128], AF.Exp)
            P.actf(tok["e4"][:], psL[:, 128:256], AF.Exp)
            P.tt(tok["t0"][:], tok["kk"][:], tok["a"][:], ALU.mult)
            P.tt(tok["Bt"][:], tok["t0"][:], tok["e2"][:], ALU.mult)
            P.tt(tok["Kt"][:], tok["kmod"][:], tok["e2"][:], ALU.mult)
            P.tt(tok["Rt"][:], tok["r"][:], tok["e3"][:], ALU.mult)
            P.tt(tok["t1"][:], tok["kk"][:], tok["e1"][:], ALU.mult)
            for h in range(2):
                hs = slice(h * 64, (h + 1) * 64)
                P.ts(RH1[h][:, 0:64], tok["t1"][:, hs], -1.0, None, ALU.mult)
                P.tt(RH2[h][:, 128:192], tok["t0"][:, hs], tok["e4"][:, hs], ALU.mult)
                P.tt(AK[h][:, 128:192], tok["kmod"][:, hs], tok["e4"][:, hs], ALU.mult)
            P.tt(tok["t1"][:], tok["rrk"][:], tok["kmod"][:], ALU.mult)
            P.op('dve', lambda e, o=small[:, 6:8], i=tok["t1"][:].rearrange("p (h k) -> p h k", h=2): e.reduce_sum(o, i, AX.X), [tok["t1"][:]], [small[:, 6:8]])
            for h in range(2):
                hs = slice(h * 64, (h + 1) * 64)
                P.ts(tok["bv"][:, hs], tok["v"][:, hs], small[:, 6 + h:7 + h], None, ALU.mult)
            ck(2)
            def head_gen(h):
                hs = slice(h * 64, (h + 1) * 64)
                tp = pl.bank()
                P.tr(tp[0:64, 0:128], RH1[h][:, 0:64], ident)
                P.tr(tp[0:64, 128:256], tok["Rt"][:, hs], ident)
                P.tr(tp[0:64, 256:384], tok["Bt"][:, hs], ident)
                P.tr(tp[0:64, 384:512], tok["Kt"][:, hs], ident)
                P.copy(CM[h][:, :], tp[0:64, :], eng='act')
                yield
                Ac, Rc, Bc, Kc = [CM[h][:, i * 128:(i + 1) * 128] for i in range(4)]
                m1 = pl.bank(); m2 = pl.bank(); m3 = pl.bank()
                P.mm(m1[:, 0:256], Bc, CM[h][:, 0:256])
                P.mm(m2[:, 0:128], Kc, Rc)
                P.mm(m3[:, 0:256], Ac, CM[h][:, 256:512])
                X0 = XX[h][0]
                P.tt(X0[:, 0:128], m1[:, 0:128], MSK1[:, 0:128], ALU.mult)
                P.tt(RH2[h][:, 0:128], m1[:, 128:256], Mincl, ALU.mult)
                P.tt(AK[h][:, 0:128], m2[:, 0:128], Mincl, ALU.mult)
                P.tt(X0[:, 128:256], m3[:, 0:128], MSK3[:, 0:128], ALU.mult)
                P.tt(RH1[h][:, 64:192], m3[:, 128:256], MSK3[:, 128:256], ALU.mult)
                yield
                ck(3)
                Tc = TT[h][0]
                P.tt(Tc[:, :], X0[:, 0:128], ident, ALU.add)
                cur = 0
                for lvl in range(LV):
                    Xc = XX[h][cur]; Xn = XX[h][1 - cur]
                    pp = pl.bank()
                    if lvl < LV - 1:
                        P.mm(pp[:, 0:128], Xc[:, 128:256], Xc[:, 0:128])
                    P.mm(pp[:, 128:256], Xc[:, 0:128], Xc[:, 128:256])
                    if lvl < LV - 1:
                        P.copy(Xn[:, :], pp[:, 0:256], eng='act')
                    else:
                        P.copy(Xn[:, 128:256], pp[:, 128:256], eng='act')
                    yield
                    pt_ = pl.bank()
                    P.mm(pt_[:, 0:128], Xn[:, 128:256], TT[h][lvl % 2][:, :])
                    P.tt(TT[h][(lvl + 1) % 2][:, :], TT[h][lvl % 2][:, :], pt_[:, 0:128], ALU.add)
                    yield
                    cur = 1 - cur
                ck(4)
                Tf = TT[h][LV % 2]
                q = pl.bank()
                P.mm(q[:, 0:192], Tf[:, :], RH1[h][:, :])
                P.copy(QS[h][:, :], q[:, 0:192], eng='act')
                yield
                rp = pl.bank()
                P.mm(rp[0:64, 0:128], QS[h][:, 0:64], RH2[h][:, 0:128])
                for c in range(NCH):
                    cs = slice(c * CH, (c + 1) * CH)
                    P.tt(RpZ[h][0:64, c, cs], rp[0:64, cs], Rc[:, cs], ALU.add)
                wk = pl.bank()
                P.mm(wk[:, 0:192], QS[h][:, 64:192], RH2[h][:, :])
                P.tt(WYKP[h][:, :], wk[:, 0:192], AK[h][:, :], ALU.add)
                yield
                gg = pl.bank()
                for c in range(NCH):
                    P.ts(msk[h][:, c, :], QS[h][:, 0:64], CInd[:, c:c + 1], None, ALU.mult)
                    P.ts(msk[h][:, NCH + c, :], WYKP[h][:, 128:192], CInd[:, c:c + 1], None, ALU.mult)
                    P.mm(gg[0:64, c * 64:(c + 1) * 64], msk[h][:, c, :], RH2[h][:, 128:192])
                P.copy(GS[h][0:64, 0:NCH * 64], gg[0:64, 0:NCH * 64], eng='act')
                lc = pl.bank()
                P.mm(lc[0:64, 0:64], tok["logw"][:, hs], CInd)
                P.actf(lamC[h][:, 0:NCH], lc[0:64, 0:NCH], AF.Exp)
                yield
                ck(5)
                for c in range(NCH):
                    cs = slice(c * CH, (c + 1) * CH)
                    sn = pl.bank()
                    STc = STS[h][:, c * 64:(c + 1) * 64]
                    P.mm(sn[0:64, 0:64], GS[h][:, c * 64:(c + 1) * 64], STc, start=True, stop=False)
                    P.mm(sn[0:64, 0:64], msk[h][:, NCH + c, :], tok["v"][:, hs], start=False, stop=True)
                    P.stt(STS[h][0:64, (c + 1) * 64:(c + 2) * 64], STS[h][0:64, c * 64:(c + 1) * 64], lamC[h][:, c:c + 1], sn[0:64, 0:64], ALU.mult, ALU.add)
                    yield
                ck(6)
                yp = pl.bank()
                P.mm(yp[:, 0:64], WYKP[h][:, 0:128], tok["v"][:, hs], start=True, stop=False)
                for c in range(NCH):
                    P.mm(yp[:, 0:64], RpZ[h][:, c, :], STS[h][:, c * 64:(c + 1) * 64], start=False, stop=(c == NCH - 1))
                P.copy(STS[h][0:64, 0:64], STS[h][0:64, NCH * 64:(NCH + 1) * 64], eng='dve')
                P.copy(tok["t0"][:, hs], yp[:, 0:64], eng='act')
                P.op('dve', lambda e, o=bnst[:, h, :], i=tok["t0"][:, hs]: e.bn_stats(o, i), [tok["t0"][:, hs]], [bnst[:, h, :]])
                P.op('dve', lambda e, o=bnag[:, h, :], i=bnst[:, h, :]: e.bn_aggr(o, i), [bnst[:, h, :]], [bnag[:, h, :]])
                P.ts(small[:, 8 + h:9 + h], bnag[:, h, 1:2], GN_EPS, None, ALU.add)
                P.actf(small[:, 8 + h:9 + h], small[:, 8 + h:9 + h], AF.Sqrt)
                P.op('dve', lambda e, o=small[:, 10 + h:11 + h], i=small[:, 8 + h:9 + h]: e.reciprocal(o, i), [small[:, 8 + h:9 + h]], [small[:, 10 + h:11 + h]])
                P.ts(tok["yn"][:, hs], tok["t0"][:, hs], bnag[:, h, 0:1], small[:, 10 + h:11 + h], ALU.subtract, ALU.mult)
            gens = [head_gen(0), head_gen(1)]
            while gens:
                for g_ in list(gens):
                    try:
                        next(g_)
                    except StopIteration:
                        gens.remove(g_)
            ck(7)
            po = pl.bank()
            P.tr(po[:, 0:128], tok["yn"][:], ident)
            P.tr(po[:, 128:256], tok["bv"][:], ident)
            P.actf(ys[:, ts_], po[:, 0:128], AF.Identity, bias=pc[:, 8:9], scale=pc[:, 7:8])
            P.tt(ys[:, ts_], ys[:, ts_], po[:, 128:256], ALU.add)
        P.dma(yT[:, blk * 512:(blk + 1) * 512], ys[:, :], q='pool')
    P.emit()
    return nc


def rwkv_inputs(d, ntok=SEQ):
    l = 0
    W = d['w_in'][l]
    cm = rwkv_consts()
    maps = []
    for c in range(8):
        b, j = c // 4, c % 4
        cols = np.concatenate([np.arange(128 * j, 128 * j + 128), 512 + np.arange(128 * j, 128 * j + 128),
                               1024 + np.arange(128 * j, 128 * j + 128), np.arange(1536, 1664)])
        hc = np.arange(128 * j, 128 * j + 128)
        pcol = np.zeros((128, 12), np.float32)
        pcol[:, 0:4] = d['rwkv_mu'][l][cols].reshape(4, 128).T
        pcol[:, 4] = d['rwkv_k_k'][l][hc]; pcol[:, 5] = d['rwkv_k_a'][l][hc]
        pcol[:, 6] = d['rwkv_r_k'][l].reshape(-1)[hc]; pcol[:, 7] = d['rwkv_gn_g'][l][hc]; pcol[:, 8] = d['rwkv_gn_b'][l][hc]
        pcol[0:64, 9] = d['rwkv_mu'][l][1536:1600]; pcol[0:64, 10] = d['rwkv_mu'][l][1600:1664]
        w2a2 = np.concatenate([d['rwkv_w2'][l][:, hc], d['rwkv_a2'][l][:, hc]], axis=1)
        w0a0 = np.tile(np.concatenate([d['rwkv_w0'][l][hc], d['rwkv_a0'][l][hc]])[None, :], (128, 1))
        maps.append(dict(xT=np.ascontiguousarray(d['x'][b, :ntok].T), w_r=np.ascontiguousarray(W[:, cols]), cmat=cm,
                         pcol=pcol, w2a2=np.ascontiguousarray(w2a2, dtype=np.float32), w0a0=np.ascontiguousarray(w0a0, dtype=np.float32)))
    return maps


def run_rwkv(d):
    if 'rwkv' not in _NC_CACHE:
        _NC_CACHE['rwkv'] = build_rwkv()
    res = run_bass_kernel_spmd(_NC_CACHE['rwkv'], rwkv_inputs(d), core_ids=list(range(8)))
    ya = np.zeros((2, SEQ, 512), np.float32)
    for c in range(8):
        b, j = c // 4, c % 4
        ya[b, :, 128 * j:128 * j + 128] = res.results[c]['yT'].T
    return ya


ROPE_THETA = 500000.0
NEGM = 30000.0


def nsa_consts(ntok):
    pos = np.arange(ntok, dtype=np.float32)
    inv = ROPE_THETA ** (-np.arange(8, dtype=np.float32) * 2.0 / 16.0)
    ang = pos[None, :] * inv[:, None]
    cosT = np.ones((64, ntok), np.float32); sinT = np.zeros((64, ntok), np.float32)
    cosT[0:8] = np.cos(ang); cosT[8:16] = np.cos(ang)
    sinT[0:8] = -np.sin(ang); sinT[8:16] = np.sin(ang)
    p = np.arange(128)
    tri_le = (p[:, None] <= p[None, :]).astype(np.float32)
    tri_gt = (p[:, None] > p[None, :]).astype(np.float32)
    cmk = np.ones((128, 32, 128), np.float32)
    for i in range(16):
        p0 = 8 * i
        u = p[:, None] - p0
        cmk[:, 16 + i, :] = (p[None, :] >= 16 * u + 31).astype(np.float32)
        if p0 == 0:
            cmk[127, i, :] = (p >= 15).astype(np.float32)
    ncmp_pad = 512
    c = np.arange(ncmp_pad); j = np.arange(128)
    ov = ((16 * c[:, None] <= 64 * j[None, :] + 63) & (16 * c[:, None] + 31 >= 64 * j[None, :])).astype(np.float32)
    ov[(ntok - 32) // 16 + 1:] = 0.0
    ovc = ov.reshape(4, 128, 128).transpose(1, 0, 2)
    jj = np.arange(128); key = np.arange(128)
    xall = np.zeros((128, 64, 128), np.float32)
    for kb in range(64):
        xall[:, kb, :] = (jj[:, None] == 2 * kb + key[None, :] // 64)
    qcol = np.zeros((128, 4), np.float32)
    qcol[:, 0] = (p >= 64) * 1e30 + (p < 64) * -1e30
    qcol[:, 1] = (p >= 64).astype(np.float32)
    qcol[:, 2] = (p < 64) * 1e30
    return dict(cosT=cosT, sinT=sinT, tri=np.concatenate([tri_le, tri_gt, np.eye(128, dtype=np.float32)], axis=1),
                cmk=cmk.reshape(128, 32 * 128), ovc=np.ascontiguousarray(ovc).reshape(128, 512),
                xall=xall.reshape(128, 64 * 128), qcol=qcol)


def build_nsa(ntok=SEQ, stop=99):
    nc = bass.Bass("TRN2", target_bir_lowering=False)
    try:
        return _build_nsa(nc, ntok, stop)
    except _Stop:
        return nc


def _build_nsa(nc, ntok, stop):
    def ck(code):
        if stop == code:
            P.emit()
            raise _Stop()
    NB = ntok // 512; NT128 = ntok // 128; NQ = NT128 // 2
    NCMP = (ntok - 32) // 16 + 1
    dr = lambda n, s, k="ExternalInput": nc.dram_tensor(n, s, F32, kind=k).ap()
    xT = dr("xT", [D, ntok])
    wA = dr("wA", [D, 15 * 64])
    wB = dr("wB", [D, 140])
    cosT = dr("cosT", [64, ntok]); sinT = dr("sinT", [64, ntok])
    tri = dr("tri", [128, 384]); cmk = dr("cmk", [128, 2048]); ovc = dr("ovc", [128, 512]); xall = dr("xall", [128, 8192]); qcol = dr("qcol", [128, 10])
    dmask = dr("dmask", [128, 256]); wmask = dr("wmask", [128, 768])
    w1k = dr("w1k", [64, 32 * 256]); w1v = dr("w1v", [64, 32 * 256]); w2k = dr("w2k", [128, 2 * 64]); w2v = dr("w2v", [128, 2 * 64])
    pek = dr("pek", [64, 32 * 64]); pev = dr("pev", [64, 32 * 64])
    xq = dr("xq", [D, NQ * 128])
    cosq = dr("cosq", [64, NQ * 128]); sinq = dr("sinq", [64, NQ * 128])
    yb = dr("yb", [NQ * 128, 256], "ExternalOutput")
    chunked = lambda ap: ap.rearrange("(c p) n -> p c n", p=128)
    P = Prog(nc)
    pl = Pools(P, nc)
    sb = lambda n, s, dt=F32: nc.alloc_sbuf_tensor(n, s, dt)
    tri_t = sb("tri_t", [128, 384]); qc_t = sb("qc_t", [128, 10])
    P.dma(tri_t[:], tri); P.dma(qc_t[:], qcol)
    identf = tri_t[:, 256:384]
    xt32 = sb("xt32", [128, 8, 512]); xb = sb("xb", [128, 8, 512], BF16)
    pl.stg = [xt32[:, 0:4, :].rearrange("p a n -> p (a n)"), xt32[:, 4:8, :].rearrange("p a n -> p (a n)")]
    BUF2 = sb("BUF2", [128, 24576], BF16)
    cmk_b = sb("cmk_b", [128, 2048], BF16); dm_b = sb("dm_b", [128, 256], BF16); wm_b = sb("wm_b", [128, 768], BF16); ov_b = sb("ov_b", [128, 4, 128], BF16)
    xall_b = BUF2[:, 16384:24576].rearrange("p (a n) -> p a n", n=128)
    pl.load_cast(cmk_b[:], cmk); pl.load_cast(dm_b[:], dmask); pl.load_cast(wm_b[:], wmask); pl.load_cast(ov_b[:].rearrange("p a n -> p (a n)"), ovc, q='pool')
    wAb = sb("wAb", [128, 8, 960], BF16); wBb = sb("wBb", [128, 8, 140], BF16)
    pl.load_cast(wAb[:], chunked(wA)); pl.load_cast(wBb[:], chunked(wB), q='pool')
    KBUF = sb("KBUF", [64, 2, ntok], BF16)
    ksT = KBUF[:, 0, :]; kwT = KBUF[:, 1, :]; kcT = KBUF[:, 0, :]; vcT = KBUF[:, 1, :]
    vsA = sb("vsA", [128, NT128, 65], BF16); vwA = sb("vwA", [128, NT128, 65], BF16)
    qT = BUF2[0:64, 0:NQ * 512].rearrange("p (a n) -> p a n", n=512)
    GT = sb("GT", [128, NQ, 12])
    P.memset(vsA[:, :, 64:65], 1.0); P.memset(vwA[:, :, 64:65], 1.0)
    cs_t = [sb("cs_t%d" % i, [64, 1024]) for i in range(1)]
    rt = [sb("rt%d" % i, [64, 512]) for i in range(2)]
    xTc = chunked(xT); xqc = chunked(xq)

    def proj_block(src_c, cosd, sind, c0, n, groups, dests, tokmajor_tiles):
        P.dma(xt32[:, 0:4, 0:n], src_c[:, 0:4, c0:c0 + n]); P.dma(xt32[:, 4:8, 0:n], src_c[:, 4:8, c0:c0 + n], q='pool')
        for c in range(8):
            P.copy(xb[:, c, 0:n], xt32[:, c, 0:n], eng=('act', 'dve', 'pool')[c % 3])
        cst = cs_t[0]
        P.dma(cst[:, 0:n], cosd[:, c0:c0 + n]); P.dma(cst[:, 512:512 + n], sind[:, c0:c0 + n], q='pool')
        for (cb, pb), dst in zip(groups, dests):
            p1 = pl.bank()
            for kc in range(8):
                P.mm(p1[0:64, 0:n], wAb[:, kc, cb:cb + 64], xb[:, kc, 0:n], start=(kc == 0), stop=(kc == 7))
            if pb is None:
                P.copy(dst, p1[0:64, 0:n], eng='act')
                continue
            p2 = pl.bank()
            for kc in range(8):
                P.mm(p2[0:64, 0:n], wAb[:, kc, pb:pb + 64], xb[:, kc, 0:n], start=(kc == 0), stop=(kc == 7))
            P.tt(rt[0][:, 0:n], p1[0:64, 0:n], cst[:, 0:n], ALU.mult)
            P.tt(rt[1][:, 0:n], p2[0:64, 0:n], cst[:, 512:512 + n], ALU.mult)
            P.tt(dst, rt[0][:, 0:n], rt[1][:, 0:n], ALU.add, eng='pool')
        for (t128, off) in tokmajor_tiles:
            pv = pl.bank()
            for kc in range(8):
                P.mm(pv[:, 0:140], xb[:, kc, off:off + 128], wBb[:, kc, :], start=(kc == 0), stop=(kc == 7))
            yield (t128, pv)

    for blk in range(NB):
        groups = [(256 + 0, 448 + 256 + 0), (896, None)]
        sl = slice(blk * 512, (blk + 1) * 512)
        for _ in proj_block(xTc, cosT, sinT, blk * 512, 512, groups, [kcT[:, sl], vcT[:, sl]], []):
            pass
    ck(1)
    w1b = [BUF2[0:64, i * 8192:(i + 1) * 8192].rearrange("p (a n) -> p a n", n=256) for i in range(2)]
    w2b = [sb("w2b%d" % i, [128, 2, 64], BF16) for i in range(2)]
    peb = [BUF2[0:64, 16384 + i * 2048:16384 + (i + 1) * 2048].rearrange("p (a n) -> p a n", n=64) for i in range(2)]
    pl.load_cast(w1b[0], w1k.rearrange("p (a n) -> p a n", n=256)); pl.load_cast(w1b[1], w1v.rearrange("p (a n) -> p a n", n=256), q='pool')
    pl.load_cast(w2b[0][:].rearrange("p a n -> p (a n)"), w2k); pl.load_cast(w2b[1][:].rearrange("p a n -> p (a n)"), w2v, q='pool')
    pl.load_cast(peb[0], pek.rearrange("p (a n) -> p a n", n=64)); pl.load_cast(peb[1], pev.rearrange("p (a n) -> p a n", n=64), q='pool')
    KC = sb("KC", [64, 512], BF16); VCA = sb("VCA", [128, 4, 65], BF16)
    P.memset(KC[:], 0.0); P.memset(VCA[:, :, 0:64], 0.0); P.memset(VCA[:, :, 64:65], 1.0)
    GH = [[sb("GH%d_%d" % (i, hc), [128, 512], BF16) for hc in range(2)] for i in range(2)]
    gtmp = [sb("gtmp%d" % i, [128, 512]) for i in range(3)]
    bcol = sb("bcol", [128, 4])
    for which, srcT in ((0, kcT), (1, vcT)):
        for hc in range(2):
            pb_ = pl.bank()
            for l in range(32):
                P.mm(pb_[:, 0:64], w1b[which][:, l, hc * 128:(hc + 1) * 128], peb[which][:, l, :], start=(l == 0), stop=(l == 31))
            P.copy(bcol[:, which * 2 + hc:which * 2 + hc + 1], pb_[:, 0:1], eng='act')
            ph = pl.bank()
            for l in range(32):
                rhs = srcT[:, l:l + 16 * (NCMP - 1) + 1:16]
                P.mm(ph[:, 0:NCMP], w1b[which][:, l, hc * 128:(hc + 1) * 128], rhs, start=(l == 0), stop=(l == 31))
            x_ = gtmp[0][:, 0:NCMP]; u_ = gtmp[1][:, 0:NCMP]; s_ = gtmp[2][:, 0:NCMP]
            P.actf(x_, ph[:, 0:NCMP], AF.Identity, bias=bcol[:, which * 2 + hc:which * 2 + hc + 1])
            P.actf(u_, x_, AF.Square)
            P.ts(u_, u_, 0.044715, 1.0, ALU.mult, ALU.add)
            P.tt(u_, u_, x_, ALU.mult)
            P.actf(s_, u_, AF.Sigmoid, scale=2.0 * 0.7978845608028654)
            P.memset(GH[which][hc][:, NCMP:512], 0.0)
            P.tt(GH[which][hc][:, 0:NCMP], x_, s_, ALU.mult)
    pk = pl.bank()
    for hc in range(2):
        P.mm(pk[0:64, 0:NCMP], w2b[0][:, hc, :], GH[0][hc][:, 0:NCMP], start=(hc == 0), stop=(hc == 1))
    P.copy(KC[:, 0:NCMP], pk[0:64, 0:NCMP], eng='act')
    for cc in range((NCMP + 127) // 128):
        pv_ = pl.bank()
        for hc in range(2):
            P.mm(pv_[:, 0:64], GH[1][hc][:, cc * 128:(cc + 1) * 128], w2b[1][:, hc, :], start=(hc == 0), stop=(hc == 1))
        P.copy(VCA[:, cc, 0:64], pv_[:, 0:64], eng='dve')
    ck(2)
    for blk in range(NB):
        groups = [(256 + 64, 448 + 256 + 64), (256 + 128, 448 + 256 + 128)]
        sl = slice(blk * 512, (blk + 1) * 512)
        for (t128, pv) in proj_block(xTc, cosT, sinT, blk * 512, 512, groups, [ksT[:, sl], kwT[:, sl]], [(blk * 4 + i, i * 128) for i in range(4)]):
            P.copy(vsA[:, t128, 0:64], pv[:, 0:64], eng='act')
            P.copy(vwA[:, t128, 0:64], pv[:, 64:128], eng='dve')
    pl.load_cast(xall_b, xall.rearrange("p (a n) -> p a n", n=128))
    qtmp = [sb("qtmp%d" % r, [64, 512], BF16) for r in range(4)]
    for qblk in range(0, NQ, 4):
        n = min(4, NQ - qblk) * 128
        groups = [(r * 64, 448 + r * 64) for r in range(4)]
        dests = [qtmp[r][:, 0:n] for r in range(4)]
        for (t128, pv) in proj_block(xqc, cosq, sinq, qblk * 128, n, groups, dests, [(qblk + i, i * 128) for i in range(n // 128)]):
            P.actf(GT[:, t128, :], pv[:, 128:140], AF.Sigmoid)
        for i in range(n // 128):
            for r in range(4):
                P.copy(qT[:, qblk + i, r * 128:(r + 1) * 128], qtmp[r][:, i * 128:(i + 1) * 128], eng=('pool', 'dve')[r % 2])
    ck(3)
    CV = sb("CV", [128, 4, 193], BF16)
    for cc in range(4):
        P.copy(CV[:, cc, 0:65], VCA[:, cc, :], eng='dve'); P.copy(CV[:, cc, 65:193], ov_b[:, cc, :], eng='pool')
    pT = [sb("pT%d" % i, [128, 512], BF16) for i in range(3)]
    imp = sb("imp", [128, 128]); imp2 = sb("imp2", [128, 128]); m8 = sb("m8", [128, 16]); sel = sb("sel", [128, 128])
    selT = sb("selT", [128, 128], BF16); negm = sb("negm", [128, 512], BF16)
    oTs = [None] + [sb("oTs%d" % i, [65, 512]) for i in range(1, 3)]
    ytile = sb("ytile", [128, 256]); sm = sb("sm", [128, 32])
    cnum = [sb("cnum%d" % i, [128, 386]) for i in range(2)]
    identb = sb("identb", [128, 128], BF16)
    P.copy(identb[:], identf)
    pi = 0
    for i in range(NQ):
        qrhs = qT[:, i, :]
        _attend(P, pl, nc, i, qrhs, dict(KC=KC, CV=CV, cmk_b=cmk_b, pT=pT, imp=imp, imp2=imp2, m8=m8, sel=sel, selT=selT, negm=negm,
                                          oTs=oTs, ytile=ytile, sm=sm, cnum=cnum, identb=identb, identf=identf, xall_b=xall_b, dm_b=dm_b, wm_b=wm_b,
                                          ksT=ksT, kwT=kwT, vsA=vsA, vwA=vwA, GT=GT, qc_t=qc_t, yb=yb, NCMP=NCMP))
    P.emit()
    return nc


def _bc4(ap):
    return ap.unsqueeze(1).broadcast_to([ap.shape[0], 4, ap.shape[1]])


def _attend(P, pl, nc, i, qrhs, T):
    KC, CV, pT = T['KC'], T['CV'], T['pT']
    NCMP = T['NCMP']
    identf, identb = T['identf'], T['identb']
    oTs = T['oTs']
    v4 = lambda t: t[:, :].rearrange("p (r q) -> p r q", r=4)
    ccb = min((16 * i + 6) // 128, (NCMP - 1) // 128)
    cn = [pl.banks[0], pl.banks[1], pl.banks[2], pl.banks[3]]
    rot = lambda: pl.banks[4 + (pl.nrot() % 4)]
    for cc in range(ccb + 1):
        ps = rot()
        P.mm(ps[:, :], KC[:, cc * 128:(cc + 1) * 128], qrhs)
        pt = pT[(pl.bi) % 3]
        P.actf(pt[:, :], ps[:, :], AF.Exp, scale=0.125)
        if cc >= ccb - 1:
            which = 1 if cc == ccb else 0
            mk = T['cmk_b'][:, (which * 8 + (i % 8)) * 128:(which * 8 + (i % 8) + 1) * 128]
            P.tt(v4(pt), v4(pt), _bc4(mk), ALU.mult)
        for r in range(4):
            P.mm(cn[r][:, 0:193], pt[:, r * 128:(r + 1) * 128], CV[:, cc, :], start=(cc == 0), stop=(cc == ccb))
    cnum = T['cnum']
    for r in range(4):
        P.copy(cnum[r // 2][:, (r % 2) * 193:(r % 2) * 193 + 193], cn[r][:, 0:193], eng=('act', 'dve')[r % 2])
    sm = T['sm']
    for r in range(4):
        P.ts(sm[:, r:r + 1], cnum[r // 2][:, (r % 2) * 193 + 64:(r % 2) * 193 + 65], 1e-30, None, ALU.add)
    P.op('dve', lambda e: e.reciprocal(sm[:, 4:8], sm[:, 0:4]), [sm[:, 0:4]], [sm[:, 4:8]])
    imp, imp2, m8, sel, selT, negm = T['imp'], T['imp2'], T['m8'], T['sel'], T['selT'], T['negm']
    for r in range(4):
        src = cnum[r // 2][:, (r % 2) * 193 + 65:(r % 2) * 193 + 193]
        if r == 0:
            P.ts(imp[:, :], src, sm[:, 4:5], None, ALU.mult)
        else:
            P.stt(imp[:, :], src, sm[:, 4 + r:5 + r], imp[:, :], ALU.mult, ALU.add)
    qc = T['qc_t']
    w0 = max(0, 4 * i - 1); wn = 4 * i + 4 - w0; t0c = w0 - (4 * i - 1)
    P.copy(imp2[:, :], imp[:, :], eng='dve')
    P.tt(imp2[:, w0:w0 + wn], imp[:, w0:w0 + wn], qc[:, t0c:t0c + wn], ALU.mult)
    P.tt(imp2[:, w0:w0 + wn], imp2[:, w0:w0 + wn], qc[:, 5 + t0c:5 + t0c + wn], ALU.add)
    if 4 * i + 4 < 128:
        P.memset(imp2[:, 4 * i + 4:128], -1e30)
    P.memset(imp2[:, 0:1], 1e30)
    P.op('dve', lambda e: e.max(out=m8[:, 0:8], in_=imp2[:, :]), [imp2[:, :]], [m8[:, 0:8]])
    P.op('dve', lambda e: e.match_replace(out=imp[:, :], in_to_replace=m8[:, 0:8], in_values=imp2[:, :], imm_value=-3e38), [m8[:, 0:8], imp2[:, :]], [imp[:, :]])
    P.op('dve', lambda e: e.max(out=m8[:, 8:16], in_=imp[:, :]), [imp[:, :]], [m8[:, 8:16]])
    P.ts(sel[:, :], imp2[:, :], m8[:, 15:16], None, ALU.is_ge)
    pst = rot()
    P.tr(pst[:, 0:128], sel[:, :], identf)
    P.copy(selT[:, :], pst[:, 0:128], eng='act')
    for r in range(4):
        P.ts(negm[:, r * 128:(r + 1) * 128], selT[:, :], -1.0, NEGM, ALU.add, ALU.mult, eng=('dve', 'pool')[r % 2])
    ksT, vsA, xall_b, dm_b = T['ksT'], T['vsA'], T['xall_b'], T['dm_b']
    po = pl.banks[0]
    nkb = 2 * i + 2
    for kb in range(nkb):
        ps = rot()
        P.mm(ps[:, :], ksT[:, kb * 128:(kb + 1) * 128], qrhs, start=True, stop=False)
        P.mm(ps[:, :], xall_b[:, kb, :], negm[:, :], start=False, stop=True)
        pt = pT[(pl.bi) % 3]
        P.actf(pt[:, :], ps[:, :], AF.Exp, scale=0.125)
        if kb >= nkb - 2:
            mk = dm_b[:, (kb - (nkb - 2)) * 128:(kb - (nkb - 2) + 1) * 128]
            P.tt(v4(pt), v4(pt), _bc4(mk), ALU.mult)
        P.mm(po[0:65, :], vsA[:, kb, :], pt[:, :], start=(kb == 0), stop=(kb == nkb - 1))
    P.copy(oTs[1][:, :], po[0:65, :], eng='act')
    kwT, vwA, wm_b = T['kwT'], T['vwA'], T['wm_b']
    pw = pl.banks[1]
    kbs = [(m, 2 * i - 4 + m) for m in range(6) if 2 * i - 4 + m >= 0]
    for n_, (m, kb) in enumerate(kbs):
        ps = rot()
        P.mm(ps[:, :], kwT[:, kb * 128:(kb + 1) * 128], qrhs)
        pt = pT[(pl.bi) % 3]
        P.actf(pt[:, :], ps[:, :], AF.Exp, scale=0.125)
        P.tt(v4(pt), v4(pt), _bc4(wm_b[:, m * 128:(m + 1) * 128]), ALU.mult)
        P.mm(pw[0:65, :], vwA[:, kb, :], pt[:, :], start=(n_ == 0), stop=(n_ == len(kbs) - 1))
    P.copy(oTs[2][:, :], pw[0:65, :], eng='dve')
    GT, ytile = T['GT'], T['ytile']
    for r in range(4):
        P.tt(sm[:, 8 + r:9 + r], sm[:, 4 + r:5 + r], GT[:, i, r * 3:r * 3 + 1], ALU.mult)
        P.ts(ytile[:, r * 64:(r + 1) * 64], cnum[r // 2][:, (r % 2) * 193:(r % 2) * 193 + 64], sm[:, 8 + r:9 + r], None, ALU.mult)
    for br in (1, 2):
        pb_ = pl.banks[br + 1]
        for r in range(4):
            P.mm(pb_[:, r * 65:(r + 1) * 65], oTs[br][0:65, r * 128:(r + 1) * 128], identf[0:65, 0:65])
        for r in range(4):
            P.ts(sm[:, 12 + r:13 + r], pb_[:, r * 65 + 64:r * 65 + 65], 1e-30, None, ALU.add)
        P.op('dve', lambda e: e.reciprocal(sm[:, 16:20], sm[:, 12:16]), [sm[:, 12:16]], [sm[:, 16:20]])
        for r in range(4):
            P.tt(sm[:, 20 + r:21 + r], sm[:, 16 + r:17 + r], GT[:, i, r * 3 + br:r * 3 + br + 1], ALU.mult)
            P.stt(ytile[:, r * 64:(r + 1) * 64], pb_[:, r * 65:r * 65 + 64], sm[:, 20 + r:21 + r], ytile[:, r * 64:(r + 1) * 64], ALU.mult, ALU.add)
    P.dma(T['yb'][i * 128:(i + 1) * 128, :], ytile[:, :], q='pool')


def nsa_inputs(d, ntok=SEQ):
    l = 0
    W = d['w_in'][l][:, 1664:1664 + 1304]
    cst = nsa_consts(ntok)
    NQ = ntok // 256
    perm = np.arange(64); perm[0:8] = np.arange(8, 16); perm[8:16] = np.arange(0, 8)
    p = np.arange(128)
    tri_le = cst['tri'][:, 0:128]; tri_gt = cst['tri'][:, 128:256]
    one = np.ones((128, 128), np.float32); zero = np.zeros((128, 128), np.float32)
    w1k = d['nsa_ck_w1'][l].reshape(32, 64, 256).transpose(1, 0, 2).reshape(64, 32 * 256)
    w1v = d['nsa_cv_w1'][l].reshape(32, 64, 256).transpose(1, 0, 2).reshape(64, 32 * 256)
    w2k = d['nsa_ck_w2'][l].reshape(2, 128, 64).transpose(1, 0, 2).reshape(128, 128)
    w2v = d['nsa_cv_w2'][l].reshape(2, 128, 64).transpose(1, 0, 2).reshape(128, 128)
    pek = np.repeat(d['nsa_pe_k'][l].T[:, :, None], 64, axis=2).reshape(64, 32 * 64)
    pev = np.repeat(d['nsa_pe_v'][l].T[:, :, None], 64, axis=2).reshape(64, 32 * 64)
    maps = []
    for c in range(8):
        b, g, par = c // 4, (c % 4) // 2, c % 2
        qcols = [256 * g + r * 64 + np.arange(64) for r in range(4)]
        kvc = lambda idx: 512 + idx * 128 + g * 64 + np.arange(64)
        roped = qcols + [kvc(0), kvc(2), kvc(4)]
        colsA = np.concatenate(roped + [cg[perm] for cg in roped] + [kvc(1)])
        gcols = np.array([1280 + (4 * g + r) * 3 + cc for r in range(4) for cc in range(3)])
        colsB = np.concatenate([kvc(3), kvc(5), gcols])
        own = np.concatenate([np.arange((2 * i + par) * 128, (2 * i + par + 1) * 128) for i in range(NQ)])
        xTb = np.ascontiguousarray(d['x'][b, :ntok].T)
        cmk_full = cst['cmk'].reshape(128, 32, 128)
        cmkc = np.stack([cmk_full[:, wh * 16 + (2 * m + par) % 16, :] for wh in range(2) for m in range(8)], axis=1)
        if par == 0:
            dmask = np.concatenate([tri_le, zero], axis=1); wmask = np.concatenate([tri_gt, one, one, one, tri_le, zero], axis=1)
        else:
            dmask = np.concatenate([one, tri_le], axis=1); wmask = np.concatenate([zero, tri_gt, one, one, one, tri_le], axis=1)
        qcol = np.zeros((128, 10), np.float32)
        hi = (p >= 64).astype(np.float32); lo = 1.0 - hi
        for w in range(5):
            dl = w - 1 - 2 * par
            if dl < -1:
                qcol[:, w] = 1.0
            elif dl == -1:
                qcol[:, w] = hi; qcol[:, 5 + w] = lo * 1e30
            elif dl == 0:
                qcol[:, 5 + w] = 1e30
            elif dl == 1:
                qcol[:, 5 + w] = hi * 1e30 - lo * 1e30
            else:
                qcol[:, 5 + w] = -1e30
        m = dict(xT=xTb, wA=W[:, colsA], wB=W[:, colsB], cosT=cst['cosT'], sinT=cst['sinT'], tri=cst['tri'], cmk=cmkc.reshape(128, 2048),
                 ovc=cst['ovc'], xall=cst['xall'], qcol=qcol, dmask=dmask, wmask=wmask, w1k=w1k, w1v=w1v, w2k=w2k, w2v=w2v, pek=pek, pev=pev,
                 xq=xTb[:, own], cosq=cst['cosT'][:, own], sinq=cst['sinT'][:, own])
        maps.append({k: np.ascontiguousarray(v, dtype=np.float32) for k, v in m.items()})
    return maps


def run_nsa(d):
    if 'nsa' not in _NC_CACHE:
        _NC_CACHE['nsa'] = build_nsa()
    res = run_bass_kernel_spmd(_NC_CACHE['nsa'], nsa_inputs(d), core_ids=list(range(8)))
    yb = np.zeros((2, SEQ, 512), np.float32)
    for c in range(8):
        b, g, par = c // 4, (c % 4) // 2, c % 2
        o = res.results[c]['yb'].reshape(SEQ // 256, 128, 256)
        yb[b].reshape(SEQ // 256, 2, 128, 512)[:, par, :, g * 256:(g + 1) * 256] = o
    return yb


def kernel(**inputs):
    d = {k: np.asarray(v) for k, v in inputs.items()}
    ya = run_rwkv(d)
    yb = run_nsa(d)
    out = run_tail(d, ya, yb)
    return out.astype(np.float32)
```

```python
import numpy as np
import concourse.bass as bass
import concourse.mybir as mybir
from concourse.bass_utils import run_bass_kernel_spmd

F32 = mybir.dt.float32
BF16 = mybir.dt.bfloat16
AF = mybir.ActivationFunctionType
ALU = mybir.AluOpType
AX = mybir.AxisListType

SAME_ENG_SYNC = False
NSLOT = 8


def _region(ap):
    t = ap.tensor
    shp = tuple(t.shape)
    sp = str(ap.space)
    off = int(ap.offset)
    pat = [(int(s), int(c)) for s, c in ap.ap]
    if 'DRAM' in sp.upper() or 'HBM' in sp.upper():
        ext = sum((c - 1) * abs(s) for s, c in pat)
        return (ap.name, 0, 0, off, off + ext)
    if 'PSUM' in sp.upper():
        return (ap.name, 0, 127, 0, 10 ** 9)
    fs = 1
    for d in shp[1:]:
        fs *= int(d)
    p0 = off // fs
    f0 = off % fs
    ps, pc = pat[0]
    p1 = p0 + (pc - 1) * (ps // fs if fs else 0)
    ext = sum((c - 1) * abs(s) for s, c in pat[1:])
    return (ap.name, p0, p1, f0, f0 + ext)


def _overlap(a, b):
    return not (a[2] < b[1] or b[2] < a[1] or a[4] < b[3] or b[4] < a[3])


def _contains(a, b):
    return a[1] <= b[1] and a[2] >= b[2] and a[3] <= b[3] and a[4] >= b[4]


class Prog:
    ENGS = ['pe', 'act', 'dve', 'pool', 'sp']

    def __init__(self, nc):
        self.nc = nc
        self.stream = {e: [] for e in self.ENGS}
        self.count = {e: 0 for e in self.ENGS}
        self.dcount = {'sp': 0, 'pool': 0, 'act': 0}
        self.hist = {}
        self.known = {e: {} for e in self.ENGS}
        self.nops = 0

    def _deps(self, eng, reads, writes, is_dma=False):
        deps = {}

        def add(tok):
            k, v, e = tok
            if e == eng and eng == 'pe' and not k.startswith('d_') and not is_dma:
                return
            if deps.get(k, (0,))[0] < v:
                deps[k] = (v, e)
        rr = [_region(a) for a in reads]
        wr = [_region(a) for a in writes]
        for r in rr:
            psum = (r[4] == 10 ** 9)
            for (reg, tok, isw) in self.hist.get(r[0], ()):
                if isw and _overlap(reg, r):
                    add(tok)
                elif psum and not isw and tok[2] != eng:
                    add(tok)
        for w in wr:
            for (reg, tok, isw) in self.hist.get(w[0], ()):
                if _overlap(reg, w):
                    add(tok)
        return deps, rr, wr

    def _update(self, tok, rr, wr):
        for w in wr:
            h = self.hist.setdefault(w[0], [])
            h[:] = [x for x in h if not _contains(w, x[0])]
            h.append((w, tok, True))
        for r in rr:
            h = self.hist.setdefault(r[0], [])
            h[:] = [x for x in h if not (not x[2] and x[1][2] == tok[2] and x[1][0] == tok[0] and x[0] == r)]
            h.append((r, tok, False))

    def _waits(self, eng, deps):
        waits = []
        kn = self.known[eng]
        for k, (v, e) in deps.items():
            if kn.get(k, 0) < v:
                kn[k] = v
                waits.append((k, v))
        return waits

    def op(self, eng, fn, reads, writes):
        deps, rr, wr = self._deps(eng, reads, writes)
        waits = self._waits(eng, deps)
        self.count[eng] += 1
        tok = ('c_' + eng, self.count[eng], eng)
        self.stream[eng].append((waits, fn, tok, 1))
        self._update(tok, rr, wr)
        self.nops += 1
        return tok

    def dma(self, out, in_, q='sp', **kw):
        eng = q
        deps, rr, wr = self._deps(eng, [in_], [out], is_dma=True)
        i = self.dcount[q]
        self.dcount[q] += 1
        slot = i % NSLOT
        val = 16 * (i // NSLOT + 1)
        key = 'd_%s_%d' % (q, slot)
        if val > 16:
            if deps.get(key, (0,))[0] < val - 16:
                deps[key] = (val - 16, q)
        waits = self._waits(eng, deps)
        tok = (key, val, eng)

        def fn(e, out=out, in_=in_, kw=kw):
            return e.dma_start(out=out, in_=in_, **kw)
        self.stream[eng].append((waits, fn, tok, 16))
        self._update(tok, rr, wr)
        self.nops += 1
        return tok

    def dma_like(self, fn, reads, writes, q='pool'):
        eng = q
        deps, rr, wr = self._deps(eng, reads, writes, is_dma=True)
        i = self.dcount[q]
        self.dcount[q] += 1
        slot = i % NSLOT
        val = 16 * (i // NSLOT + 1)
        key = 'd_%s_%d' % (q, slot)
        if val > 16:
            if deps.get(key, (0,))[0] < val - 16:
                deps[key] = (val - 16, q)
        waits = self._waits(eng, deps)
        tok = (key, val, eng)
        self.stream[eng].append((waits, fn, tok, 16))
        self._update(tok, rr, wr)
        self.nops += 1
        return tok

    def mm(self, out, lhsT, rhs, start=True, stop=True, **kw):
        return self.op('pe', lambda e: e.matmul(out, lhsT, rhs, start=start, stop=stop, **kw),
                       [lhsT, rhs] + ([] if start else [out]), [out])

    def tr(self, out, in_, ident):
        return self.op('pe', lambda e: e.transpose(out, in_, ident), [in_, ident], [out])

    def actf(self, out, in_, func, bias=None, scale=None, accum_out=None, eng='act'):
        kw = {}
        rd = [in_]
        wr = [out]
        if bias is not None:
            kw['bias'] = bias
            if not isinstance(bias, (int, float)):
                rd.append(bias)
        if scale is not None:
            kw['scale'] = scale
            if not isinstance(scale, (int, float)):
                rd.append(scale)
        if accum_out is not None:
            kw['accum_out'] = accum_out
            wr.append(accum_out)
        return self.op(eng, lambda e: e.activation(out, in_, func, **kw), rd, wr)

    def tt(self, out, in0, in1, op, eng='dve'):
        return self.op(eng, lambda e: e.tensor_tensor(out, in0, in1, op), [in0, in1], [out])

    def ts(self, out, in0, s1, s2, op0, op1=None, accum_out=None, eng='dve'):
        rd = [in0] + [s for s in (s1, s2) if s is not None and not isinstance(s, (int, float))]
        wr = [out] + ([accum_out] if accum_out is not None else [])
        kw = {}
        if op1 is not None:
            kw['op1'] = op1
        if accum_out is not None:
            kw['accum_out'] = accum_out
        return self.op(eng, lambda e: e.tensor_scalar(out, in0, s1, s2, op0, **kw), rd, wr)

    def stt(self, out, in0, scalar, in1, op0, op1, eng='dve'):
        rd = [in0, in1] + ([scalar] if not isinstance(scalar, (int, float)) else [])
        return self.op(eng, lambda e: e.scalar_tensor_tensor(out, in0, scalar, in1, op0, op1), rd, [out])

    def copy(self, out, in_, eng='dve'):
        if eng == 'act':
            return self.op('act', lambda e: e.copy(out, in_), [in_], [out])
        return self.op(eng, lambda e: e.tensor_copy(out, in_), [in_], [out])

    def memset(self, ap, v, eng='dve'):
        return self.op(eng, lambda e: e.memset(ap, v), [], [ap])

    def emit(self):
        nc = self.nc
        import contextlib
        es = contextlib.ExitStack()
        sems = {}

        def sem(k):
            if k not in sems:
                sems[k] = es.enter_context(nc.semaphore(k))
            return sems[k]
        for e in self.ENGS:
            sem('c_' + e)
        for q in self.dcount:
            for s in range(NSLOT):
                sem('d_%s_%d' % (q, s))
        final = []
        for e in self.ENGS:
            if e != 'sp' and self.count[e] > 0:
                final.append(('c_' + e, self.count[e]))
        for q, n in self.dcount.items():
            for s in range(NSLOT):
                cnt = (n - s + NSLOT - 1) // NSLOT if n > s else 0
                if cnt > 0:
                    final.append(('d_%s_%d' % (q, s), 16 * cnt))
        streams = self.stream

        def run(engname, e):
            for (waits, fn, tok, inc) in streams[engname]:
                for (k, v) in waits:
                    e.wait_ge(sem(k), v)
                ins = fn(e)
                ins.then_inc(sem(tok[0]), inc)
            if engname == 'sp':
                for (k, v) in final:
                    e.wait_ge(sem(k), v)
        with nc.Block() as block:
            @block.tensor
            def _(e):
                run('pe', e)

            @block.scalar
            def _(e):
                run('act', e)

            @block.vector
            def _(e):
                run('dve', e)

            @block.gpsimd
            def _(e):
                run('pool', e)

            @block.sync
            def _(e):
                run('sp', e)
        es.close()


D = 1024
ALPHA = 2.0 ** 0.25
LN_EPS = 1e-5
DFF = 2816
NT = 2050
NJ = DFF // 128


class Pools:
    def __init__(self, P, nc):
        self.P = P
        self.nc = nc
        self.banks = [nc.alloc_psum_tensor("bank%d" % i, [128, 512], F32) for i in range(8)]
        self.bi = 0
        self.stg = [nc.alloc_sbuf_tensor("stg%d" % i, [128, 2048], F32) for i in range(2)]
        self.si = 0
        self.ci = 0

    def bank(self):
        b = self.banks[self.bi % 8]
        self.bi += 1
        return b

    def nrot(self):
        self.bi += 1
        return self.bi

    def load_cast(self, dst, src, q='sp'):
        P = self.P
        shp = dst.shape
        npart = shp[0]
        if len(shp) == 2:
            n = shp[1]
            step = 2048
            for c0 in range(0, n, step):
                c1 = min(n, c0 + step)
                st = self.stg[self.si % 2]
                self.si += 1
                P.dma(st[0:npart, 0:c1 - c0], src[:, c0:c1], q=q)
                self._cast(dst[:, c0:c1], st[0:npart, 0:c1 - c0])
        else:
            a, n = shp[1], shp[2]
            assert n <= 2048
            per = max(1, 2048 // n)
            for a0 in range(0, a, per):
                a1 = min(a, a0 + per)
                st = self.stg[self.si % 2]
                self.si += 1
                sv = st[0:npart, 0:(a1 - a0) * n].rearrange("p (a n) -> p a n", n=n)
                P.dma(sv, src[:, a0:a1, :], q=q)
                self._cast(dst[:, a0:a1, :], sv)

    def _cast(self, dst, src):
        engs = ['pool', 'dve', 'act', 'pool']
        e = engs[self.ci % len(engs)]
        self.ci += 1
        self.P.copy(dst, src, eng=e)


def chan_ln(P, pl, res, resb, gcol, bcol, N, ones_b, scr_b, scr_sq, tmp):
    pm = pl.bank()
    psq = pl.bank()
    for c in range(8):
        P.actf(scr_b[:, c, 0:N], res[:, c, 0:N], AF.Copy)
        P.actf(scr_sq[:, c, 0:N], res[:, c, 0:N], AF.Square)
    for c in range(8):
        P.mm(pm[:, 0:N], ones_b[:], scr_b[:, c, 0:N], start=(c == 0), stop=(c == 7))
    for c in range(8):
        P.mm(psq[:, 0:N], ones_b[:], scr_sq[:, c, 0:N], start=(c == 0), stop=(c == 7))
    mean, msq, var, rstd = [t[:, 0:N] for t in tmp[:4]]
    P.actf(mean, pm[:, 0:N], AF.Copy, scale=1.0 / D)
    P.tt(msq, mean, mean, ALU.mult)
    P.stt(var, psq[:, 0:N], 1.0 / D, msq, ALU.mult, ALU.subtract)
    P.ts(var, var, LN_EPS, None, ALU.add)
    P.actf(var, var, AF.Sqrt)
    P.op('dve', lambda e: e.reciprocal(rstd, var), [var], [rstd])
    for c in range(8):
        P.tt(res[:, c, 0:N], res[:, c, 0:N], mean, ALU.subtract)
        P.tt(res[:, c, 0:N], res[:, c, 0:N], rstd, ALU.mult)
        P.actf(res[:, c, 0:N], res[:, c, 0:N], AF.Identity, bias=bcol[:, c:c + 1], scale=gcol[:, c:c + 1])
        P.copy(resb[:, c, 0:N], res[:, c, 0:N], eng='pool')


class _Stop(Exception):
    pass


def build_tail(stop=99):
    nc = bass.Bass("TRN2", target_bir_lowering=False)
    try:
        return _build_tail(nc, stop)
    except _Stop:
        return nc


def _build_tail(nc, stop):
    def ck(code):
        if stop == code:
            P.emit()
            raise _Stop()
    dr = lambda n, s, k="ExternalInput": nc.dram_tensor(n, s, F32, kind=k).ap()
    xT = dr("xT", [D, NT]); yaT = dr("yaT", [512, NT]); ybT = dr("ybT", [512, NT]); memT = dr("memT", [D, 256])
    wg = dr("wg", [D, 2048]); pa = dr("pa", [512, D]); pb = dr("pb", [512, D]); wo = dr("wo", [D, D])
    wq = dr("wq", [D, D]); wk = dr("wk", [D, D]); wv = dr("wv", [D, D]); xwo = dr("xwo", [D, D])
    wup = dr("wup", [D, 2 * DFF]); wdn = dr("wdn", [DFF, D])
    lnp = dr("lnp", [128, 48]); cw = dr("cw", [128, 44 * 3]); cb = dr("cb", [128, 44]); hmask = dr("hmask", [128, 1])
    outT = dr("outT", [D, 2048], "ExternalOutput")
    x2s = dr("x2s", [D, NT], "Internal")
    chunked = lambda ap: ap.rearrange("(c p) n -> p c n", p=128)

    P = Prog(nc)
    pl = Pools(P, nc)
    sb = lambda n, s, dt=F32: nc.alloc_sbuf_tensor(n, s, dt)
    NA = 256
    arena = sb("arena", [128, 49152], BF16)
    lnp_t = sb("lnp_t", [128, 48]); cw_t = sb("cw_t", [128, 132]); cb_t = sb("cb_t", [128, 44]); hm_t = sb("hm_t", [128, 1])
    ones_b = sb("ones_b", [128, 128], BF16)
    res = sb("res", [128, 8, 258]); resb = sb("resb", [128, 8, 258], BF16)
    yab = sb("yab", [128, 4, NA], BF16); ybb = sb("ybb", [128, 4, NA], BF16)
    actb2 = sb("actb2", [128, 8, NA], BF16); qT = sb("qT", [128, 8, NA], BF16)
    scr_b = sb("scr_b", [128, 8, 258], BF16); scr_sq = sb("scr_sq", [128, 8, 258], BF16)
    tmp = [sb("tmp%d" % i, [128, 258]) for i in range(8)]
    KT = sb("KT", [128, 8, 256], BF16); V = sb("V", [128, 2, D], BF16); memb = sb("memb", [128, 8, 256], BF16)
    pbuf = [sb("pbuf%d" % i, [128, NA], BF16) for i in range(2)]
    wdn_r = [sb("wdn_r%d" % i, [128, NJ, 128], BF16) for i in range(2)]
    gall = sb("gall", [128, NJ, 256], BF16)
    hs = [sb("hs%d" % i, [128, 258]) for i in range(2)]
    ostg = [sb("ostg%d" % i, [128, 256]) for i in range(2)]

    P.dma(lnp_t[:], lnp); P.dma(cw_t[:], cw); P.dma(cb_t[:], cb); P.dma(hm_t[:], hmask)
    P.memset(ones_b[:], 1.0)

    if stop <= 0:
        P.emit(); return nc
    def A(off, kc, n):
        return arena[:, off:off + kc * n].rearrange("p (c n) -> p c n", n=n)
    Wk_b = A(0, 8, D); Wv_b = A(8192, 8, D)
    pl.load_cast(memb[:], chunked(memT))
    pl.load_cast(Wk_b, chunked(wk)); pl.load_cast(Wv_b, chunked(wv), q='pool')
    if stop == 1:
        P.emit(); return nc
    for oc in range(8):
        ps = pl.bank()
        for kc in range(8):
            P.mm(ps[:, 0:256], Wk_b[:, kc, oc * 128:(oc + 1) * 128], memb[:, kc, :], start=(kc == 0), stop=(kc == 7))
        P.copy(KT[:, oc, :], ps[:, 0:256], eng='act')
    for mc in range(2):
        for hf in range(2):
            ps = pl.bank()
            for kc in range(8):
                P.mm(ps[:, :], memb[:, kc, mc * 128:(mc + 1) * 128], Wv_b[:, kc, hf * 512:(hf + 1) * 512], start=(kc == 0), stop=(kc == 7))
            P.copy(V[:, mc, hf * 512:(hf + 1) * 512], ps[:, :], eng='dve')
    if stop <= 1:
        P.dma(outT[0:128, 0:256], KT[:, 0, :].bitcast(F32)[:, 0:128] if False else tmp[0][:, 0:256]); P.emit(); return nc
    Wg_b = A(0, 8, 2048); Pa_b = A(16384, 4, D); Pb_b = A(20480, 4, D); Wo_b = A(24576, 8, D)
    Wq_b = A(32768, 8, D); XWo_b = A(40960, 8, D)
    pl.load_cast(Pa_b, chunked(pa)); pl.load_cast(Pb_b, chunked(pb), q='pool')
    pl.load_cast(Wo_b, chunked(wo)); pl.load_cast(Wq_b, chunked(wq), q='pool'); pl.load_cast(XWo_b, chunked(xwo))
    pl.load_cast(Wg_b, chunked(wg), q='pool')
    g1, b1, g2, b2, g3, b3 = [lnp_t[:, i * 8:(i + 1) * 8] for i in range(6)]

    if stop <= 2:
        P.emit(); return nc
    tiles = [(0, 2)] + [(2 + i * NA, NA) for i in range(8)]
    if stop <= 3:
        tiles = tiles[:1]
    if stop == 4 or 40 < stop < 50:
        tiles = tiles[1:2]
    xTc, yaTc, ybTc, x2sc, outTc = chunked(xT), chunked(yaT), chunked(ybT), chunked(x2s), chunked(outT)
    for (c0, N) in tiles:
        P.dma(res[:, :, 0:N], xTc[:, :, c0:c0 + N])
        st = pl.stg[pl.si % 2]; pl.si += 1
        sv = st[:, 0:8 * N].rearrange("p (a n) -> p a n", n=N)
        P.dma(sv[:, 0:4, :], yaTc[:, :, c0:c0 + N], q='pool'); P.dma(sv[:, 4:8, :], ybTc[:, :, c0:c0 + N], q='pool')
        P.copy(yab[:, :, 0:N], sv[:, 0:4, :], eng='pool'); P.copy(ybb[:, :, 0:N], sv[:, 4:8, :], eng='pool')
        for c in range(8):
            P.copy(resb[:, c, 0:N], res[:, c, 0:N], eng='act' if c % 2 else 'dve')
        ck(41)
        for oc in range(8):
            osl = slice(oc * 128, (oc + 1) * 128)
            za = pl.bank(); ga = pl.bank()
            for kc in range(4):
                P.mm(za[:, 0:N], Pa_b[:, kc, osl], yab[:, kc, 0:N], start=(kc == 0), stop=(kc == 3))
            for kc in range(4):
                P.mm(za[:, 256:256 + N], Pb_b[:, kc, osl], ybb[:, kc, 0:N], start=(kc == 0), stop=(kc == 3))
            for kc in range(8):
                P.mm(ga[:, 0:N], Wg_b[:, kc, osl], resb[:, kc, 0:N], start=(kc == 0), stop=(kc == 7))
            for kc in range(8):
                P.mm(ga[:, 256:256 + N], Wg_b[:, kc, 1024 + oc * 128:1024 + (oc + 1) * 128], resb[:, kc, 0:N], start=(kc == 0), stop=(kc == 7))
            sa, sbb, t1, t2 = tmp[4][:, 0:N], tmp[5][:, 0:N], tmp[6][:, 0:N], tmp[7][:, 0:N]
            P.actf(sa, ga[:, 0:N], AF.Sigmoid)
            P.actf(sbb, ga[:, 256:256 + N], AF.Sigmoid)
            P.tt(t1, sa, za[:, 0:N], ALU.mult)
            P.tt(t2, sbb, za[:, 256:256 + N], ALU.mult)
            P.tt(actb2[:, oc, 0:N], t1, t2, ALU.add)
        ck(42)
        for oc in range(8):
            osl = slice(oc * 128, (oc + 1) * 128)
            ps = pl.bank()
            for kc in range(8):
                P.mm(ps[:, 0:N], Wo_b[:, kc, osl], actb2[:, kc, 0:N], start=(kc == 0), stop=(kc == 7))
            P.stt(res[:, oc, 0:N], res[:, oc, 0:N], ALPHA, ps[:, 0:N], ALU.mult, ALU.add)
        ck(43)
        chan_ln(P, pl, res, resb, g1, b1, N, ones_b, scr_b, scr_sq, tmp)
        ck(44)
        for oc in range(8):
            osl = slice(oc * 128, (oc + 1) * 128)
            ps = pl.bank()
            for kc in range(8):
                P.mm(ps[:, 0:N], Wq_b[:, kc, osl], resb[:, kc, 0:N], start=(kc == 0), stop=(kc == 7))
            P.copy(qT[:, oc, 0:N], ps[:, 0:N], eng='act' if oc % 2 else 'dve')
        for hh in range(4):
            pden = pl.bank()
            for mc in range(2):
                ps = pl.bank()
                for dc in range(2):
                    P.mm(ps[:, 0:N], KT[:, 2 * hh + dc, mc * 128:(mc + 1) * 128], qT[:, 2 * hh + dc, 0:N], start=(dc == 0), stop=(dc == 1))
                P.actf(pbuf[mc][:, 0:N], ps[:, 0:N], AF.Exp, scale=1.0 / 16.0)
            for mc in range(2):
                P.mm(pden[:, 0:N], ones_b[:], pbuf[mc][:, 0:N], start=(mc == 0), stop=(mc == 1))
            rec = tmp[4][:, 0:N]
            P.op('dve', lambda e, rec=rec, pden=pden, N=N: e.reciprocal(rec, pden[:, 0:N]), [pden[:, 0:N]], [rec])
            for dc in range(2):
                po = pl.bank()
                for mc in range(2):
                    P.mm(po[:, 0:N], V[:, mc, (2 * hh + dc) * 128:(2 * hh + dc + 1) * 128], pbuf[mc][:, 0:N], start=(mc == 0), stop=(mc == 1))
                P.tt(actb2[:, 2 * hh + dc, 0:N], po[:, 0:N], rec, ALU.mult)
        for oc in range(8):
            osl = slice(oc * 128, (oc + 1) * 128)
            ps = pl.bank()
            for kc in range(8):
                P.mm(ps[:, 0:N], XWo_b[:, kc, osl], actb2[:, kc, 0:N], start=(kc == 0), stop=(kc == 7))
            P.stt(res[:, oc, 0:N], res[:, oc, 0:N], ALPHA, ps[:, 0:N], ALU.mult, ALU.add)
        ck(45)
        chan_ln(P, pl, res, resb, g2, b2, N, ones_b, scr_b, scr_sq, tmp)
        ck(46)
        P.dma(x2sc[:, :, c0:c0 + N], res[:, :, 0:N], q='pool')

    if stop <= 5 or 40 < stop < 50:
        P.emit(); return nc
    Wup_b = A(0, 8, 2 * DFF)
    wupc = chunked(wup)
    wdnc = chunked(wdn)
    for kc in range(8):
        for h0 in range(0, 2 * DFF, 2048):
            h1 = min(2 * DFF, h0 + 2048)
            pl.load_cast(Wup_b[:, kc, h0:h1], wupc[:, kc, h0:h1], q='sp' if kc % 2 else 'pool')
    cw3 = cw_t[:].rearrange("p (c k) -> p c k", k=3)
    for ti in range(8):
        c0 = ti * 256
        P.dma(res[:, :, 0:258], x2sc[:, :, c0:c0 + 258])
        for c in range(8):
            P.copy(resb[:, c, 0:258], res[:, c, 0:258], eng='act' if c % 2 else 'dve')
        for j in range(NJ):
            hp = pl.bank(); hv = pl.bank()
            for kc in range(8):
                P.mm(hp[:, 0:258], Wup_b[:, kc, j * 128:(j + 1) * 128], resb[:, kc, 0:258], start=(kc == 0), stop=(kc == 7))
            for kc in range(8):
                P.mm(hv[:, 0:258], Wup_b[:, kc, DFF + j * 128:DFF + (j + 1) * 128], resb[:, kc, 0:258], start=(kc == 0), stop=(kc == 7))
            srcs = []
            for (hsrc, k) in ((hp, 0), (hv, 1)):
                if ti == 0:
                    hb_ = hs[k]
                    P.copy(hb_[:, 0:258], hsrc[:, 0:258], eng='act')
                    P.ts(hb_[:, 0:2], hb_[:, 0:2], hm_t[:, 0:1], None, ALU.mult)
                    srcs.append(hb_)
                else:
                    srcs.append(hsrc)
            outs = []
            for k, (src, ch) in enumerate(zip(srcs, (j, NJ + j))):
                t = tmp[k * 2]; t2 = tmp[k * 2 + 1]
                P.actf(t[:, 0:256], src[:, 2:258], AF.Identity, bias=cb_t[:, ch:ch + 1], scale=cw3[:, ch, 2:3])
                P.stt(t2[:, 0:256], src[:, 1:257], cw3[:, ch, 1:2], t[:, 0:256], ALU.mult, ALU.add)
                P.stt(t[:, 0:256], src[:, 0:256], cw3[:, ch, 0:1], t2[:, 0:256], ALU.mult, ALU.add)
                outs.append(t)
            sg = tmp[4]
            P.actf(sg[:, 0:256], outs[0][:, 0:256], AF.Silu)
            P.tt(gall[:, j, :], sg[:, 0:256], outs[1][:, 0:256], ALU.mult)
        for oc in range(8):
            wr = wdn_r[oc % 2]
            pl.load_cast(wr[:, :, :], wdnc[:, :, oc * 128:(oc + 1) * 128], q='sp')
            a = pl.bank()
            for j in range(NJ):
                P.mm(a[:, 0:256], wr[:, j, :], gall[:, j, :], start=(j == 0), stop=(j == NJ - 1))
            P.stt(res[:, oc, 2:258], res[:, oc, 2:258], ALPHA, a[:, 0:256], ALU.mult, ALU.add)
        res_v = res[:, :, 2:258]; resb_v = resb[:, :, 2:258]
        chan_ln(P, pl, res_v, resb_v, g3, b3, 256, ones_b, scr_b, scr_sq, tmp)
        P.dma(outTc[:, :, ti * 256:(ti + 1) * 256], res[:, :, 2:258], q='pool')
    P.emit()
    return nc


def tail_inputs(d, ya, yb):
    l = 0
    x = d['x']
    maps = []
    RC, NC_ = 1664, 1304
    wg = np.ascontiguousarray(d['w_in'][l][:, RC + NC_:])
    lnp = np.concatenate([d[k][l].reshape(8, 128).T for k in ('ln1_g', 'ln1_b', 'ln2_g', 'ln2_b', 'ln3_g', 'ln3_b')], axis=1)
    cw = np.ascontiguousarray(d['ffn_conv_w'][l].reshape(3, 44, 128).transpose(2, 1, 0)).reshape(128, 132)
    cb = np.ascontiguousarray(d['ffn_conv_b'][l].reshape(44, 128).T)
    common = dict(wg=wg, pa=d['merge_p_a'][l], pb=d['merge_p_b'][l], wo=d['mix_w_o'][l], wq=d['xa_wq'][l], wk=d['xa_wk'][l],
                  wv=d['xa_wv'][l], xwo=d['xa_wo'][l], wup=d['ffn_w_up'][l], wdn=d['ffn_w_down'][l],
                  lnp=np.ascontiguousarray(lnp), cw=cw, cb=cb)
    common = {k: np.ascontiguousarray(v, dtype=np.float32) for k, v in common.items()}

    def halo_T(a, b, t0):
        C = a.shape[-1]
        o = np.zeros((C, NT), np.float32)
        lo = max(0, t0 - 2)
        o[:, 2 - (t0 - lo):] = a[b, lo:t0 + 2048].T
        return o
    for c in range(8):
        b, t0 = c // 4, (c % 4) * 2048
        m = dict(common)
        m['xT'] = halo_T(x, b, t0); m['yaT'] = halo_T(ya, b, t0); m['ybT'] = halo_T(yb, b, t0)
        m['memT'] = np.ascontiguousarray(d['mem'][b].T)
        m['hmask'] = np.full((128, 1), 0.0 if t0 == 0 else 1.0, np.float32)
        maps.append(m)
    return maps


_NC_CACHE = {}


def run_tail(d, ya, yb):
    if 'tail' not in _NC_CACHE:
        _NC_CACHE['tail'] = build_tail()
    nc = _NC_CACHE['tail']
    res = run_bass_kernel_spmd(nc, tail_inputs(d, ya, yb), core_ids=list(range(8)))
    out = np.zeros((2, 8192, D), np.float32)
    for c in range(8):
        b, t0 = c // 4, (c % 4) * 2048
        out[b, t0:t0 + 2048] = res.results[c]['outT'].T
    return out


SEQ = 8192
GN_EPS = 64e-5
CH = 64
NCH = 128 // CH
LV = 5


def rwkv_consts():
    idx = np.arange(128)
    same = (idx[:, None] // CH == idx[None, :] // CH)
    m_strict = (same & (idx[:, None] < idx[None, :])).astype(np.float32)
    m_incl = (same & (idx[:, None] <= idx[None, :])).astype(np.float32)
    tri_gt = (same & (idx[:, None] > idx[None, :])).astype(np.float32)
    cind = (idx[:, None] // CH == np.arange(4)[None, :]).astype(np.float32)
    ident = np.eye(128, dtype=np.float32)
    return np.concatenate([ident, m_incl, tri_gt, m_strict, m_incl, m_strict.T, m_strict.T, cind, np.zeros((128, 60), np.float32)], axis=1).astype(np.float32)


def build_rwkv(ntok=SEQ, stop=99):
    nc = bass.Bass("TRN2", target_bir_lowering=False)
    try:
        return _build_rwkv(nc, ntok, stop)
    except _Stop:
        return nc


def _build_rwkv(nc, ntok, stop):
    def ck(code):
        if stop == code:
            P.emit()
            raise _Stop()
    dr = lambda n, s, k="ExternalInput": nc.dram_tensor(n, s, F32, kind=k).ap()
    xT = dr("xT", [D, ntok]); w_r = dr("w_r", [D, 512]); cmat = dr("cmat", [128, 960])
    pcol = dr("pcol", [128, 12])
    w2a2 = dr("w2a2", [64, 256])
    w0a0 = dr("w0a0", [128, 256])
    yT = dr("yT", [128, ntok], "ExternalOutput")
    chunked = lambda ap: ap.rearrange("(c p) n -> p c n", p=128)
    P = Prog(nc)
    pl = Pools(P, nc)
    sb = lambda n, s, dt=F32: nc.alloc_sbuf_tensor(n, s, dt)
    cm = sb("cm", [128, 960]); pc = sb("pc", [128, 12]); wa = sb("wa", [64, 256]); w0 = sb("w0", [128, 256])
    P.dma(cm[:], cmat); P.dma(pc[:], pcol); P.dma(wa[:], w2a2); P.dma(w0[:], w0a0)
    ident = cm[:, 0:128]; TriInc = cm[:, 128:256]; TriGt = cm[:, 256:384]
    MSK1 = cm[:, 384:640]; Mincl = cm[:, 512:640]; MSK3 = cm[:, 640:896]; CInd = cm[:, 896:960]
    dkk = sb("dkk", [128, 128]); dka = sb("dka", [128, 128]); drk = sb("drk", [128, 128])
    P.ts(dkk[:], ident, pc[:, 4:5], None, ALU.mult)
    P.ts(dka[:], ident, pc[:, 5:6], None, ALU.mult)
    P.ts(drk[:], ident, pc[:, 6:7], None, ALU.mult)
    wrb = sb("wrb", [128, 8, 512], BF16)
    pl.load_cast(wrb[:], chunked(w_r))
    xt32 = [sb("xt32_%d" % i, [128, 8, 512]) for i in range(1)]
    xb = sb("xb", [128, 8, 512], BF16)
    pS = [[sb("pS%d_%d" % (i, c), [128, 513]) for c in range(5)] for i in range(2)]
    pL = [sb("pL%d" % c, [128, 512]) for c in range(5)]
    for c in range(5):
        P.memset(pS[1][c][:, 512:513], 0.0)
    tok = {n: sb("tk_" + n, [128, 128]) for n in ("r", "k", "v", "kkp", "kka", "rrk", "logw", "a", "kk", "kmod", "e1", "e2", "e3", "e4", "Bt", "Kt", "Rt", "Kh", "t0", "t1", "yn", "bv")}
    small = sb("small", [128, 16])
    RH1 = [sb("RH1_%d" % h, [128, 192]) for h in range(2)]
    RH2 = [sb("RH2_%d" % h, [128, 192]) for h in range(2)]
    AK = [sb("AK_%d" % h, [128, 192]) for h in range(2)]
    CM = [sb("CM_%d" % h, [64, 512]) for h in range(2)]
    XX = [[sb("XX%d_%d" % (h, i), [128, 256]) for i in range(2)] for h in range(2)]
    TT = [[sb("TT%d_%d" % (h, i), [128, 128]) for i in range(2)] for h in range(2)]
    QS = [sb("QS_%d" % h, [128, 192]) for h in range(2)]
    RpZ = [sb("RpZ_%d" % h, [128, NCH, 128]) for h in range(2)]
    msk = [sb("msk_%d" % h, [128, 2 * NCH, 64]) for h in range(2)]
    WYKP = [sb("WYKP_%d" % h, [128, 192]) for h in range(2)]
    GS = [sb("GS_%d" % h, [128, 256]) for h in range(2)]
    lamC = [sb("lamC_%d" % h, [64, 4]) for h in range(2)]
    STS = [sb("STS_%d" % h, [128, 5 * 64]) for h in range(2)]
    for h in range(2):
        P.memset(STS[h][:, :], 0.0); P.memset(GS[h][:, :], 0.0); P.memset(RpZ[h][:, :, :], 0.0)
    bnst = sb("bnst", [128, 2, 6]); bnag = sb("bnag", [128, 2, 2])
    ystg = [sb("ystg%d" % i, [128, 512]) for i in range(2)]
    lh = sb("lh", [128, 512])
    xTc = chunked(xT)
    NB = ntok // 512
    for blk in range(NB):
        par = blk % 2
        P.dma(xt32[0][:, 0:4, :], xTc[:, 0:4, blk * 512:(blk + 1) * 512])
        P.dma(xt32[0][:, 4:8, :], xTc[:, 4:8, blk * 512:(blk + 1) * 512], q='pool')
        for c in range(8):
            P.copy(xb[:, c, :], xt32[0][:, c, :], eng=('act', 'dve', 'pool')[c % 3])
        for c in range(5):
            ps = pl.bank()
            c0, c1 = ((c * 128, (c + 1) * 128) if c < 3 else ((384, 448) if c == 3 else (448, 512)))
            R = c1 - c0
            for kc in range(8):
                P.mm(ps[0:R, :], wrb[:, kc, c0:c1], xb[:, kc, :], start=(kc == 0), stop=(kc == 7))
            cur = pS[par][c]; prev = pS[1 - par][c]
            P.copy(cur[0:R, 1:513], ps[0:R, :], eng='act')
            P.copy(cur[0:R, 0:1], prev[0:R, 512:513], eng='dve')
            t = pL[c]
            mucol = pc[0:R, c:c + 1] if c < 3 else pc[0:R, 6 + c:7 + c]
            P.tt(t[0:R, :], cur[0:R, 0:512], cur[0:R, 1:513], ALU.subtract)
            P.stt(t[0:R, :], t[0:R, :], mucol, cur[0:R, 1:513], ALU.mult, ALU.add)
        P.actf(lh[0:64, :], pL[3][0:64, :], AF.Tanh)
        ys = ystg[blk % 2]
        ck(1)
        for sub in range(4):
            ts_ = slice(sub * 128, (sub + 1) * 128)
            ps = pl.bank()
            P.mm(ps[:, 0:128], pL[0][:, ts_], ident); P.mm(ps[:, 128:256], pL[1][:, ts_], ident); P.mm(ps[:, 256:384], pL[2][:, ts_], ident)
            P.copy(tok["r"][:], ps[:, 0:128], eng='act'); P.copy(tok["k"][:], ps[:, 128:256], eng='dve'); P.copy(tok["v"][:], ps[:, 256:384], eng='act')
            ps2 = pl.bank()
            P.mm(ps2[:, 0:128], pL[1][:, ts_], dkk[:]); P.mm(ps2[:, 128:256], pL[1][:, ts_], dka[:]); P.mm(ps2[:, 256:384], pL[0][:, ts_], drk[:])
            P.copy(tok["kkp"][:], ps2[:, 0:128], eng='dve'); P.copy(tok["kka"][:], ps2[:, 128:256], eng='act'); P.copy(tok["rrk"][:], ps2[:, 256:384], eng='dve')
            ps3 = pl.bank()
            P.mm(ps3[:, 0:128], lh[0:64, ts_], wa[0:64, 0:128])
            P.mm(ps3[:, 128:256], pL[4][0:64, ts_], wa[0:64, 128:256])
            P.tt(tok["t0"][:], ps3[:, 0:128], w0[:, 0:128], ALU.add)
            P.tt(tok["t1"][:], ps3[:, 128:256], w0[:, 128:256], ALU.add)
            P.actf(tok["logw"][:], tok["t0"][:], AF.Sigmoid)
            P.ts(tok["logw"][:], tok["logw"][:], -float(np.exp(-0.5)), None, ALU.mult)
            P.actf(tok["a"][:], tok["t1"][:], AF.Sigmoid)
            P.actf(tok["t0"][:], tok["kkp"][:], AF.Square)
            P.op('dve', lambda e, o=small[:, 0:2], i=tok["t0"][:].rearrange("p (h k) -> p h k", h=2): e.reduce_sum(o, i, AX.X), [tok["t0"][:]], [small[:, 0:2]])
            P.actf(small[:, 2:4], small[:, 0:2], AF.Sqrt)
            P.ts(small[:, 2:4], small[:, 2:4], 1e-12, None, ALU.max)
            P.op('dve', lambda e, o=small[:, 4:6], i=small[:, 2:4]: e.reciprocal(o, i), [small[:, 2:4]], [small[:, 4:6]])
            for h in range(2):
                hs = slice(h * 64, (h + 1) * 64)
                P.ts(tok["kk"][:, hs], tok["kkp"][:, hs], small[:, 4 + h:5 + h], None, ALU.mult)
            P.ts(tok["t0"][:], tok["a"][:], -1.0, None, ALU.add)
            P.tt(tok["t0"][:], tok["t0"][:], tok["kka"][:], ALU.mult)
            P.tt(tok["kmod"][:], tok["t0"][:], tok["k"][:], ALU.add)
            psL = pl.bank()
            P.mm(psL[:, 0:128], TriInc, tok["logw"][:]); P.mm(psL[:, 128:256], TriGt, tok["logw"][:])
            P.tt(tok["t1"][:], psL[:, 0:128], tok["logw"][:], ALU.subtract)
            P.actf(tok["e1"][:], tok["t1"][:], AF.Exp)
            P.actf(tok["e2"][:], psL[:, 0:128], AF.Exp, scale=-1.0)
            P.actf(tok["e3"][:], psL[:, 0:128], AF.Exp)
            P.actf(tok["e4"][:], psL[:, 128:256], AF.Exp)
            P.tt(tok["t0"][:], tok["kk"][:], tok["a"][:], ALU.mult)
            P.tt(tok["Bt"][:], tok["t0"][:], tok["e2"][:], ALU.mult)
            P.tt(tok["Kt"][:], tok["kmod"][:], tok["e2"][:], ALU.mult)
            P.tt(tok["Rt"][:], tok["r"][:], tok["e3"][:], ALU.mult)
            P.tt(tok["t1"][:], tok["kk"][:], tok["e1"][:], ALU.mult)
            for h in range(2):
                hs = slice(h * 64, (h + 1) * 64)
                P.ts(RH1[h][:, 0:64], tok["t1"][:, hs], -1.0, None, ALU.mult)
                P.tt(RH2[h][:, 128:192], tok["t0"][:, hs], tok["e4"][:, hs], ALU.mult)
                P.tt(AK[h][:, 128:192], tok["kmod"][:, hs], tok["e4"][:, hs], ALU.mult)
            P.tt(tok["t1"][:], tok["rrk"][:], tok["kmod"][:], ALU.mult)
            P.op('dve', lambda e, o=small[:, 6:8], i=tok["t1"][:].rearrange("p (h k) -> p h k", h=2): e.reduce_sum(o, i, AX.X), [tok["t1"][:]], [small[:, 6:8]])
            for h in range(2):
                hs = slice(h * 64, (h + 1) * 64)
                P.ts(tok["bv"][:, hs], tok["v"][:, hs], small[:, 6 + h:7 + h], None, ALU.mult)
            ck(2)
            def head_gen(h):
                hs = slice(h * 64, (h + 1) * 64)
                tp = pl.bank()
                P.tr(tp[0:64, 0:128], RH1[h][:, 0:64], ident)
                P.tr(tp[0:64, 128:256], tok["Rt"][:, hs], ident)
                P.tr(tp[0:64, 256:384], tok["Bt"][:, hs], ident)
                P.tr(tp[0:64, 384:512], tok["Kt"][:, hs], ident)
                P.copy(CM[h][:, :], tp[0:64, :], eng='act')
                yield
                Ac, Rc, Bc, Kc = [CM[h][:, i * 128:(i + 1) * 128] for i in range(4)]
                m1 = pl.bank(); m2 = pl.bank(); m3 = pl.bank()
                P.mm(m1[:, 0:256], Bc, CM[h][:, 0:256])
                P.mm(m2[:, 0:128], Kc, Rc)
                P.mm(m3[:, 0:256], Ac, CM[h][:, 256:512])
                X0 = XX[h][0]
                P.tt(X0[:, 0:128], m1[:, 0:128], MSK1[:, 0:128], ALU.mult)
                P.tt(RH2[h][:, 0:128], m1[:, 128:256], Mincl, ALU.mult)
                P.tt(AK[h][:, 0:128], m2[:, 0:128], Mincl, ALU.mult)
                P.tt(X0[:, 128:256], m3[:, 0:128], MSK3[:, 0:128], ALU.mult)
                P.tt(RH1[h][:, 64:192], m3[:, 128:256], MSK3[:, 128:256], ALU.mult)
                yield
                ck(3)
                Tc = TT[h][0]
                P.tt(Tc[:, :], X0[:, 0:128], ident, ALU.add)
                cur = 0
                for lvl in range(LV):
                    Xc = XX[h][cur]; Xn = XX[h][1 - cur]
                    pp = pl.bank()
                    if lvl < LV - 1:
                        P.mm(pp[:, 0:128], Xc[:, 128:256], Xc[:, 0:128])
                    P.mm(pp[:, 128:256], Xc[:, 0:128], Xc[:, 128:256])
                    if lvl < LV - 1:
                        P.copy(Xn[:, :], pp[:, 0:256], eng='act')
                    else:
                        P.copy(Xn[:, 128:256], pp[:, 128:256], eng='act')
                    yield
                    pt_ = pl.bank()
                    P.mm(pt_[:, 0:128], Xn[:, 128:256], TT[h][lvl % 2][:, :])
                    P.tt(TT[h][(lvl + 1) % 2][:, :], TT[h][lvl % 2][:, :], pt_[:, 0:128], ALU.add)
                    yield
                    cur = 1 - cur
                ck(4)
                Tf = TT[h][LV % 2]
                q = pl.bank()
                P.mm(q[:, 0:192], Tf[:, :], RH1[h][:, :])
                P.copy(QS[h][:, :], q[:, 0:192], eng='act')
                yield
                rp = pl.bank()
                P.mm(rp[0:64, 0:128], QS[h][:, 0:64], RH2[h][:, 0:128])
                for c in range(NCH):
                    cs = slice(c * CH, (c + 1) * CH)
                    P.tt(RpZ[h][0:64, c, cs], rp[0:64, cs], Rc[:, cs], ALU.add)
                wk = pl.bank()
                P.mm(wk[:, 0:192], QS[h][:, 64:192], RH2[h][:, :])
                P.tt(WYKP[h][:, :], wk[:, 0:192], AK[h][:, :], ALU.add)
                yield
                gg = pl.bank()
                for c in range(NCH):
                    P.ts(msk[h][:, c, :], QS[h][:, 0:64], CInd[:, c:c + 1], None, ALU.mult)
                    P.ts(msk[h][:, NCH + c, :], WYKP[h][:, 128:192], CInd[:, c:c + 1], None, ALU.mult)
                    P.mm(gg[0:64, c * 64:(c + 1) * 64], msk[h][:, c, :], RH2[h][:, 128:192])
                P.copy(GS[h][0:64, 0:NCH * 64], gg[0:64, 0:NCH * 64], eng='act')
                lc = pl.bank()
                P.mm(lc[0:64, 0:64], tok["logw"][:, hs], CInd)
                P.actf(lamC[h][:, 0:NCH], lc[0:64, 0:NCH], AF.Exp)
                yield
                ck(5)
                for c in range(NCH):
                    cs = slice(c * CH, (c + 1) * CH)
                    sn = pl.bank()
                    STc = STS[h][:, c * 64:(c + 1) * 64]
                    P.mm(sn[0:64, 0:64], GS[h][:, c * 64:(c + 1) * 64], STc, start=True, stop=False)
                    P.mm(sn[0:64, 0:64], msk[h][:, NCH + c, :], tok["v"][:, hs], start=False, stop=True)
                    P.stt(STS[h][0:64, (c + 1) * 64:(c + 2) * 64], STS[h][0:64, c * 64:(c + 1) * 64], lamC[h][:, c:c + 1], sn[0:64, 0:64], ALU.mult, ALU.add)
                    yield
                ck(6)
                yp = pl.bank()
                P.mm(yp[:, 0:64], WYKP[h][:, 0:128], tok["v"][:, hs], start=True, stop=False)
                for c in range(NCH):
                    P.mm(yp[:, 0:64], RpZ[h][:, c, :], STS[h][:, c * 64:(c + 1) * 64], start=False, stop=(c == NCH - 1))
                P.copy(STS[h][0:64, 0:64], STS[h][0:64, NCH * 64:(NCH + 1) * 64], eng='dve')
                P.copy(tok["t0"][:, hs], yp[:, 0:64], eng='act')
                P.op('dve', lambda e, o=bnst[:, h, :], i=tok["t0"][:, hs]: e.bn_stats(o, i), [tok["t0"][:, hs]], [bnst[:, h, :]])
                P.op('dve', lambda e, o=bnag[:, h, :], i=bnst[:, h, :]: e.bn_aggr(o, i), [bnst[:, h, :]], [bnag[:, h, :]])
                P.ts(small[:, 8 + h:9 + h], bnag[:, h, 1:2], GN_EPS, None, ALU.add)
                P.actf(small[:, 8 + h:9 + h], small[:, 8 + h:9 + h], AF.Sqrt)
                P.op('dve', lambda e, o=small[:, 10 + h:11 + h], i=small[:, 8 + h:9 + h]: e.reciprocal(o, i), [small[:, 8 + h:9 + h]], [small[:, 10 + h:11 + h]])
                P.ts(tok["yn"][:, hs], tok["t0"][:, hs], bnag[:, h, 0:1], small[:, 10 + h:11 + h], ALU.subtract, ALU.mult)
            gens = [head_gen(0), head_gen(1)]
            while gens:
                for g_ in list(gens):
                    try:
                        next(g_)
                    except StopIteration:
                        gens.remove(g_)
            ck(7)
            po = pl.bank()
            P.tr(po[:, 0:128], tok["yn"][:], ident)
            P.tr(po[:, 128:256], tok["bv"][:], ident)
            P.actf(ys[:, ts_], po[:, 0:128], AF.Identity, bias=pc[:, 8:9], scale=pc[:, 7:8])
            P.tt(ys[:, ts_], ys[:, ts_], po[:, 128:256], ALU.add)
        P.dma(yT[:, blk * 512:(blk + 1) * 512], ys[:, :], q='pool')
    P.emit()
    return nc


def rwkv_inputs(d, ntok=SEQ):
    l = 0
    W = d['w_in'][l]
    cm = rwkv_consts()
    maps = []
    for c in range(8):
        b, j = c // 4, c % 4
        cols = np.concatenate([np.arange(128 * j, 128 * j + 128), 512 + np.arange(128 * j, 128 * j + 128),
                               1024 + np.arange(128 * j, 128 * j + 128), np.arange(1536, 1664)])
        hc = np.arange(128 * j, 128 * j + 128)
        pcol = np.zeros((128, 12), np.float32)
        pcol[:, 0:4] = d['rwkv_mu'][l][cols].reshape(4, 128).T
        pcol[:, 4] = d['rwkv_k_k'][l][hc]; pcol[:, 5] = d['rwkv_k_a'][l][hc]
        pcol[:, 6] = d['rwkv_r_k'][l].reshape(-1)[hc]; pcol[:, 7] = d['rwkv_gn_g'][l][hc]; pcol[:, 8] = d['rwkv_gn_b'][l][hc]
        pcol[0:64, 9] = d['rwkv_mu'][l][1536:1600]; pcol[0:64, 10] = d['rwkv_mu'][l][1600:1664]
        w2a2 = np.concatenate([d['rwkv_w2'][l][:, hc], d['rwkv_a2'][l][:, hc]], axis=1)
        w0a0 = np.tile(np.concatenate([d['rwkv_w0'][l][hc], d['rwkv_a0'][l][hc]])[None, :], (128, 1))
        maps.append(dict(xT=np.ascontiguousarray(d['x'][b, :ntok].T), w_r=np.ascontiguousarray(W[:, cols]), cmat=cm,
                         pcol=pcol, w2a2=np.ascontiguousarray(w2a2, dtype=np.float32), w0a0=np.ascontiguousarray(w0a0, dtype=np.float32)))
    return maps


def run_rwkv(d):
    if 'rwkv' not in _NC_CACHE:
        _NC_CACHE['rwkv'] = build_rwkv()
    res = run_bass_kernel_spmd(_NC_CACHE['rwkv'], rwkv_inputs(d), core_ids=list(range(8)))
    ya = np.zeros((2, SEQ, 512), np.float32)
    for c in range(8):
        b, j = c // 4, c % 4
        ya[b, :, 128 * j:128 * j + 128] = res.results[c]['yT'].T
    return ya


ROPE_THETA = 500000.0
NEGM = 30000.0


def nsa_consts(ntok):
    pos = np.arange(ntok, dtype=np.float32)
    inv = ROPE_THETA ** (-np.arange(8, dtype=np.float32) * 2.0 / 16.0)
    ang = pos[None, :] * inv[:, None]
    cosT = np.ones((64, ntok), np.float32); sinT = np.zeros((64, ntok), np.float32)
    cosT[0:8] = np.cos(ang); cosT[8:16] = np.cos(ang)
    sinT[0:8] = -np.sin(ang); sinT[8:16] = np.sin(ang)
    p = np.arange(128)
    tri_le = (p[:, None] <= p[None, :]).astype(np.float32)
    tri_gt = (p[:, None] > p[None, :]).astype(np.float32)
    cmk = np.ones((128, 32, 128), np.float32)
    for i in range(16):
        p0 = 8 * i
        u = p[:, None] - p0
        cmk[:, 16 + i, :] = (p[None, :] >= 16 * u + 31).astype(np.float32)
        if p0 == 0:
            cmk[127, i, :] = (p >= 15).astype(np.float32)
    ncmp_pad = 512
    c = np.arange(ncmp_pad); j = np.arange(128)
    ov = ((16 * c[:, None] <= 64 * j[None, :] + 63) & (16 * c[:, None] + 31 >= 64 * j[None, :])).astype(np.float32)
    ov[(ntok - 32) // 16 + 1:] = 0.0
    ovc = ov.reshape(4, 128, 128).transpose(1, 0, 2)
    jj = np.arange(128); key = np.arange(128)
    xall = np.zeros((128, 64, 128), np.float32)
    for kb in range(64):
        xall[:, kb, :] = (jj[:, None] == 2 * kb + key[None, :] // 64)
    qcol = np.zeros((128, 4), np.float32)
    qcol[:, 0] = (p >= 64) * 1e30 + (p < 64) * -1e30
    qcol[:, 1] = (p >= 64).astype(np.float32)
    qcol[:, 2] = (p < 64) * 1e30
    return dict(cosT=cosT, sinT=sinT, tri=np.concatenate([tri_le, tri_gt, np.eye(128, dtype=np.float32)], axis=1),
                cmk=cmk.reshape(128, 32 * 128), ovc=np.ascontiguousarray(ovc).reshape(128, 512),
                xall=xall.reshape(128, 64 * 128), qcol=qcol)


def build_nsa(ntok=SEQ, stop=99):
    nc = bass.Bass("TRN2", target_bir_lowering=False)
    try:
        return _build_nsa(nc, ntok, stop)
    except _Stop:
        return nc


def _build_nsa(nc, ntok, stop):
    def ck(code):
        if stop == code:
            P.emit()
            raise _Stop()
    NB = ntok // 512; NT128 = ntok // 128; NQ = NT128 // 2
    NCMP = (ntok - 32) // 16 + 1
    dr = lambda n, s, k="ExternalInput": nc.dram_tensor(n, s, F32, kind=k).ap()
    xT = dr("xT", [D, ntok])
    wA = dr("wA", [D, 15 * 64])
    wB = dr("wB", [D, 140])
    cosT = dr("cosT", [64, ntok]); sinT = dr("sinT", [64, ntok])
    tri = dr("tri", [128, 384]); cmk = dr("cmk", [128, 2048]); ovc = dr("ovc", [128, 512]); xall = dr("xall", [128, 8192]); qcol = dr("qcol", [128, 10])
    dmask = dr("dmask", [128, 256]); wmask = dr("wmask", [128, 768])
    w1k = dr("w1k", [64, 32 * 256]); w1v = dr("w1v", [64, 32 * 256]); w2k = dr("w2k", [128, 2 * 64]); w2v = dr("w2v", [128, 2 * 64])
    pek = dr("pek", [64, 32 * 64]); pev = dr("pev", [64, 32 * 64])
    xq = dr("xq", [D, NQ * 128])
    cosq = dr("cosq", [64, NQ * 128]); sinq = dr("sinq", [64, NQ * 128])
    yb = dr("yb", [NQ * 128, 256], "ExternalOutput")
    chunked = lambda ap: ap.rearrange("(c p) n -> p c n", p=128)
    P = Prog(nc)
    pl = Pools(P, nc)
    sb = lambda n, s, dt=F32: nc.alloc_sbuf_tensor(n, s, dt)
    tri_t = sb("tri_t", [128, 384]); qc_t = sb("qc_t", [128, 10])
    P.dma(tri_t[:], tri); P.dma(qc_t[:], qcol)
    identf = tri_t[:, 256:384]
    xt32 = sb("xt32", [128, 8, 512]); xb = sb("xb", [128, 8, 512], BF16)
    pl.stg = [xt32[:, 0:4, :].rearrange("p a n -> p (a n)"), xt32[:, 4:8, :].rearrange("p a n -> p (a n)")]
    BUF2 = sb("BUF2", [128, 24576], BF16)
    cmk_b = sb("cmk_b", [128, 2048], BF16); dm_b = sb("dm_b", [128, 256], BF16); wm_b = sb("wm_b", [128, 768], BF16); ov_b = sb("ov_b", [128, 4, 128], BF16)
    xall_b = BUF2[:, 16384:24576].rearrange("p (a n) -> p a n", n=128)
    pl.load_cast(cmk_b[:], cmk); pl.load_cast(dm_b[:], dmask); pl.load_cast(wm_b[:], wmask); pl.load_cast(ov_b[:].rearrange("p a n -> p (a n)"), ovc, q='pool')
    wAb = sb("wAb", [128, 8, 960], BF16); wBb = sb("wBb", [128, 8, 140], BF16)
    pl.load_cast(wAb[:], chunked(wA)); pl.load_cast(wBb[:], chunked(wB), q='pool')
    KBUF = sb("KBUF", [64, 2, ntok], BF16)
    ksT = KBUF[:, 0, :]; kwT = KBUF[:, 1, :]; kcT = KBUF[:, 0, :]; vcT = KBUF[:, 1, :]
    vsA = sb("vsA", [128, NT128, 65], BF16); vwA = sb("vwA", [128, NT128, 65], BF16)
    qT = BUF2[0:64, 0:NQ * 512].rearrange("p (a n) -> p a n", n=512)
    GT = sb("GT", [128, NQ, 12])
    P.memset(vsA[:, :, 64:65], 1.0); P.memset(vwA[:, :, 64:65], 1.0)
    cs_t = [sb("cs_t%d" % i, [64, 1024]) for i in range(1)]
    rt = [sb("rt%d" % i, [64, 512]) for i in range(2)]
    xTc = chunked(xT); xqc = chunked(xq)

    def proj_block(src_c, cosd, sind, c0, n, groups, dests, tokmajor_tiles):
        P.dma(xt32[:, 0:4, 0:n], src_c[:, 0:4, c0:c0 + n]); P.dma(xt32[:, 4:8, 0:n], src_c[:, 4:8, c0:c0 + n], q='pool')
        for c in range(8):
            P.copy(xb[:, c, 0:n], xt32[:, c, 0:n], eng=('act', 'dve', 'pool')[c % 3])
        cst = cs_t[0]
        P.dma(cst[:, 0:n], cosd[:, c0:c0 + n]); P.dma(cst[:, 512:512 + n], sind[:, c0:c0 + n], q='pool')
        for (cb, pb), dst in zip(groups, dests):
            p1 = pl.bank()
            for kc in range(8):
                P.mm(p1[0:64, 0:n], wAb[:, kc, cb:cb + 64], xb[:, kc, 0:n], start=(kc == 0), stop=(kc == 7))
            if pb is None:
                P.copy(dst, p1[0:64, 0:n], eng='act')
                continue
            p2 = pl.bank()
            for kc in range(8):
                P.mm(p2[0:64, 0:n], wAb[:, kc, pb:pb + 64], xb[:, kc, 0:n], start=(kc == 0), stop=(kc == 7))
            P.tt(rt[0][:, 0:n], p1[0:64, 0:n], cst[:, 0:n], ALU.mult)
            P.tt(rt[1][:, 0:n], p2[0:64, 0:n], cst[:, 512:512 + n], ALU.mult)
            P.tt(dst, rt[0][:, 0:n], rt[1][:, 0:n], ALU.add, eng='pool')
        for (t128, off) in tokmajor_tiles:
            pv = pl.bank()
            for kc in range(8):
                P.mm(pv[:, 0:140], xb[:, kc, off:off + 128], wBb[:, kc, :], start=(kc == 0), stop=(kc == 7))
            yield (t128, pv)

    for blk in range(NB):
        groups = [(256 + 0, 448 + 256 + 0), (896, None)]
        sl = slice(blk * 512, (blk + 1) * 512)
        for _ in proj_block(xTc, cosT, sinT, blk * 512, 512, groups, [kcT[:, sl], vcT[:, sl]], []):
            pass
    ck(1)
    w1b = [BUF2[0:64, i * 8192:(i + 1) * 8192].rearrange("p (a n) -> p a n", n=256) for i in range(2)]
    w2b = [sb("w2b%d" % i, [128, 2, 64], BF16) for i in range(2)]
    peb = [BUF2[0:64, 16384 + i * 2048:16384 + (i + 1) * 2048].rearrange("p (a n) -> p a n", n=64) for i in range(2)]
    pl.load_cast(w1b[0], w1k.rearrange("p (a n) -> p a n", n=256)); pl.load_cast(w1b[1], w1v.rearrange("p (a n) -> p a n", n=256), q='pool')
    pl.load_cast(w2b[0][:].rearrange("p a n -> p (a n)"), w2k); pl.load_cast(w2b[1][:].rearrange("p a n -> p (a n)"), w2v, q='pool')
    pl.load_cast(peb[0], pek.rearrange("p (a n) -> p a n", n=64)); pl.load_cast(peb[1], pev.rearrange("p (a n) -> p a n", n=64), q='pool')
    KC = sb("KC", [64, 512], BF16); VCA = sb("VCA", [128, 4, 65], BF16)
    P.memset(KC[:], 0.0); P.memset(VCA[:, :, 0:64], 0.0); P.memset(VCA[:, :, 64:65], 1.0)
    GH = [[sb("GH%d_%d" % (i, hc), [128, 512], BF16) for hc in range(2)] for i in range(2)]
    gtmp = [sb("gtmp%d" % i, [128, 512]) for i in range(3)]
    bcol = sb("bcol", [128, 4])
    for which, srcT in ((0, kcT), (1, vcT)):
        for hc in range(2):
            pb_ = pl.bank()
            for l in range(32):
                P.mm(pb_[:, 0:64], w1b[which][:, l, hc * 128:(hc + 1) * 128], peb[which][:, l, :], start=(l == 0), stop=(l == 31))
            P.copy(bcol[:, which * 2 + hc:which * 2 + hc + 1], pb_[:, 0:1], eng='act')
            ph = pl.bank()
            for l in range(32):
                rhs = srcT[:, l:l + 16 * (NCMP - 1) + 1:16]
                P.mm(ph[:, 0:NCMP], w1b[which][:, l, hc * 128:(hc + 1) * 128], rhs, start=(l == 0), stop=(l == 31))
            x_ = gtmp[0][:, 0:NCMP]; u_ = gtmp[1][:, 0:NCMP]; s_ = gtmp[2][:, 0:NCMP]
            P.actf(x_, ph[:, 0:NCMP], AF.Identity, bias=bcol[:, which * 2 + hc:which * 2 + hc + 1])
            P.actf(u_, x_, AF.Square)
            P.ts(u_, u_, 0.044715, 1.0, ALU.mult, ALU.add)
            P.tt(u_, u_, x_, ALU.mult)
            P.actf(s_, u_, AF.Sigmoid, scale=2.0 * 0.7978845608028654)
            P.memset(GH[which][hc][:, NCMP:512], 0.0)
            P.tt(GH[which][hc][:, 0:NCMP], x_, s_, ALU.mult)
    pk = pl.bank()
    for hc in range(2):
        P.mm(pk[0:64, 0:NCMP], w2b[0][:, hc, :], GH[0][hc][:, 0:NCMP], start=(hc == 0), stop=(hc == 1))
    P.copy(KC[:, 0:NCMP], pk[0:64, 0:NCMP], eng='act')
    for cc in range((NCMP + 127) // 128):
        pv_ = pl.bank()
        for hc in range(2):
            P.mm(pv_[:, 0:64], GH[1][hc][:, cc * 128:(cc + 1) * 128], w2b[1][:, hc, :], start=(hc == 0), stop=(hc == 1))
        P.copy(VCA[:, cc, 0:64], pv_[:, 0:64], eng='dve')
    ck(2)
    for blk in range(NB):
        groups = [(256 + 64, 448 + 256 + 64), (256 + 128, 448 + 256 + 128)]
        sl = slice(blk * 512, (blk + 1) * 512)
        for (t128, pv) in proj_block(xTc, cosT, sinT, blk * 512, 512, groups, [ksT[:, sl], kwT[:, sl]], [(blk * 4 + i, i * 128) for i in range(4)]):
            P.copy(vsA[:, t128, 0:64], pv[:, 0:64], eng='act')
            P.copy(vwA[:, t128, 0:64], pv[:, 64:128], eng='dve')
    pl.load_cast(xall_b, xall.rearrange("p (a n) -> p a n", n=128))
    qtmp = [sb("qtmp%d" % r, [64, 512], BF16) for r in range(4)]
    for qblk in range(0, NQ, 4):
        n = min(4, NQ - qblk) * 128
        groups = [(r * 64, 448 + r * 64) for r in range(4)]
        dests = [qtmp[r][:, 0:n] for r in range(4)]
        for (t128, pv) in proj_block(xqc, cosq, sinq, qblk * 128, n, groups, dests, [(qblk + i, i * 128) for i in range(n // 128)]):
            P.actf(GT[:, t128, :], pv[:, 128:140], AF.Sigmoid)
        for i in range(n // 128):
            for r in range(4):
                P.copy(qT[:, qblk + i, r * 128:(r + 1) * 128], qtmp[r][:, i * 128:(i + 1) * 128], eng=('pool', 'dve')[r % 2])
    ck(3)
    CV = sb("CV", [128, 4, 193], BF16)
    for cc in range(4):
        P.copy(CV[:, cc, 0:65], VCA[:, cc, :], eng='dve'); P.copy(CV[:, cc, 65:193], ov_b[:, cc, :], eng='pool')
    pT = [sb("pT%d" % i, [128, 512], BF16) for i in range(3)]
    imp = sb("imp", [128, 128]); imp2 = sb("imp2", [128, 128]); m8 = sb("m8", [128, 16]); sel = sb("sel", [128, 128])
    selT = sb("selT", [128, 128], BF16); negm = sb("negm", [128, 512], BF16)
    oTs = [None] + [sb("oTs%d" % i, [65, 512]) for i in range(1, 3)]
    ytile = sb("ytile", [128, 256]); sm = sb("sm", [128, 32])
    cnum = [sb("cnum%d" % i, [128, 386]) for i in range(2)]
    identb = sb("identb", [128, 128], BF16)
    P.copy(identb[:], identf)
    pi = 0
    for i in range(NQ):
        qrhs = qT[:, i, :]
        _attend(P, pl, nc, i, qrhs, dict(KC=KC, CV=CV, cmk_b=cmk_b, pT=pT, imp=imp, imp2=imp2, m8=m8, sel=sel, selT=selT, negm=negm,
                                          oTs=oTs, ytile=ytile, sm=sm, cnum=cnum, identb=identb, identf=identf, xall_b=xall_b, dm_b=dm_b, wm_b=wm_b,
                                          ksT=ksT, kwT=kwT, vsA=vsA, vwA=vwA, GT=GT, qc_t=qc_t, yb=yb, NCMP=NCMP))
    P.emit()
    return nc


def _bc4(ap):
    return ap.unsqueeze(1).broadcast_to([ap.shape[0], 4, ap.shape[1]])


def _attend(P, pl, nc, i, qrhs, T):
    KC, CV, pT = T['KC'], T['CV'], T['pT']
    NCMP = T['NCMP']
    identf, identb = T['identf'], T['identb']
    oTs = T['oTs']
    v4 = lambda t: t[:, :].rearrange("p (r q) -> p r q", r=4)
    ccb = min((16 * i + 6) // 128, (NCMP - 1) // 128)
    cn = [pl.banks[0], pl.banks[1], pl.banks[2], pl.banks[3]]
    rot = lambda: pl.banks[4 + (pl.nrot() % 4)]
    for cc in range(ccb + 1):
        ps = rot()
        P.mm(ps[:, :], KC[:, cc * 128:(cc + 1) * 128], qrhs)
        pt = pT[(pl.bi) % 3]
        P.actf(pt[:, :], ps[:, :], AF.Exp, scale=0.125)
        if cc >= ccb - 1:
            which = 1 if cc == ccb else 0
            mk = T['cmk_b'][:, (which * 8 + (i % 8)) * 128:(which * 8 + (i % 8) + 1) * 128]
            P.tt(v4(pt), v4(pt), _bc4(mk), ALU.mult)
        for r in range(4):
            P.mm(cn[r][:, 0:193], pt[:, r * 128:(r + 1) * 128], CV[:, cc, :], start=(cc == 0), stop=(cc == ccb))
    cnum = T['cnum']
    for r in range(4):
        P.copy(cnum[r // 2][:, (r % 2) * 193:(r % 2) * 193 + 193], cn[r][:, 0:193], eng=('act', 'dve')[r % 2])
    sm = T['sm']
    for r in range(4):
        P.ts(sm[:, r:r + 1], cnum[r // 2][:, (r % 2) * 193 + 64:(r % 2) * 193 + 65], 1e-30, None, ALU.add)
    P.op('dve', lambda e: e.reciprocal(sm[:, 4:8], sm[:, 0:4]), [sm[:, 0:4]], [sm[:, 4:8]])
    imp, imp2, m8, sel, selT, negm = T['imp'], T['imp2'], T['m8'], T['sel'], T['selT'], T['negm']
    for r in range(4):
        src = cnum[r // 2][:, (r % 2) * 193 + 65:(r % 2) * 193 + 193]
        if r == 0:
            P.ts(imp[:, :], src, sm[:, 4:5], None, ALU.mult)
        else:
            P.stt(imp[:, :], src, sm[:, 4 + r:5 + r], imp[:, :], ALU.mult, ALU.add)
    qc = T['qc_t']
    w0 = max(0, 4 * i - 1); wn = 4 * i + 4 - w0; t0c = w0 - (4 * i - 1)
    P.copy(imp2[:, :], imp[:, :], eng='dve')
    P.tt(imp2[:, w0:w0 + wn], imp[:, w0:w0 + wn], qc[:, t0c:t0c + wn], ALU.mult)
    P.tt(imp2[:, w0:w0 + wn], imp2[:, w0:w0 + wn], qc[:, 5 + t0c:5 + t0c + wn], ALU.add)
    if 4 * i + 4 < 128:
        P.memset(imp2[:, 4 * i + 4:128], -1e30)
    P.memset(imp2[:, 0:1], 1e30)
    P.op('dve', lambda e: e.max(out=m8[:, 0:8], in_=imp2[:, :]), [imp2[:, :]], [m8[:, 0:8]])
    P.op('dve', lambda e: e.match_replace(out=imp[:, :], in_to_replace=m8[:, 0:8], in_values=imp2[:, :], imm_value=-3e38), [m8[:, 0:8], imp2[:, :]], [imp[:, :]])
    P.op('dve', lambda e: e.max(out=m8[:, 8:16], in_=imp[:, :]), [imp[:, :]], [m8[:, 8:16]])
    P.ts(sel[:, :], imp2[:, :], m8[:, 15:16], None, ALU.is_ge)
    pst = rot()
    P.tr(pst[:, 0:128], sel[:, :], identf)
    P.copy(selT[:, :], pst[:, 0:128], eng='act')
    for r in range(4):
        P.ts(negm[:, r * 128:(r + 1) * 128], selT[:, :], -1.0, NEGM, ALU.add, ALU.mult, eng=('dve', 'pool')[r % 2])
    ksT, vsA, xall_b, dm_b = T['ksT'], T['vsA'], T['xall_b'], T['dm_b']
    po = pl.banks[0]
    nkb = 2 * i + 2
    def qk_sel(kb):
        ps = rot()
        P.mm(ps[:, :], ksT[:, kb * 128:(kb + 1) * 128], qrhs, start=True, stop=False)
        P.mm(ps[:, :], xall_b[:, kb, :], negm[:, :], start=False, stop=True)
        return ps
    ps_next = qk_sel(0)
    for kb in range(nkb):
        ps = ps_next
        pt = pT[kb % 3]
        P.actf(pt[:, :], ps[:, :], AF.Exp, scale=0.125)
        if kb + 1 < nkb:
            ps_next = qk_sel(kb + 1)
        if kb >= nkb - 2:
            mk = dm_b[:, (kb - (nkb - 2)) * 128:(kb - (nkb - 2) + 1) * 128]
            P.tt(v4(pt), v4(pt), _bc4(mk), ALU.mult)
        P.mm(po[0:65, :], vsA[:, kb, :], pt[:, :], start=(kb == 0), stop=(kb == nkb - 1))
    P.copy(oTs[1][:, :], po[0:65, :], eng='act')
    kwT, vwA, wm_b = T['kwT'], T['vwA'], T['wm_b']
    pw = pl.banks[1]
    kbs = [(m, 2 * i - 4 + m) for m in range(6) if 2 * i - 4 + m >= 0]
    def qk_w(kb):
        ps = rot()
        P.mm(ps[:, :], kwT[:, kb * 128:(kb + 1) * 128], qrhs)
        return ps
    ps_next = qk_w(kbs[0][1])
    for n_, (m, kb) in enumerate(kbs):
        ps = ps_next
        pt = pT[n_ % 3]
        P.actf(pt[:, :], ps[:, :], AF.Exp, scale=0.125)
        if n_ + 1 < len(kbs):
            ps_next = qk_w(kbs[n_ + 1][1])
        P.tt(v4(pt), v4(pt), _bc4(wm_b[:, m * 128:(m + 1) * 128]), ALU.mult)
        P.mm(pw[0:65, :], vwA[:, kb, :], pt[:, :], start=(n_ == 0), stop=(n_ == len(kbs) - 1))
    P.copy(oTs[2][:, :], pw[0:65, :], eng='dve')
    GT, ytile = T['GT'], T['ytile']
    for r in range(4):
        P.tt(sm[:, 8 + r:9 + r], sm[:, 4 + r:5 + r], GT[:, i, r * 3:r * 3 + 1], ALU.mult)
        P.ts(ytile[:, r * 64:(r + 1) * 64], cnum[r // 2][:, (r % 2) * 193:(r % 2) * 193 + 64], sm[:, 8 + r:9 + r], None, ALU.mult)
    for br in (1, 2):
        pb_ = pl.banks[br + 1]
        for r in range(4):
            P.mm(pb_[:, r * 65:(r + 1) * 65], oTs[br][0:65, r * 128:(r + 1) * 128], identf[0:65, 0:65])
        for r in range(4):
            P.ts(sm[:, 12 + r:13 + r], pb_[:, r * 65 + 64:r * 65 + 65], 1e-30, None, ALU.add)
        P.op('dve', lambda e: e.reciprocal(sm[:, 16:20], sm[:, 12:16]), [sm[:, 12:16]], [sm[:, 16:20]])
        for r in range(4):
            P.tt(sm[:, 20 + r:21 + r], sm[:, 16 + r:17 + r], GT[:, i, r * 3 + br:r * 3 + br + 1], ALU.mult)
            P.stt(ytile[:, r * 64:(r + 1) * 64], pb_[:, r * 65:r * 65 + 64], sm[:, 20 + r:21 + r], ytile[:, r * 64:(r + 1) * 64], ALU.mult, ALU.add)
    P.dma(T['yb'][i * 128:(i + 1) * 128, :], ytile[:, :], q='pool')


def nsa_inputs(d, ntok=SEQ):
    l = 0
    W = d['w_in'][l][:, 1664:1664 + 1304]
    cst = nsa_consts(ntok)
    NQ = ntok // 256
    perm = np.arange(64); perm[0:8] = np.arange(8, 16); perm[8:16] = np.arange(0, 8)
    p = np.arange(128)
    tri_le = cst['tri'][:, 0:128]; tri_gt = cst['tri'][:, 128:256]
    one = np.ones((128, 128), np.float32); zero = np.zeros((128, 128), np.float32)
    w1k = d['nsa_ck_w1'][l].reshape(32, 64, 256).transpose(1, 0, 2).reshape(64, 32 * 256)
    w1v = d['nsa_cv_w1'][l].reshape(32, 64, 256).transpose(1, 0, 2).reshape(64, 32 * 256)
    w2k = d['nsa_ck_w2'][l].reshape(2, 128, 64).transpose(1, 0, 2).reshape(128, 128)
    w2v = d['nsa_cv_w2'][l].reshape(2, 128, 64).transpose(1, 0, 2).reshape(128, 128)
    pek = np.repeat(d['nsa_pe_k'][l].T[:, :, None], 64, axis=2).reshape(64, 32 * 64)
    pev = np.repeat(d['nsa_pe_v'][l].T[:, :, None], 64, axis=2).reshape(64, 32 * 64)
    maps = []
    for c in range(8):
        b, g, par = c // 4, (c % 4) // 2, c % 2
        qcols = [256 * g + r * 64 + np.arange(64) for r in range(4)]
        kvc = lambda idx: 512 + idx * 128 + g * 64 + np.arange(64)
        roped = qcols + [kvc(0), kvc(2), kvc(4)]
        colsA = np.concatenate(roped + [cg[perm] for cg in roped] + [kvc(1)])
        gcols = np.array([1280 + (4 * g + r) * 3 + cc for r in range(4) for cc in range(3)])
        colsB = np.concatenate([kvc(3), kvc(5), gcols])
        own = np.concatenate([np.arange((2 * i + par) * 128, (2 * i + par + 1) * 128) for i in range(NQ)])
        xTb = np.ascontiguousarray(d['x'][b, :ntok].T)
        cmk_full = cst['cmk'].reshape(128, 32, 128)
        cmkc = np.stack([cmk_full[:, wh * 16 + (2 * m + par) % 16, :] for wh in range(2) for m in range(8)], axis=1)
        if par == 0:
            dmask = np.concatenate([tri_le, zero], axis=1); wmask = np.concatenate([tri_gt, one, one, one, tri_le, zero], axis=1)
        else:
            dmask = np.concatenate([one, tri_le], axis=1); wmask = np.concatenate([zero, tri_gt, one, one, one, tri_le], axis=1)
        qcol = np.zeros((128, 10), np.float32)
        hi = (p >= 64).astype(np.float32); lo = 1.0 - hi
        for w in range(5):
            dl = w - 1 - 2 * par
            if dl < -1:
                qcol[:, w] = 1.0
            elif dl == -1:
                qcol[:, w] = hi; qcol[:, 5 + w] = lo * 1e30
            elif dl == 0:
                qcol[:, 5 + w] = 1e30
            elif dl == 1:
                qcol[:, 5 + w] = hi * 1e30 - lo * 1e30
            else:
                qcol[:, 5 + w] = -1e30
        m = dict(xT=xTb, wA=W[:, colsA], wB=W[:, colsB], cosT=cst['cosT'], sinT=cst['sinT'], tri=cst['tri'], cmk=cmkc.reshape(128, 2048),
                 ovc=cst['ovc'], xall=cst['xall'], qcol=qcol, dmask=dmask, wmask=wmask, w1k=w1k, w1v=w1v, w2k=w2k, w2v=w2v, pek=pek, pev=pev,
                 xq=xTb[:, own], cosq=cst['cosT'][:, own], sinq=cst['sinT'][:, own])
        maps.append({k: np.ascontiguousarray(v, dtype=np.float32) for k, v in m.items()})
    return maps


def run_nsa(d):
    if 'nsa' not in _NC_CACHE:
        _NC_CACHE['nsa'] = build_nsa()
    res = run_bass_kernel_spmd(_NC_CACHE['nsa'], nsa_inputs(d), core_ids=list(range(8)))
    yb = np.zeros((2, SEQ, 512), np.float32)
    for c in range(8):
        b, g, par = c // 4, (c % 4) // 2, c % 2
        o = res.results[c]['yb'].reshape(SEQ // 256, 128, 256)
        yb[b].reshape(SEQ // 256, 2, 128, 512)[:, par, :, g * 256:(g + 1) * 256] = o
    return yb


def kernel(**inputs):
    d = {k: np.asarray(v) for k, v in inputs.items()}
    ya = run_rwkv(d)
    yb = run_nsa(d)
    out = run_tail(d, ya, yb)
    return out.astype(np.float32)
```

```python
import numpy as np
import concourse.bass as bass
import concourse.mybir as mybir
from concourse.bass_utils import run_bass_kernel_spmd

F32 = mybir.dt.float32
BF16 = mybir.dt.bfloat16
AF = mybir.ActivationFunctionType
ALU = mybir.AluOpType
AX = mybir.AxisListType

SAME_ENG_SYNC = False
NSLOT = 8


def _region(ap):
    t = ap.tensor
    shp = tuple(t.shape)
    sp = str(ap.space)
    off = int(ap.offset)
    pat = [(int(s), int(c)) for s, c in ap.ap]
    if 'DRAM' in sp.upper() or 'HBM' in sp.upper():
        ext = sum((c - 1) * abs(s) for s, c in pat)
        return (ap.name, 0, 0, off, off + ext)
    if 'PSUM' in sp.upper():
        return (ap.name, 0, 127, 0, 10 ** 9)
    fs = 1
    for d in shp[1:]:
        fs *= int(d)
    p0 = off // fs
    f0 = off % fs
    ps, pc = pat[0]
    p1 = p0 + (pc - 1) * (ps // fs if fs else 0)
    ext = sum((c - 1) * abs(s) for s, c in pat[1:])
    return (ap.name, p0, p1, f0, f0 + ext)


def _overlap(a, b):
    return not (a[2] < b[1] or b[2] < a[1] or a[4] < b[3] or b[4] < a[3])


def _contains(a, b):
    return a[1] <= b[1] and a[2] >= b[2] and a[3] <= b[3] and a[4] >= b[4]


class Prog:
    ENGS = ['pe', 'act', 'dve', 'pool', 'sp']

    def __init__(self, nc):
        self.nc = nc
        self.stream = {e: [] for e in self.ENGS}
        self.count = {e: 0 for e in self.ENGS}
        self.dcount = {'sp': 0, 'pool': 0, 'act': 0}
        self.hist = {}
        self.known = {e: {} for e in self.ENGS}
        self.nops = 0

    def _deps(self, eng, reads, writes, is_dma=False):
        deps = {}

        def add(tok):
            k, v, e = tok
            if e == eng and eng == 'pe' and not k.startswith('d_') and not is_dma:
                return
            if deps.get(k, (0,))[0] < v:
                deps[k] = (v, e)
        rr = [_region(a) for a in reads]
        wr = [_region(a) for a in writes]
        for r in rr:
            psum = (r[4] == 10 ** 9)
            for (reg, tok, isw) in self.hist.get(r[0], ()):
                if isw and _overlap(reg, r):
                    add(tok)
                elif psum and not isw and tok[2] != eng:
                    add(tok)
        for w in wr:
            for (reg, tok, isw) in self.hist.get(w[0], ()):
                if _overlap(reg, w):
                    add(tok)
        return deps, rr, wr

    def _update(self, tok, rr, wr):
        for w in wr:
            h = self.hist.setdefault(w[0], [])
            h[:] = [x for x in h if not _contains(w, x[0])]
            h.append((w, tok, True))
        for r in rr:
            h = self.hist.setdefault(r[0], [])
            h[:] = [x for x in h if not (not x[2] and x[1][2] == tok[2] and x[1][0] == tok[0] and x[0] == r)]
            h.append((r, tok, False))

    def _waits(self, eng, deps):
        waits = []
        kn = self.known[eng]
        for k, (v, e) in deps.items():
            if kn.get(k, 0) < v:
                kn[k] = v
                waits.append((k, v))
        return waits

    def op(self, eng, fn, reads, writes):
        deps, rr, wr = self._deps(eng, reads, writes)
        waits = self._waits(eng, deps)
        self.count[eng] += 1
        tok = ('c_' + eng, self.count[eng], eng)
        self.stream[eng].append((waits, fn, tok, 1))
        self._update(tok, rr, wr)
        self.nops += 1
        return tok

    def dma(self, out, in_, q='sp', **kw):
        eng = q
        deps, rr, wr = self._deps(eng, [in_], [out], is_dma=True)
        i = self.dcount[q]
        self.dcount[q] += 1
        slot = i % NSLOT
        val = 16 * (i // NSLOT + 1)
        key = 'd_%s_%d' % (q, slot)
        if val > 16:
            if deps.get(key, (0,))[0] < val - 16:
                deps[key] = (val - 16, q)
        waits = self._waits(eng, deps)
        tok = (key, val, eng)

        def fn(e, out=out, in_=in_, kw=kw):
            return e.dma_start(out=out, in_=in_, **kw)
        self.stream[eng].append((waits, fn, tok, 16))
        self._update(tok, rr, wr)
        self.nops += 1
        return tok

    def dma_like(self, fn, reads, writes, q='pool'):
        eng = q
        deps, rr, wr = self._deps(eng, reads, writes, is_dma=True)
        i = self.dcount[q]
        self.dcount[q] += 1
        slot = i % NSLOT
        val = 16 * (i // NSLOT + 1)
        key = 'd_%s_%d' % (q, slot)
        if val > 16:
            if deps.get(key, (0,))[0] < val - 16:
                deps[key] = (val - 16, q)
        waits = self._waits(eng, deps)
        tok = (key, val, eng)
        self.stream[eng].append((waits, fn, tok, 16))
        self._update(tok, rr, wr)
        self.nops += 1
        return tok

    F32R = False

    def mm(self, out, lhsT, rhs, start=True, stop=True, **kw):
        l2, r2 = lhsT, rhs
        if self.F32R and lhsT.dtype == F32 and rhs.dtype == F32 and lhsT.shape[-1] == 128 and lhsT.shape[0] == 128 and rhs.shape[-1] % 2 == 0:
            l2 = lhsT.bitcast(mybir.dt.float32r); r2 = rhs.bitcast(mybir.dt.float32r)
        return self.op('pe', lambda e: e.matmul(out, l2, r2, start=start, stop=stop, **kw),
                       [lhsT, rhs] + ([] if start else [out]), [out])

    def tr(self, out, in_, ident):
        return self.op('pe', lambda e: e.transpose(out, in_, ident), [in_, ident], [out])

    def actf(self, out, in_, func, bias=None, scale=None, accum_out=None, eng='act'):
        kw = {}
        rd = [in_]
        wr = [out]
        if bias is not None:
            kw['bias'] = bias
            if not isinstance(bias, (int, float)):
                rd.append(bias)
        if scale is not None:
            kw['scale'] = scale
            if not isinstance(scale, (int, float)):
                rd.append(scale)
        if accum_out is not None:
            kw['accum_out'] = accum_out
            wr.append(accum_out)
        return self.op(eng, lambda e: e.activation(out, in_, func, **kw), rd, wr)

    def tt(self, out, in0, in1, op, eng='dve'):
        return self.op(eng, lambda e: e.tensor_tensor(out, in0, in1, op), [in0, in1], [out])

    def ts(self, out, in0, s1, s2, op0, op1=None, accum_out=None, eng='dve'):
        rd = [in0] + [s for s in (s1, s2) if s is not None and not isinstance(s, (int, float))]
        wr = [out] + ([accum_out] if accum_out is not None else [])
        kw = {}
        if op1 is not None:
            kw['op1'] = op1
        if accum_out is not None:
            kw['accum_out'] = accum_out
        return self.op(eng, lambda e: e.tensor_scalar(out, in0, s1, s2, op0, **kw), rd, wr)

    def stt(self, out, in0, scalar, in1, op0, op1, eng='dve'):
        rd = [in0, in1] + ([scalar] if not isinstance(scalar, (int, float)) else [])
        return self.op(eng, lambda e: e.scalar_tensor_tensor(out, in0, scalar, in1, op0, op1), rd, [out])

    def copy(self, out, in_, eng='dve'):
        if eng == 'act':
            return self.op('act', lambda e: e.copy(out, in_), [in_], [out])
        return self.op(eng, lambda e: e.tensor_copy(out, in_), [in_], [out])

    def memset(self, ap, v, eng='dve'):
        return self.op(eng, lambda e: e.memset(ap, v), [], [ap])

    def emit(self):
        nc = self.nc
        import contextlib
        es = contextlib.ExitStack()
        sems = {}

        def sem(k):
            if k not in sems:
                sems[k] = es.enter_context(nc.semaphore(k))
            return sems[k]
        for e in self.ENGS:
            sem('c_' + e)
        for q in self.dcount:
            for s in range(NSLOT):
                sem('d_%s_%d' % (q, s))
        final = []
        for e in self.ENGS:
            if e != 'sp' and self.count[e] > 0:
                final.append(('c_' + e, self.count[e]))
        for q, n in self.dcount.items():
            for s in range(NSLOT):
                cnt = (n - s + NSLOT - 1) // NSLOT if n > s else 0
                if cnt > 0:
                    final.append(('d_%s_%d' % (q, s), 16 * cnt))
        streams = self.stream

        def run(engname, e):
            for (waits, fn, tok, inc) in streams[engname]:
                for (k, v) in waits:
                    e.wait_ge(sem(k), v)
                ins = fn(e)
                ins.then_inc(sem(tok[0]), inc)
            if engname == 'sp':
                for (k, v) in final:
                    e.wait_ge(sem(k), v)
        with nc.Block() as block:
            @block.tensor
            def _(e):
                run('pe', e)

            @block.scalar
            def _(e):
                run('act', e)

            @block.vector
            def _(e):
                run('dve', e)

            @block.gpsimd
            def _(e):
                run('pool', e)

            @block.sync
            def _(e):
                run('sp', e)
        es.close()


D = 1024
ALPHA = 2.0 ** 0.25
LN_EPS = 1e-5
DFF = 2816
NT = 2050
NJ = DFF // 128


class Pools:
    def __init__(self, P, nc):
        self.P = P
        self.nc = nc
        self.banks = [nc.alloc_psum_tensor("bank%d" % i, [128, 512], F32) for i in range(8)]
        self.bi = 0
        self.stg = [nc.alloc_sbuf_tensor("stg%d" % i, [128, 2048], F32) for i in range(2)]
        self.si = 0
        self.ci = 0

    def bank(self):
        b = self.banks[self.bi % 8]
        self.bi += 1
        return b

    def nrot(self):
        self.bi += 1
        return self.bi

    def load_cast(self, dst, src, q='sp'):
        P = self.P
        shp = dst.shape
        npart = shp[0]
        if len(shp) == 2:
            n = shp[1]
            step = 2048
            for c0 in range(0, n, step):
                c1 = min(n, c0 + step)
                st = self.stg[self.si % 2]
                self.si += 1
                P.dma(st[0:npart, 0:c1 - c0], src[:, c0:c1], q=q)
                self._cast(dst[:, c0:c1], st[0:npart, 0:c1 - c0])
        else:
            a, n = shp[1], shp[2]
            assert n <= 2048
            per = max(1, 2048 // n)
            for a0 in range(0, a, per):
                a1 = min(a, a0 + per)
                st = self.stg[self.si % 2]
                self.si += 1
                sv = st[0:npart, 0:(a1 - a0) * n].rearrange("p (a n) -> p a n", n=n)
                P.dma(sv, src[:, a0:a1, :], q=q)
                self._cast(dst[:, a0:a1, :], sv)

    def _cast(self, dst, src):
        engs = ['pool', 'dve', 'act', 'pool']
        e = engs[self.ci % len(engs)]
        self.ci += 1
        self.P.copy(dst, src, eng=e)


def chan_ln(P, pl, res, resb, gcol, bcol, N, ones_b, scr_b, scr_sq, tmp):
    pm = pl.bank()
    psq = pl.bank()
    for c in range(8):
        P.actf(scr_b[:, c, 0:N], res[:, c, 0:N], AF.Copy)
        P.actf(scr_sq[:, c, 0:N], res[:, c, 0:N], AF.Square)
    for c in range(8):
        P.mm(pm[:, 0:N], ones_b[:], scr_b[:, c, 0:N], start=(c == 0), stop=(c == 7))
    for c in range(8):
        P.mm(psq[:, 0:N], ones_b[:], scr_sq[:, c, 0:N], start=(c == 0), stop=(c == 7))
    mean, msq, var, rstd = [t[:, 0:N] for t in tmp[:4]]
    P.actf(mean, pm[:, 0:N], AF.Copy, scale=1.0 / D)
    P.tt(msq, mean, mean, ALU.mult)
    P.stt(var, psq[:, 0:N], 1.0 / D, msq, ALU.mult, ALU.subtract)
    P.ts(var, var, LN_EPS, None, ALU.add)
    P.actf(var, var, AF.Sqrt)
    P.op('dve', lambda e: e.reciprocal(rstd, var), [var], [rstd])
    for c in range(8):
        P.tt(res[:, c, 0:N], res[:, c, 0:N], mean, ALU.subtract)
        P.tt(res[:, c, 0:N], res[:, c, 0:N], rstd, ALU.mult)
        P.actf(res[:, c, 0:N], res[:, c, 0:N], AF.Identity, bias=bcol[:, c:c + 1], scale=gcol[:, c:c + 1])
        P.copy(resb[:, c, 0:N], res[:, c, 0:N], eng='pool')


class _Stop(Exception):
    pass


def build_tail(stop=99):
    nc = bass.Bass("TRN2", target_bir_lowering=False)
    try:
        return _build_tail(nc, stop)
    except _Stop:
        return nc


def _build_tail(nc, stop):
    def ck(code):
        if stop == code:
            P.emit()
            raise _Stop()
    dr = lambda n, s, k="ExternalInput": nc.dram_tensor(n, s, F32, kind=k).ap()
    xT = dr("xT", [D, NT]); yaT = dr("yaT", [512, NT]); ybT = dr("ybT", [512, NT]); memT = dr("memT", [D, 256])
    wg = dr("wg", [D, 2048]); pa = dr("pa", [512, D]); pb = dr("pb", [512, D]); wo = dr("wo", [D, D])
    wq = dr("wq", [D, D]); wk = dr("wk", [D, D]); wv = dr("wv", [D, D]); xwo = dr("xwo", [D, D])
    wup = dr("wup", [D, 2 * DFF]); wdn = dr("wdn", [DFF, D])
    lnp = dr("lnp", [128, 48]); cw = dr("cw", [128, 44 * 3]); cb = dr("cb", [128, 44]); hmask = dr("hmask", [128, 1])
    outT = dr("outT", [D, 2048], "ExternalOutput")
    x2s = dr("x2s", [D, NT], "Internal")
    chunked = lambda ap: ap.rearrange("(c p) n -> p c n", p=128)

    P = Prog(nc)
    pl = Pools(P, nc)
    sb = lambda n, s, dt=F32: nc.alloc_sbuf_tensor(n, s, dt)
    NA = 256
    arena = sb("arena", [128, 49152], BF16)
    lnp_t = sb("lnp_t", [128, 48]); cw_t = sb("cw_t", [128, 132]); cb_t = sb("cb_t", [128, 44]); hm_t = sb("hm_t", [128, 1])
    ones_b = sb("ones_b", [128, 128], BF16)
    res = sb("res", [128, 8, 258]); resb = sb("resb", [128, 8, 258], BF16)
    yab = sb("yab", [128, 4, NA], BF16); ybb = sb("ybb", [128, 4, NA], BF16)
    actb2 = sb("actb2", [128, 8, NA], BF16); qT = sb("qT", [128, 8, NA], BF16)
    scr_b = sb("scr_b", [128, 8, 258], BF16); scr_sq = sb("scr_sq", [128, 8, 258], BF16)
    tmp = [sb("tmp%d" % i, [128, 258]) for i in range(8)]
    KT = sb("KT", [128, 8, 256], BF16); V = sb("V", [128, 2, D], BF16); memb = sb("memb", [128, 8, 256], BF16)
    pbuf = [sb("pbuf%d" % i, [128, NA], BF16) for i in range(2)]
    wdn_r = [sb("wdn_r%d" % i, [128, NJ, 128], BF16) for i in range(2)]
    gall = sb("gall", [128, NJ, 256], BF16)
    hs = [sb("hs%d" % i, [128, 258]) for i in range(2)]
    ostg = [sb("ostg%d" % i, [128, 256]) for i in range(2)]

    P.dma(lnp_t[:], lnp); P.dma(cw_t[:], cw); P.dma(cb_t[:], cb); P.dma(hm_t[:], hmask)
    P.memset(ones_b[:], 1.0)

    if stop <= 0:
        P.emit(); return nc
    def A(off, kc, n):
        return arena[:, off:off + kc * n].rearrange("p (c n) -> p c n", n=n)
    Wk_b = A(0, 8, D); Wv_b = A(8192, 8, D)
    pl.load_cast(memb[:], chunked(memT))
    pl.load_cast(Wk_b, chunked(wk)); pl.load_cast(Wv_b, chunked(wv), q='pool')
    if stop == 1:
        P.emit(); return nc
    for oc in range(8):
        ps = pl.bank()
        for kc in range(8):
            P.mm(ps[:, 0:256], Wk_b[:, kc, oc * 128:(oc + 1) * 128], memb[:, kc, :], start=(kc == 0), stop=(kc == 7))
        P.copy(KT[:, oc, :], ps[:, 0:256], eng='act')
    for mc in range(2):
        for hf in range(2):
            ps = pl.bank()
            for kc in range(8):
                P.mm(ps[:, :], memb[:, kc, mc * 128:(mc + 1) * 128], Wv_b[:, kc, hf * 512:(hf + 1) * 512], start=(kc == 0), stop=(kc == 7))
            P.copy(V[:, mc, hf * 512:(hf + 1) * 512], ps[:, :], eng='dve')
    if stop <= 1:
        P.dma(outT[0:128, 0:256], KT[:, 0, :].bitcast(F32)[:, 0:128] if False else tmp[0][:, 0:256]); P.emit(); return nc
    Wg_b = A(0, 8, 2048); Pa_b = A(16384, 4, D); Pb_b = A(20480, 4, D); Wo_b = A(24576, 8, D)
    Wq_b = A(32768, 8, D); XWo_b = A(40960, 8, D)
    pl.load_cast(Pa_b, chunked(pa)); pl.load_cast(Pb_b, chunked(pb), q='pool')
    pl.load_cast(Wo_b, chunked(wo)); pl.load_cast(Wq_b, chunked(wq), q='pool'); pl.load_cast(XWo_b, chunked(xwo))
    pl.load_cast(Wg_b, chunked(wg), q='pool')
    g1, b1, g2, b2, g3, b3 = [lnp_t[:, i * 8:(i + 1) * 8] for i in range(6)]

    if stop <= 2:
        P.emit(); return nc
    tiles = [(0, 2)] + [(2 + i * NA, NA) for i in range(8)]
    if stop <= 3:
        tiles = tiles[:1]
    if stop == 4 or 40 < stop < 50:
        tiles = tiles[1:2]
    xTc, yaTc, ybTc, x2sc, outTc = chunked(xT), chunked(yaT), chunked(ybT), chunked(x2s), chunked(outT)
    for (c0, N) in tiles:
        P.dma(res[:, :, 0:N], xTc[:, :, c0:c0 + N])
        st = pl.stg[pl.si % 2]; pl.si += 1
        sv = st[:, 0:8 * N].rearrange("p (a n) -> p a n", n=N)
        P.dma(sv[:, 0:4, :], yaTc[:, :, c0:c0 + N], q='pool'); P.dma(sv[:, 4:8, :], ybTc[:, :, c0:c0 + N], q='pool')
        P.copy(yab[:, :, 0:N], sv[:, 0:4, :], eng='pool'); P.copy(ybb[:, :, 0:N], sv[:, 4:8, :], eng='pool')
        for c in range(8):
            P.copy(resb[:, c, 0:N], res[:, c, 0:N], eng='act' if c % 2 else 'dve')
        ck(41)
        for oc in range(8):
            osl = slice(oc * 128, (oc + 1) * 128)
            za = pl.bank(); ga = pl.bank()
            for kc in range(4):
                P.mm(za[:, 0:N], Pa_b[:, kc, osl], yab[:, kc, 0:N], start=(kc == 0), stop=(kc == 3))
            for kc in range(4):
                P.mm(za[:, 256:256 + N], Pb_b[:, kc, osl], ybb[:, kc, 0:N], start=(kc == 0), stop=(kc == 3))
            for kc in range(8):
                P.mm(ga[:, 0:N], Wg_b[:, kc, osl], resb[:, kc, 0:N], start=(kc == 0), stop=(kc == 7))
            for kc in range(8):
                P.mm(ga[:, 256:256 + N], Wg_b[:, kc, 1024 + oc * 128:1024 + (oc + 1) * 128], resb[:, kc, 0:N], start=(kc == 0), stop=(kc == 7))
            sa, sbb, t1, t2 = tmp[4][:, 0:N], tmp[5][:, 0:N], tmp[6][:, 0:N], tmp[7][:, 0:N]
            P.actf(sa, ga[:, 0:N], AF.Sigmoid)
            P.actf(sbb, ga[:, 256:256 + N], AF.Sigmoid)
            P.tt(t1, sa, za[:, 0:N], ALU.mult)
            P.tt(t2, sbb, za[:, 256:256 + N], ALU.mult)
            P.tt(actb2[:, oc, 0:N], t1, t2, ALU.add)
        ck(42)
        for oc in range(8):
            osl = slice(oc * 128, (oc + 1) * 128)
            ps = pl.bank()
            for kc in range(8):
                P.mm(ps[:, 0:N], Wo_b[:, kc, osl], actb2[:, kc, 0:N], start=(kc == 0), stop=(kc == 7))
            P.stt(res[:, oc, 0:N], res[:, oc, 0:N], ALPHA, ps[:, 0:N], ALU.mult, ALU.add)
        ck(43)
        chan_ln(P, pl, res, resb, g1, b1, N, ones_b, scr_b, scr_sq, tmp)
        ck(44)
        for oc in range(8):
            osl = slice(oc * 128, (oc + 1) * 128)
            ps = pl.bank()
            for kc in range(8):
                P.mm(ps[:, 0:N], Wq_b[:, kc, osl], resb[:, kc, 0:N], start=(kc == 0), stop=(kc == 7))
            P.copy(qT[:, oc, 0:N], ps[:, 0:N], eng='act' if oc % 2 else 'dve')
        for hh in range(4):
            pden = pl.bank()
            for mc in range(2):
                ps = pl.bank()
                for dc in range(2):
                    P.mm(ps[:, 0:N], KT[:, 2 * hh + dc, mc * 128:(mc + 1) * 128], qT[:, 2 * hh + dc, 0:N], start=(dc == 0), stop=(dc == 1))
                P.actf(pbuf[mc][:, 0:N], ps[:, 0:N], AF.Exp, scale=1.0 / 16.0)
            for mc in range(2):
                P.mm(pden[:, 0:N], ones_b[:], pbuf[mc][:, 0:N], start=(mc == 0), stop=(mc == 1))
            rec = tmp[4][:, 0:N]
            P.op('dve', lambda e, rec=rec, pden=pden, N=N: e.reciprocal(rec, pden[:, 0:N]), [pden[:, 0:N]], [rec])
            for dc in range(2):
                po = pl.bank()
                for mc in range(2):
                    P.mm(po[:, 0:N], V[:, mc, (2 * hh + dc) * 128:(2 * hh + dc + 1) * 128], pbuf[mc][:, 0:N], start=(mc == 0), stop=(mc == 1))
                P.tt(actb2[:, 2 * hh + dc, 0:N], po[:, 0:N], rec, ALU.mult)
        for oc in range(8):
            osl = slice(oc * 128, (oc + 1) * 128)
            ps = pl.bank()
            for kc in range(8):
                P.mm(ps[:, 0:N], XWo_b[:, kc, osl], actb2[:, kc, 0:N], start=(kc == 0), stop=(kc == 7))
            P.stt(res[:, oc, 0:N], res[:, oc, 0:N], ALPHA, ps[:, 0:N], ALU.mult, ALU.add)
        ck(45)
        chan_ln(P, pl, res, resb, g2, b2, N, ones_b, scr_b, scr_sq, tmp)
        ck(46)
        P.dma(x2sc[:, :, c0:c0 + N], res[:, :, 0:N], q='pool')

    if stop <= 5 or 40 < stop < 50:
        P.emit(); return nc
    Wup_b = A(0, 8, 2 * DFF)
    wupc = chunked(wup)
    wdnc = chunked(wdn)
    for kc in range(8):
        for h0 in range(0, 2 * DFF, 2048):
            h1 = min(2 * DFF, h0 + 2048)
            pl.load_cast(Wup_b[:, kc, h0:h1], wupc[:, kc, h0:h1], q='sp' if kc % 2 else 'pool')
    cw3 = cw_t[:].rearrange("p (c k) -> p c k", k=3)
    for ti in range(8):
        c0 = ti * 256
        P.dma(res[:, :, 0:258], x2sc[:, :, c0:c0 + 258])
        for c in range(8):
            P.copy(resb[:, c, 0:258], res[:, c, 0:258], eng='act' if c % 2 else 'dve')
        for j in range(NJ):
            hp = pl.bank(); hv = pl.bank()
            for kc in range(8):
                P.mm(hp[:, 0:258], Wup_b[:, kc, j * 128:(j + 1) * 128], resb[:, kc, 0:258], start=(kc == 0), stop=(kc == 7))
            for kc in range(8):
                P.mm(hv[:, 0:258], Wup_b[:, kc, DFF + j * 128:DFF + (j + 1) * 128], resb[:, kc, 0:258], start=(kc == 0), stop=(kc == 7))
            srcs = []
            for (hsrc, k) in ((hp, 0), (hv, 1)):
                if ti == 0:
                    hb_ = hs[k]
                    P.copy(hb_[:, 0:258], hsrc[:, 0:258], eng='act')
                    P.ts(hb_[:, 0:2], hb_[:, 0:2], hm_t[:, 0:1], None, ALU.mult)
                    srcs.append(hb_)
                else:
                    srcs.append(hsrc)
            outs = []
            for k, (src, ch) in enumerate(zip(srcs, (j, NJ + j))):
                t = tmp[k * 2]; t2 = tmp[k * 2 + 1]
                P.actf(t[:, 0:256], src[:, 2:258], AF.Identity, bias=cb_t[:, ch:ch + 1], scale=cw3[:, ch, 2:3])
                P.stt(t2[:, 0:256], src[:, 1:257], cw3[:, ch, 1:2], t[:, 0:256], ALU.mult, ALU.add)
                P.stt(t[:, 0:256], src[:, 0:256], cw3[:, ch, 0:1], t2[:, 0:256], ALU.mult, ALU.add)
                outs.append(t)
            sg = tmp[4]
            P.actf(sg[:, 0:256], outs[0][:, 0:256], AF.Silu)
            P.tt(gall[:, j, :], sg[:, 0:256], outs[1][:, 0:256], ALU.mult)
        for oc in range(8):
            wr = wdn_r[oc % 2]
            pl.load_cast(wr[:, :, :], wdnc[:, :, oc * 128:(oc + 1) * 128], q='sp')
            a = pl.bank()
            for j in range(NJ):
                P.mm(a[:, 0:256], wr[:, j, :], gall[:, j, :], start=(j == 0), stop=(j == NJ - 1))
            P.stt(res[:, oc, 2:258], res[:, oc, 2:258], ALPHA, a[:, 0:256], ALU.mult, ALU.add)
        res_v = res[:, :, 2:258]; resb_v = resb[:, :, 2:258]
        chan_ln(P, pl, res_v, resb_v, g3, b3, 256, ones_b, scr_b, scr_sq, tmp)
        P.dma(outTc[:, :, ti * 256:(ti + 1) * 256], res[:, :, 2:258], q='pool')
    P.emit()
    return nc


def tail_inputs(d, ya, yb):
    l = 0
    x = d['x']
    maps = []
    RC, NC_ = 1664, 1304
    wg = np.ascontiguousarray(d['w_in'][l][:, RC + NC_:])
    lnp = np.concatenate([d[k][l].reshape(8, 128).T for k in ('ln1_g', 'ln1_b', 'ln2_g', 'ln2_b', 'ln3_g', 'ln3_b')], axis=1)
    cw = np.ascontiguousarray(d['ffn_conv_w'][l].reshape(3, 44, 128).transpose(2, 1, 0)).reshape(128, 132)
    cb = np.ascontiguousarray(d['ffn_conv_b'][l].reshape(44, 128).T)
    common = dict(wg=wg, pa=d['merge_p_a'][l], pb=d['merge_p_b'][l], wo=d['mix_w_o'][l], wq=d['xa_wq'][l], wk=d['xa_wk'][l],
                  wv=d['xa_wv'][l], xwo=d['xa_wo'][l], wup=d['ffn_w_up'][l], wdn=d['ffn_w_down'][l],
                  lnp=np.ascontiguousarray(lnp), cw=cw, cb=cb)
    common = {k: np.ascontiguousarray(v, dtype=np.float32) for k, v in common.items()}

    def halo_T(a, b, t0):
        C = a.shape[-1]
        o = np.zeros((C, NT), np.float32)
        lo = max(0, t0 - 2)
        o[:, 2 - (t0 - lo):] = a[b, lo:t0 + 2048].T
        return o
    for c in range(8):
        b, t0 = c // 4, (c % 4) * 2048
        m = dict(common)
        m['xT'] = halo_T(x, b, t0); m['yaT'] = halo_T(ya, b, t0); m['ybT'] = halo_T(yb, b, t0)
        m['memT'] = np.ascontiguousarray(d['mem'][b].T)
        m['hmask'] = np.full((128, 1), 0.0 if t0 == 0 else 1.0, np.float32)
        maps.append(m)
    return maps


_NC_CACHE = {}


def run_tail(d, ya, yb):
    if 'tail' not in _NC_CACHE:
        _NC_CACHE['tail'] = build_tail()
    nc = _NC_CACHE['tail']
    res = run_bass_kernel_spmd(nc, tail_inputs(d, ya, yb), core_ids=list(range(8)))
    out = np.zeros((2, 8192, D), np.float32)
    for c in range(8):
        b, t0 = c // 4, (c % 4) * 2048
        out[b, t0:t0 + 2048] = res.results[c]['outT'].T
    return out


SEQ = 8192
GN_EPS = 64e-5
CH = 64
NCH = 128 // CH
LV = 5


def rwkv_consts():
    idx = np.arange(128)
    same = (idx[:, None] // CH == idx[None, :] // CH)
    m_strict = (same & (idx[:, None] < idx[None, :])).astype(np.float32)
    m_incl = (same & (idx[:, None] <= idx[None, :])).astype(np.float32)
    tri_gt = (same & (idx[:, None] > idx[None, :])).astype(np.float32)
    cind = (idx[:, None] // CH == np.arange(4)[None, :]).astype(np.float32)
    ident = np.eye(128, dtype=np.float32)
    return np.concatenate([ident, m_incl, tri_gt, m_strict, m_incl, m_strict.T, m_strict.T, cind, np.zeros((128, 60), np.float32)], axis=1).astype(np.float32)


def build_rwkv(ntok=SEQ, stop=99):
    nc = bass.Bass("TRN2", target_bir_lowering=False)
    try:
        return _build_rwkv(nc, ntok, stop)
    except _Stop:
        return nc


def _build_rwkv(nc, ntok, stop):
    def ck(code):
        if stop == code:
            P.emit()
            raise _Stop()
    dr = lambda n, s, k="ExternalInput": nc.dram_tensor(n, s, F32, kind=k).ap()
    xT = dr("xT", [D, ntok]); w_r = dr("w_r", [D, 512]); cmat = dr("cmat", [128, 960])
    pcol = dr("pcol", [128, 12])
    w2a2 = dr("w2a2", [64, 256])
    w0a0 = dr("w0a0", [128, 256])
    yT = dr("yT", [128, ntok], "ExternalOutput")
    chunked = lambda ap: ap.rearrange("(c p) n -> p c n", p=128)
    P = Prog(nc)
    pl = Pools(P, nc)
    sb = lambda n, s, dt=F32: nc.alloc_sbuf_tensor(n, s, dt)
    cm = sb("cm", [128, 960]); pc = sb("pc", [128, 12]); wa = sb("wa", [64, 256]); w0 = sb("w0", [128, 256])
    P.dma(cm[:], cmat); P.dma(pc[:], pcol); P.dma(wa[:], w2a2); P.dma(w0[:], w0a0)
    ident = cm[:, 0:128]; TriInc = cm[:, 128:256]; TriGt = cm[:, 256:384]
    MSK1 = cm[:, 384:640]; Mincl = cm[:, 512:640]; MSK3 = cm[:, 640:896]; CInd = cm[:, 896:960]
    dkk = sb("dkk", [128, 128]); dka = sb("dka", [128, 128]); drk = sb("drk", [128, 128])
    P.ts(dkk[:], ident, pc[:, 4:5], None, ALU.mult)
    P.ts(dka[:], ident, pc[:, 5:6], None, ALU.mult)
    P.ts(drk[:], ident, pc[:, 6:7], None, ALU.mult)
    wrb = sb("wrb", [128, 8, 512], BF16)
    pl.load_cast(wrb[:], chunked(w_r))
    xt32 = [sb("xt32_%d" % i, [128, 8, 512]) for i in range(1)]
    xb = sb("xb", [128, 8, 512], BF16)
    pS = [[sb("pS%d_%d" % (i, c), [128, 513]) for c in range(5)] for i in range(2)]
    pL = [sb("pL%d" % c, [128, 512]) for c in range(5)]
    for c in range(5):
        P.memset(pS[1][c][:, 512:513], 0.0)
    toks = [{n: sb("tk%d_%s" % (i, n), [128, 128]) for n in ("r", "k", "v", "kkp", "kka", "rrk", "logw", "a", "kk", "kmod", "e1", "e2", "e3", "e4", "Bt", "Kt", "Rt", "t0", "t1", "yn", "bv")} for i in range(2)]
    smalls = [sb("small%d" % i, [128, 16]) for i in range(2)]
    RH1s = [[sb("RH1_%d_%d" % (i, h), [128, 192]) for h in range(2)] for i in range(2)]
    RH2s = [[sb("RH2_%d_%d" % (i, h), [128, 192]) for h in range(2)] for i in range(2)]
    AKs = [[sb("AK_%d_%d" % (i, h), [128, 192]) for h in range(2)] for i in range(2)]
    CM = [sb("CM_%d" % h, [64, 512]) for h in range(2)]
    XX = [[sb("XX%d_%d" % (h, i), [128, 256]) for i in range(2)] for h in range(2)]
    TT = [[sb("TT%d_%d" % (h, i), [128, 128]) for i in range(2)] for h in range(2)]
    QS = [sb("QS_%d" % h, [128, 192]) for h in range(2)]
    RpZ = [sb("RpZ_%d" % h, [128, NCH, 128]) for h in range(2)]
    msk = [sb("msk_%d" % h, [128, 2 * NCH, 64]) for h in range(2)]
    WYKP = [sb("WYKP_%d" % h, [128, 192]) for h in range(2)]
    GS = [sb("GS_%d" % h, [128, 256]) for h in range(2)]
    lamC = [sb("lamC_%d" % h, [64, 4]) for h in range(2)]
    STS = [sb("STS_%d" % h, [128, 5 * 64]) for h in range(2)]
    for h in range(2):
        P.memset(STS[h][:, :], 0.0); P.memset(GS[h][:, :], 0.0); P.memset(RpZ[h][:, :, :], 0.0)
    bnst = sb("bnst", [128, 2, 6]); bnag = sb("bnag", [128, 2, 2])
    ystg = [sb("ystg%d" % i, [128, 512]) for i in range(2)]
    lh = sb("lh", [128, 512])
    xTc = chunked(xT)
    NB = ntok // 512
    for blk in range(NB):
        par = blk % 2
        P.dma(xt32[0][:, 0:4, :], xTc[:, 0:4, blk * 512:(blk + 1) * 512])
        P.dma(xt32[0][:, 4:8, :], xTc[:, 4:8, blk * 512:(blk + 1) * 512], q='pool')
        for c in range(8):
            P.copy(xb[:, c, :], xt32[0][:, c, :], eng=('act', 'dve', 'pool')[c % 3])
        for c in range(5):
            ps = pl.bank()
            c0, c1 = ((c * 128, (c + 1) * 128) if c < 3 else ((384, 448) if c == 3 else (448, 512)))
            R = c1 - c0
            for kc in range(8):
                P.mm(ps[0:R, :], wrb[:, kc, c0:c1], xb[:, kc, :], start=(kc == 0), stop=(kc == 7))
            cur = pS[par][c]; prev = pS[1 - par][c]
            P.copy(cur[0:R, 1:513], ps[0:R, :], eng='act')
            P.copy(cur[0:R, 0:1], prev[0:R, 512:513], eng='dve')
            t = pL[c]
            mucol = pc[0:R, c:c + 1] if c < 3 else pc[0:R, 6 + c:7 + c]
            P.tt(t[0:R, :], cur[0:R, 0:512], cur[0:R, 1:513], ALU.subtract)
            P.stt(t[0:R, :], t[0:R, :], mucol, cur[0:R, 1:513], ALU.mult, ALU.add)
        P.actf(lh[0:64, :], pL[3][0:64, :], AF.Tanh)
        ys = ystg[blk % 2]
        ck(1)
        def prep_gen(sub):
            tok = toks[sub % 2]; small = smalls[sub % 2]; RH1 = RH1s[sub % 2]; RH2 = RH2s[sub % 2]; AK = AKs[sub % 2]
            ts_ = slice(sub * 128, (sub + 1) * 128)
            ps = pl.bank()
            P.mm(ps[:, 0:128], pL[0][:, ts_], ident); P.mm(ps[:, 128:256], pL[1][:, ts_], ident); P.mm(ps[:, 256:384], pL[2][:, ts_], ident)
            yield
            P.copy(tok["r"][:], ps[:, 0:128], eng='act'); P.copy(tok["k"][:], ps[:, 128:256], eng='dve'); P.copy(tok["v"][:], ps[:, 256:384], eng='act')
            yield
            ps2 = pl.bank()
            P.mm(ps2[:, 0:128], pL[1][:, ts_], dkk[:]); P.mm(ps2[:, 128:256], pL[1][:, ts_], dka[:]); P.mm(ps2[:, 256:384], pL[0][:, ts_], drk[:])
            yield
            P.copy(tok["kkp"][:], ps2[:, 0:128], eng='dve'); P.copy(tok["kka"][:], ps2[:, 128:256], eng='act'); P.copy(tok["rrk"][:], ps2[:, 256:384], eng='dve')
            yield
            ps3 = pl.bank()
            P.mm(ps3[:, 0:128], lh[0:64, ts_], wa[0:64, 0:128])
            yield
            P.mm(ps3[:, 128:256], pL[4][0:64, ts_], wa[0:64, 128:256])
            yield
            P.tt(tok["t0"][:], ps3[:, 0:128], w0[:, 0:128], ALU.add)
            yield
            P.tt(tok["t1"][:], ps3[:, 128:256], w0[:, 128:256], ALU.add)
            yield
            P.actf(tok["logw"][:], tok["t0"][:], AF.Sigmoid)
            yield
            P.ts(tok["logw"][:], tok["logw"][:], -float(np.exp(-0.5)), None, ALU.mult)
            yield
            P.actf(tok["a"][:], tok["t1"][:], AF.Sigmoid)
            yield
            P.actf(tok["t0"][:], tok["kkp"][:], AF.Square)
            yield
            P.op('dve', lambda e, o=small[:, 0:2], i=tok["t0"][:].rearrange("p (h k) -> p h k", h=2): e.reduce_sum(o, i, AX.X), [tok["t0"][:]], [small[:, 0:2]])
            yield
            P.actf(small[:, 2:4], small[:, 0:2], AF.Sqrt)
            yield
            P.ts(small[:, 2:4], small[:, 2:4], 1e-12, None, ALU.max)
            yield
            P.op('dve', lambda e, o=small[:, 4:6], i=small[:, 2:4]: e.reciprocal(o, i), [small[:, 2:4]], [small[:, 4:6]])
            yield
            for h in range(2):
                hs = slice(h * 64, (h + 1) * 64)
                P.ts(tok["kk"][:, hs], tok["kkp"][:, hs], small[:, 4 + h:5 + h], None, ALU.mult)
            P.ts(tok["t0"][:], tok["a"][:], -1.0, None, ALU.add)
            yield
            P.tt(tok["t0"][:], tok["t0"][:], tok["kka"][:], ALU.mult)
            yield
            P.tt(tok["kmod"][:], tok["t0"][:], tok["k"][:], ALU.add)
            yield
            psL = pl.bank()
            P.mm(psL[:, 0:128], TriInc, tok["logw"][:]); P.mm(psL[:, 128:256], TriGt, tok["logw"][:])
            yield
            P.tt(tok["t1"][:], psL[:, 0:128], tok["logw"][:], ALU.subtract)
            yield
            P.actf(tok["e1"][:], tok["t1"][:], AF.Exp)
            yield
            P.actf(tok["e2"][:], psL[:, 0:128], AF.Exp, scale=-1.0)
            yield
            P.actf(tok["e3"][:], psL[:, 0:128], AF.Exp)
            yield
            P.actf(tok["e4"][:], psL[:, 128:256], AF.Exp)
            yield
            P.tt(tok["t0"][:], tok["kk"][:], tok["a"][:], ALU.mult)
            yield
            P.tt(tok["Bt"][:], tok["t0"][:], tok["e2"][:], ALU.mult)
            yield
            P.tt(tok["Kt"][:], tok["kmod"][:], tok["e2"][:], ALU.mult)
            yield
            P.tt(tok["Rt"][:], tok["r"][:], tok["e3"][:], ALU.mult)
            yield
            P.tt(tok["t1"][:], tok["kk"][:], tok["e1"][:], ALU.mult)
            yield
            for h in range(2):
                hs = slice(h * 64, (h + 1) * 64)
                P.ts(RH1[h][:, 0:64], tok["t1"][:, hs], -1.0, None, ALU.mult)
                P.tt(RH2[h][:, 128:192], tok["t0"][:, hs], tok["e4"][:, hs], ALU.mult)
                P.tt(AK[h][:, 128:192], tok["kmod"][:, hs], tok["e4"][:, hs], ALU.mult)
            P.tt(tok["t1"][:], tok["rrk"][:], tok["kmod"][:], ALU.mult)
            yield
            P.op('dve', lambda e, o=small[:, 6:8], i=tok["t1"][:].rearrange("p (h k) -> p h k", h=2): e.reduce_sum(o, i, AX.X), [tok["t1"][:]], [small[:, 6:8]])
            yield
            for h in range(2):
                hs = slice(h * 64, (h + 1) * 64)
                P.ts(tok["bv"][:, hs], tok["v"][:, hs], small[:, 6 + h:7 + h], None, ALU.mult)
        for _ in prep_gen(0):
            pass
        for sub in range(4):
            tok = toks[sub % 2]; small = smalls[sub % 2]; RH1 = RH1s[sub % 2]; RH2 = RH2s[sub % 2]; AK = AKs[sub % 2]
            ts_ = slice(sub * 128, (sub + 1) * 128)
            ck(2)
            def head_gen(h):
                hs = slice(h * 64, (h + 1) * 64)
                tp = pl.bank()
                P.tr(tp[0:64, 0:128], RH1[h][:, 0:64], ident)
                P.tr(tp[0:64, 128:256], tok["Rt"][:, hs], ident)
                P.tr(tp[0:64, 256:384], tok["Bt"][:, hs], ident)
                P.tr(tp[0:64, 384:512], tok["Kt"][:, hs], ident)
                P.copy(CM[h][:, :], tp[0:64, :], eng='act')
                yield
                Ac, Rc, Bc, Kc = [CM[h][:, i * 128:(i + 1) * 128] for i in range(4)]
                m1 = pl.bank(); m2 = pl.bank(); m3 = pl.bank()
                P.mm(m1[:, 0:256], Bc, CM[h][:, 0:256])
                P.mm(m2[:, 0:128], Kc, Rc)
                P.mm(m3[:, 0:256], Ac, CM[h][:, 256:512])
                X0 = XX[h][0]
                P.tt(X0[:, 0:128], m1[:, 0:128], MSK1[:, 0:128], ALU.mult)
                P.tt(RH2[h][:, 0:128], m1[:, 128:256], Mincl, ALU.mult)
                P.tt(AK[h][:, 0:128], m2[:, 0:128], Mincl, ALU.mult)
                P.tt(X0[:, 128:256], m3[:, 0:128], MSK3[:, 0:128], ALU.mult)
                P.tt(RH1[h][:, 64:192], m3[:, 128:256], MSK3[:, 128:256], ALU.mult)
                yield
                ck(3)
                Tc = TT[h][0]
                P.tt(Tc[:, :], X0[:, 0:128], ident, ALU.add)
                cur = 0
                for lvl in range(LV):
                    Xc = XX[h][cur]; Xn = XX[h][1 - cur]
                    pp = pl.bank()
                    if lvl < LV - 1:
                        P.mm(pp[:, 0:128], Xc[:, 128:256], Xc[:, 0:128])
                    P.mm(pp[:, 128:256], Xc[:, 0:128], Xc[:, 128:256])
                    if lvl < LV - 1:
                        P.copy(Xn[:, :], pp[:, 0:256], eng='act')
                    else:
                        P.copy(Xn[:, 128:256], pp[:, 128:256], eng='act')
                    yield
                    pt_ = pl.bank()
                    P.mm(pt_[:, 0:128], Xn[:, 128:256], TT[h][lvl % 2][:, :])
                    P.tt(TT[h][(lvl + 1) % 2][:, :], TT[h][lvl % 2][:, :], pt_[:, 0:128], ALU.add)
                    yield
                    cur = 1 - cur
                ck(4)
                Tf = TT[h][LV % 2]
                q = pl.bank()
                P.mm(q[:, 0:192], Tf[:, :], RH1[h][:, :])
                P.copy(QS[h][:, :], q[:, 0:192], eng='act')
                yield
                rp = pl.bank()
                P.mm(rp[0:64, 0:128], QS[h][:, 0:64], RH2[h][:, 0:128])
                for c in range(NCH):
                    cs = slice(c * CH, (c + 1) * CH)
                    P.tt(RpZ[h][0:64, c, cs], rp[0:64, cs], Rc[:, cs], ALU.add)
                wk = pl.bank()
                P.mm(wk[:, 0:192], QS[h][:, 64:192], RH2[h][:, :])
                P.tt(WYKP[h][:, :], wk[:, 0:192], AK[h][:, :], ALU.add)
                yield
                gg = pl.bank()
                for c in range(NCH):
                    P.ts(msk[h][:, c, :], QS[h][:, 0:64], CInd[:, c:c + 1], None, ALU.mult)
                    P.ts(msk[h][:, NCH + c, :], WYKP[h][:, 128:192], CInd[:, c:c + 1], None, ALU.mult)
                    P.mm(gg[0:64, c * 64:(c + 1) * 64], msk[h][:, c, :], RH2[h][:, 128:192])
                P.copy(GS[h][0:64, 0:NCH * 64], gg[0:64, 0:NCH * 64], eng='act')
                lc = pl.bank()
                P.mm(lc[0:64, 0:64], tok["logw"][:, hs], CInd)
                P.actf(lamC[h][:, 0:NCH], lc[0:64, 0:NCH], AF.Exp)
                yield
                ck(5)
                for c in range(NCH):
                    cs = slice(c * CH, (c + 1) * CH)
                    sn = pl.bank()
                    STc = STS[h][:, c * 64:(c + 1) * 64]
                    P.mm(sn[0:64, 0:64], GS[h][:, c * 64:(c + 1) * 64], STc, start=True, stop=False)
                    P.mm(sn[0:64, 0:64], msk[h][:, NCH + c, :], tok["v"][:, hs], start=False, stop=True)
                    P.stt(STS[h][0:64, (c + 1) * 64:(c + 2) * 64], STS[h][0:64, c * 64:(c + 1) * 64], lamC[h][:, c:c + 1], sn[0:64, 0:64], ALU.mult, ALU.add)
                    yield
                ck(6)
                yp = pl.bank()
                P.mm(yp[:, 0:64], WYKP[h][:, 0:128], tok["v"][:, hs], start=True, stop=False)
                for c in range(NCH):
                    P.mm(yp[:, 0:64], RpZ[h][:, c, :], STS[h][:, c * 64:(c + 1) * 64], start=False, stop=(c == NCH - 1))
                P.copy(STS[h][0:64, 0:64], STS[h][0:64, NCH * 64:(NCH + 1) * 64], eng='dve')
                P.copy(tok["t0"][:, hs], yp[:, 0:64], eng='act')
                P.op('dve', lambda e, o=bnst[:, h, :], i=tok["t0"][:, hs]: e.bn_stats(o, i), [tok["t0"][:, hs]], [bnst[:, h, :]])
                P.op('dve', lambda e, o=bnag[:, h, :], i=bnst[:, h, :]: e.bn_aggr(o, i), [bnst[:, h, :]], [bnag[:, h, :]])
                P.ts(small[:, 8 + h:9 + h], bnag[:, h, 1:2], GN_EPS, None, ALU.add)
                P.actf(small[:, 8 + h:9 + h], small[:, 8 + h:9 + h], AF.Sqrt)
                P.op('dve', lambda e, o=small[:, 10 + h:11 + h], i=small[:, 8 + h:9 + h]: e.reciprocal(o, i), [small[:, 8 + h:9 + h]], [small[:, 10 + h:11 + h]])
                P.ts(tok["yn"][:, hs], tok["t0"][:, hs], bnag[:, h, 0:1], small[:, 10 + h:11 + h], ALU.subtract, ALU.mult)
            gens = [head_gen(0), head_gen(1)] + ([prep_gen(sub + 1)] if sub < 3 else [])
            while gens:
                for g_ in list(gens):
                    try:
                        next(g_)
                    except StopIteration:
                        gens.remove(g_)
            ck(7)
            po = pl.bank()
            P.tr(po[:, 0:128], tok["yn"][:], ident)
            P.tr(po[:, 128:256], tok["bv"][:], ident)
            P.actf(ys[:, ts_], po[:, 0:128], AF.Identity, bias=pc[:, 8:9], scale=pc[:, 7:8])
            P.tt(ys[:, ts_], ys[:, ts_], po[:, 128:256], ALU.add)
        P.dma(yT[:, blk * 512:(blk + 1) * 512], ys[:, :], q='pool')
    P.emit()
    return nc


def rwkv_inputs(d, ntok=SEQ):
    l = 0
    W = d['w_in'][l]
    cm = rwkv_consts()
    maps = []
    for c in range(8):
        b, j = c // 4, c % 4
        cols = np.concatenate([np.arange(128 * j, 128 * j + 128), 512 + np.arange(128 * j, 128 * j + 128),
                               1024 + np.arange(128 * j, 128 * j + 128), np.arange(1536, 1664)])
        hc = np.arange(128 * j, 128 * j + 128)
        pcol = np.zeros((128, 12), np.float32)
        pcol[:, 0:4] = d['rwkv_mu'][l][cols].reshape(4, 128).T
        pcol[:, 4] = d['rwkv_k_k'][l][hc]; pcol[:, 5] = d['rwkv_k_a'][l][hc]
        pcol[:, 6] = d['rwkv_r_k'][l].reshape(-1)[hc]; pcol[:, 7] = d['rwkv_gn_g'][l][hc]; pcol[:, 8] = d['rwkv_gn_b'][l][hc]
        pcol[0:64, 9] = d['rwkv_mu'][l][1536:1600]; pcol[0:64, 10] = d['rwkv_mu'][l][1600:1664]
        w2a2 = np.concatenate([d['rwkv_w2'][l][:, hc], d['rwkv_a2'][l][:, hc]], axis=1)
        w0a0 = np.tile(np.concatenate([d['rwkv_w0'][l][hc], d['rwkv_a0'][l][hc]])[None, :], (128, 1))
        maps.append(dict(xT=np.ascontiguousarray(d['x'][b, :ntok].T), w_r=np.ascontiguousarray(W[:, cols]), cmat=cm,
                         pcol=pcol, w2a2=np.ascontiguousarray(w2a2, dtype=np.float32), w0a0=np.ascontiguousarray(w0a0, dtype=np.float32)))
    return maps


def run_rwkv(d):
    if 'rwkv' not in _NC_CACHE:
        _NC_CACHE['rwkv'] = build_rwkv()
    res = run_bass_kernel_spmd(_NC_CACHE['rwkv'], rwkv_inputs(d), core_ids=list(range(8)))
    ya = np.zeros((2, SEQ, 512), np.float32)
    for c in range(8):
        b, j = c // 4, c % 4
        ya[b, :, 128 * j:128 * j + 128] = res.results[c]['yT'].T
    return ya


ROPE_THETA = 500000.0
NEGM = 30000.0


def nsa_consts(ntok):
    pos = np.arange(ntok, dtype=np.float32)
    inv = ROPE_THETA ** (-np.arange(8, dtype=np.float32) * 2.0 / 16.0)
    ang = pos[None, :] * inv[:, None]
    cosT = np.ones((64, ntok), np.float32); sinT = np.zeros((64, ntok), np.float32)
    cosT[0:8] = np.cos(ang); cosT[8:16] = np.cos(ang)
    sinT[0:8] = -np.sin(ang); sinT[8:16] = np.sin(ang)
    p = np.arange(128)
    tri_le = (p[:, None] <= p[None, :]).astype(np.float32)
    tri_gt = (p[:, None] > p[None, :]).astype(np.float32)
    cmk = np.ones((128, 32, 128), np.float32)
    for i in range(16):
        p0 = 8 * i
        u = p[:, None] - p0
        cmk[:, 16 + i, :] = (p[None, :] >= 16 * u + 31).astype(np.float32)
        if p0 == 0:
            cmk[127, i, :] = (p >= 15).astype(np.float32)
    ncmp_pad = 512
    c = np.arange(ncmp_pad); j = np.arange(128)
    ov = ((16 * c[:, None] <= 64 * j[None, :] + 63) & (16 * c[:, None] + 31 >= 64 * j[None, :])).astype(np.float32)
    ov[(ntok - 32) // 16 + 1:] = 0.0
    ovc = ov.reshape(4, 128, 128).transpose(1, 0, 2)
    jj = np.arange(128); key = np.arange(128)
    xall = np.zeros((128, 64, 128), np.float32)
    for kb in range(64):
        xall[:, kb, :] = (jj[:, None] == 2 * kb + key[None, :] // 64)
    qcol = np.zeros((128, 4), np.float32)
    qcol[:, 0] = (p >= 64) * 1e30 + (p < 64) * -1e30
    qcol[:, 1] = (p >= 64).astype(np.float32)
    qcol[:, 2] = (p < 64) * 1e30
    return dict(cosT=cosT, sinT=sinT, tri=np.concatenate([tri_le, tri_gt, np.eye(128, dtype=np.float32)], axis=1),
                cmk=cmk.reshape(128, 32 * 128), ovc=np.ascontiguousarray(ovc).reshape(128, 512),
                xall=xall.reshape(128, 64 * 128), qcol=qcol)


def build_nsa(ntok=SEQ, stop=99):
    nc = bass.Bass("TRN2", target_bir_lowering=False)
    try:
        return _build_nsa(nc, ntok, stop)
    except _Stop:
        return nc


def _build_nsa(nc, ntok, stop):
    def ck(code):
        if stop == code:
            P.emit()
            raise _Stop()
    NB = ntok // 512; NT128 = ntok // 128; NQ = NT128 // 2
    NCMP = (ntok - 32) // 16 + 1
    dr = lambda n, s, k="ExternalInput": nc.dram_tensor(n, s, F32, kind=k).ap()
    xT = dr("xT", [D, ntok])
    wA = dr("wA", [D, 15 * 64])
    wB = dr("wB", [D, 140])
    cosT = dr("cosT", [64, ntok]); sinT = dr("sinT", [64, ntok])
    tri = dr("tri", [128, 384]); cmk = dr("cmk", [128, 2048]); ovc = dr("ovc", [128, 512]); xall = dr("xall", [128, 8192]); qcol = dr("qcol", [128, 10])
    dmask = dr("dmask", [128, 256]); wmask = dr("wmask", [128, 768])
    w1k = dr("w1k", [64, 32 * 256]); w1v = dr("w1v", [64, 32 * 256]); w2k = dr("w2k", [128, 2 * 64]); w2v = dr("w2v", [128, 2 * 64])
    pek = dr("pek", [64, 32 * 64]); pev = dr("pev", [64, 32 * 64])
    xq = dr("xq", [D, NQ * 128])
    cosq = dr("cosq", [64, NQ * 128]); sinq = dr("sinq", [64, NQ * 128])
    yb = dr("yb", [NQ * 128, 256], "ExternalOutput")
    chunked = lambda ap: ap.rearrange("(c p) n -> p c n", p=128)
    P = Prog(nc)
    pl = Pools(P, nc)
    sb = lambda n, s, dt=F32: nc.alloc_sbuf_tensor(n, s, dt)
    tri_t = sb("tri_t", [128, 384]); qc_t = sb("qc_t", [128, 10])
    P.dma(tri_t[:], tri); P.dma(qc_t[:], qcol)
    identf = tri_t[:, 256:384]
    xt32 = sb("xt32", [128, 8, 512]); xb = sb("xb", [128, 8, 512], BF16)
    pl.stg = [xt32[:, 0:4, :].rearrange("p a n -> p (a n)"), xt32[:, 4:8, :].rearrange("p a n -> p (a n)")]
    BUF2 = sb("BUF2", [128, 24576], BF16)
    cmk_b = sb("cmk_b", [128, 2048], BF16); dm_b = sb("dm_b", [128, 256], BF16); wm_b = sb("wm_b", [128, 768], BF16); ov_b = sb("ov_b", [128, 4, 128], BF16)
    xall_b = BUF2[:, 16384:24576].rearrange("p (a n) -> p a n", n=128)
    pl.load_cast(cmk_b[:], cmk); pl.load_cast(dm_b[:], dmask); pl.load_cast(wm_b[:], wmask); pl.load_cast(ov_b[:].rearrange("p a n -> p (a n)"), ovc, q='pool')
    wAb = sb("wAb", [128, 8, 960], BF16); wBb = sb("wBb", [128, 8, 140], BF16)
    pl.load_cast(wAb[:], chunked(wA)); pl.load_cast(wBb[:], chunked(wB), q='pool')
    KBUF = sb("KBUF", [64, 2, ntok], BF16)
    ksT = KBUF[:, 0, :]; kwT = KBUF[:, 1, :]; kcT = KBUF[:, 0, :]; vcT = KBUF[:, 1, :]
    vsA = sb("vsA", [128, NT128, 65], BF16); vwA = sb("vwA", [128, NT128, 65], BF16)
    qT = BUF2[0:64, 0:NQ * 512].rearrange("p (a n) -> p a n", n=512)
    GT = sb("GT", [128, NQ, 12])
    P.memset(vsA[:, :, 64:65], 1.0); P.memset(vwA[:, :, 64:65], 1.0)
    cs_t = [sb("cs_t%d" % i, [64, 1024]) for i in range(1)]
    rt = [sb("rt%d" % i, [64, 512]) for i in range(2)]
    xTc = chunked(xT); xqc = chunked(xq)

    def proj_block(src_c, cosd, sind, c0, n, groups, dests, tokmajor_tiles):
        P.dma(xt32[:, 0:4, 0:n], src_c[:, 0:4, c0:c0 + n]); P.dma(xt32[:, 4:8, 0:n], src_c[:, 4:8, c0:c0 + n], q='pool')
        for c in range(8):
            P.copy(xb[:, c, 0:n], xt32[:, c, 0:n], eng=('act', 'dve', 'pool')[c % 3])
        cst = cs_t[0]
        P.dma(cst[:, 0:n], cosd[:, c0:c0 + n]); P.dma(cst[:, 512:512 + n], sind[:, c0:c0 + n], q='pool')
        for (cb, pb), dst in zip(groups, dests):
            p1 = pl.bank()
            for kc in range(8):
                P.mm(p1[0:64, 0:n], wAb[:, kc, cb:cb + 64], xb[:, kc, 0:n], start=(kc == 0), stop=(kc == 7))
            if pb is None:
                P.copy(dst, p1[0:64, 0:n], eng='act')
                continue
            p2 = pl.bank()
            for kc in range(8):
                P.mm(p2[0:64, 0:n], wAb[:, kc, pb:pb + 64], xb[:, kc, 0:n], start=(kc == 0), stop=(kc == 7))
            P.tt(rt[0][:, 0:n], p1[0:64, 0:n], cst[:, 0:n], ALU.mult)
            P.tt(rt[1][:, 0:n], p2[0:64, 0:n], cst[:, 512:512 + n], ALU.mult)
            P.tt(dst, rt[0][:, 0:n], rt[1][:, 0:n], ALU.add, eng='pool')
        for (t128, off) in tokmajor_tiles:
            pv = pl.bank()
            for kc in range(8):
                P.mm(pv[:, 0:140], xb[:, kc, off:off + 128], wBb[:, kc, :], start=(kc == 0), stop=(kc == 7))
            yield (t128, pv)

    for blk in range(NB):
        groups = [(256 + 0, 448 + 256 + 0), (896, None)]
        sl = slice(blk * 512, (blk + 1) * 512)
        for _ in proj_block(xTc, cosT, sinT, blk * 512, 512, groups, [kcT[:, sl], vcT[:, sl]], []):
            pass
    ck(1)
    w1b = [BUF2[0:64, i * 8192:(i + 1) * 8192].rearrange("p (a n) -> p a n", n=256) for i in range(2)]
    w2b = [sb("w2b%d" % i, [128, 2, 64], BF16) for i in range(2)]
    peb = [BUF2[0:64, 16384 + i * 2048:16384 + (i + 1) * 2048].rearrange("p (a n) -> p a n", n=64) for i in range(2)]
    pl.load_cast(w1b[0], w1k.rearrange("p (a n) -> p a n", n=256)); pl.load_cast(w1b[1], w1v.rearrange("p (a n) -> p a n", n=256), q='pool')
    pl.load_cast(w2b[0][:].rearrange("p a n -> p (a n)"), w2k); pl.load_cast(w2b[1][:].rearrange("p a n -> p (a n)"), w2v, q='pool')
    pl.load_cast(peb[0], pek.rearrange("p (a n) -> p a n", n=64)); pl.load_cast(peb[1], pev.rearrange("p (a n) -> p a n", n=64), q='pool')
    KC = sb("KC", [64, 512], BF16); VCA = sb("VCA", [128, 4, 65], BF16)
    P.memset(KC[:], 0.0); P.memset(VCA[:, :, 0:64], 0.0); P.memset(VCA[:, :, 64:65], 1.0)
    GH = [[sb("GH%d_%d" % (i, hc), [128, 512], BF16) for hc in range(2)] for i in range(2)]
    gtmp = [sb("gtmp%d" % i, [128, 512]) for i in range(3)]
    bcol = sb("bcol", [128, 4])
    for which, srcT in ((0, kcT), (1, vcT)):
        for hc in range(2):
            pb_ = pl.bank()
            for l in range(32):
                P.mm(pb_[:, 0:64], w1b[which][:, l, hc * 128:(hc + 1) * 128], peb[which][:, l, :], start=(l == 0), stop=(l == 31))
            P.copy(bcol[:, which * 2 + hc:which * 2 + hc + 1], pb_[:, 0:1], eng='act')
            ph = pl.bank()
            for l in range(32):
                rhs = srcT[:, l:l + 16 * (NCMP - 1) + 1:16]
                P.mm(ph[:, 0:NCMP], w1b[which][:, l, hc * 128:(hc + 1) * 128], rhs, start=(l == 0), stop=(l == 31))
            x_ = gtmp[0][:, 0:NCMP]; u_ = gtmp[1][:, 0:NCMP]; s_ = gtmp[2][:, 0:NCMP]
            P.actf(x_, ph[:, 0:NCMP], AF.Identity, bias=bcol[:, which * 2 + hc:which * 2 + hc + 1])
            P.actf(u_, x_, AF.Square)
            P.ts(u_, u_, 0.044715, 1.0, ALU.mult, ALU.add)
            P.tt(u_, u_, x_, ALU.mult)
            P.actf(s_, u_, AF.Sigmoid, scale=2.0 * 0.7978845608028654)
            P.memset(GH[which][hc][:, NCMP:512], 0.0)
            P.tt(GH[which][hc][:, 0:NCMP], x_, s_, ALU.mult)
    pk = pl.bank()
    for hc in range(2):
        P.mm(pk[0:64, 0:NCMP], w2b[0][:, hc, :], GH[0][hc][:, 0:NCMP], start=(hc == 0), stop=(hc == 1))
    P.copy(KC[:, 0:NCMP], pk[0:64, 0:NCMP], eng='act')
    for cc in range((NCMP + 127) // 128):
        pv_ = pl.bank()
        for hc in range(2):
            P.mm(pv_[:, 0:64], GH[1][hc][:, cc * 128:(cc + 1) * 128], w2b[1][:, hc, :], start=(hc == 0), stop=(hc == 1))
        P.copy(VCA[:, cc, 0:64], pv_[:, 0:64], eng='dve')
    ck(2)
    for blk in range(NB):
        groups = [(256 + 64, 448 + 256 + 64), (256 + 128, 448 + 256 + 128)]
        sl = slice(blk * 512, (blk + 1) * 512)
        for (t128, pv) in proj_block(xTc, cosT, sinT, blk * 512, 512, groups, [ksT[:, sl], kwT[:, sl]], [(blk * 4 + i, i * 128) for i in range(4)]):
            P.copy(vsA[:, t128, 0:64], pv[:, 0:64], eng='act')
            P.copy(vwA[:, t128, 0:64], pv[:, 64:128], eng='dve')
    pl.load_cast(xall_b, xall.rearrange("p (a n) -> p a n", n=128))
    qtmp = [sb("qtmp%d" % r, [64, 512], BF16) for r in range(4)]
    for qblk in range(0, NQ, 4):
        n = min(4, NQ - qblk) * 128
        groups = [(r * 64, 448 + r * 64) for r in range(4)]
        dests = [qtmp[r][:, 0:n] for r in range(4)]
        for (t128, pv) in proj_block(xqc, cosq, sinq, qblk * 128, n, groups, dests, [(qblk + i, i * 128) for i in range(n // 128)]):
            P.actf(GT[:, t128, :], pv[:, 128:140], AF.Sigmoid)
        for i in range(n // 128):
            for r in range(4):
                P.copy(qT[:, qblk + i, r * 128:(r + 1) * 128], qtmp[r][:, i * 128:(i + 1) * 128], eng=('pool', 'dve')[r % 2])
    ck(3)
    CV = sb("CV", [128, 4, 193], BF16)
    for cc in range(4):
        P.copy(CV[:, cc, 0:65], VCA[:, cc, :], eng='dve'); P.copy(CV[:, cc, 65:193], ov_b[:, cc, :], eng='pool')
    pT = [sb("pT%d" % i, [128, 512], BF16) for i in range(3)]
    imp = sb("imp", [128, 128]); imp2 = sb("imp2", [128, 128]); m8 = sb("m8", [128, 16]); sel = sb("sel", [128, 128])
    selT = sb("selT", [128, 128], BF16); negm = sb("negm", [128, 512], BF16)
    oTs = [None] + [sb("oTs%d" % i, [65, 512]) for i in range(1, 3)]
    ytile = sb("ytile", [128, 256]); sm = sb("sm", [128, 32])
    cnum = [sb("cnum%d" % i, [128, 386]) for i in range(2)]
    identb = sb("identb", [128, 128], BF16)
    P.copy(identb[:], identf)
    pi = 0
    for i in range(NQ):
        qrhs = qT[:, i, :]
        _attend(P, pl, nc, i, qrhs, dict(KC=KC, CV=CV, cmk_b=cmk_b, pT=pT, imp=imp, imp2=imp2, m8=m8, sel=sel, selT=selT, negm=negm,
                                          oTs=oTs, ytile=ytile, sm=sm, cnum=cnum, identb=identb, identf=identf, xall_b=xall_b, dm_b=dm_b, wm_b=wm_b,
                                          ksT=ksT, kwT=kwT, vsA=vsA, vwA=vwA, GT=GT, qc_t=qc_t, yb=yb, NCMP=NCMP))
    P.emit()
    return nc


def _bc4(ap):
    return ap.unsqueeze(1).broadcast_to([ap.shape[0], 4, ap.shape[1]])


def _attend(P, pl, nc, i, qrhs, T):
    KC, CV, pT = T['KC'], T['CV'], T['pT']
    NCMP = T['NCMP']
    identf, identb = T['identf'], T['identb']
    oTs = T['oTs']
    v4 = lambda t: t[:, :].rearrange("p (r q) -> p r q", r=4)
    ccb = min((16 * i + 6) // 128, (NCMP - 1) // 128)
    cn = [pl.banks[0], pl.banks[1], pl.banks[2], pl.banks[3]]
    rot = lambda: pl.banks[4 + (pl.nrot() % 4)]
    for cc in range(ccb + 1):
        ps = rot()
        P.mm(ps[:, :], KC[:, cc * 128:(cc + 1) * 128], qrhs)
        pt = pT[(pl.bi) % 3]
        P.actf(pt[:, :], ps[:, :], AF.Exp, scale=0.125)
        if cc >= ccb - 1:
            which = 1 if cc == ccb else 0
            mk = T['cmk_b'][:, (which * 8 + (i % 8)) * 128:(which * 8 + (i % 8) + 1) * 128]
            P.tt(v4(pt), v4(pt), _bc4(mk), ALU.mult)
        for r in range(4):
            P.mm(cn[r][:, 0:193], pt[:, r * 128:(r + 1) * 128], CV[:, cc, :], start=(cc == 0), stop=(cc == ccb))
    cnum = T['cnum']
    for r in range(4):
        P.copy(cnum[r // 2][:, (r % 2) * 193:(r % 2) * 193 + 193], cn[r][:, 0:193], eng=('act', 'dve')[r % 2])
    sm = T['sm']
    for r in range(4):
        P.ts(sm[:, r:r + 1], cnum[r // 2][:, (r % 2) * 193 + 64:(r % 2) * 193 + 65], 1e-30, None, ALU.add)
    P.op('dve', lambda e: e.reciprocal(sm[:, 4:8], sm[:, 0:4]), [sm[:, 0:4]], [sm[:, 4:8]])
    imp, imp2, m8, sel, selT, negm = T['imp'], T['imp2'], T['m8'], T['sel'], T['selT'], T['negm']
    for r in range(4):
        src = cnum[r // 2][:, (r % 2) * 193 + 65:(r % 2) * 193 + 193]
        if r == 0:
            P.ts(imp[:, :], src, sm[:, 4:5], None, ALU.mult)
        else:
            P.stt(imp[:, :], src, sm[:, 4 + r:5 + r], imp[:, :], ALU.mult, ALU.add)
    qc = T['qc_t']
    w0 = max(0, 4 * i - 1); wn = 4 * i + 4 - w0; t0c = w0 - (4 * i - 1)
    P.copy(imp2[:, :], imp[:, :], eng='dve')
    P.tt(imp2[:, w0:w0 + wn], imp[:, w0:w0 + wn], qc[:, t0c:t0c + wn], ALU.mult)
    P.tt(imp2[:, w0:w0 + wn], imp2[:, w0:w0 + wn], qc[:, 5 + t0c:5 + t0c + wn], ALU.add)
    if 4 * i + 4 < 128:
        P.memset(imp2[:, 4 * i + 4:128], -1e30)
    P.memset(imp2[:, 0:1], 1e30)
    P.op('dve', lambda e: e.max(out=m8[:, 0:8], in_=imp2[:, :]), [imp2[:, :]], [m8[:, 0:8]])
    P.op('dve', lambda e: e.match_replace(out=imp[:, :], in_to_replace=m8[:, 0:8], in_values=imp2[:, :], imm_value=-3e38), [m8[:, 0:8], imp2[:, :]], [imp[:, :]])
    P.op('dve', lambda e: e.max(out=m8[:, 8:16], in_=imp[:, :]), [imp[:, :]], [m8[:, 8:16]])
    P.ts(sel[:, :], imp2[:, :], m8[:, 15:16], None, ALU.is_ge)
    pst = rot()
    P.tr(pst[:, 0:128], sel[:, :], identf)
    P.copy(selT[:, :], pst[:, 0:128], eng='act')
    for r in range(4):
        P.ts(negm[:, r * 128:(r + 1) * 128], selT[:, :], -1.0, NEGM, ALU.add, ALU.mult, eng=('dve', 'pool')[r % 2])
    ksT, vsA, xall_b, dm_b = T['ksT'], T['vsA'], T['xall_b'], T['dm_b']
    po = pl.banks[0]
    nkb = 2 * i + 2
    def qk_sel(kb):
        ps = rot()
        P.mm(ps[:, :], ksT[:, kb * 128:(kb + 1) * 128], qrhs, start=True, stop=False)
        P.mm(ps[:, :], xall_b[:, kb, :], negm[:, :], start=False, stop=True)
        return ps
    ps_next = qk_sel(0)
    for kb in range(nkb):
        ps = ps_next
        pt = pT[kb % 3]
        P.actf(pt[:, :], ps[:, :], AF.Exp, scale=0.125)
        if kb + 1 < nkb:
            ps_next = qk_sel(kb + 1)
        if kb >= nkb - 2:
            mk = dm_b[:, (kb - (nkb - 2)) * 128:(kb - (nkb - 2) + 1) * 128]
            P.tt(v4(pt), v4(pt), _bc4(mk), ALU.mult)
        P.mm(po[0:65, :], vsA[:, kb, :], pt[:, :], start=(kb == 0), stop=(kb == nkb - 1))
    P.copy(oTs[1][:, :], po[0:65, :], eng='act')
    kwT, vwA, wm_b = T['kwT'], T['vwA'], T['wm_b']
    pw = pl.banks[1]
    kbs = [(m, 2 * i - 4 + m) for m in range(6) if 2 * i - 4 + m >= 0]
    def qk_w(kb):
        ps = rot()
        P.mm(ps[:, :], kwT[:, kb * 128:(kb + 1) * 128], qrhs)
        return ps
    ps_next = qk_w(kbs[0][1])
    for n_, (m, kb) in enumerate(kbs):
        ps = ps_next
        pt = pT[n_ % 3]
        P.actf(pt[:, :], ps[:, :], AF.Exp, scale=0.125)
        if n_ + 1 < len(kbs):
            ps_next = qk_w(kbs[n_ + 1][1])
        P.tt(v4(pt), v4(pt), _bc4(wm_b[:, m * 128:(m + 1) * 128]), ALU.mult)
        P.mm(pw[0:65, :], vwA[:, kb, :], pt[:, :], start=(n_ == 0), stop=(n_ == len(kbs) - 1))
    P.copy(oTs[2][:, :], pw[0:65, :], eng='dve')
    GT, ytile = T['GT'], T['ytile']
    for r in range(4):
        P.tt(sm[:, 8 + r:9 + r], sm[:, 4 + r:5 + r], GT[:, i, r * 3:r * 3 + 1], ALU.mult)
        P.ts(ytile[:, r * 64:(r + 1) * 64], cnum[r // 2][:, (r % 2) * 193:(r % 2) * 193 + 64], sm[:, 8 + r:9 + r], None, ALU.mult)
    for br in (1, 2):
        pb_ = pl.banks[br + 1]
        for r in range(4):
            P.mm(pb_[:, r * 65:(r + 1) * 65], oTs[br][0:65, r * 128:(r + 1) * 128], identf[0:65, 0:65])
        for r in range(4):
            P.ts(sm[:, 12 + r:13 + r], pb_[:, r * 65 + 64:r * 65 + 65], 1e-30, None, ALU.add)
        P.op('dve', lambda e: e.reciprocal(sm[:, 16:20], sm[:, 12:16]), [sm[:, 12:16]], [sm[:, 16:20]])
        for r in range(4):
            P.tt(sm[:, 20 + r:21 + r], sm[:, 16 + r:17 + r], GT[:, i, r * 3 + br:r * 3 + br + 1], ALU.mult)
            P.stt(ytile[:, r * 64:(r + 1) * 64], pb_[:, r * 65:r * 65 + 64], sm[:, 20 + r:21 + r], ytile[:, r * 64:(r + 1) * 64], ALU.mult, ALU.add)
    P.dma(T['yb'][i * 128:(i + 1) * 128, :], ytile[:, :], q='pool')


def nsa_inputs(d, ntok=SEQ):
    l = 0
    W = d['w_in'][l][:, 1664:1664 + 1304]
    cst = nsa_consts(ntok)
    NQ = ntok // 256
    perm = np.arange(64); perm[0:8] = np.arange(8, 16); perm[8:16] = np.arange(0, 8)
    p = np.arange(128)
    tri_le = cst['tri'][:, 0:128]; tri_gt = cst['tri'][:, 128:256]
    one = np.ones((128, 128), np.float32); zero = np.zeros((128, 128), np.float32)
    w1k = d['nsa_ck_w1'][l].reshape(32, 64, 256).transpose(1, 0, 2).reshape(64, 32 * 256)
    w1v = d['nsa_cv_w1'][l].reshape(32, 64, 256).transpose(1, 0, 2).reshape(64, 32 * 256)
    w2k = d['nsa_ck_w2'][l].reshape(2, 128, 64).transpose(1, 0, 2).reshape(128, 128)
    w2v = d['nsa_cv_w2'][l].reshape(2, 128, 64).transpose(1, 0, 2).reshape(128, 128)
    pek = np.repeat(d['nsa_pe_k'][l].T[:, :, None], 64, axis=2).reshape(64, 32 * 64)
    pev = np.repeat(d['nsa_pe_v'][l].T[:, :, None], 64, axis=2).reshape(64, 32 * 64)
    maps = []
    for c in range(8):
        b, g, par = c // 4, (c % 4) // 2, c % 2
        qcols = [256 * g + r * 64 + np.arange(64) for r in range(4)]
        kvc = lambda idx: 512 + idx * 128 + g * 64 + np.arange(64)
        roped = qcols + [kvc(0), kvc(2), kvc(4)]
        colsA = np.concatenate(roped + [cg[perm] for cg in roped] + [kvc(1)])
        gcols = np.array([1280 + (4 * g + r) * 3 + cc for r in range(4) for cc in range(3)])
        colsB = np.concatenate([kvc(3), kvc(5), gcols])
        own = np.concatenate([np.arange((2 * i + par) * 128, (2 * i + par + 1) * 128) for i in range(NQ)])
        xTb = np.ascontiguousarray(d['x'][b, :ntok].T)
        cmk_full = cst['cmk'].reshape(128, 32, 128)
        cmkc = np.stack([cmk_full[:, wh * 16 + (2 * m + par) % 16, :] for wh in range(2) for m in range(8)], axis=1)
        if par == 0:
            dmask = np.concatenate([tri_le, zero], axis=1); wmask = np.concatenate([tri_gt, one, one, one, tri_le, zero], axis=1)
        else:
            dmask = np.concatenate([one, tri_le], axis=1); wmask = np.concatenate([zero, tri_gt, one, one, one, tri_le], axis=1)
        qcol = np.zeros((128, 10), np.float32)
        hi = (p >= 64).astype(np.float32); lo = 1.0 - hi
        for w in range(5):
            dl = w - 1 - 2 * par
            if dl < -1:
                qcol[:, w] = 1.0
            elif dl == -1:
                qcol[:, w] = hi; qcol[:, 5 + w] = lo * 1e30
            elif dl == 0:
                qcol[:, 5 + w] = 1e30
            elif dl == 1:
                qcol[:, 5 + w] = hi * 1e30 - lo * 1e30
            else:
                qcol[:, 5 + w] = -1e30
        m = dict(xT=xTb, wA=W[:, colsA], wB=W[:, colsB], cosT=cst['cosT'], sinT=cst['sinT'], tri=cst['tri'], cmk=cmkc.reshape(128, 2048),
                 ovc=cst['ovc'], xall=cst['xall'], qcol=qcol, dmask=dmask, wmask=wmask, w1k=w1k, w1v=w1v, w2k=w2k, w2v=w2v, pek=pek, pev=pev,
                 xq=xTb[:, own], cosq=cst['cosT'][:, own], sinq=cst['sinT'][:, own])
        maps.append({k: np.ascontiguousarray(v, dtype=np.float32) for k, v in m.items()})
    return maps


def run_nsa(d):
    if 'nsa' not in _NC_CACHE:
        _NC_CACHE['nsa'] = build_nsa()
    res = run_bass_kernel_spmd(_NC_CACHE['nsa'], nsa_inputs(d), core_ids=list(range(8)))
    yb = np.zeros((2, SEQ, 512), np.float32)
    for c in range(8):
        b, g, par = c // 4, (c % 4) // 2, c % 2
        o = res.results[c]['yb'].reshape(SEQ // 256, 128, 256)
        yb[b].reshape(SEQ // 256, 2, 128, 512)[:, par, :, g * 256:(g + 1) * 256] = o
    return yb


def kernel(**inputs):
    d = {k: np.asarray(v) for k, v in inputs.items()}
    ya = run_rwkv(d)
    yb = run_nsa(d)
    out = run_tail(d, ya, yb)
    return out.astype(np.float32)
```

```python
import numpy as np
import concourse.bass as bass
import concourse.mybir as mybir
from concourse.bass_utils import run_bass_kernel_spmd

F32 = mybir.dt.float32
BF16 = mybir.dt.bfloat16
AF = mybir.ActivationFunctionType
ALU = mybir.AluOpType
AX = mybir.AxisListType

SAME_ENG_SYNC = False
NSLOT = 8


def _region(ap):
    t = ap.tensor
    shp = tuple(t.shape)
    sp = str(ap.space)
    off = int(ap.offset)
    pat = [(int(s), int(c)) for s, c in ap.ap]
    if 'DRAM' in sp.upper() or 'HBM' in sp.upper():
        ext = sum((c - 1) * abs(s) for s, c in pat)
        return (ap.name, 0, 0, off, off + ext)
    if 'PSUM' in sp.upper():
        return (ap.name, 0, 127, 0, 10 ** 9)
    fs = 1
    for d in shp[1:]:
        fs *= int(d)
    p0 = off // fs
    f0 = off % fs
    ps, pc = pat[0]
    p1 = p0 + (pc - 1) * (ps // fs if fs else 0)
    ext = sum((c - 1) * abs(s) for s, c in pat[1:])
    return (ap.name, p0, p1, f0, f0 + ext)


def _overlap(a, b):
    return not (a[2] < b[1] or b[2] < a[1] or a[4] < b[3] or b[4] < a[3])


def _contains(a, b):
    return a[1] <= b[1] and a[2] >= b[2] and a[3] <= b[3] and a[4] >= b[4]


class Prog:
    ENGS = ['pe', 'act', 'dve', 'pool', 'sp']

    def __init__(self, nc):
        self.nc = nc
        self.stream = {e: [] for e in self.ENGS}
        self.count = {e: 0 for e in self.ENGS}
        self.dcount = {'sp': 0, 'pool': 0, 'act': 0}
        self.hist = {}
        self.known = {e: {} for e in self.ENGS}
        self.nops = 0

    def _deps(self, eng, reads, writes, is_dma=False):
        deps = {}

        def add(tok):
            k, v, e = tok
            if e == eng and eng == 'pe' and not k.startswith('d_') and not is_dma:
                return
            if deps.get(k, (0,))[0] < v:
                deps[k] = (v, e)
        rr = [_region(a) for a in reads]
        wr = [_region(a) for a in writes]
        for r in rr:
            psum = (r[4] == 10 ** 9)
            for (reg, tok, isw) in self.hist.get(r[0], ()):
                if isw and _overlap(reg, r):
                    add(tok)
                elif psum and not isw and tok[2] != eng:
                    add(tok)
        for w in wr:
            for (reg, tok, isw) in self.hist.get(w[0], ()):
                if _overlap(reg, w):
                    add(tok)
        return deps, rr, wr

    def _update(self, tok, rr, wr):
        for w in wr:
            h = self.hist.setdefault(w[0], [])
            h[:] = [x for x in h if not _contains(w, x[0])]
            h.append((w, tok, True))
        for r in rr:
            h = self.hist.setdefault(r[0], [])
            h[:] = [x for x in h if not (not x[2] and x[1][2] == tok[2] and x[1][0] == tok[0] and x[0] == r)]
            h.append((r, tok, False))

    def _waits(self, eng, deps):
        waits = []
        kn = self.known[eng]
        for k, (v, e) in deps.items():
            if kn.get(k, 0) < v:
                kn[k] = v
                waits.append((k, v))
        return waits

    def op(self, eng, fn, reads, writes):
        deps, rr, wr = self._deps(eng, reads, writes)
        waits = self._waits(eng, deps)
        self.count[eng] += 1
        tok = ('c_' + eng, self.count[eng], eng)
        self.stream[eng].append((waits, fn, tok, 1))
        self._update(tok, rr, wr)
        self.nops += 1
        return tok

    def dma(self, out, in_, q='sp', **kw):
        eng = q
        deps, rr, wr = self._deps(eng, [in_], [out], is_dma=True)
        i = self.dcount[q]
        self.dcount[q] += 1
        slot = i % NSLOT
        val = 16 * (i // NSLOT + 1)
        key = 'd_%s_%d' % (q, slot)
        if val > 16:
            if deps.get(key, (0,))[0] < val - 16:
                deps[key] = (val - 16, q)
        waits = self._waits(eng, deps)
        tok = (key, val, eng)

        def fn(e, out=out, in_=in_, kw=kw):
            return e.dma_start(out=out, in_=in_, **kw)
        self.stream[eng].append((waits, fn, tok, 16))
        self._update(tok, rr, wr)
        self.nops += 1
        return tok

    def dma_like(self, fn, reads, writes, q='pool'):
        eng = q
        deps, rr, wr = self._deps(eng, reads, writes, is_dma=True)
        i = self.dcount[q]
        self.dcount[q] += 1
        slot = i % NSLOT
        val = 16 * (i // NSLOT + 1)
        key = 'd_%s_%d' % (q, slot)
        if val > 16:
            if deps.get(key, (0,))[0] < val - 16:
                deps[key] = (val - 16, q)
        waits = self._waits(eng, deps)
        tok = (key, val, eng)
        self.stream[eng].append((waits, fn, tok, 16))
        self._update(tok, rr, wr)
        self.nops += 1
        return tok

    F32R = False

    def mm(self, out, lhsT, rhs, start=True, stop=True, **kw):
        l2, r2 = lhsT, rhs
        if self.F32R and lhsT.dtype == F32 and rhs.dtype == F32 and lhsT.shape[-1] == 128 and lhsT.shape[0] == 128 and rhs.shape[-1] % 2 == 0:
            l2 = lhsT.bitcast(mybir.dt.float32r); r2 = rhs.bitcast(mybir.dt.float32r)
        return self.op('pe', lambda e: e.matmul(out, l2, r2, start=start, stop=stop, **kw),
                       [lhsT, rhs] + ([] if start else [out]), [out])

    def tr(self, out, in_, ident):
        return self.op('pe', lambda e: e.transpose(out, in_, ident), [in_, ident], [out])

    def actf(self, out, in_, func, bias=None, scale=None, accum_out=None, eng='act'):
        kw = {}
        rd = [in_]
        wr = [out]
        if bias is not None:
            kw['bias'] = bias
            if not isinstance(bias, (int, float)):
                rd.append(bias)
        if scale is not None:
            kw['scale'] = scale
            if not isinstance(scale, (int, float)):
                rd.append(scale)
        if accum_out is not None:
            kw['accum_out'] = accum_out
            wr.append(accum_out)
        return self.op(eng, lambda e: e.activation(out, in_, func, **kw), rd, wr)

    def tt(self, out, in0, in1, op, eng='dve'):
        return self.op(eng, lambda e: e.tensor_tensor(out, in0, in1, op), [in0, in1], [out])

    def ts(self, out, in0, s1, s2, op0, op1=None, accum_out=None, eng='dve'):
        rd = [in0] + [s for s in (s1, s2) if s is not None and not isinstance(s, (int, float))]
        wr = [out] + ([accum_out] if accum_out is not None else [])
        kw = {}
        if op1 is not None:
            kw['op1'] = op1
        if accum_out is not None:
            kw['accum_out'] = accum_out
        return self.op(eng, lambda e: e.tensor_scalar(out, in0, s1, s2, op0, **kw), rd, wr)

    def stt(self, out, in0, scalar, in1, op0, op1, eng='dve'):
        rd = [in0, in1] + ([scalar] if not isinstance(scalar, (int, float)) else [])
        return self.op(eng, lambda e: e.scalar_tensor_tensor(out, in0, scalar, in1, op0, op1), rd, [out])

    def copy(self, out, in_, eng='dve'):
        if eng == 'act':
            return self.op('act', lambda e: e.copy(out, in_), [in_], [out])
        return self.op(eng, lambda e: e.tensor_copy(out, in_), [in_], [out])

    def memset(self, ap, v, eng='dve'):
        return self.op(eng, lambda e: e.memset(ap, v), [], [ap])

    def emit(self):
        nc = self.nc
        import contextlib
        es = contextlib.ExitStack()
        sems = {}

        def sem(k):
            if k not in sems:
                sems[k] = es.enter_context(nc.semaphore(k))
            return sems[k]
        for e in self.ENGS:
            sem('c_' + e)
        for q in self.dcount:
            for s in range(NSLOT):
                sem('d_%s_%d' % (q, s))
        final = []
        for e in self.ENGS:
            if e != 'sp' and self.count[e] > 0:
                final.append(('c_' + e, self.count[e]))
        for q, n in self.dcount.items():
            for s in range(NSLOT):
                cnt = (n - s + NSLOT - 1) // NSLOT if n > s else 0
                if cnt > 0:
                    final.append(('d_%s_%d' % (q, s), 16 * cnt))
        streams = self.stream

        def run(engname, e):
            for (waits, fn, tok, inc) in streams[engname]:
                for (k, v) in waits:
                    e.wait_ge(sem(k), v)
                ins = fn(e)
                ins.then_inc(sem(tok[0]), inc)
            if engname == 'sp':
                for (k, v) in final:
                    e.wait_ge(sem(k), v)
        with nc.Block() as block:
            @block.tensor
            def _(e):
                run('pe', e)

            @block.scalar
            def _(e):
                run('act', e)

            @block.vector
            def _(e):
                run('dve', e)

            @block.gpsimd
            def _(e):
                run('pool', e)

            @block.sync
            def _(e):
                run('sp', e)
        es.close()


D = 1024
ALPHA = 2.0 ** 0.25
LN_EPS = 1e-5
DFF = 2816
NT = 2050
NJ = DFF // 128


class Pools:
    def __init__(self, P, nc):
        self.P = P
        self.nc = nc
        self.banks = [nc.alloc_psum_tensor("bank%d" % i, [128, 512], F32) for i in range(8)]
        self.bi = 0
        self.stg = [nc.alloc_sbuf_tensor("stg%d" % i, [128, 2048], F32) for i in range(2)]
        self.si = 0
        self.ci = 0

    def bank(self):
        b = self.banks[self.bi % 8]
        self.bi += 1
        return b

    def nrot(self):
        self.bi += 1
        return self.bi

    def load_cast(self, dst, src, q='sp'):
        P = self.P
        shp = dst.shape
        npart = shp[0]
        if len(shp) == 2:
            n = shp[1]
            step = 2048
            for c0 in range(0, n, step):
                c1 = min(n, c0 + step)
                st = self.stg[self.si % 2]
                self.si += 1
                P.dma(st[0:npart, 0:c1 - c0], src[:, c0:c1], q=q)
                self._cast(dst[:, c0:c1], st[0:npart, 0:c1 - c0])
        else:
            a, n = shp[1], shp[2]
            assert n <= 2048
            per = max(1, 2048 // n)
            for a0 in range(0, a, per):
                a1 = min(a, a0 + per)
                st = self.stg[self.si % 2]
                self.si += 1
                sv = st[0:npart, 0:(a1 - a0) * n].rearrange("p (a n) -> p a n", n=n)
                P.dma(sv, src[:, a0:a1, :], q=q)
                self._cast(dst[:, a0:a1, :], sv)

    def _cast(self, dst, src):
        engs = ['pool', 'dve', 'act', 'pool']
        e = engs[self.ci % len(engs)]
        self.ci += 1
        self.P.copy(dst, src, eng=e)


def chan_ln(P, pl, res, resb, gcol, bcol, N, ones_b, scr_b, scr_sq, tmp):
    pm = pl.bank()
    psq = pl.bank()
    for c in range(8):
        P.actf(scr_b[:, c, 0:N], res[:, c, 0:N], AF.Copy)
        P.actf(scr_sq[:, c, 0:N], res[:, c, 0:N], AF.Square)
    for c in range(8):
        P.mm(pm[:, 0:N], ones_b[:], scr_b[:, c, 0:N], start=(c == 0), stop=(c == 7))
    for c in range(8):
        P.mm(psq[:, 0:N], ones_b[:], scr_sq[:, c, 0:N], start=(c == 0), stop=(c == 7))
    mean, msq, var, rstd = [t[:, 0:N] for t in tmp[:4]]
    P.actf(mean, pm[:, 0:N], AF.Copy, scale=1.0 / D)
    P.tt(msq, mean, mean, ALU.mult)
    P.stt(var, psq[:, 0:N], 1.0 / D, msq, ALU.mult, ALU.subtract)
    P.ts(var, var, LN_EPS, None, ALU.add)
    P.actf(var, var, AF.Sqrt)
    P.op('dve', lambda e: e.reciprocal(rstd, var), [var], [rstd])
    for c in range(8):
        P.tt(res[:, c, 0:N], res[:, c, 0:N], mean, ALU.subtract)
        P.tt(res[:, c, 0:N], res[:, c, 0:N], rstd, ALU.mult)
        P.actf(res[:, c, 0:N], res[:, c, 0:N], AF.Identity, bias=bcol[:, c:c + 1], scale=gcol[:, c:c + 1])
        P.copy(resb[:, c, 0:N], res[:, c, 0:N], eng='pool')


class _Stop(Exception):
    pass


def build_tail(stop=99):
    nc = bass.Bass("TRN2", target_bir_lowering=False)
    try:
        return _build_tail(nc, stop)
    except _Stop:
        return nc


def _build_tail(nc, stop):
    def ck(code):
        if stop == code:
            P.emit()
            raise _Stop()
    dr = lambda n, s, k="ExternalInput": nc.dram_tensor(n, s, F32, kind=k).ap()
    xT = dr("xT", [D, NT]); yaT = dr("yaT", [512, NT]); ybT = dr("ybT", [512, NT]); memT = dr("memT", [D, 256])
    wg = dr("wg", [D, 2048]); pa = dr("pa", [512, D]); pb = dr("pb", [512, D]); wo = dr("wo", [D, D])
    wq = dr("wq", [D, D]); wk = dr("wk", [D, D]); wv = dr("wv", [D, D]); xwo = dr("xwo", [D, D])
    wup = dr("wup", [D, 2 * DFF]); wdn = dr("wdn", [DFF, D])
    lnp = dr("lnp", [128, 48]); cw = dr("cw", [128, 44 * 3]); cb = dr("cb", [128, 44]); hmask = dr("hmask", [128, 1])
    outT = dr("outT", [D, 2048], "ExternalOutput")
    x2s = dr("x2s", [D, NT], "Internal")
    chunked = lambda ap: ap.rearrange("(c p) n -> p c n", p=128)

    P = Prog(nc)
    pl = Pools(P, nc)
    sb = lambda n, s, dt=F32: nc.alloc_sbuf_tensor(n, s, dt)
    NA = 256
    arena = sb("arena", [128, 49152], BF16)
    lnp_t = sb("lnp_t", [128, 48]); cw_t = sb("cw_t", [128, 132]); cb_t = sb("cb_t", [128, 44]); hm_t = sb("hm_t", [128, 1])
    ones_b = sb("ones_b", [128, 128], BF16)
    res = sb("res", [128, 8, 258]); resb = sb("resb", [128, 8, 258], BF16)
    shr = sb("shr", [128, 6144], BF16)
    v3 = lambda ap, n: ap.rearrange("p (a n) -> p a n", n=n)
    actb2 = v3(shr[:, 0:2048], NA); qT = v3(shr[:, 2048:4096], NA); yab = v3(shr[:, 4096:5120], NA); ybb = v3(shr[:, 5120:6144], NA)
    gall = v3(shr[:, 0:NJ * 256], 256)
    scr_b = sb("scr_b", [128, 8, 258], BF16); scr_sq = sb("scr_sq", [128, 8, 258], BF16)
    tmp = [sb("tmp%d" % i, [128, 258]) for i in range(8)]
    big2 = sb("big2", [128, NJ * D], BF16)
    KT = v3(big2[:, 0:2048], 256); V = v3(big2[:, 2048:4096], D); memb = v3(big2[:, 4096:6144], 256)
    wdn_all = v3(big2[:, :], D)
    pbuf = [sb("pbuf%d" % i, [128, NA], BF16) for i in range(2)]
    hs = [sb("hs%d" % i, [128, 258]) for i in range(2)]

    P.dma(lnp_t[:], lnp); P.dma(cw_t[:], cw); P.dma(cb_t[:], cb); P.dma(hm_t[:], hmask)
    P.memset(ones_b[:], 1.0)

    if stop <= 0:
        P.emit(); return nc
    def A(off, kc, n):
        return arena[:, off:off + kc * n].rearrange("p (c n) -> p c n", n=n)
    Wk_b = A(0, 8, D); Wv_b = A(8192, 8, D)
    pl.load_cast(memb[:], chunked(memT))
    pl.load_cast(Wk_b, chunked(wk)); pl.load_cast(Wv_b, chunked(wv), q='pool')
    if stop == 1:
        P.emit(); return nc
    for oc in range(8):
        ps = pl.bank()
        for kc in range(8):
            P.mm(ps[:, 0:256], Wk_b[:, kc, oc * 128:(oc + 1) * 128], memb[:, kc, :], start=(kc == 0), stop=(kc == 7))
        P.copy(KT[:, oc, :], ps[:, 0:256], eng='act')
    for mc in range(2):
        for hf in range(2):
            ps = pl.bank()
            for kc in range(8):
                P.mm(ps[:, :], memb[:, kc, mc * 128:(mc + 1) * 128], Wv_b[:, kc, hf * 512:(hf + 1) * 512], start=(kc == 0), stop=(kc == 7))
            P.copy(V[:, mc, hf * 512:(hf + 1) * 512], ps[:, :], eng='dve')
    if stop <= 1:
        P.dma(outT[0:128, 0:256], KT[:, 0, :].bitcast(F32)[:, 0:128] if False else tmp[0][:, 0:256]); P.emit(); return nc
    Wg_b = A(0, 8, 2048); Pa_b = A(16384, 4, D); Pb_b = A(20480, 4, D); Wo_b = A(24576, 8, D)
    Wq_b = A(32768, 8, D); XWo_b = A(40960, 8, D)
    pl.load_cast(Pa_b, chunked(pa)); pl.load_cast(Pb_b, chunked(pb), q='pool')
    pl.load_cast(Wo_b, chunked(wo)); pl.load_cast(Wq_b, chunked(wq), q='pool'); pl.load_cast(XWo_b, chunked(xwo))
    pl.load_cast(Wg_b, chunked(wg), q='pool')
    g1, b1, g2, b2, g3, b3 = [lnp_t[:, i * 8:(i + 1) * 8] for i in range(6)]

    if stop <= 2:
        P.emit(); return nc
    tiles = [(0, 2)] + [(2 + i * NA, NA) for i in range(8)]
    if stop <= 3:
        tiles = tiles[:1]
    if stop == 4 or 40 < stop < 50:
        tiles = tiles[1:2]
    xTc, yaTc, ybTc, x2sc, outTc = chunked(xT), chunked(yaT), chunked(ybT), chunked(x2s), chunked(outT)
    for (c0, N) in tiles:
        P.dma(res[:, :, 0:N], xTc[:, :, c0:c0 + N])
        st = pl.stg[pl.si % 2]; pl.si += 1
        sv = st[:, 0:8 * N].rearrange("p (a n) -> p a n", n=N)
        P.dma(sv[:, 0:4, :], yaTc[:, :, c0:c0 + N], q='pool'); P.dma(sv[:, 4:8, :], ybTc[:, :, c0:c0 + N], q='pool')
        P.copy(yab[:, :, 0:N], sv[:, 0:4, :], eng='pool'); P.copy(ybb[:, :, 0:N], sv[:, 4:8, :], eng='pool')
        for c in range(8):
            P.copy(resb[:, c, 0:N], res[:, c, 0:N], eng='act' if c % 2 else 'dve')
        ck(41)
        for oc in range(8):
            osl = slice(oc * 128, (oc + 1) * 128)
            za = pl.bank(); ga = pl.bank()
            for kc in range(4):
                P.mm(za[:, 0:N], Pa_b[:, kc, osl], yab[:, kc, 0:N], start=(kc == 0), stop=(kc == 3))
            for kc in range(4):
                P.mm(za[:, 256:256 + N], Pb_b[:, kc, osl], ybb[:, kc, 0:N], start=(kc == 0), stop=(kc == 3))
            for kc in range(8):
                P.mm(ga[:, 0:N], Wg_b[:, kc, osl], resb[:, kc, 0:N], start=(kc == 0), stop=(kc == 7))
            for kc in range(8):
                P.mm(ga[:, 256:256 + N], Wg_b[:, kc, 1024 + oc * 128:1024 + (oc + 1) * 128], resb[:, kc, 0:N], start=(kc == 0), stop=(kc == 7))
            sa, sbb, t1, t2 = tmp[4][:, 0:N], tmp[5][:, 0:N], tmp[6][:, 0:N], tmp[7][:, 0:N]
            P.actf(sa, ga[:, 0:N], AF.Sigmoid)
            P.actf(sbb, ga[:, 256:256 + N], AF.Sigmoid)
            P.tt(t1, sa, za[:, 0:N], ALU.mult)
            P.tt(t2, sbb, za[:, 256:256 + N], ALU.mult)
            P.tt(actb2[:, oc, 0:N], t1, t2, ALU.add)
        ck(42)
        for oc in range(8):
            osl = slice(oc * 128, (oc + 1) * 128)
            ps = pl.bank()
            for kc in range(8):
                P.mm(ps[:, 0:N], Wo_b[:, kc, osl], actb2[:, kc, 0:N], start=(kc == 0), stop=(kc == 7))
            P.stt(res[:, oc, 0:N], res[:, oc, 0:N], ALPHA, ps[:, 0:N], ALU.mult, ALU.add)
        ck(43)
        chan_ln(P, pl, res, resb, g1, b1, N, ones_b, scr_b, scr_sq, tmp)
        ck(44)
        for oc in range(8):
            osl = slice(oc * 128, (oc + 1) * 128)
            ps = pl.bank()
            for kc in range(8):
                P.mm(ps[:, 0:N], Wq_b[:, kc, osl], resb[:, kc, 0:N], start=(kc == 0), stop=(kc == 7))
            P.copy(qT[:, oc, 0:N], ps[:, 0:N], eng='act' if oc % 2 else 'dve')
        for hh in range(4):
            pden = pl.bank()
            for mc in range(2):
                ps = pl.bank()
                for dc in range(2):
                    P.mm(ps[:, 0:N], KT[:, 2 * hh + dc, mc * 128:(mc + 1) * 128], qT[:, 2 * hh + dc, 0:N], start=(dc == 0), stop=(dc == 1))
                P.actf(pbuf[mc][:, 0:N], ps[:, 0:N], AF.Exp, scale=1.0 / 16.0)
            for mc in range(2):
                P.mm(pden[:, 0:N], ones_b[:], pbuf[mc][:, 0:N], start=(mc == 0), stop=(mc == 1))
            rec = tmp[4][:, 0:N]
            P.op('dve', lambda e, rec=rec, pden=pden, N=N: e.reciprocal(rec, pden[:, 0:N]), [pden[:, 0:N]], [rec])
            for dc in range(2):
                po = pl.bank()
                for mc in range(2):
                    P.mm(po[:, 0:N], V[:, mc, (2 * hh + dc) * 128:(2 * hh + dc + 1) * 128], pbuf[mc][:, 0:N], start=(mc == 0), stop=(mc == 1))
                P.tt(actb2[:, 2 * hh + dc, 0:N], po[:, 0:N], rec, ALU.mult)
        for oc in range(8):
            osl = slice(oc * 128, (oc + 1) * 128)
            ps = pl.bank()
            for kc in range(8):
                P.mm(ps[:, 0:N], XWo_b[:, kc, osl], actb2[:, kc, 0:N], start=(kc == 0), stop=(kc == 7))
            P.stt(res[:, oc, 0:N], res[:, oc, 0:N], ALPHA, ps[:, 0:N], ALU.mult, ALU.add)
        ck(45)
        chan_ln(P, pl, res, resb, g2, b2, N, ones_b, scr_b, scr_sq, tmp)
        ck(46)
        P.dma(x2sc[:, :, c0:c0 + N], res[:, :, 0:N], q='pool')

    if stop <= 5 or 40 < stop < 50:
        P.emit(); return nc
    Wup_b = A(0, 8, 2 * DFF)
    wupc = chunked(wup)
    wdnc = chunked(wdn)
    for kc in range(8):
        for h0 in range(0, 2 * DFF, 2048):
            h1 = min(2 * DFF, h0 + 2048)
            pl.load_cast(Wup_b[:, kc, h0:h1], wupc[:, kc, h0:h1], q='sp' if kc % 2 else 'pool')
    for j0 in range(0, NJ, 2):
        pl.load_cast(wdn_all[:, j0:j0 + 2, :], wdnc[:, j0:j0 + 2, :], q='sp' if (j0 // 2) % 2 else 'pool')
    cw3 = cw_t[:].rearrange("p (c k) -> p c k", k=3)
    for ti in range(8):
        c0 = ti * 256
        P.dma(res[:, :, 0:258], x2sc[:, :, c0:c0 + 258])
        for c in range(8):
            P.copy(resb[:, c, 0:258], res[:, c, 0:258], eng='act' if c % 2 else 'dve')
        for j in range(NJ):
            hp = pl.bank(); hv = pl.bank()
            for kc in range(8):
                P.mm(hp[:, 0:258], Wup_b[:, kc, j * 128:(j + 1) * 128], resb[:, kc, 0:258], start=(kc == 0), stop=(kc == 7))
            for kc in range(8):
                P.mm(hv[:, 0:258], Wup_b[:, kc, DFF + j * 128:DFF + (j + 1) * 128], resb[:, kc, 0:258], start=(kc == 0), stop=(kc == 7))
            srcs = []
            for (hsrc, k) in ((hp, 0), (hv, 1)):
                if ti == 0:
                    hb_ = hs[k]
                    P.copy(hb_[:, 0:258], hsrc[:, 0:258], eng='act')
                    P.ts(hb_[:, 0:2], hb_[:, 0:2], hm_t[:, 0:1], None, ALU.mult)
                    srcs.append(hb_)
                else:
                    srcs.append(hsrc)
            outs = []
            for k, (src, ch) in enumerate(zip(srcs, (j, NJ + j))):
                t = tmp[k * 2]; t2 = tmp[k * 2 + 1]
                P.actf(t[:, 0:256], src[:, 2:258], AF.Identity, bias=cb_t[:, ch:ch + 1], scale=cw3[:, ch, 2:3])
                P.stt(t2[:, 0:256], src[:, 1:257], cw3[:, ch, 1:2], t[:, 0:256], ALU.mult, ALU.add)
                P.stt(t[:, 0:256], src[:, 0:256], cw3[:, ch, 0:1], t2[:, 0:256], ALU.mult, ALU.add)
                outs.append(t)
            sg = tmp[4]
            P.actf(sg[:, 0:256], outs[0][:, 0:256], AF.Silu)
            P.tt(gall[:, j, :], sg[:, 0:256], outs[1][:, 0:256], ALU.mult)
        for oc in range(8):
            a = pl.bank()
            for j in range(NJ):
                P.mm(a[:, 0:256], wdn_all[:, j, oc * 128:(oc + 1) * 128], gall[:, j, :], start=(j == 0), stop=(j == NJ - 1))
            P.stt(res[:, oc, 2:258], res[:, oc, 2:258], ALPHA, a[:, 0:256], ALU.mult, ALU.add)
        res_v = res[:, :, 2:258]; resb_v = resb[:, :, 2:258]
        chan_ln(P, pl, res_v, resb_v, g3, b3, 256, ones_b, scr_b, scr_sq, tmp)
        P.dma(outTc[:, :, ti * 256:(ti + 1) * 256], res[:, :, 2:258], q='pool')
    P.emit()
    return nc


def tail_inputs(d, ya, yb):
    l = 0
    x = d['x']
    maps = []
    RC, NC_ = 1664, 1304
    wg = np.ascontiguousarray(d['w_in'][l][:, RC + NC_:])
    lnp = np.concatenate([d[k][l].reshape(8, 128).T for k in ('ln1_g', 'ln1_b', 'ln2_g', 'ln2_b', 'ln3_g', 'ln3_b')], axis=1)
    cw = np.ascontiguousarray(d['ffn_conv_w'][l].reshape(3, 44, 128).transpose(2, 1, 0)).reshape(128, 132)
    cb = np.ascontiguousarray(d['ffn_conv_b'][l].reshape(44, 128).T)
    common = dict(wg=wg, pa=d['merge_p_a'][l], pb=d['merge_p_b'][l], wo=d['mix_w_o'][l], wq=d['xa_wq'][l], wk=d['xa_wk'][l],
                  wv=d['xa_wv'][l], xwo=d['xa_wo'][l], wup=d['ffn_w_up'][l], wdn=d['ffn_w_down'][l],
                  lnp=np.ascontiguousarray(lnp), cw=cw, cb=cb)
    common = {k: np.ascontiguousarray(v, dtype=np.float32) for k, v in common.items()}

    def halo_T(a, b, t0):
        C = a.shape[-1]
        o = np.zeros((C, NT), np.float32)
        lo = max(0, t0 - 2)
        o[:, 2 - (t0 - lo):] = a[b, lo:t0 + 2048].T
        return o
    for c in range(8):
        b, t0 = c // 4, (c % 4) * 2048
        m = dict(common)
        m['xT'] = halo_T(x, b, t0); m['yaT'] = halo_T(ya, b, t0); m['ybT'] = halo_T(yb, b, t0)
        m['memT'] = np.ascontiguousarray(d['mem'][b].T)
        m['hmask'] = np.full((128, 1), 0.0 if t0 == 0 else 1.0, np.float32)
        maps.append(m)
    return maps


_NC_CACHE = {}


def run_tail(d, ya, yb):
    if 'tail' not in _NC_CACHE:
        _NC_CACHE['tail'] = build_tail()
    nc = _NC_CACHE['tail']
    res = run_bass_kernel_spmd(nc, tail_inputs(d, ya, yb), core_ids=list(range(8)))
    out = np.zeros((2, 8192, D), np.float32)
    for c in range(8):
        b, t0 = c // 4, (c % 4) * 2048
        out[b, t0:t0 + 2048] = res.results[c]['outT'].T
    return out


SEQ = 8192
GN_EPS = 64e-5
CH = 64
NCH = 128 // CH
LV = 5


def rwkv_consts():
    idx = np.arange(128)
    same = (idx[:, None] // CH == idx[None, :] // CH)
    m_strict = (same & (idx[:, None] < idx[None, :])).astype(np.float32)
    m_incl = (same & (idx[:, None] <= idx[None, :])).astype(np.float32)
    tri_gt = (same & (idx[:, None] > idx[None, :])).astype(np.float32)
    cind = (idx[:, None] // CH == np.arange(4)[None, :]).astype(np.float32)
    ident = np.eye(128, dtype=np.float32)
    return np.concatenate([ident, m_incl, tri_gt, m_strict, m_incl, m_strict.T, m_strict.T, cind, np.zeros((128, 60), np.float32)], axis=1).astype(np.float32)


def build_rwkv(ntok=SEQ, stop=99):
    nc = bass.Bass("TRN2", target_bir_lowering=False)
    try:
        return _build_rwkv(nc, ntok, stop)
    except _Stop:
        return nc


def _build_rwkv(nc, ntok, stop):
    def ck(code):
        if stop == code:
            P.emit()
            raise _Stop()
    dr = lambda n, s, k="ExternalInput": nc.dram_tensor(n, s, F32, kind=k).ap()
    xT = dr("xT", [D, ntok]); w_r = dr("w_r", [D, 512]); cmat = dr("cmat", [128, 960])
    pcol = dr("pcol", [128, 12])
    w2a2 = dr("w2a2", [64, 256])
    w0a0 = dr("w0a0", [128, 256])
    yT = dr("yT", [128, ntok], "ExternalOutput")
    chunked = lambda ap: ap.rearrange("(c p) n -> p c n", p=128)
    P = Prog(nc)
    pl = Pools(P, nc)
    sb = lambda n, s, dt=F32: nc.alloc_sbuf_tensor(n, s, dt)
    cm = sb("cm", [128, 960]); pc = sb("pc", [128, 12]); wa = sb("wa", [64, 256]); w0 = sb("w0", [128, 256])
    P.dma(cm[:], cmat); P.dma(pc[:], pcol); P.dma(wa[:], w2a2); P.dma(w0[:], w0a0)
    ident = cm[:, 0:128]; TriInc = cm[:, 128:256]; TriGt = cm[:, 256:384]
    MSK1 = cm[:, 384:640]; Mincl = cm[:, 512:640]; MSK3 = cm[:, 640:896]; CInd = cm[:, 896:960]
    dkk = sb("dkk", [128, 128]); dka = sb("dka", [128, 128]); drk = sb("drk", [128, 128])
    P.ts(dkk[:], ident, pc[:, 4:5], None, ALU.mult)
    P.ts(dka[:], ident, pc[:, 5:6], None, ALU.mult)
    P.ts(drk[:], ident, pc[:, 6:7], None, ALU.mult)
    wrb = sb("wrb", [128, 8, 512], BF16)
    pl.load_cast(wrb[:], chunked(w_r))
    xt32 = [sb("xt32_%d" % i, [128, 8, 512]) for i in range(1)]
    xb = sb("xb", [128, 8, 512], BF16)
    pS = [[sb("pS%d_%d" % (i, c), [128, 513]) for c in range(5)] for i in range(2)]
    pL = [sb("pL%d" % c, [128, 512]) for c in range(5)]
    for c in range(5):
        P.memset(pS[1][c][:, 512:513], 0.0)
    toks = [{n: sb("tk%d_%s" % (i, n), [128, 128]) for n in ("r", "k", "v", "kkp", "kka", "rrk", "logw", "a", "kk", "kmod", "e1", "e2", "e3", "e4", "Bt", "Kt", "Rt", "t0", "t1", "yn", "bv")} for i in range(2)]
    smalls = [sb("small%d" % i, [128, 16]) for i in range(2)]
    RH1s = [[sb("RH1_%d_%d" % (i, h), [128, 192]) for h in range(2)] for i in range(2)]
    RH2s = [[sb("RH2_%d_%d" % (i, h), [128, 192]) for h in range(2)] for i in range(2)]
    AKs = [[sb("AK_%d_%d" % (i, h), [128, 192]) for h in range(2)] for i in range(2)]
    CM = [sb("CM_%d" % h, [64, 512]) for h in range(2)]
    XX = [[sb("XX%d_%d" % (h, i), [128, 256]) for i in range(2)] for h in range(2)]
    TT = [[sb("TT%d_%d" % (h, i), [128, 128]) for i in range(2)] for h in range(2)]
    QS = [sb("QS_%d" % h, [128, 192]) for h in range(2)]
    RpZ = [sb("RpZ_%d" % h, [128, NCH, 128]) for h in range(2)]
    msk = [sb("msk_%d" % h, [128, 2 * NCH, 64]) for h in range(2)]
    WYKP = [sb("WYKP_%d" % h, [128, 192]) for h in range(2)]
    GS = [sb("GS_%d" % h, [128, 256]) for h in range(2)]
    lamC = [sb("lamC_%d" % h, [64, 4]) for h in range(2)]
    STS = [sb("STS_%d" % h, [128, 5 * 64]) for h in range(2)]
    for h in range(2):
        P.memset(STS[h][:, :], 0.0); P.memset(GS[h][:, :], 0.0); P.memset(RpZ[h][:, :, :], 0.0)
    bnst = sb("bnst", [128, 2, 6]); bnag = sb("bnag", [128, 2, 2])
    ystg = [sb("ystg%d" % i, [128, 512]) for i in range(2)]
    lh = sb("lh", [128, 512])
    xTc = chunked(xT)
    NB = ntok // 512
    for blk in range(NB):
        par = blk % 2
        P.dma(xt32[0][:, 0:4, :], xTc[:, 0:4, blk * 512:(blk + 1) * 512])
        P.dma(xt32[0][:, 4:8, :], xTc[:, 4:8, blk * 512:(blk + 1) * 512], q='pool')
        for c in range(8):
            P.copy(xb[:, c, :], xt32[0][:, c, :], eng=('act', 'dve', 'pool')[c % 3])
        for c in range(5):
            ps = pl.bank()
            c0, c1 = ((c * 128, (c + 1) * 128) if c < 3 else ((384, 448) if c == 3 else (448, 512)))
            R = c1 - c0
            for kc in range(8):
                P.mm(ps[0:R, :], wrb[:, kc, c0:c1], xb[:, kc, :], start=(kc == 0), stop=(kc == 7))
            cur = pS[par][c]; prev = pS[1 - par][c]
            P.copy(cur[0:R, 1:513], ps[0:R, :], eng='act')
            P.copy(cur[0:R, 0:1], prev[0:R, 512:513], eng='dve')
            t = pL[c]
            mucol = pc[0:R, c:c + 1] if c < 3 else pc[0:R, 6 + c:7 + c]
            P.tt(t[0:R, :], cur[0:R, 0:512], cur[0:R, 1:513], ALU.subtract)
            P.stt(t[0:R, :], t[0:R, :], mucol, cur[0:R, 1:513], ALU.mult, ALU.add)
        P.actf(lh[0:64, :], pL[3][0:64, :], AF.Tanh)
        ys = ystg[blk % 2]
        ck(1)
        def prep_gen(sub):
            tok = toks[sub % 2]; small = smalls[sub % 2]; RH1 = RH1s[sub % 2]; RH2 = RH2s[sub % 2]; AK = AKs[sub % 2]
            ts_ = slice(sub * 128, (sub + 1) * 128)
            ps = pl.bank()
            P.mm(ps[:, 0:128], pL[0][:, ts_], ident); P.mm(ps[:, 128:256], pL[1][:, ts_], ident); P.mm(ps[:, 256:384], pL[2][:, ts_], ident)
            yield
            P.copy(tok["r"][:], ps[:, 0:128], eng='act'); P.copy(tok["k"][:], ps[:, 128:256], eng='dve'); P.copy(tok["v"][:], ps[:, 256:384], eng='act')
            yield
            ps2 = pl.bank()
            P.mm(ps2[:, 0:128], pL[1][:, ts_], dkk[:]); P.mm(ps2[:, 128:256], pL[1][:, ts_], dka[:]); P.mm(ps2[:, 256:384], pL[0][:, ts_], drk[:])
            yield
            P.copy(tok["kkp"][:], ps2[:, 0:128], eng='dve'); P.copy(tok["kka"][:], ps2[:, 128:256], eng='act'); P.copy(tok["rrk"][:], ps2[:, 256:384], eng='dve')
            yield
            ps3 = pl.bank()
            P.mm(ps3[:, 0:128], lh[0:64, ts_], wa[0:64, 0:128])
            yield
            P.mm(ps3[:, 128:256], pL[4][0:64, ts_], wa[0:64, 128:256])
            yield
            P.tt(tok["t0"][:], ps3[:, 0:128], w0[:, 0:128], ALU.add)
            yield
            P.tt(tok["t1"][:], ps3[:, 128:256], w0[:, 128:256], ALU.add)
            yield
            P.actf(tok["logw"][:], tok["t0"][:], AF.Sigmoid)
            yield
            P.ts(tok["logw"][:], tok["logw"][:], -float(np.exp(-0.5)), None, ALU.mult)
            yield
            P.actf(tok["a"][:], tok["t1"][:], AF.Sigmoid)
            yield
            P.actf(tok["t0"][:], tok["kkp"][:], AF.Square)
            yield
            P.op('dve', lambda e, o=small[:, 0:2], i=tok["t0"][:].rearrange("p (h k) -> p h k", h=2): e.reduce_sum(o, i, AX.X), [tok["t0"][:]], [small[:, 0:2]])
            yield
            P.actf(small[:, 2:4], small[:, 0:2], AF.Sqrt)
            yield
            P.ts(small[:, 2:4], small[:, 2:4], 1e-12, None, ALU.max)
            yield
            P.op('dve', lambda e, o=small[:, 4:6], i=small[:, 2:4]: e.reciprocal(o, i), [small[:, 2:4]], [small[:, 4:6]])
            yield
            for h in range(2):
                hs = slice(h * 64, (h + 1) * 64)
                P.ts(tok["kk"][:, hs], tok["kkp"][:, hs], small[:, 4 + h:5 + h], None, ALU.mult)
            P.ts(tok["t0"][:], tok["a"][:], -1.0, None, ALU.add)
            yield
            P.tt(tok["t0"][:], tok["t0"][:], tok["kka"][:], ALU.mult)
            yield
            P.tt(tok["kmod"][:], tok["t0"][:], tok["k"][:], ALU.add)
            yield
            psL = pl.bank()
            P.mm(psL[:, 0:128], TriInc, tok["logw"][:]); P.mm(psL[:, 128:256], TriGt, tok["logw"][:])
            yield
            P.tt(tok["t1"][:], psL[:, 0:128], tok["logw"][:], ALU.subtract)
            yield
            P.actf(tok["e1"][:], tok["t1"][:], AF.Exp)
            yield
            P.actf(tok["e2"][:], psL[:, 0:128], AF.Exp, scale=-1.0)
            yield
            P.actf(tok["e3"][:], psL[:, 0:128], AF.Exp)
            yield
            P.actf(tok["e4"][:], psL[:, 128:256], AF.Exp)
            yield
            P.tt(tok["t0"][:], tok["kk"][:], tok["a"][:], ALU.mult)
            yield
            P.tt(tok["Bt"][:], tok["t0"][:], tok["e2"][:], ALU.mult)
            yield
            P.tt(tok["Kt"][:], tok["kmod"][:], tok["e2"][:], ALU.mult)
            yield
            P.tt(tok["Rt"][:], tok["r"][:], tok["e3"][:], ALU.mult)
            yield
            P.tt(tok["t1"][:], tok["kk"][:], tok["e1"][:], ALU.mult)
            yield
            for h in range(2):
                hs = slice(h * 64, (h + 1) * 64)
                P.ts(RH1[h][:, 0:64], tok["t1"][:, hs], -1.0, None, ALU.mult)
                P.tt(RH2[h][:, 128:192], tok["t0"][:, hs], tok["e4"][:, hs], ALU.mult)
                P.tt(AK[h][:, 128:192], tok["kmod"][:, hs], tok["e4"][:, hs], ALU.mult)
            P.tt(tok["t1"][:], tok["rrk"][:], tok["kmod"][:], ALU.mult)
            yield
            P.op('dve', lambda e, o=small[:, 6:8], i=tok["t1"][:].rearrange("p (h k) -> p h k", h=2): e.reduce_sum(o, i, AX.X), [tok["t1"][:]], [small[:, 6:8]])
            yield
            for h in range(2):
                hs = slice(h * 64, (h + 1) * 64)
                P.ts(tok["bv"][:, hs], tok["v"][:, hs], small[:, 6 + h:7 + h], None, ALU.mult)
        for _ in prep_gen(0):
            pass
        for sub in range(4):
            tok = toks[sub % 2]; small = smalls[sub % 2]; RH1 = RH1s[sub % 2]; RH2 = RH2s[sub % 2]; AK = AKs[sub % 2]
            ts_ = slice(sub * 128, (sub + 1) * 128)
            ck(2)
            def head_gen(h):
                hs = slice(h * 64, (h + 1) * 64)
                tp = pl.bank()
                P.tr(tp[0:64, 0:128], RH1[h][:, 0:64], ident)
                P.tr(tp[0:64, 128:256], tok["Rt"][:, hs], ident)
                P.tr(tp[0:64, 256:384], tok["Bt"][:, hs], ident)
                P.tr(tp[0:64, 384:512], tok["Kt"][:, hs], ident)
                P.copy(CM[h][:, :], tp[0:64, :], eng='act')
                yield
                Ac, Rc, Bc, Kc = [CM[h][:, i * 128:(i + 1) * 128] for i in range(4)]
                m1 = pl.bank(); m2 = pl.bank(); m3 = pl.bank()
                P.mm(m1[:, 0:256], Bc, CM[h][:, 0:256])
                P.mm(m2[:, 0:128], Kc, Rc)
                P.mm(m3[:, 0:256], Ac, CM[h][:, 256:512])
                X0 = XX[h][0]
                P.tt(X0[:, 0:128], m1[:, 0:128], MSK1[:, 0:128], ALU.mult)
                P.tt(RH2[h][:, 0:128], m1[:, 128:256], Mincl, ALU.mult)
                P.tt(AK[h][:, 0:128], m2[:, 0:128], Mincl, ALU.mult)
                P.tt(X0[:, 128:256], m3[:, 0:128], MSK3[:, 0:128], ALU.mult)
                P.tt(RH1[h][:, 64:192], m3[:, 128:256], MSK3[:, 128:256], ALU.mult)
                yield
                ck(3)
                Tc = TT[h][0]
                P.tt(Tc[:, :], X0[:, 0:128], ident, ALU.add)
                cur = 0
                for lvl in range(LV):
                    Xc = XX[h][cur]; Xn = XX[h][1 - cur]
                    pp = pl.bank()
                    if lvl < LV - 1:
                        P.mm(pp[:, 0:128], Xc[:, 128:256], Xc[:, 0:128])
                    P.mm(pp[:, 128:256], Xc[:, 0:128], Xc[:, 128:256])
                    if lvl < LV - 1:
                        P.copy(Xn[:, :], pp[:, 0:256], eng='act')
                    else:
                        P.copy(Xn[:, 128:256], pp[:, 128:256], eng='act')
                    yield
                    pt_ = pl.bank()
                    P.mm(pt_[:, 0:128], Xn[:, 128:256], TT[h][lvl % 2][:, :])
                    P.tt(TT[h][(lvl + 1) % 2][:, :], TT[h][lvl % 2][:, :], pt_[:, 0:128], ALU.add)
                    yield
                    cur = 1 - cur
                ck(4)
                Tf = TT[h][LV % 2]
                q = pl.bank()
                P.mm(q[:, 0:192], Tf[:, :], RH1[h][:, :])
                P.copy(QS[h][:, :], q[:, 0:192], eng='act')
                yield
                rp = pl.bank()
                P.mm(rp[0:64, 0:128], QS[h][:, 0:64], RH2[h][:, 0:128])
                for c in range(NCH):
                    cs = slice(c * CH, (c + 1) * CH)
                    P.tt(RpZ[h][0:64, c, cs], rp[0:64, cs], Rc[:, cs], ALU.add)
                wk = pl.bank()
                P.mm(wk[:, 0:192], QS[h][:, 64:192], RH2[h][:, :])
                P.tt(WYKP[h][:, :], wk[:, 0:192], AK[h][:, :], ALU.add)
                yield
                gg = pl.bank()
                for c in range(NCH):
                    P.ts(msk[h][:, c, :], QS[h][:, 0:64], CInd[:, c:c + 1], None, ALU.mult)
                    P.ts(msk[h][:, NCH + c, :], WYKP[h][:, 128:192], CInd[:, c:c + 1], None, ALU.mult)
                    P.mm(gg[0:64, c * 64:(c + 1) * 64], msk[h][:, c, :], RH2[h][:, 128:192])
                P.copy(GS[h][0:64, 0:NCH * 64], gg[0:64, 0:NCH * 64], eng='act')
                lc = pl.bank()
                P.mm(lc[0:64, 0:64], tok["logw"][:, hs], CInd)
                P.actf(lamC[h][:, 0:NCH], lc[0:64, 0:NCH], AF.Exp)
                yield
                ck(5)
                for c in range(NCH):
                    cs = slice(c * CH, (c + 1) * CH)
                    sn = pl.bank()
                    STc = STS[h][:, c * 64:(c + 1) * 64]
                    P.mm(sn[0:64, 0:64], GS[h][:, c * 64:(c + 1) * 64], STc, start=True, stop=False)
                    P.mm(sn[0:64, 0:64], msk[h][:, NCH + c, :], tok["v"][:, hs], start=False, stop=True)
                    P.stt(STS[h][0:64, (c + 1) * 64:(c + 2) * 64], STS[h][0:64, c * 64:(c + 1) * 64], lamC[h][:, c:c + 1], sn[0:64, 0:64], ALU.mult, ALU.add)
                    yield
                ck(6)
                yp = pl.bank()
                P.mm(yp[:, 0:64], WYKP[h][:, 0:128], tok["v"][:, hs], start=True, stop=False)
                for c in range(NCH):
                    P.mm(yp[:, 0:64], RpZ[h][:, c, :], STS[h][:, c * 64:(c + 1) * 64], start=False, stop=(c == NCH - 1))
                P.copy(STS[h][0:64, 0:64], STS[h][0:64, NCH * 64:(NCH + 1) * 64], eng='dve')
                P.copy(tok["t0"][:, hs], yp[:, 0:64], eng='act')
                P.op('dve', lambda e, o=bnst[:, h, :], i=tok["t0"][:, hs]: e.bn_stats(o, i), [tok["t0"][:, hs]], [bnst[:, h, :]])
                P.op('dve', lambda e, o=bnag[:, h, :], i=bnst[:, h, :]: e.bn_aggr(o, i), [bnst[:, h, :]], [bnag[:, h, :]])
                P.ts(small[:, 8 + h:9 + h], bnag[:, h, 1:2], GN_EPS, None, ALU.add)
                P.actf(small[:, 8 + h:9 + h], small[:, 8 + h:9 + h], AF.Sqrt)
                P.op('dve', lambda e, o=small[:, 10 + h:11 + h], i=small[:, 8 + h:9 + h]: e.reciprocal(o, i), [small[:, 8 + h:9 + h]], [small[:, 10 + h:11 + h]])
                P.ts(tok["yn"][:, hs], tok["t0"][:, hs], bnag[:, h, 0:1], small[:, 10 + h:11 + h], ALU.subtract, ALU.mult)
            gens = [head_gen(0), head_gen(1)] + ([prep_gen(sub + 1)] if sub < 3 else [])
            while gens:
                for g_ in list(gens):
                    try:
                        next(g_)
                    except StopIteration:
                        gens.remove(g_)
            ck(7)
            po = pl.bank()
            P.tr(po[:, 0:128], tok["yn"][:], ident)
            P.tr(po[:, 128:256], tok["bv"][:], ident)
            P.actf(ys[:, ts_], po[:, 0:128], AF.Identity, bias=pc[:, 8:9], scale=pc[:, 7:8])
            P.tt(ys[:, ts_], ys[:, ts_], po[:, 128:256], ALU.add)
        P.dma(yT[:, blk * 512:(blk + 1) * 512], ys[:, :], q='pool')
    P.emit()
    return nc


def rwkv_inputs(d, ntok=SEQ):
    l = 0
    W = d['w_in'][l]
    cm = rwkv_consts()
    maps = []
    for c in range(8):
        b, j = c // 4, c % 4
        cols = np.concatenate([np.arange(128 * j, 128 * j + 128), 512 + np.arange(128 * j, 128 * j + 128),
                               1024 + np.arange(128 * j, 128 * j + 128), np.arange(1536, 1664)])
        hc = np.arange(128 * j, 128 * j + 128)
        pcol = np.zeros((128, 12), np.float32)
        pcol[:, 0:4] = d['rwkv_mu'][l][cols].reshape(4, 128).T
        pcol[:, 4] = d['rwkv_k_k'][l][hc]; pcol[:, 5] = d['rwkv_k_a'][l][hc]
        pcol[:, 6] = d['rwkv_r_k'][l].reshape(-1)[hc]; pcol[:, 7] = d['rwkv_gn_g'][l][hc]; pcol[:, 8] = d['rwkv_gn_b'][l][hc]
        pcol[0:64, 9] = d['rwkv_mu'][l][1536:1600]; pcol[0:64, 10] = d['rwkv_mu'][l][1600:1664]
        w2a2 = np.concatenate([d['rwkv_w2'][l][:, hc], d['rwkv_a2'][l][:, hc]], axis=1)
        w0a0 = np.tile(np.concatenate([d['rwkv_w0'][l][hc], d['rwkv_a0'][l][hc]])[None, :], (128, 1))
        maps.append(dict(xT=np.ascontiguousarray(d['x'][b, :ntok].T), w_r=np.ascontiguousarray(W[:, cols]), cmat=cm,
                         pcol=pcol, w2a2=np.ascontiguousarray(w2a2, dtype=np.float32), w0a0=np.ascontiguousarray(w0a0, dtype=np.float32)))
    return maps


def run_rwkv(d):
    if 'rwkv' not in _NC_CACHE:
        _NC_CACHE['rwkv'] = build_rwkv()
    res = run_bass_kernel_spmd(_NC_CACHE['rwkv'], rwkv_inputs(d), core_ids=list(range(8)))
    ya = np.zeros((2, SEQ, 512), np.float32)
    for c in range(8):
        b, j = c // 4, c % 4
        ya[b, :, 128 * j:128 * j + 128] = res.results[c]['yT'].T
    return ya


ROPE_THETA = 500000.0
NEGM = 30000.0


def nsa_consts(ntok):
    pos = np.arange(ntok, dtype=np.float32)
    inv = ROPE_THETA ** (-np.arange(8, dtype=np.float32) * 2.0 / 16.0)
    ang = pos[None, :] * inv[:, None]
    cosT = np.ones((64, ntok), np.float32); sinT = np.zeros((64, ntok), np.float32)
    cosT[0:8] = np.cos(ang); cosT[8:16] = np.cos(ang)
    sinT[0:8] = -np.sin(ang); sinT[8:16] = np.sin(ang)
    p = np.arange(128)
    tri_le = (p[:, None] <= p[None, :]).astype(np.float32)
    tri_gt = (p[:, None] > p[None, :]).astype(np.float32)
    cmk = np.ones((128, 32, 128), np.float32)
    for i in range(16):
        p0 = 8 * i
        u = p[:, None] - p0
        cmk[:, 16 + i, :] = (p[None, :] >= 16 * u + 31).astype(np.float32)
        if p0 == 0:
            cmk[127, i, :] = (p >= 15).astype(np.float32)
    ncmp_pad = 512
    c = np.arange(ncmp_pad); j = np.arange(128)
    ov = ((16 * c[:, None] <= 64 * j[None, :] + 63) & (16 * c[:, None] + 31 >= 64 * j[None, :])).astype(np.float32)
    ov[(ntok - 32) // 16 + 1:] = 0.0
    ovc = ov.reshape(4, 128, 128).transpose(1, 0, 2)
    jj = np.arange(128); key = np.arange(128)
    xall = np.zeros((128, 64, 128), np.float32)
    for kb in range(64):
        xall[:, kb, :] = (jj[:, None] == 2 * kb + key[None, :] // 64)
    qcol = np.zeros((128, 4), np.float32)
    qcol[:, 0] = (p >= 64) * 1e30 + (p < 64) * -1e30
    qcol[:, 1] = (p >= 64).astype(np.float32)
    qcol[:, 2] = (p < 64) * 1e30
    return dict(cosT=cosT, sinT=sinT, tri=np.concatenate([tri_le, tri_gt, np.eye(128, dtype=np.float32)], axis=1),
                cmk=cmk.reshape(128, 32 * 128), ovc=np.ascontiguousarray(ovc).reshape(128, 512),
                xall=xall.reshape(128, 64 * 128), qcol=qcol)


def build_nsa(ntok=SEQ, stop=99):
    nc = bass.Bass("TRN2", target_bir_lowering=False)
    try:
        return _build_nsa(nc, ntok, stop)
    except _Stop:
        return nc


def _build_nsa(nc, ntok, stop):
    def ck(code):
        if stop == code:
            P.emit()
            raise _Stop()
    NB = ntok // 512; NT128 = ntok // 128; NQ = NT128 // 2
    NCMP = (ntok - 32) // 16 + 1
    dr = lambda n, s, k="ExternalInput": nc.dram_tensor(n, s, F32, kind=k).ap()
    xT = dr("xT", [D, ntok])
    wA = dr("wA", [D, 15 * 64])
    wB = dr("wB", [D, 140])
    cosT = dr("cosT", [64, ntok]); sinT = dr("sinT", [64, ntok])
    tri = dr("tri", [128, 384]); cmk = dr("cmk", [128, 2048]); ovc = dr("ovc", [128, 512]); xall = dr("xall", [128, 8192]); qcol = dr("qcol", [128, 10])
    dmask = dr("dmask", [128, 256]); wmask = dr("wmask", [128, 768])
    w1k = dr("w1k", [64, 32 * 256]); w1v = dr("w1v", [64, 32 * 256]); w2k = dr("w2k", [128, 2 * 64]); w2v = dr("w2v", [128, 2 * 64])
    pek = dr("pek", [64, 32 * 64]); pev = dr("pev", [64, 32 * 64])
    xq = dr("xq", [D, NQ * 128])
    cosq = dr("cosq", [64, NQ * 128]); sinq = dr("sinq", [64, NQ * 128])
    yb = dr("yb", [NQ * 128, 256], "ExternalOutput")
    chunked = lambda ap: ap.rearrange("(c p) n -> p c n", p=128)
    P = Prog(nc)
    pl = Pools(P, nc)
    sb = lambda n, s, dt=F32: nc.alloc_sbuf_tensor(n, s, dt)
    tri_t = sb("tri_t", [128, 384]); qc_t = sb("qc_t", [128, 10])
    P.dma(tri_t[:], tri); P.dma(qc_t[:], qcol)
    identf = tri_t[:, 256:384]
    xt32 = sb("xt32", [128, 8, 512]); xb = sb("xb", [128, 8, 512], BF16)
    pl.stg = [xt32[:, 0:4, :].rearrange("p a n -> p (a n)"), xt32[:, 4:8, :].rearrange("p a n -> p (a n)")]
    BUF2 = sb("BUF2", [128, 24576], BF16)
    cmk_b = sb("cmk_b", [128, 2048], BF16); dm_b = sb("dm_b", [128, 256], BF16); wm_b = sb("wm_b", [128, 768], BF16); ov_b = sb("ov_b", [128, 4, 128], BF16)
    xall_b = BUF2[:, 16384:24576].rearrange("p (a n) -> p a n", n=128)
    pl.load_cast(cmk_b[:], cmk); pl.load_cast(dm_b[:], dmask); pl.load_cast(wm_b[:], wmask); pl.load_cast(ov_b[:].rearrange("p a n -> p (a n)"), ovc, q='pool')
    wAb = sb("wAb", [128, 8, 960], BF16); wBb = sb("wBb", [128, 8, 140], BF16)
    pl.load_cast(wAb[:], chunked(wA)); pl.load_cast(wBb[:], chunked(wB), q='pool')
    KBUF = sb("KBUF", [64, 2, ntok], BF16)
    ksT = KBUF[:, 0, :]; kwT = KBUF[:, 1, :]; kcT = KBUF[:, 0, :]; vcT = KBUF[:, 1, :]
    vsA = sb("vsA", [128, NT128, 65], BF16); vwA = sb("vwA", [128, NT128, 65], BF16)
    qT = BUF2[0:64, 0:NQ * 512].rearrange("p (a n) -> p a n", n=512)
    GT = sb("GT", [128, NQ, 12])
    P.memset(vsA[:, :, 64:65], 1.0); P.memset(vwA[:, :, 64:65], 1.0)
    cs_t = [sb("cs_t%d" % i, [64, 1024]) for i in range(1)]
    rt = [sb("rt%d" % i, [64, 512]) for i in range(2)]
    xTc = chunked(xT); xqc = chunked(xq)

    def proj_block(src_c, cosd, sind, c0, n, groups, dests, tokmajor_tiles):
        P.dma(xt32[:, 0:4, 0:n], src_c[:, 0:4, c0:c0 + n]); P.dma(xt32[:, 4:8, 0:n], src_c[:, 4:8, c0:c0 + n], q='pool')
        for c in range(8):
            P.copy(xb[:, c, 0:n], xt32[:, c, 0:n], eng=('act', 'dve', 'pool')[c % 3])
        cst = cs_t[0]
        P.dma(cst[:, 0:n], cosd[:, c0:c0 + n]); P.dma(cst[:, 512:512 + n], sind[:, c0:c0 + n], q='pool')
        for (cb, pb), dst in zip(groups, dests):
            p1 = pl.bank()
            for kc in range(8):
                P.mm(p1[0:64, 0:n], wAb[:, kc, cb:cb + 64], xb[:, kc, 0:n], start=(kc == 0), stop=(kc == 7))
            if pb is None:
                P.copy(dst, p1[0:64, 0:n], eng='act')
                continue
            p2 = pl.bank()
            for kc in range(8):
                P.mm(p2[0:64, 0:n], wAb[:, kc, pb:pb + 64], xb[:, kc, 0:n], start=(kc == 0), stop=(kc == 7))
            P.tt(rt[0][:, 0:n], p1[0:64, 0:n], cst[:, 0:n], ALU.mult)
            P.tt(rt[1][:, 0:n], p2[0:64, 0:n], cst[:, 512:512 + n], ALU.mult)
            P.tt(dst, rt[0][:, 0:n], rt[1][:, 0:n], ALU.add, eng='pool')
        for (t128, off) in tokmajor_tiles:
            pv = pl.bank()
            for kc in range(8):
                P.mm(pv[:, 0:140], xb[:, kc, off:off + 128], wBb[:, kc, :], start=(kc == 0), stop=(kc == 7))
            yield (t128, pv)

    for blk in range(NB):
        groups = [(256 + 0, 448 + 256 + 0), (896, None)]
        sl = slice(blk * 512, (blk + 1) * 512)
        for _ in proj_block(xTc, cosT, sinT, blk * 512, 512, groups, [kcT[:, sl], vcT[:, sl]], []):
            pass
    ck(1)
    w1b = [BUF2[0:64, i * 8192:(i + 1) * 8192].rearrange("p (a n) -> p a n", n=256) for i in range(2)]
    w2b = [sb("w2b%d" % i, [128, 2, 64], BF16) for i in range(2)]
    peb = [BUF2[0:64, 16384 + i * 2048:16384 + (i + 1) * 2048].rearrange("p (a n) -> p a n", n=64) for i in range(2)]
    pl.load_cast(w1b[0], w1k.rearrange("p (a n) -> p a n", n=256)); pl.load_cast(w1b[1], w1v.rearrange("p (a n) -> p a n", n=256), q='pool')
    pl.load_cast(w2b[0][:].rearrange("p a n -> p (a n)"), w2k); pl.load_cast(w2b[1][:].rearrange("p a n -> p (a n)"), w2v, q='pool')
    pl.load_cast(peb[0], pek.rearrange("p (a n) -> p a n", n=64)); pl.load_cast(peb[1], pev.rearrange("p (a n) -> p a n", n=64), q='pool')
    KC = sb("KC", [64, 512], BF16); VCA = sb("VCA", [128, 4, 65], BF16)
    P.memset(KC[:], 0.0); P.memset(VCA[:, :, 0:64], 0.0); P.memset(VCA[:, :, 64:65], 1.0)
    GH = [[sb("GH%d_%d" % (i, hc), [128, 512], BF16) for hc in range(2)] for i in range(2)]
    gtmp = [sb("gtmp%d" % i, [128, 512]) for i in range(3)]
    bcol = sb("bcol", [128, 4])
    for which, srcT in ((0, kcT), (1, vcT)):
        for hc in range(2):
            pb_ = pl.bank()
            for l in range(32):
                P.mm(pb_[:, 0:64], w1b[which][:, l, hc * 128:(hc + 1) * 128], peb[which][:, l, :], start=(l == 0), stop=(l == 31))
            P.copy(bcol[:, which * 2 + hc:which * 2 + hc + 1], pb_[:, 0:1], eng='act')
            ph = pl.bank()
            for l in range(32):
                rhs = srcT[:, l:l + 16 * (NCMP - 1) + 1:16]
                P.mm(ph[:, 0:NCMP], w1b[which][:, l, hc * 128:(hc + 1) * 128], rhs, start=(l == 0), stop=(l == 31))
            x_ = gtmp[0][:, 0:NCMP]; u_ = gtmp[1][:, 0:NCMP]; s_ = gtmp[2][:, 0:NCMP]
            P.actf(x_, ph[:, 0:NCMP], AF.Identity, bias=bcol[:, which * 2 + hc:which * 2 + hc + 1])
            P.actf(u_, x_, AF.Square)
            P.ts(u_, u_, 0.044715, 1.0, ALU.mult, ALU.add)
            P.tt(u_, u_, x_, ALU.mult)
            P.actf(s_, u_, AF.Sigmoid, scale=2.0 * 0.7978845608028654)
            P.memset(GH[which][hc][:, NCMP:512], 0.0)
            P.tt(GH[which][hc][:, 0:NCMP], x_, s_, ALU.mult)
    pk = pl.bank()
    for hc in range(2):
        P.mm(pk[0:64, 0:NCMP], w2b[0][:, hc, :], GH[0][hc][:, 0:NCMP], start=(hc == 0), stop=(hc == 1))
    P.copy(KC[:, 0:NCMP], pk[0:64, 0:NCMP], eng='act')
    for cc in range((NCMP + 127) // 128):
        pv_ = pl.bank()
        for hc in range(2):
            P.mm(pv_[:, 0:64], GH[1][hc][:, cc * 128:(cc + 1) * 128], w2b[1][:, hc, :], start=(hc == 0), stop=(hc == 1))
        P.copy(VCA[:, cc, 0:64], pv_[:, 0:64], eng='dve')
    ck(2)
    for blk in range(NB):
        groups = [(256 + 64, 448 + 256 + 64), (256 + 128, 448 + 256 + 128)]
        sl = slice(blk * 512, (blk + 1) * 512)
        for (t128, pv) in proj_block(xTc, cosT, sinT, blk * 512, 512, groups, [ksT[:, sl], kwT[:, sl]], [(blk * 4 + i, i * 128) for i in range(4)]):
            P.copy(vsA[:, t128, 0:64], pv[:, 0:64], eng='act')
            P.copy(vwA[:, t128, 0:64], pv[:, 64:128], eng='dve')
    pl.load_cast(xall_b, xall.rearrange("p (a n) -> p a n", n=128))
    qtmp = [sb("qtmp%d" % r, [64, 512], BF16) for r in range(4)]
    for qblk in range(0, NQ, 4):
        n = min(4, NQ - qblk) * 128
        groups = [(r * 64, 448 + r * 64) for r in range(4)]
        dests = [qtmp[r][:, 0:n] for r in range(4)]
        for (t128, pv) in proj_block(xqc, cosq, sinq, qblk * 128, n, groups, dests, [(qblk + i, i * 128) for i in range(n // 128)]):
            P.actf(GT[:, t128, :], pv[:, 128:140], AF.Sigmoid)
        for i in range(n // 128):
            for r in range(4):
                P.copy(qT[:, qblk + i, r * 128:(r + 1) * 128], qtmp[r][:, i * 128:(i + 1) * 128], eng=('pool', 'dve')[r % 2])
    ck(3)
    CV = sb("CV", [128, 4, 193], BF16)
    for cc in range(4):
        P.copy(CV[:, cc, 0:65], VCA[:, cc, :], eng='dve'); P.copy(CV[:, cc, 65:193], ov_b[:, cc, :], eng='pool')
    pT = [sb("pT%d" % i, [128, 512], BF16) for i in range(3)]
    imp = sb("imp", [128, 128]); imp2 = sb("imp2", [128, 128]); m8 = sb("m8", [128, 16]); sel = sb("sel", [128, 128])
    selT = sb("selT", [128, 128], BF16); negm = sb("negm", [128, 512], BF16)
    oTs = [None] + [sb("oTs%d" % i, [65, 512]) for i in range(1, 3)]
    ytile = sb("ytile", [128, 256]); sm = sb("sm", [128, 32])
    cnum = [sb("cnum%d" % i, [128, 386]) for i in range(2)]
    identb = sb("identb", [128, 128], BF16)
    P.copy(identb[:], identf)
    pi = 0
    for i in range(NQ):
        qrhs = qT[:, i, :]
        _attend(P, pl, nc, i, qrhs, dict(KC=KC, CV=CV, cmk_b=cmk_b, pT=pT, imp=imp, imp2=imp2, m8=m8, sel=sel, selT=selT, negm=negm,
                                          oTs=oTs, ytile=ytile, sm=sm, cnum=cnum, identb=identb, identf=identf, xall_b=xall_b, dm_b=dm_b, wm_b=wm_b,
                                          ksT=ksT, kwT=kwT, vsA=vsA, vwA=vwA, GT=GT, qc_t=qc_t, yb=yb, NCMP=NCMP))
    P.emit()
    return nc


def _bc4(ap):
    return ap.unsqueeze(1).broadcast_to([ap.shape[0], 4, ap.shape[1]])


def _attend(P, pl, nc, i, qrhs, T):
    KC, CV, pT = T['KC'], T['CV'], T['pT']
    NCMP = T['NCMP']
    identf, identb = T['identf'], T['identb']
    oTs = T['oTs']
    v4 = lambda t: t[:, :].rearrange("p (r q) -> p r q", r=4)
    ccb = min((16 * i + 6) // 128, (NCMP - 1) // 128)
    cn = [pl.banks[0], pl.banks[1], pl.banks[2], pl.banks[3]]
    rot = lambda: pl.banks[4 + (pl.nrot() % 4)]
    for cc in range(ccb + 1):
        ps = rot()
        P.mm(ps[:, :], KC[:, cc * 128:(cc + 1) * 128], qrhs)
        pt = pT[(pl.bi) % 3]
        P.actf(pt[:, :], ps[:, :], AF.Exp, scale=0.125)
        if cc >= ccb - 1:
            which = 1 if cc == ccb else 0
            mk = T['cmk_b'][:, (which * 8 + (i % 8)) * 128:(which * 8 + (i % 8) + 1) * 128]
            P.tt(v4(pt), v4(pt), _bc4(mk), ALU.mult)
        for r in range(4):
            P.mm(cn[r][:, 0:193], pt[:, r * 128:(r + 1) * 128], CV[:, cc, :], start=(cc == 0), stop=(cc == ccb))
    cnum = T['cnum']
    for r in range(4):
        P.copy(cnum[r // 2][:, (r % 2) * 193:(r % 2) * 193 + 193], cn[r][:, 0:193], eng=('act', 'dve')[r % 2])
    sm = T['sm']
    for r in range(4):
        P.ts(sm[:, r:r + 1], cnum[r // 2][:, (r % 2) * 193 + 64:(r % 2) * 193 + 65], 1e-30, None, ALU.add)
    P.op('dve', lambda e: e.reciprocal(sm[:, 4:8], sm[:, 0:4]), [sm[:, 0:4]], [sm[:, 4:8]])
    imp, imp2, m8, sel, selT, negm = T['imp'], T['imp2'], T['m8'], T['sel'], T['selT'], T['negm']
    for r in range(4):
        src = cnum[r // 2][:, (r % 2) * 193 + 65:(r % 2) * 193 + 193]
        if r == 0:
            P.ts(imp[:, :], src, sm[:, 4:5], None, ALU.mult)
        else:
            P.stt(imp[:, :], src, sm[:, 4 + r:5 + r], imp[:, :], ALU.mult, ALU.add)
    qc = T['qc_t']
    w0 = max(0, 4 * i - 1); wn = 4 * i + 4 - w0; t0c = w0 - (4 * i - 1)
    P.copy(imp2[:, :], imp[:, :], eng='dve')
    P.tt(imp2[:, w0:w0 + wn], imp[:, w0:w0 + wn], qc[:, t0c:t0c + wn], ALU.mult)
    P.tt(imp2[:, w0:w0 + wn], imp2[:, w0:w0 + wn], qc[:, 5 + t0c:5 + t0c + wn], ALU.add)
    if 4 * i + 4 < 128:
        P.memset(imp2[:, 4 * i + 4:128], -1e30)
    P.memset(imp2[:, 0:1], 1e30)
    P.op('dve', lambda e: e.max(out=m8[:, 0:8], in_=imp2[:, :]), [imp2[:, :]], [m8[:, 0:8]])
    P.op('dve', lambda e: e.match_replace(out=imp[:, :], in_to_replace=m8[:, 0:8], in_values=imp2[:, :], imm_value=-3e38), [m8[:, 0:8], imp2[:, :]], [imp[:, :]])
    P.op('dve', lambda e: e.max(out=m8[:, 8:16], in_=imp[:, :]), [imp[:, :]], [m8[:, 8:16]])
    P.ts(sel[:, :], imp2[:, :], m8[:, 15:16], None, ALU.is_ge)
    pst = rot()
    P.tr(pst[:, 0:128], sel[:, :], identf)
    P.copy(selT[:, :], pst[:, 0:128], eng='act')
    for r in range(4):
        P.ts(negm[:, r * 128:(r + 1) * 128], selT[:, :], -1.0, NEGM, ALU.add, ALU.mult, eng=('dve', 'pool')[r % 2])
    ksT, vsA, xall_b, dm_b = T['ksT'], T['vsA'], T['xall_b'], T['dm_b']
    po = pl.banks[0]
    nkb = 2 * i + 2
    def qk_sel(kb):
        ps = rot()
        P.mm(ps[:, :], ksT[:, kb * 128:(kb + 1) * 128], qrhs, start=True, stop=False)
        P.mm(ps[:, :], xall_b[:, kb, :], negm[:, :], start=False, stop=True)
        return ps
    ps_next = qk_sel(0)
    for kb in range(nkb):
        ps = ps_next
        pt = pT[kb % 3]
        P.actf(pt[:, :], ps[:, :], AF.Exp, scale=0.125)
        if kb + 1 < nkb:
            ps_next = qk_sel(kb + 1)
        if kb >= nkb - 2:
            mk = dm_b[:, (kb - (nkb - 2)) * 128:(kb - (nkb - 2) + 1) * 128]
            P.tt(v4(pt), v4(pt), _bc4(mk), ALU.mult)
        P.mm(po[0:65, :], vsA[:, kb, :], pt[:, :], start=(kb == 0), stop=(kb == nkb - 1))
    P.copy(oTs[1][:, :], po[0:65, :], eng='act')
    kwT, vwA, wm_b = T['kwT'], T['vwA'], T['wm_b']
    pw = pl.banks[1]
    kbs = [(m, 2 * i - 4 + m) for m in range(6) if 2 * i - 4 + m >= 0]
    def qk_w(kb):
        ps = rot()
        P.mm(ps[:, :], kwT[:, kb * 128:(kb + 1) * 128], qrhs)
        return ps
    ps_next = qk_w(kbs[0][1])
    for n_, (m, kb) in enumerate(kbs):
        ps = ps_next
        pt = pT[n_ % 3]
        P.actf(pt[:, :], ps[:, :], AF.Exp, scale=0.125)
        if n_ + 1 < len(kbs):
            ps_next = qk_w(kbs[n_ + 1][1])
        P.tt(v4(pt), v4(pt), _bc4(wm_b[:, m * 128:(m + 1) * 128]), ALU.mult)
        P.mm(pw[0:65, :], vwA[:, kb, :], pt[:, :], start=(n_ == 0), stop=(n_ == len(kbs) - 1))
    P.copy(oTs[2][:, :], pw[0:65, :], eng='dve')
    GT, ytile = T['GT'], T['ytile']
    for r in range(4):
        P.tt(sm[:, 8 + r:9 + r], sm[:, 4 + r:5 + r], GT[:, i, r * 3:r * 3 + 1], ALU.mult)
        P.ts(ytile[:, r * 64:(r + 1) * 64], cnum[r // 2][:, (r % 2) * 193:(r % 2) * 193 + 64], sm[:, 8 + r:9 + r], None, ALU.mult)
    for br in (1, 2):
        pb_ = pl.banks[br + 1]
        for r in range(4):
            P.mm(pb_[:, r * 65:(r + 1) * 65], oTs[br][0:65, r * 128:(r + 1) * 128], identf[0:65, 0:65])
        for r in range(4):
            P.ts(sm[:, 12 + r:13 + r], pb_[:, r * 65 + 64:r * 65 + 65], 1e-30, None, ALU.add)
        P.op('dve', lambda e: e.reciprocal(sm[:, 16:20], sm[:, 12:16]), [sm[:, 12:16]], [sm[:, 16:20]])
        for r in range(4):
            P.tt(sm[:, 20 + r:21 + r], sm[:, 16 + r:17 + r], GT[:, i, r * 3 + br:r * 3 + br + 1], ALU.mult)
            P.stt(ytile[:, r * 64:(r + 1) * 64], pb_[:, r * 65:r * 65 + 64], sm[:, 20 + r:21 + r], ytile[:, r * 64:(r + 1) * 64], ALU.mult, ALU.add)
    P.dma(T['yb'][i * 128:(i + 1) * 128, :], ytile[:, :], q='pool')


def nsa_inputs(d, ntok=SEQ):
    l = 0
    W = d['w_in'][l][:, 1664:1664 + 1304]
    cst = nsa_consts(ntok)
    NQ = ntok // 256
    perm = np.arange(64); perm[0:8] = np.arange(8, 16); perm[8:16] = np.arange(0, 8)
    p = np.arange(128)
    tri_le = cst['tri'][:, 0:128]; tri_gt = cst['tri'][:, 128:256]
    one = np.ones((128, 128), np.float32); zero = np.zeros((128, 128), np.float32)
    w1k = d['nsa_ck_w1'][l].reshape(32, 64, 256).transpose(1, 0, 2).reshape(64, 32 * 256)
    w1v = d['nsa_cv_w1'][l].reshape(32, 64, 256).transpose(1, 0, 2).reshape(64, 32 * 256)
    w2k = d['nsa_ck_w2'][l].reshape(2, 128, 64).transpose(1, 0, 2).reshape(128, 128)
    w2v = d['nsa_cv_w2'][l].reshape(2, 128, 64).transpose(1, 0, 2).reshape(128, 128)
    pek = np.repeat(d['nsa_pe_k'][l].T[:, :, None], 64, axis=2).reshape(64, 32 * 64)
    pev = np.repeat(d['nsa_pe_v'][l].T[:, :, None], 64, axis=2).reshape(64, 32 * 64)
    maps = []
    for c in range(8):
        b, g, par = c // 4, (c % 4) // 2, c % 2
        qcols = [256 * g + r * 64 + np.arange(64) for r in range(4)]
        kvc = lambda idx: 512 + idx * 128 + g * 64 + np.arange(64)
        roped = qcols + [kvc(0), kvc(2), kvc(4)]
        colsA = np.concatenate(roped + [cg[perm] for cg in roped] + [kvc(1)])
        gcols = np.array([1280 + (4 * g + r) * 3 + cc for r in range(4) for cc in range(3)])
        colsB = np.concatenate([kvc(3), kvc(5), gcols])
        own = np.concatenate([np.arange((2 * i + par) * 128, (2 * i + par + 1) * 128) for i in range(NQ)])
        xTb = np.ascontiguousarray(d['x'][b, :ntok].T)
        cmk_full = cst['cmk'].reshape(128, 32, 128)
        cmkc = np.stack([cmk_full[:, wh * 16 + (2 * m + par) % 16, :] for wh in range(2) for m in range(8)], axis=1)
        if par == 0:
            dmask = np.concatenate([tri_le, zero], axis=1); wmask = np.concatenate([tri_gt, one, one, one, tri_le, zero], axis=1)
        else:
            dmask = np.concatenate([one, tri_le], axis=1); wmask = np.concatenate([zero, tri_gt, one, one, one, tri_le], axis=1)
        qcol = np.zeros((128, 10), np.float32)
        hi = (p >= 64).astype(np.float32); lo = 1.0 - hi
        for w in range(5):
            dl = w - 1 - 2 * par
            if dl < -1:
                qcol[:, w] = 1.0
            elif dl == -1:
                qcol[:, w] = hi; qcol[:, 5 + w] = lo * 1e30
            elif dl == 0:
                qcol[:, 5 + w] = 1e30
            elif dl == 1:
                qcol[:, 5 + w] = hi * 1e30 - lo * 1e30
            else:
                qcol[:, 5 + w] = -1e30
        m = dict(xT=xTb, wA=W[:, colsA], wB=W[:, colsB], cosT=cst['cosT'], sinT=cst['sinT'], tri=cst['tri'], cmk=cmkc.reshape(128, 2048),
                 ovc=cst['ovc'], xall=cst['xall'], qcol=qcol, dmask=dmask, wmask=wmask, w1k=w1k, w1v=w1v, w2k=w2k, w2v=w2v, pek=pek, pev=pev,
                 xq=xTb[:, own], cosq=cst['cosT'][:, own], sinq=cst['sinT'][:, own])
        maps.append({k: np.ascontiguousarray(v, dtype=np.float32) for k, v in m.items()})
    return maps


def run_nsa(d):
    if 'nsa' not in _NC_CACHE:
        _NC_CACHE['nsa'] = build_nsa()
    res = run_bass_kernel_spmd(_NC_CACHE['nsa'], nsa_inputs(d), core_ids=list(range(8)))
    yb = np.zeros((2, SEQ, 512), np.float32)
    for c in range(8):
        b, g, par = c // 4, (c % 4) // 2, c % 2
        o = res.results[c]['yb'].reshape(SEQ // 256, 128, 256)
        yb[b].reshape(SEQ // 256, 2, 128, 512)[:, par, :, g * 256:(g + 1) * 256] = o
    return yb


def kernel(**inputs):
    d = {k: np.asarray(v) for k, v in inputs.items()}
    ya = run_rwkv(d)
    yb = run_nsa(d)
    out = run_tail(d, ya, yb)
    return out.astype(np.float32)
```
